# Optimizing a Trainium2 kernel written in Bass

```python
import jax, jax.numpy as jnp
from jax import lax
import numpy as np

D_MODEL = 1024
BATCH = 4
SEQ = 4096
DEPTH = 1
DEC_BATCH = 32
DEC_SEQ = 32
PAST_LEN = 2048

CHUNK = 64
GDN_HEADS = 4
GDN_DK = 128
GDN_DV = 128
GDN_CONV = 4
ATT_HEADS = 8
ATT_DH = 64
BAND_CHUNKS = 8
WINDOW = BAND_CHUNKS * CHUNK
MAX_REL = 128
D_FF = 2816
FFN_CONV = 3
EPS = 1e-6

GDN_QKV = GDN_HEADS * (2 * GDN_DK + GDN_DV)
GDN_Z = GDN_HEADS * GDN_DV
ATT_W = ATT_HEADS * ATT_DH
MIX_WIDTH = GDN_HEADS * GDN_DV + ATT_W
IN_COLS = GDN_QKV + GDN_Z + 2 * GDN_HEADS + 3 * ATT_W

kernel_name = 'hybrid_gdn_bandattn_convffn_step'


def rmsnorm(x, w):
    xf = x.astype(jnp.float32)
    y = xf * lax.rsqrt(jnp.mean(xf * xf, axis=-1, keepdims=True) + EPS)
    return (y * w.astype(jnp.float32)).astype(x.dtype)


def l2norm(x):
    xf = x.astype(jnp.float32)
    return xf * lax.rsqrt(jnp.sum(xf * xf, axis=-1, keepdims=True) + EPS)


def causal_dwconv(x, prev, w):
    width = w.shape[0]
    t = x.shape[1]
    xp = jnp.concatenate([prev.astype(x.dtype), x], axis=1)
    out = sum(xp[:, i:i + t] * w[i].astype(x.dtype) for i in range(width))
    return out, xp[:, xp.shape[1] - (width - 1):]


def gated_delta_chunked(q, k, v, g, beta, s0, chunk):
    B, T, H, DK = q.shape
    DV = v.shape[-1]
    N = T // chunk
    f32 = jnp.float32

    def blocks(a):
        a = a.astype(f32).reshape((B, N, chunk) + a.shape[2:])
        return jnp.moveaxis(a, (1, 3), (0, 2))

    q = blocks(q) * (DK ** -0.5)
    k = blocks(k)
    v = blocks(v)
    g = blocks(g)
    beta = blocks(beta)
    G = jnp.cumsum(g, axis=-1)
    incl = jnp.tril(jnp.ones((chunk, chunk), bool))
    strict = jnp.tril(jnp.ones((chunk, chunk), bool), k=-1)
    diff = G[..., :, None] - G[..., None, :]
    gam = jnp.where(incl, jnp.exp(jnp.where(incl, diff, 0.0)), 0.0)
    kb = k * beta[..., None]
    a_mat = jnp.where(strict, jnp.einsum('nbhid,nbhjd->nbhij', kb, k) * gam, 0.0)
    eye = jnp.eye(chunk, dtype=f32)
    rhs = jnp.concatenate([v * beta[..., None], kb * jnp.exp(G)[..., None]], axis=-1)
    sol = lax.linalg.triangular_solve(eye + a_mat, rhs, left_side=True, lower=True, unit_diagonal=True)
    u_blk, w_blk = sol[..., :DV], sol[..., DV:]
    qk = jnp.where(incl, jnp.einsum('nbhid,nbhjd->nbhij', q, k) * gam, 0.0)
    qg = q * jnp.exp(G)[..., None]
    kg = k * jnp.exp(G[..., -1:] - G)[..., None]
    decay_last = jnp.exp(G[..., -1])

    def step(S, xs):
        qg_c, kg_c, u_c, w_c, qk_c, dl = xs
        v_new = u_c - jnp.einsum('bhck,bhkv->bhcv', w_c, S)
        o = jnp.einsum('bhck,bhkv->bhcv', qg_c, S) + jnp.einsum('bhij,bhjv->bhiv', qk_c, v_new)
        S = S * dl[..., None, None] + jnp.einsum('bhck,bhcv->bhkv', kg_c, v_new)
        return S, o

    S, o = lax.scan(step, s0.astype(f32), (qg, kg, u_blk, w_blk, qk, decay_last))
    o = jnp.moveaxis(o, (0, 2), (1, 3)).reshape(B, T, H, DV)
    return o, S


def gated_deltanet(qkv, z, b_raw, a_raw, conv_prev, s0, conv_w, a_log, dt_bias, norm_w, chunk):
    B, T, _ = qkv.shape
    qkv, conv_state = causal_dwconv(qkv, conv_prev, conv_w)
    qkv = jax.nn.silu(qkv)
    q, k, v = jnp.split(qkv, [GDN_HEADS * GDN_DK, 2 * GDN_HEADS * GDN_DK], axis=-1)
    q = l2norm(q.reshape(B, T, GDN_HEADS, GDN_DK))
    k = l2norm(k.reshape(B, T, GDN_HEADS, GDN_DK))
    v = v.reshape(B, T, GDN_HEADS, GDN_DV)
    beta = jax.nn.sigmoid(b_raw.astype(jnp.float32))
    g = -jnp.exp(a_log.astype(jnp.float32)) * jax.nn.softplus(a_raw.astype(jnp.float32) + dt_bias.astype(jnp.float32))
    o, S = gated_delta_chunked(q, k, v, g, beta, s0, chunk)
    o = rmsnorm(o.astype(qkv.dtype), norm_w) * jax.nn.silu(z.reshape(B, T, GDN_HEADS, GDN_DV))
    return o.reshape(B, T, GDN_Z), S.astype(s0.dtype), conv_state


def rel_bias_lookup(table, rel):
    idx = jnp.clip(rel, -MAX_REL, MAX_REL) + MAX_REL
    return table[:, idx].astype(jnp.float32)


def band_attention_prompt(q, k, v, table):
    B, T, H, DH = q.shape
    N = T // CHUNK
    nb = BAND_CHUNKS + 1
    qc = q.reshape(B, N, CHUNK, H, DH)
    pad = jnp.zeros((B, WINDOW, H, DH), k.dtype)
    kc = jnp.concatenate([pad, k], axis=1).reshape(B, N + BAND_CHUNKS, CHUNK, H, DH)
    vc = jnp.concatenate([pad, v], axis=1).reshape(B, N + BAND_CHUNKS, CHUNK, H, DH)
    kband = jnp.concatenate([kc[:, m:m + N] for m in range(nb)], axis=2)
    vband = jnp.concatenate([vc[:, m:m + N] for m in range(nb)], axis=2)
    s = jnp.einsum('bnqhd,bnkhd->bnhqk', qc, kband).astype(jnp.float32) * (DH ** -0.5)
    rel = WINDOW + jnp.arange(CHUNK)[:, None] - jnp.arange(nb * CHUNK)[None, :]
    s = s + rel_bias_lookup(table, rel)[None, None]
    key_pos = (jnp.arange(N)[:, None] - BAND_CHUNKS) * CHUNK + jnp.arange(nb * CHUNK)[None, :]
    s = jnp.where((key_pos >= 0)[None, :, None, None, :], s, -jnp.inf)
    p = jax.nn.softmax(s, axis=-1).astype(v.dtype)
    o = jnp.einsum('bnhqk,bnkhd->bnqhd', p, vband)
    return o.reshape(B, T, H, DH)


def band_attention_sample(q, k_new, v_new, k_cache, v_cache, table):
    T = q.shape[1]
    lc = k_cache.shape[1]
    kk = jnp.concatenate([k_cache.astype(k_new.dtype), k_new], axis=1)
    vv = jnp.concatenate([v_cache.astype(v_new.dtype), v_new], axis=1)
    s = jnp.einsum('bqhd,bkhd->bhqk', q, kk).astype(jnp.float32) * (ATT_DH ** -0.5)
    rel = lc + jnp.arange(T)[:, None] - jnp.arange(lc + T)[None, :]
    s = s + rel_bias_lookup(table, rel)[None]
    p = jax.nn.softmax(s, axis=-1).astype(vv.dtype)
    return jnp.einsum('bhqk,bkhd->bqhd', p, vv)


def hybrid_layer(x, band_cache, s0, qkv_prev, ffn_prev, gdn_chunk, lw):
    (norm_mix_pre, w_in, qkv_conv_w, a_log, dt_bias, gdn_norm_w, rel_bias,
     attn_norm_w, w_out, norm_mix_post, norm_ffn_pre, w_gate_up, ffn_conv_w,
     ffn_conv_b, w_down, norm_ffn_post) = lw
    B, T, _ = x.shape
    u = rmsnorm(x, norm_mix_pre)
    p = u @ w_in
    c0 = GDN_QKV
    c1 = c0 + GDN_Z
    c2 = c1 + GDN_HEADS
    c3 = c2 + GDN_HEADS
    c4 = c3 + ATT_W
    c5 = c4 + ATT_W
    qkv_a, z_a, b_a, a_a, q_b, k_b, v_b = jnp.split(p, [c0, c1, c2, c3, c4, c5], axis=-1)
    o_a, s_new, qkv_state = gated_deltanet(qkv_a, z_a, b_a, a_a, qkv_prev, s0, qkv_conv_w,
                                           a_log, dt_bias, gdn_norm_w, gdn_chunk)
    q_b = q_b.reshape(B, T, ATT_HEADS, ATT_DH)
    k_b = k_b.reshape(B, T, ATT_HEADS, ATT_DH)
    v_b = v_b.reshape(B, T, ATT_HEADS, ATT_DH)
    if band_cache is None:
        o_b = band_attention_prompt(q_b, k_b, v_b, rel_bias)
        keep = min(WINDOW, T)
        k_rows, v_rows = k_b[:, T - keep:], v_b[:, T - keep:]
    else:
        o_b = band_attention_sample(q_b, k_b, v_b, band_cache[0], band_cache[1], rel_bias)
        k_rows, v_rows = k_b, v_b
    o_b = rmsnorm(o_b, attn_norm_w).reshape(B, T, ATT_W)
    mix = jnp.concatenate([o_a, o_b], axis=-1) @ w_out
    x = x + rmsnorm(mix, norm_mix_post)
    u = rmsnorm(x, norm_ffn_pre)
    gate, up = jnp.split(u @ w_gate_up, 2, axis=-1)
    gate, ffn_state = causal_dwconv(gate, ffn_prev, ffn_conv_w)
    h = jax.nn.gelu(gate + ffn_conv_b.astype(gate.dtype), approximate=True) * up
    x = x + rmsnorm(h @ w_down, norm_ffn_post)
    return x, k_rows, v_rows, s_new, qkv_state, ffn_state


def setup_inputs(seed: int = 0) -> dict:
    key = jax.random.key(seed)
    ks = jax.random.split(key, 32)
    f32 = jnp.float32
    lc = min(WINDOW, PAST_LEN)

    def nrm(k, shape, scale):
        return jax.random.normal(k, shape, f32) * scale

    def gain(k, n):
        return 1.0 + 0.01 * jax.random.normal(k, (DEPTH, n), f32)

    dt = jnp.exp(jax.random.uniform(ks[9], (DEPTH, GDN_HEADS), f32, np.log(0.001), np.log(0.1)))
    return {
        'x_prompt': nrm(ks[0], (BATCH, SEQ, D_MODEL), 1.0),
        'x_sample': nrm(ks[1], (DEC_BATCH, DEC_SEQ, D_MODEL), 1.0),
        'cache_band_k': nrm(ks[2], (DEPTH, DEC_BATCH, lc, ATT_HEADS, ATT_DH), 1.0),
        'cache_band_v': nrm(ks[3], (DEPTH, DEC_BATCH, lc, ATT_HEADS, ATT_DH), 1.0),
        'state_delta': nrm(ks[4], (DEPTH, DEC_BATCH, GDN_HEADS, GDN_DK, GDN_DV), 0.1),
        'state_qkv_conv': nrm(ks[5], (DEPTH, DEC_BATCH, GDN_CONV - 1, GDN_QKV), 1.0),
        'state_ffn_conv': nrm(ks[6], (DEPTH, DEC_BATCH, FFN_CONV - 1, D_FF), 1.0),
        'norm_mix_pre': gain(ks[7], D_MODEL),
        'w_in': nrm(ks[8], (DEPTH, D_MODEL, IN_COLS), D_MODEL ** -0.5),
        'qkv_conv_w': nrm(ks[10], (DEPTH, GDN_CONV, GDN_QKV), GDN_CONV ** -0.5),
        'a_log': jnp.log(jax.random.uniform(ks[11], (DEPTH, GDN_HEADS), f32, 1.0, 16.0)),
        'dt_bias': dt + jnp.log(-jnp.expm1(-dt)),
        'gdn_norm_w': gain(ks[12], GDN_DV),
        'rel_bias': nrm(ks[13], (DEPTH, ATT_HEADS, 2 * MAX_REL + 1), 0.1),
        'attn_norm_w': gain(ks[14], ATT_DH),
        'w_out': nrm(ks[15], (DEPTH, MIX_WIDTH, D_MODEL), MIX_WIDTH ** -0.5),
        'norm_mix_post': gain(ks[16], D_MODEL),
        'norm_ffn_pre': gain(ks[17], D_MODEL),
        'w_gate_up': nrm(ks[18], (DEPTH, D_MODEL, 2 * D_FF), D_MODEL ** -0.5),
        'ffn_conv_w': nrm(ks[19], (DEPTH, FFN_CONV, D_FF), FFN_CONV ** -0.5),
        'ffn_conv_b': nrm(ks[20], (DEPTH, D_FF), 0.01),
        'w_down': nrm(ks[21], (DEPTH, D_FF, D_MODEL), D_FF ** -0.5),
        'norm_ffn_post': gain(ks[22], D_MODEL),
    }


def _stack(outs, i):
    return jnp.stack([o[i] for o in outs], axis=0)


def reference(x_prompt, x_sample, cache_band_k, cache_band_v, state_delta, state_qkv_conv,
              state_ffn_conv, norm_mix_pre, w_in, qkv_conv_w, a_log, dt_bias, gdn_norm_w,
              rel_bias, attn_norm_w, w_out, norm_mix_post, norm_ffn_pre, w_gate_up,
              ffn_conv_w, ffn_conv_b, w_down, norm_ffn_post):
    weights = (norm_mix_pre, w_in, qkv_conv_w, a_log, dt_bias, gdn_norm_w, rel_bias,
               attn_norm_w, w_out, norm_mix_post, norm_ffn_pre, w_gate_up, ffn_conv_w,
               ffn_conv_b, w_down, norm_ffn_post)
    xp, xs = x_prompt, x_sample
    bp = xp.shape[0]
    outs_p, outs_s = [], []
    for l in range(DEPTH):
        lw = tuple(w[l] for w in weights)
        s0_p = jnp.zeros((bp, GDN_HEADS, GDN_DK, GDN_DV), xp.dtype)
        qkv0_p = jnp.zeros((bp, GDN_CONV - 1, GDN_QKV), xp.dtype)
        ffn0_p = jnp.zeros((bp, FFN_CONV - 1, D_FF), xp.dtype)
        xp, kp, vp, sp, cqp, cfp = hybrid_layer(xp, None, s0_p, qkv0_p, ffn0_p, CHUNK, lw)
        xs, ksm, vsm, ssm, cqs, cfs = hybrid_layer(
            xs, (cache_band_k[l], cache_band_v[l]), state_delta[l], state_qkv_conv[l],
            state_ffn_conv[l], xs.shape[1], lw)
        outs_p.append((kp, vp, sp, cqp, cfp))
        outs_s.append((ksm, vsm, ssm, cqs, cfs))
    return (xp, xs,
            _stack(outs_p, 0), _stack(outs_p, 1), _stack(outs_p, 2), _stack(outs_p, 3), _stack(outs_p, 4),
            _stack(outs_s, 0), _stack(outs_s, 1), _stack(outs_s, 2), _stack(outs_s, 3), _stack(outs_s, 4))
```

```python
import os
import numpy as np
from contextlib import ExitStack
import ml_dtypes
import concourse.bass as bass
import concourse.mybir as mybir
from concourse.bass_utils import run_bass_kernel_spmd

F32 = mybir.dt.float32
BF16 = mybir.dt.bfloat16
AF = mybir.ActivationFunctionType
ALU = mybir.AluOpType

D_MODEL = 1024
SEQ = 4096
GDN_H = 4
ATT_H = 8
D_FF = 2816
NFF = 22
IN_COLS = 3592
EPS = 1e-6
NEG = -30000.0
C_Q, C_K, C_V, C_Z, C_B, C_A, C_QB, C_KB, C_VB = 0, 512, 1024, 1536, 2048, 2052, 2056, 2568, 3080
CP_TRI, CP_SAME, CP_STRICT, CP_INCL, CP_IDENT, CP_BM, CP_NBM, CP_CM = 0, 128, 256, 768, 1280, 1792, 1796, 1800
CP_W = 1800 + 512


class Res:
    __slots__ = ("name", "lw", "rd")

    def __init__(self, name):
        self.name = name
        self.lw = None
        self.rd = {}


class Buf:
    def __init__(self, h, name, nslots=1):
        self.h = h
        self.r = [Res(f"{name}.{i}") for i in range(nslots)]

    def __getitem__(self, idx):
        return self.h[idx]


def _res(x):
    out = []
    for a in x:
        if isinstance(a, Buf):
            out.extend(a.r)
        else:
            out.append(a)
    return out


class Sched:
    CE = ("pe", "dve", "act", "pool")
    ROT = 20000

    def __init__(self, nc, es):
        self.nc, self.es = nc, es
        self.ops = {e: [] for e in ("pe", "dve", "act", "pool", "sp")}
        self.sems = []
        self.cur = {}
        self.cnt = {}
        for e in self.CE:
            self._newsem(e)
        self.waited = {e: {} for e in self.ops}
        self.dpool = {q: [] for q in ("sp", "pool", "act")}
        self.dnext = {q: 0 for q in self.dpool}
        for q, n in (("sp", 12), ("pool", 12), ("act", 4)):
            for i in range(n):
                s = es.enter_context(nc.semaphore(f"d_{q}{i}"))
                self.sems.append(s)
                self.dpool[q].append([len(self.sems) - 1, 0])
        self.ninstr = 0

    def _newsem(self, e):
        s = self.es.enter_context(self.nc.semaphore(f"s_{e}{len(self.sems)}"))
        self.sems.append(s)
        self.cur[e] = len(self.sems) - 1
        self.cnt[e] = 0

    def _deps(self, eng, reads, writes, strict=True):
        deps = {}

        def add(t):
            if t is None:
                return
            if deps.get(t[0], -1) < t[1]:
                deps[t[0]] = t[1]
        for r in reads:
            add(r.lw)
        for w in writes:
            add(w.lw)
            for k, v in w.rd.items():
                add((k, v))
        waits = []
        wd = self.waited[eng]
        for k, v in deps.items():
            if not strict and eng in self.cur and k == self.cur[eng]:
                continue
            if wd.get(k, -1) >= v:
                continue
            wd[k] = v
            waits.append((k, v))
        return waits

    def _mark(self, ticket, reads, writes):
        for w in writes:
            w.lw = ticket
            w.rd = {}
        for r in reads:
            if r.rd.get(ticket[0], -1) < ticket[1]:
                r.rd[ticket[0]] = ticket[1]

    def op(self, eng, fns, reads=(), writes=()):
        reads, writes = _res(reads), _res(writes)
        if not isinstance(fns, (list, tuple)):
            fns = [fns]
        waits = self._deps(eng, reads, writes, strict=(eng != "pe"))
        if self.cnt[eng] >= self.ROT:
            self._newsem(eng)
        self.cnt[eng] += 1
        ticket = (self.cur[eng], self.cnt[eng])
        n = len(fns)
        for i, f in enumerate(fns):
            self.ops[eng].append((waits if i == 0 else (), f, ticket[0] if i == n - 1 else None, 1))
        self.ninstr += n
        self._mark(ticket, reads, writes)

    def dma(self, q, fn, reads=(), writes=()):
        reads, writes = _res(reads), _res(writes)
        waits = self._deps(q, reads, writes)
        pool = self.dpool[q]
        slot = pool[self.dnext[q] % len(pool)]
        self.dnext[q] += 1
        wd = self.waited[q]
        if slot[1] > 0 and wd.get(slot[0], -1) < slot[1]:
            wd[slot[0]] = slot[1]
            waits.append((slot[0], slot[1]))
        slot[1] += 16
        ticket = (slot[0], slot[1])
        self.ops[q].append((waits, fn, slot[0], 16))
        self.ninstr += 1
        self._mark(ticket, reads, writes)

    def barrier(self):
        allw = []
        for q in self.dpool:
            for si, tgt in self.dpool[q]:
                if tgt > 0:
                    allw.append((si, tgt))
        for e in self.CE:
            if self.cnt[e] > 0:
                allw.append((self.cur[e], self.cnt[e]))
        for eng in self.ops:
            wd = self.waited[eng]
            waits = [(k, v) for k, v in allw if wd.get(k, -1) < v]
            for k, v in waits:
                wd[k] = v
            self.ops[eng].append((waits, None, None, 0))

    def finish(self):
        waits = []
        for q in self.dpool:
            for si, tgt in self.dpool[q]:
                if tgt > 0:
                    waits.append((si, tgt))
        for e in self.CE:
            if self.cnt[e] > 0:
                waits.append((self.cur[e], self.cnt[e]))
        self.ops["sp"].append((waits, None, None, 0))

    def replay(self, eng, e):
        sems = self.sems
        for waits, fn, inc, amt in self.ops[eng]:
            for k, v in waits:
                e.wait_ge(sems[k], v)
            if fn is None:
                continue
            ins = fn(e)
            if inc is not None:
                ins.then_inc(sems[inc], amt)


ARENA_F32 = 53000
SMK_W = 4 * 512 + 4 * 128 + 128 + 512
NSCR = 19


def build_program(cfg=None):
    cfg = cfg or {}
    nc = bass.Bass("TRN2", target_bir_lowering=False)
    es = ExitStack()

    def DI(name, shape, dt=F32):
        return nc.dram_tensor(name, list(shape), dt, kind="ExternalInput").ap()

    def DO(name, shape, dt=F32):
        return nc.dram_tensor(name, list(shape), dt, kind="ExternalOutput").ap()

    xp_d = DI("xp", [4096, 1024])
    xs_d = DI("xs", [128, 1024])
    kvalid_d = DI("kvalid", [128, 32])
    ck_d = DI("ck", [4, 512, 512])
    cv_d = DI("cv", [4, 512, 512])
    s0_d = DI("s0", [4, 4, 128, 128])
    qc0_d = DI("qc0", [128, 12 * 4 * 3])
    fc0_d = DI("fc0", [128, NFF * 4 * 2])
    w_in_d = DI("w_in", [1024, IN_COLS])
    w_out_d = DI("w_out", [1024, 1024])
    w_gu_d = DI("w_gu", [1024, 2 * D_FF])
    w_d_d = DI("w_d", [D_FF, 1024])
    nmpre_d = DI("nmpre", [128, 8])
    nfpre_d = DI("nfpre", [128, 8])
    nmpost_d = DI("nmpost", [128, 1024])
    nfpost_d = DI("nfpost", [128, 1024])
    cw_d = DI("cw", [128, 48])
    fw_d = DI("fw", [128, NFF * 3])
    fb_d = DI("fb", [128, NFF])
    alog_d = DI("alog", [128, 4])
    dtb_d = DI("dtb", [128, 4])
    gnw_d = DI("gnw", [128, 512])
    anw_d = DI("anw", [128, 512])
    btp_d = DI("btp", [128, 5 * 8 * 128])
    btm_d = DI("btm", [128, 5 * 8 * 128])
    bts_d = DI("bts", [128, 5 * 8 * 128])
    cpp_d = DI("cpp", [128, CP_W])
    cps_d = DI("cps", [128, CP_W])
    smk_d = DI("smk", [128, SMK_W])

    y_p = DO("y_p", [2048, 1024])
    bk_p = DO("bk_p", [512, 512])
    bv_p = DO("bv_p", [512, 512])
    S_p = DO("S_p", [128, 512])
    qc_p = DO("qc_p", [128, 36])
    fc_p = DO("fc_p", [128, NFF * 2])
    y_s = DO("y_s", [128, 1024])
    bk_s = DO("bk_s", [128, 512])
    bv_s = DO("bv_s", [128, 512])
    S_s = DO("S_s", [128, 4 * 512])
    qc_s = DO("qc_s", [128, 12 * 4 * 3])
    fc_s = DO("fc_s", [128, NFF * 4 * 2])
    ocat_scr = nc.dram_tensor("ocat_scr", [NSCR * 128, 1024], BF16).ap()
    dbg = {}
    for name, shape in (cfg.get("dbg") or {}).items():
        dbg[name] = DO("dbg_" + name, shape, BF16 if name.startswith(("ocat", "bf_")) else F32)

    S = Sched(nc, es)
    dump_tile = cfg.get("dump_tile", -1)

    def DUMP(name, ap, res, tag):
        if name in dbg and tag == dump_tile:
            S.dma("pool", lambda e: e.dma_start(out=dbg[name][:, :], in_=ap), reads=res)

    arena = es.enter_context(nc.sbuf_tensor("arena", [128, ARENA_F32], F32))
    aptr = [0]

    def SB(name, shape, dt=F32, nslots=1):
        n = int(np.prod(shape[1:]))
        nf = n if dt == F32 else (n + 1) // 2
        nf = (nf + 1) // 2 * 2
        off = aptr[0]
        aptr[0] += nf
        assert aptr[0] <= ARENA_F32, f"SBUF arena overflow at {name}: {aptr[0]}"
        v = arena[:, off:off + nf]
        if dt != F32:
            v = v.bitcast(dt)[:, 0:n]
        if len(shape) == 3:
            v = v.rearrange("p (a b) -> p a b", a=shape[1])
        elif len(shape) == 4:
            v = v.rearrange("p (a b c) -> p a b c", a=shape[1], b=shape[2])
        return Buf(v, name, nslots)

    def flat(ap):
        nd = len(ap.shape)
        if nd == 2:
            return ap
        if nd == 3:
            return ap.rearrange("p a b -> p (a b)")
        return ap.rearrange("p a b c -> p (a b c)")

    def PS(name, shape, dt=F32):
        h = es.enter_context(nc.psum_tensor(name, list(shape), dt))
        return Buf(h, name, 1)

    psT = PS("psT", [128, 1024], BF16)
    psP = [PS("psP0", [128, 512]), PS("psP1", [128, 512])]
    psA = PS("psA", [128, 512])
    psB = PS("psB", [128, 512])
    psS = PS("psS", [128, 512])
    psO = PS("psO", [128, 512])
    psM = PS("psM", [128, 512])
    pctr = [0]

    def next_psP():
        pctr[0] += 1
        return psP[pctr[0] % 2]

    identb = SB("identb", [128, 128], BF16)
    stage = SB("stage", [128, 2, 1024], F32, nslots=2)
    xin = SB("xin", [128, 2, 1024], F32, nslots=2)
    xn = SB("xn", [128, 1024], BF16)
    st = SB("st", [128, 8])
    epsb = SB("epsb", [128, 2])
    shared_end = aptr[0]

    stctr = [0]

    def load_cast_weight(dst_fn, src_ap_fn, ncols, nk, scale_ap_fn, dstbuf):
        for k in range(nk):
            for c0 in range(0, ncols, 1024):
                c1 = min(ncols, c0 + 1024)
                sl = stctr[0] % 2
                stctr[0] += 1
                w = c1 - c0
                S.dma("sp", (lambda e, sl=sl, k=k, c0=c0, c1=c1, w=w: e.dma_start(out=stage[:, sl, 0:w], in_=src_ap_fn(k, c0, c1))),
                      writes=[stage.r[sl]])
                sc = scale_ap_fn(k) if scale_ap_fn else None
                rds = [stage.r[sl]] + ([scale_ap_fn.buf] if scale_ap_fn else [])
                if stctr[0] % 2:
                    if sc is None:
                        S.op("act", (lambda e, sl=sl, k=k, c0=c0, c1=c1, w=w: e.activation(out=dst_fn(k, c0, c1), in_=stage[:, sl, 0:w], func=AF.Copy)),
                             reads=rds, writes=[dstbuf])
                    else:
                        S.op("act", (lambda e, sl=sl, k=k, c0=c0, c1=c1, w=w, sc=sc: e.activation(out=dst_fn(k, c0, c1), in_=stage[:, sl, 0:w], func=AF.Copy, scale=sc)),
                             reads=rds, writes=[dstbuf])
                else:
                    if sc is None:
                        S.op("dve", (lambda e, sl=sl, k=k, c0=c0, c1=c1, w=w: e.tensor_copy(out=dst_fn(k, c0, c1), in_=stage[:, sl, 0:w])),
                             reads=rds, writes=[dstbuf])
                    else:
                        S.op("dve", (lambda e, sl=sl, k=k, c0=c0, c1=c1, w=w, sc=sc: e.tensor_scalar(out=dst_fn(k, c0, c1), in0=stage[:, sl, 0:w], scalar1=sc, scalar2=None, op0=ALU.mult)),
                             reads=rds, writes=[dstbuf])

    def load_small(buf, dram_ap):
        S.dma("sp", (lambda e: e.dma_start(out=flat(buf[:]), in_=dram_ap)), writes=[buf])

    def rms_stats(src_ap, src_res, junk_ap, junk_res, ncols):
        S.op("act", lambda e: e.activation(out=junk_ap, in_=src_ap, func=AF.Square, accum_out=st[:, 0:1]),
             reads=src_res, writes=[st] + junk_res)
        S.op("act", lambda e: e.activation(out=st[:, 1:2], in_=st[:, 0:1], func=AF.Ln, scale=1.0 / ncols, bias=epsb[:, 0:1]),
             reads=[st, epsb], writes=[st])
        S.op("act", lambda e: e.activation(out=st[:, 2:3], in_=st[:, 1:2], func=AF.Exp, scale=-0.5),
             reads=[st], writes=[st])

    S.op("pool", lambda e: e.memset(epsb[:, 0:1], EPS), writes=[epsb])
    S.op("pool", lambda e: e.memset(epsb[:, 1:2], 1.0), writes=[epsb])

    do_p1 = cfg.get("p1", True)
    do_p2 = cfg.get("p2", True)
    NTL1 = cfg.get("ntl1", 2)
    p_tiles = 32
    first_own = 16
    p1_from = cfg.get("p1_from", 0)
    p1_to = cfg.get("p1_to", p_tiles)
    xctr = [0]

    def norm_transpose(src_dram_ap, dstT, col0, junk=None):
        sl = xctr[0] % 2
        xctr[0] += 1
        S.dma("sp", lambda e: e.dma_start(out=xin[:, sl, :], in_=src_dram_ap), writes=[xin.r[sl]])
        rms_stats(xin[:, sl, :], [xin.r[sl]], xn[:], [xn], 1024)
        S.op("dve", lambda e: e.tensor_scalar(out=xn[:], in0=xin[:, sl, :], scalar1=st[:, 2:3], scalar2=None, op0=ALU.mult),
             reads=[xin.r[sl], st], writes=[xn])
        S.op("pe", [(lambda e, k=k: e.transpose(psT[:, k * 128:(k + 1) * 128], xn[:, k * 128:(k + 1) * 128], identb[:])) for k in range(8)],
             reads=[xn, identb], writes=[psT])
        S.op("act", lambda e: e.activation(out=dstT[:, :, col0:col0 + 128], in_=psT[:].rearrange("p (a b) -> p a b", a=8), func=AF.Copy),
             reads=[psT], writes=[dstT])
        return sl

    def proj_fm(wbuf, wcol0, rhsT, NT, ps, nk=8):
        S.op("pe", [(lambda e, k=k: e.matmul(ps[:, 0:NT], lhsT=wbuf[:, k, wcol0:wcol0 + 128], rhs=rhsT[:, k, 0:NT], start=(k == 0), stop=(k == nk - 1))) for k in range(nk)],
             reads=[wbuf, rhsT], writes=[ps])

    def proj_tm(wbuf, wcol0, ncols, lhsT_buf, col0, ps, pcol0=0, nk=8):
        S.op("pe", [(lambda e, k=k: e.matmul(ps[:, pcol0:pcol0 + ncols], lhsT=lhsT_buf[:, k, col0:col0 + 128], rhs=wbuf[:, k, wcol0:wcol0 + ncols], start=(k == 0), stop=(k == nk - 1))) for k in range(nk)],
             reads=[wbuf, lhsT_buf], writes=[ps])

    if do_p1:
        aptr[0] = shared_end
        NT1 = NTL1 * 128
        cp = SB("cp", [128, CP_W])
        tri2 = cp[:, CP_TRI:CP_TRI + 128]
        same2 = cp[:, CP_SAME:CP_SAME + 128]
        identf = cp[:, CP_IDENT:CP_IDENT + 128]

        def c4(off):
            return cp[:, off:off + 512].rearrange("p (a b) -> p a b", a=4)

        strict4, incl4, ident4, cm4 = c4(CP_STRICT), c4(CP_INCL), c4(CP_IDENT), c4(CP_CM)
        w_in_bf = SB("w_in_bf", [128, 8, IN_COLS], BF16)
        nmpre = SB("nmpre", [128, 8])
        cw = SB("cw", [128, 12, 4])
        alog = SB("alog", [128, 4])
        nea = SB("nea", [128, 4])
        dtb = SB("dtb", [128, 4])
        gnw4 = SB("gnw4", [128, 4, 128])
        anw8 = SB("anw8", [128, 8, 64])
        kvalid = SB("kvalid", [128, 32])
        BT = SB("BT", [128, 5, 8, 128], BF16)
        uT = SB("uT", [128, 8, NT1], BF16)
        xpb = SB("xpb", [128, 2, NT1 + 16], F32, nslots=2)
        cvb = SB("cvb", [128, 2, NT1], F32, nslots=2)
        sqb = SB("sqb", [128, NT1], F32)
        rnb = SB("rnb", [128, NT1], F32)
        knT = SB("knT", [128, 4, NT1], BF16)
        qnT = SB("qnT", [128, 4, NT1], BF16)
        vT = SB("vT", [128, 4, NT1], BF16)
        zs = SB("zs", [128, NTL1, 512], BF16, nslots=NTL1)
        kbT = SB("kbT", [128, 4, 8, 128], BF16, nslots=8)
        V1 = SB("V1", [128, 8, 8, 64], BF16, nslots=8)
        vcol = SB("vcol", [128, 8], BF16, nslots=8)
        qbz = SB("qbz", [128, 4, 2, NT1], BF16)
        qtail = SB("qtail", [128, 12, 4, 3])
        ones128 = SB("ones128", [128, 128])
        gt = SB("gt", [128, 80])
        Rb = SB("Rb", [128, 4, 128])
        Eb = SB("Eb", [128, 4, 128])
        tmpb = SB("tmpb", [128, 4, 128])
        Qb = [SB("Qb0", [128, 4, 128]), SB("Qb1", [128, 4, 128])]
        Pb = [SB("Pb0", [128, 4, 128]), SB("Pb1", [128, 4, 128])]
        Xb = SB("Xb", [128, 4, 128])
        Xbf = SB("Xbf", [128, 4, 128], BF16)
        qkb = SB("qkb", [128, 4, 128], BF16)
        qkT = SB("qkT", [128, 4, 128], BF16)
        kbg = SB("kbg", [128, 4, 128], BF16)
        kgm = SB("kgm", [128, 4, 4, 128], BF16)
        vbb = SB("vbb", [128, 4, 128], BF16)
        vn32 = SB("vn32", [128, 4, 128])
        vnb = SB("vnb", [128, 4, 128], BF16)
        wTb = SB("wTb", [128, 4, 128], BF16)
        oacc = SB("oacc", [128, 4, 128])
        Sp = SB("Sp", [128, 4, 128])
        Spb = SB("Spb", [128, 4, 128], BF16)
        Ss = SB("Ss", [128, 4, 4, 128], F32, nslots=4)
        Ssb = SB("Ssb", [128, 4, 4, 128], BF16, nslots=4)
        scb = SB("scb", [128, 4, 128])
        PT = SB("PT", [128, 2, 4, 128], BF16, nslots=2)
        ob32 = SB("ob32", [128, 8, 64])
        ocat = SB("ocat", [128, 2, 1024], BF16, nslots=2)
        kvout = SB("kvout", [128, 2, 512], F32, nslots=2)
        smk = SB("smk", [128, SMK_W], BF16)
        print("phase1 arena used", aptr[0], "of", ARENA_F32)

        load_small(cp, cpp_d[:, :])
        S.op("dve", lambda e: e.tensor_copy(out=identb[:], in_=identf), reads=[cp], writes=[identb])
        load_small(nmpre, nmpre_d[:, :])
        load_small(cw, cw_d[:, :])
        load_small(alog, alog_d[:, :])
        load_small(dtb, dtb_d[:, :])
        load_small(gnw4, gnw_d[:, :])
        load_small(anw8, anw_d[:, :])
        load_small(kvalid, kvalid_d[:, :])
        S.op("act", lambda e: e.activation(out=nea[:], in_=alog[:], func=AF.Exp), reads=[alog], writes=[nea])
        S.op("dve", lambda e: e.tensor_scalar(out=nea[:], in0=nea[:], scalar1=-1.0, scalar2=None, op0=ALU.mult), reads=[nea], writes=[nea])
        S.op("pool", lambda e: e.memset(ones128[:], 1.0), writes=[ones128])
        S.op("pool", lambda e: e.memset(flat(qbz[:]), 0.0), writes=[qbz])
        S.op("pool", lambda e: e.memset(flat(qtail[:]), 0.0), writes=[qtail])
        S.op("pool", lambda e: e.memset(flat(Sp[:]), 0.0), writes=[Sp])
        S.op("pool", lambda e: e.memset(flat(Spb[:]), 0.0), writes=[Spb])
        S.op("pool", lambda e: e.tensor_copy(out=vcol[:], in_=kvalid[:, 0:8]), reads=[kvalid], writes=[vcol])

        def load_BT(tab_d, mask_d):
            BTf = flat(BT[:])
            for i in range(5):
                S.dma("sp", (lambda e, i=i: e.dma_start(out=stage[:, 0, :], in_=tab_d[:, i * 1024:(i + 1) * 1024])), writes=[stage.r[0]])
                if mask_d is not None:
                    S.dma("sp", (lambda e, i=i: e.dma_start(out=stage[:, 1, :], in_=mask_d[:, i * 1024:(i + 1) * 1024])), writes=[stage.r[1]])
                    S.op("dve", (lambda e, i=i: e.tensor_tensor(out=BTf[:, i * 1024:(i + 1) * 1024], in0=stage[:, 0, :], in1=stage[:, 1, :], op=ALU.add)),
                         reads=[stage], writes=[BT])
                else:
                    S.op("dve", (lambda e, i=i: e.tensor_copy(out=BTf[:, i * 1024:(i + 1) * 1024], in_=stage[:, 0, :])),
                         reads=[stage.r[0]], writes=[BT])

        load_BT(btp_d, btm_d)
        sfn = lambda k: nmpre[:, k:k + 1]
        sfn.buf = nmpre
        load_cast_weight(lambda k, c0, c1: w_in_bf[:, k, c0:c1],
                         lambda k, c0, c1: w_in_d[k * 128:(k + 1) * 128, c0:c1],
                         IN_COLS, 8, sfn, w_in_bf)

        cvctr = [0]

        def conv_silu(ps, c, nseq, L, NT, dst_ap, dst_res, eng, sl):
            xpv = xpb[:, sl, 0:nseq * (L + 3)].rearrange("p (s t) -> p s t", s=nseq)
            cvv = cvb[:, sl, 0:NT].rearrange("p (s t) -> p s t", s=nseq)
            S.op("pool", lambda e: e.tensor_copy(out=xpv[:, :, 0:3], in_=qtail[:, c, 0:nseq, :]), reads=[qtail], writes=[xpb.r[sl]])
            S.op("act", lambda e: e.activation(out=xpv[:, :, 3:3 + L], in_=ps[:, 0:NT].rearrange("p (s t) -> p s t", s=nseq), func=AF.Copy),
                 reads=[ps], writes=[xpb.r[sl]])
            S.op("pool", lambda e: e.tensor_copy(out=qtail[:, c, 0:nseq, :], in_=xpv[:, :, L:L + 3]), reads=[xpb.r[sl]], writes=[qtail])
            S.op(eng, lambda e: e.tensor_scalar(out=cvv, in0=xpv[:, :, 0:L], scalar1=cw[:, c, 0:1], scalar2=None, op0=ALU.mult),
                 reads=[xpb.r[sl], cw], writes=[cvb.r[sl]])
            for i in range(1, 4):
                S.op(eng, (lambda e, i=i: e.scalar_tensor_tensor(out=cvv, in0=xpv[:, :, i:i + L], scalar=cw[:, c, i:i + 1], in1=cvv, op0=ALU.mult, op1=ALU.add)),
                     reads=[xpb.r[sl], cw, cvb.r[sl]], writes=[cvb.r[sl]])
            S.op("act", lambda e: e.activation(out=dst_ap, in_=cvb[:, sl, 0:NT], func=AF.Silu), reads=[cvb.r[sl]], writes=dst_res)

        def l2norm_chunk(src_ap, src_res, dstT, h, NT, scale):
            S.op("pool", lambda e: e.tensor_tensor(out=sqb[:, 0:NT], in0=src_ap, in1=src_ap, op=ALU.mult), reads=src_res, writes=[sqb])
            S.op("pe", lambda e: e.matmul(psM[:, 0:NT], lhsT=ones128[:], rhs=sqb[:, 0:NT], start=True, stop=True), reads=[ones128, sqb], writes=[psM])
            S.op("act", lambda e: e.activation(out=rnb[:, 0:NT], in_=psM[:, 0:NT], func=AF.Ln, bias=epsb[:, 0:1]), reads=[psM, epsb], writes=[rnb])
            S.op("act", lambda e: e.activation(out=rnb[:, 0:NT], in_=rnb[:, 0:NT], func=AF.Exp, scale=-0.5), reads=[rnb], writes=[rnb])
            S.op("dve", lambda e: e.scalar_tensor_tensor(out=dstT[:, h, 0:NT], in0=src_ap, scalar=float(scale), in1=rnb[:, 0:NT], op0=ALU.mult, op1=ALU.mult),
                 reads=src_res + [rnb], writes=[dstT])

        def bc(ap2, n, w):
            return ap2.unsqueeze(2).to_broadcast([128, n, w])

        def gdn_tile(tcol, nb, full, S_in, Sb_in, S_out, Sb_out, zslot, oc_slot, tag=-1):
            tc = slice(tcol, tcol + 128)
            beta, t1, g, Gs, eG, ekg, ckbg, nbeta = (gt[:, 0:4], gt[:, 4:8], gt[:, 8:12], gt[:, 12:20], gt[:, 20:24], gt[:, 24:28], gt[:, 28:32], gt[:, 32:36])
            dlb = gt[:, 36:36 + 4 * nb]
            ssq = gt[:, 52:56]
            cog = gt[:, 56:60]
            S.op("act", lambda e: e.activation(out=beta, in_=psM[:, 0:4], func=AF.Sigmoid), reads=[psM], writes=[gt])
            S.op("dve", lambda e: e.tensor_tensor(out=t1, in0=psM[:, 4:8], in1=dtb[:], op=ALU.add), reads=[psM, dtb], writes=[gt])
            S.op("act", lambda e: e.activation(out=t1, in_=t1, func=AF.Exp), reads=[gt], writes=[gt])
            S.op("act", lambda e: e.activation(out=t1, in_=t1, func=AF.Ln, bias=epsb[:, 1:2]), reads=[gt, epsb], writes=[gt])
            S.op("dve", lambda e: e.tensor_tensor(out=g, in0=t1, in1=nea[:], op=ALU.mult), reads=[gt, nea], writes=[gt])
            fns = [lambda e: e.matmul(psM[:, 16:20], lhsT=tri2, rhs=g, start=True, stop=True),
                   lambda e: e.matmul(psM[:, 20:24], lhsT=same2, rhs=g, start=True, stop=True)]
            for b in range(nb):
                fns.append(lambda e, b=b: e.matmul(psM[:, 24 + 4 * b:28 + 4 * b], lhsT=cm4[:, b, :], rhs=g, start=True, stop=True))
            S.op("pe", fns, reads=[cp, gt], writes=[psM])
            S.op("act", lambda e: e.activation(out=Gs, in_=psM[:, 16:24], func=AF.Copy), reads=[psM], writes=[gt])
            S.op("act", lambda e: e.activation(out=dlb, in_=psM[:, 24:24 + 4 * nb], func=AF.Exp), reads=[psM], writes=[gt])
            S.op("dve", lambda e: e.tensor_tensor(out=ekg, in0=Gs[:, 4:8], in1=Gs[:, 0:4], op=ALU.subtract), reads=[gt], writes=[gt])
            S.op("act", lambda e: e.activation(out=ekg, in_=ekg, func=AF.Exp), reads=[gt], writes=[gt])
            S.op("act", lambda e: e.activation(out=eG, in_=Gs[:, 0:4], func=AF.Exp), reads=[gt], writes=[gt])
            S.op("dve", lambda e: e.tensor_tensor(out=ckbg, in0=beta, in1=eG, op=ALU.mult), reads=[gt], writes=[gt])
            S.op("dve", lambda e: e.tensor_scalar(out=nbeta, in0=beta, scalar1=-1.0, scalar2=None, op0=ALU.mult), reads=[gt], writes=[gt])
            DUMP("gt", gt[:, 0:64], [gt], tag)
            DUMP("bf_knT", flat(knT[:]), [knT], tag)
            DUMP("bf_vT", flat(vT[:]), [vT], tag)
            S.op("pool", lambda e: e.tensor_tensor(out=Rb[:], in0=strict4, in1=bc(g, 4, 128), op=ALU.mult), reads=[cp, gt], writes=[Rb])
            S.op("pe", [(lambda e, h=h: e.matmul(psA[:, h * 128:(h + 1) * 128], lhsT=tri2, rhs=Rb[:, h, :], start=True, stop=True)) for h in range(4)],
                 reads=[cp, Rb], writes=[psA])
            S.op("act", lambda e: e.activation(out=flat(Eb[:]), in_=psA[:], func=AF.Exp), reads=[psA], writes=[Eb])
            S.op("pe", [(lambda e, h=h: e.matmul(psB[:, h * 128:(h + 1) * 128], lhsT=knT[:, h, tc], rhs=knT[:, h, tc], start=True, stop=True)) for h in range(4)],
                 reads=[knT], writes=[psB])
            S.op("dve", lambda e: e.tensor_tensor(out=flat(tmpb[:]), in0=psB[:], in1=flat(Eb[:]), op=ALU.mult), reads=[psB, Eb], writes=[tmpb])
            S.op("dve", lambda e: e.tensor_tensor(out=tmpb[:], in0=tmpb[:], in1=bc(nbeta, 4, 128), op=ALU.mult), reads=[tmpb, gt], writes=[tmpb])
            Q, Q2, P, P2 = Qb[0], Qb[1], Pb[0], Pb[1]
            DUMP("E", flat(Eb[:]), [Eb], tag)
            S.op("dve", lambda e, Q=Q: e.tensor_tensor(out=Q[:], in0=tmpb[:], in1=strict4, op=ALU.mult), reads=[tmpb, cp], writes=[Q])
            DUMP("Q0", flat(Q[:]), [Q], tag)
            S.op("pe", [(lambda e, h=h, Q=Q: e.transpose(psA[:, h * 128:(h + 1) * 128], Q[:, h, :], identf)) for h in range(4)],
                 reads=[Q, cp], writes=[psA])
            S.op("act", lambda e, P=P: e.activation(out=flat(P[:]), in_=psA[:], func=AF.Copy), reads=[psA], writes=[P])
            if full:
                S.op("pe", [(lambda e, h=h: e.matmul(psB[:, h * 128:(h + 1) * 128], lhsT=qnT[:, h, tc], rhs=knT[:, h, tc], start=True, stop=True)) for h in range(4)],
                     reads=[qnT, knT], writes=[psB])
                S.op("dve", lambda e: e.tensor_tensor(out=flat(tmpb[:]), in0=psB[:], in1=flat(Eb[:]), op=ALU.mult), reads=[psB, Eb], writes=[tmpb])
                S.op("pool", lambda e: e.tensor_tensor(out=qkb[:], in0=tmpb[:], in1=incl4, op=ALU.mult), reads=[tmpb, cp], writes=[qkb])
                S.op("pe", [(lambda e, h=h: e.transpose(psT[:, h * 128:(h + 1) * 128], qkb[:, h, :], identb[:])) for h in range(4)],
                     reads=[qkb, identb], writes=[psT])
                S.op("act", lambda e: e.activation(out=flat(qkT[:]), in_=psT[:, 0:512], func=AF.Copy), reads=[psT], writes=[qkT])
            DUMP("P0", flat(P[:]), [P], tag)
            S.op("dve", lambda e, P=P: e.tensor_tensor(out=Xb[:], in0=P[:], in1=ident4, op=ALU.add), reads=[P, cp], writes=[Xb])
            nlev = 5
            for lv in range(nlev):
                S.op("pe", [(lambda e, h=h, P=P, Q=Q: e.matmul(psB[:, h * 128:(h + 1) * 128], lhsT=P[:, h, :], rhs=Q[:, h, :], start=True, stop=True)) for h in range(4)],
                     reads=[P, Q], writes=[psB])
                S.op("act", (lambda e, Q2=Q2: e.activation(out=flat(Q2[:]), in_=psB[:], func=AF.Copy)), reads=[psB], writes=[Q2])
                S.op("pe", [(lambda e, h=h, Q2=Q2: e.matmul(psA[:, h * 128:(h + 1) * 128], lhsT=Q2[:, h, :], rhs=Xb[:, h, :], start=True, stop=True)) for h in range(4)],
                     reads=[Q2, Xb], writes=[psA])
                S.op("dve", lambda e: e.tensor_tensor(out=flat(Xb[:]), in0=flat(Xb[:]), in1=psA[:], op=ALU.add), reads=[Xb, psA], writes=[Xb])
                if lv < nlev - 1:
                    S.op("pe", [(lambda e, h=h, P=P, Q=Q: e.matmul(psB[:, h * 128:(h + 1) * 128], lhsT=Q[:, h, :], rhs=P[:, h, :], start=True, stop=True)) for h in range(4)],
                         reads=[P, Q], writes=[psB])
                    S.op("act", (lambda e, P2=P2: e.activation(out=flat(P2[:]), in_=psB[:], func=AF.Copy)), reads=[psB], writes=[P2])
                    P, P2 = P2, P
                Q, Q2 = Q2, Q
            DUMP("X", flat(Xb[:]), [Xb], tag)
            S.op("act", lambda e: e.activation(out=Xbf[:], in_=Xb[:], func=AF.Copy), reads=[Xb], writes=[Xbf])
            S.op("pe", [(lambda e, h=h: e.transpose(psT[:, h * 128:(h + 1) * 128], knT[:, h, tc], identb[:])) for h in range(4)] +
                 [(lambda e, h=h: e.transpose(psT[:, 512 + h * 128:512 + (h + 1) * 128], vT[:, h, tc], identb[:])) for h in range(4)],
                 reads=[knT, vT, identb], writes=[psT])
            kTv = psT[:, 0:512].rearrange("p (a b) -> p a b", a=4)
            vTv = psT[:, 512:1024].rearrange("p (a b) -> p a b", a=4)
            S.op("dve", lambda e: e.tensor_tensor(out=kbg[:], in0=kTv, in1=bc(ckbg, 4, 128), op=ALU.mult), reads=[psT, gt], writes=[kbg])
            S.op("dve", lambda e: e.tensor_tensor(out=vbb[:], in0=vTv, in1=bc(beta, 4, 128), op=ALU.mult), reads=[psT, gt], writes=[vbb])
            for b in range(nb):
                cof = gt[:, 60 + 4 * b:64 + 4 * b]
                S.op("dve", (lambda e, b=b, cof=cof: e.tensor_scalar(out=cof, in0=ekg, scalar1=cp[:, CP_BM + b:CP_BM + b + 1], scalar2=None, op0=ALU.mult)),
                     reads=[gt, cp], writes=[gt])
                S.op("dve", (lambda e, b=b, cof=cof: e.tensor_tensor(out=kgm[:, b], in0=kTv, in1=bc(cof, 4, 128), op=ALU.mult)),
                     reads=[psT, gt], writes=[kgm])
            S.op("pe", [(lambda e, h=h: e.matmul(psA[:, h * 128:(h + 1) * 128], lhsT=Xbf[:, h, :], rhs=vbb[:, h, :], start=True, stop=True)) for h in range(4)],
                 reads=[Xbf, vbb], writes=[psA])
            S.op("act", lambda e: e.activation(out=flat(vn32[:]), in_=psA[:], func=AF.Copy), reads=[psA], writes=[vn32])
            S.op("pe", [(lambda e, h=h: e.matmul(psB[:, h * 128:(h + 1) * 128], lhsT=kbg[:, h, :], rhs=Xbf[:, h, :], start=True, stop=True)) for h in range(4)],
                 reads=[kbg, Xbf], writes=[psB])
            S.op("act", lambda e: e.activation(out=flat(wTb[:]), in_=psB[:], func=AF.Copy), reads=[psB], writes=[wTb])
            DUMP("u", flat(vn32[:]), [vn32], tag)
            DUMP("bf_wT", flat(wTb[:]), [wTb], tag)
            DUMP("bf_kbg", flat(kbg[:]), [kbg], tag)
            DUMP("bf_vb", flat(vbb[:]), [vbb], tag)
            for b in range(nb):
                Si, Sbi, So, Sbo = S_in(b), Sb_in(b), S_out(b), Sb_out(b)
                S.op("pe", [(lambda e, h=h, Sbi=Sbi: e.matmul(psA[:, h * 128:(h + 1) * 128], lhsT=wTb[:, h, :], rhs=Sbi[0][:, h, :], start=True, stop=True)) for h in range(4)],
                     reads=[wTb, Sbi[1]], writes=[psA])
                S.op("dve", (lambda e, b=b: e.scalar_tensor_tensor(out=flat(vn32[:]), in0=psA[:], scalar=cp[:, CP_NBM + b:CP_NBM + b + 1],
                                                                    in1=flat(vn32[:]), op0=ALU.mult, op1=ALU.add)),
                     reads=[psA, cp, vn32], writes=[vn32])
                S.op("act", lambda e: e.activation(out=vnb[:], in_=vn32[:], func=AF.Copy), reads=[vn32], writes=[vnb])
                if full:
                    S.op("dve", (lambda e, b=b: e.tensor_scalar(out=cog, in0=eG, scalar1=cp[:, CP_BM + b:CP_BM + b + 1], scalar2=None, op0=ALU.mult)),
                         reads=[gt, cp], writes=[gt])
                    S.op("pe", [(lambda e, h=h, Sbi=Sbi: e.matmul(psB[:, h * 128:(h + 1) * 128], lhsT=qnT[:, h, tc], rhs=Sbi[0][:, h, :], start=True, stop=True)) for h in range(4)],
                         reads=[qnT, Sbi[1]], writes=[psB])
                    psBv = psB[:].rearrange("p (a b) -> p a b", a=4)
                    if b == 0:
                        S.op("dve", lambda e: e.tensor_tensor(out=oacc[:], in0=psBv, in1=bc(cog, 4, 128), op=ALU.mult), reads=[psB, gt], writes=[oacc])
                    else:
                        S.op("dve", lambda e: e.tensor_tensor(out=tmpb[:], in0=psBv, in1=bc(cog, 4, 128), op=ALU.mult), reads=[psB, gt], writes=[tmpb])
                        S.op("pool", lambda e: e.tensor_tensor(out=oacc[:], in0=oacc[:], in1=tmpb[:], op=ALU.add), reads=[oacc, tmpb], writes=[oacc])
                S.op("pe", [(lambda e, h=h, b=b: e.matmul(psA[:, h * 128:(h + 1) * 128], lhsT=kgm[:, b, h, :], rhs=vnb[:, h, :], start=True, stop=True)) for h in range(4)],
                     reads=[kgm, vnb], writes=[psA])
                for h in range(4):
                    S.op("dve", (lambda e, h=h, b=b, Si=Si, So=So: e.scalar_tensor_tensor(out=So[0][:, h, :], in0=Si[0][:, h, :], scalar=dlb[:, 4 * b + h:4 * b + h + 1],
                                                                                      in1=psA[:, h * 128:(h + 1) * 128], op0=ALU.mult, op1=ALU.add)),
                         reads=[Si[1], gt, psA], writes=[So[1]])
                if Sbo is not None:
                    S.op("act", (lambda e, So=So, Sbo=Sbo: e.activation(out=Sbo[0], in_=So[0], func=AF.Copy)), reads=[So[1]], writes=[Sbo[1]])
            DUMP("vn", flat(vn32[:]), [vn32], tag)
            DUMP("Safter", flat(S_out(nb - 1)[0]), [S_out(nb - 1)[1]], tag)
            if full:
                S.op("pe", [(lambda e, h=h: e.matmul(psB[:, h * 128:(h + 1) * 128], lhsT=qkT[:, h, :], rhs=vnb[:, h, :], start=True, stop=True)) for h in range(4)],
                     reads=[qkT, vnb], writes=[psB])
                S.op("dve", lambda e: e.tensor_tensor(out=flat(oacc[:]), in0=flat(oacc[:]), in1=psB[:], op=ALU.add), reads=[oacc, psB], writes=[oacc])
                for h in range(4):
                    S.op("act", (lambda e, h=h: e.activation(out=tmpb[:, h, :], in_=oacc[:, h, :], func=AF.Square, accum_out=ssq[:, h:h + 1])),
                         reads=[oacc], writes=[tmpb, gt])
                S.op("act", lambda e: e.activation(out=ssq, in_=ssq, func=AF.Ln, scale=1.0 / 128, bias=epsb[:, 0:1]), reads=[gt, epsb], writes=[gt])
                S.op("act", lambda e: e.activation(out=ssq, in_=ssq, func=AF.Exp, scale=-0.5), reads=[gt], writes=[gt])
                DUMP("o", flat(oacc[:]), [oacc], tag)
                S.op("dve", lambda e: e.tensor_tensor(out=tmpb[:], in0=oacc[:], in1=bc(ssq, 4, 128), op=ALU.mult), reads=[oacc, gt], writes=[tmpb])
                S.op("pool", lambda e: e.tensor_tensor(out=tmpb[:], in0=tmpb[:], in1=gnw4[:], op=ALU.mult), reads=[tmpb, gnw4], writes=[tmpb])
                S.op("dve", lambda e: e.tensor_tensor(out=ocat[:, oc_slot, 0:512], in0=flat(tmpb[:]), in1=zs[:, zslot, :], op=ALU.mult),
                     reads=[tmpb, zs.r[zslot]], writes=[ocat.r[oc_slot]])

        def attn_tile(pieces, qcol, oc_slot):
            qc = slice(qcol, qcol + 128)
            npc = len(pieces)
            first_pv = [True]
            pctr2 = [0]
            for pi, pc in enumerate(pieces):
                if pc.get("prep"):
                    pc["prep"]()
                for hh in range(2):
                    fns = []
                    first = True
                    if pc.get("seq") is not None:
                        s = pc["seq"]
                        fns.append(lambda e, s=s: e.matmul(psS[:], lhsT=smk[:, 2560:2688], rhs=smk[:, s * 512:(s + 1) * 512], start=True, stop=False, skip_group_check=True))
                        first = False
                        if pc.get("newk"):
                            fns.append(lambda e, s=s: e.matmul(psS[:], lhsT=smk[:, 2048 + s * 128:2048 + (s + 1) * 128], rhs=smk[:, 2688:3200], start=False, stop=False, skip_group_check=True))
                    for j in range(4):
                        h = 4 * hh + j
                        fns.append(lambda e, h=h, j=j, pc=pc, first=first: e.matmul(psS[:, j * 128:(j + 1) * 128], lhsT=kbT[:, h // 2, pc["kslot"], :],
                                                                                   rhs=qbz[:, h // 2, h % 2, qc], start=first, stop=True, skip_group_check=True))
                    S.op("pe", fns, reads=[kbT.r[pc["kslot"]], qbz, smk], writes=[psS])
                    S.op("dve", (lambda e, pc=pc, hh=hh: e.tensor_tensor(out=scb[:], in0=psS[:].rearrange("p (a b) -> p a b", a=4), in1=BT[:, pc["r"], 4 * hh:4 * hh + 4, :], op=ALU.add)),
                         reads=[psS, BT], writes=[scb])
                    psl = pctr2[0] % 2
                    pctr2[0] += 1
                    S.op("act", (lambda e, psl=psl: e.activation(out=PT[:, psl], in_=scb[:], func=AF.Exp)), reads=[scb], writes=[PT.r[psl]])
                    fns = []
                    for j in range(4):
                        h = 4 * hh + j
                        stt = first_pv[0]
                        first_pv[0] = False
                        fns.append(lambda e, j=j, h=h, pc=pc, psl=psl, stt=stt: e.matmul(psO[:, h * 64:(h + 1) * 64], lhsT=PT[:, psl, j, :], rhs=V1[:, pc["vslot"], h, :],
                                                                                        start=stt, stop=False, skip_group_check=True))
                        fns.append(lambda e, j=j, h=h, pc=pc, psl=psl, pi=pi: e.matmul(psM[:, 64 + pi * 8 + h:64 + pi * 8 + h + 1], lhsT=PT[:, psl, j, :], rhs=vcol[:, pc["vslot"]:pc["vslot"] + 1],
                                                                                      start=True, stop=True, skip_group_check=True))
                    S.op("pe", fns, reads=[PT.r[psl], V1.r[pc["vslot"]], vcol.r[pc["vslot"]]], writes=[psO, psM])
            rden = gt[:, 64:72]
            ss8 = gt[:, 72:80]
            S.op("dve", lambda e: e.tensor_reduce(out=rden, in_=psM[:, 64:64 + npc * 8].rearrange("p (a b) -> p b a", b=8), axis=mybir.AxisListType.X, op=ALU.add),
                 reads=[psM], writes=[gt])
            S.op("dve", lambda e: e.tensor_scalar(out=rden, in0=rden, scalar1=1e-30, scalar2=None, op0=ALU.max), reads=[gt], writes=[gt])
            S.op("dve", lambda e: e.reciprocal(out=rden, in_=rden), reads=[gt], writes=[gt])
            S.op("dve", lambda e: e.tensor_tensor(out=ob32[:], in0=psO[:].rearrange("p (a b) -> p a b", a=8), in1=bc(rden, 8, 64), op=ALU.mult),
                 reads=[psO, gt], writes=[ob32])
            for h in range(8):
                S.op("act", (lambda e, h=h: e.activation(out=scb[:, 0, 0:64], in_=ob32[:, h, :], func=AF.Square, accum_out=ss8[:, h:h + 1])),
                     reads=[ob32], writes=[scb, gt])
            S.op("act", lambda e: e.activation(out=ss8, in_=ss8, func=AF.Ln, scale=1.0 / 64, bias=epsb[:, 0:1]), reads=[gt, epsb], writes=[gt])
            S.op("act", lambda e: e.activation(out=ss8, in_=ss8, func=AF.Exp, scale=-0.5), reads=[gt], writes=[gt])
            S.op("dve", lambda e: e.tensor_tensor(out=ob32[:], in0=ob32[:], in1=bc(ss8, 8, 64), op=ALU.mult), reads=[ob32, gt], writes=[ob32])
            S.op("pool", lambda e: e.tensor_tensor(out=ocat[:, oc_slot, 512:1024].rearrange("p (a b) -> p a b", a=8), in0=ob32[:], in1=anw8[:], op=ALU.mult),
                 reads=[ob32, anw8], writes=[ocat.r[oc_slot]])

        occtr = [0]
        kvctr = [0]

        def p1_group(tls, gq, gkv, full, sample=False):
            ntl = len(tls)
            NT = ntl * 128
            nseq, L = (4, 32) if sample else (1, NT)
            for ti, tl in enumerate(tls):
                src = xs_d[:, :] if sample else xp_d[tl * 128:(tl + 1) * 128, :]
                norm_transpose(src, uT, ti * 128)
            chunks = list(range(12)) if gq else list(range(4, 12))
            for ci, c in enumerate(chunks):
                ps = next_psP()
                proj_fm(w_in_bf, c * 128, uT, NT, ps)
                eng = "dve"
                sl = cvctr[0] % 2
                cvctr[0] += 1
                if c < 8:
                    conv_silu(ps, c, nseq, L, NT, cvb[:, sl, 0:NT], [cvb.r[sl]], eng, sl)
                    if c < 4:
                        if full:
                            l2norm_chunk(cvb[:, sl, 0:NT], [cvb.r[sl]], qnT, c, NT, 128.0 ** -0.5)
                    else:
                        l2norm_chunk(cvb[:, sl, 0:NT], [cvb.r[sl]], knT, c - 4, NT, 1.0)
                else:
                    conv_silu(ps, c, nseq, L, NT, vT[:, c - 8, 0:NT], [vT], eng, sl)
            if full:
                for c in range(4):
                    ps = next_psP()
                    proj_fm(w_in_bf, C_QB + c * 128, uT, NT, ps)
                    S.op("act", (lambda e, c=c, ps=ps: e.activation(out=qbz[0:64, c, 0, 0:NT], in_=ps[0:64, 0:NT], func=AF.Copy, scale=0.125)), reads=[ps], writes=[qbz])
                    S.op("act", (lambda e, c=c, ps=ps: e.activation(out=qbz[64:128, c, 1, 0:NT], in_=ps[64:128, 0:NT], func=AF.Copy, scale=0.125)), reads=[ps], writes=[qbz])
            if gkv:
                for c in range(4):
                    ps = next_psP()
                    proj_fm(w_in_bf, C_KB + c * 128, uT, NT, ps)
                    for ti, tl in enumerate(tls):
                        slot = tl % 8
                        S.op("act", (lambda e, c=c, ps=ps, ti=ti, slot=slot: e.activation(out=kbT[:, c, slot, :], in_=ps[:, ti * 128:(ti + 1) * 128], func=AF.Copy)),
                             reads=[ps], writes=[kbT.r[slot]])
            for ti, tl in enumerate(tls):
                tcol = ti * 128
                if gkv:
                    slot = tl % 8
                    ps = next_psP()
                    proj_tm(w_in_bf, C_VB, 512, uT, tcol, ps)
                    S.op("act", (lambda e, ps=ps, slot=slot: e.activation(out=V1[:, slot], in_=ps[:].rearrange("p (a b) -> p a b", a=8), func=AF.Copy)),
                         reads=[ps], writes=[V1.r[slot]])
                    if sample:
                        S.op("pool", (lambda e, slot=slot: e.memset(vcol[:, slot:slot + 1], 1.0)), writes=[vcol.r[slot]])
                    else:
                        S.op("pool", (lambda e, slot=slot, tl=tl: e.tensor_copy(out=vcol[:, slot:slot + 1], in_=kvalid[:, tl:tl + 1])),
                             reads=[kvalid], writes=[vcol.r[slot]])
                    want_out = sample or (tl >= p_tiles - 4)
                    if want_out:
                        orow = 0 if sample else (tl - (p_tiles - 4)) * 128
                        kd, vd = (bk_s, bv_s) if sample else (bk_p, bv_p)
                        ks = kvctr[0] % 2
                        kvctr[0] += 1
                        S.op("act", (lambda e, ps=ps, ks=ks: e.activation(out=kvout[:, ks, :], in_=ps[:], func=AF.Copy)), reads=[ps], writes=[kvout.r[ks]])
                        S.dma("pool", (lambda e, ks=ks, vd=vd, orow=orow: e.dma_start(out=vd[orow:orow + 128, :], in_=kvout[:, ks, :])), reads=[kvout.r[ks]])
                        ps2 = next_psP()
                        proj_tm(w_in_bf, C_KB, 512, uT, tcol, ps2)
                        ks = kvctr[0] % 2
                        kvctr[0] += 1
                        S.op("act", (lambda e, ps2=ps2, ks=ks: e.activation(out=kvout[:, ks, :], in_=ps2[:], func=AF.Copy)), reads=[ps2], writes=[kvout.r[ks]])
                        S.dma("pool", (lambda e, ks=ks, kd=kd, orow=orow: e.dma_start(out=kd[orow:orow + 128, :], in_=kvout[:, ks, :])), reads=[kvout.r[ks]])
                if full:
                    ps = next_psP()
                    proj_tm(w_in_bf, C_Z, 512, uT, tcol, ps)
                    S.op("act", (lambda e, ps=ps, ti=ti: e.activation(out=zs[:, ti, :], in_=ps[:], func=AF.Silu)), reads=[ps], writes=[zs.r[ti]])
                proj_tm(w_in_bf, C_B, 8, uT, tcol, psM, 0)
                oc_slot = occtr[0] % 2
                if full:
                    occtr[0] += 1
                if sample:
                    gdn_tile(tcol, 4, True,
                             lambda b: (Ss[:, b], Ss.r[b]), lambda b: (Ssb[:, b], Ssb.r[b]),
                             lambda b: (Ss[:, b], Ss.r[b]), lambda b: None, ti, oc_slot)
                else:
                    gdn_tile(tcol, 2, full,
                             lambda b: (Sp[:], Sp.r[0]), lambda b: (Spb[:], Spb.r[0]),
                             lambda b: (Sp[:], Sp.r[0]), lambda b: (Spb[:], Spb.r[0]), ti, oc_slot, tag=tl)
                if full and not sample:
                    pieces = [dict(kslot=(tl - 4 + r) % 8, vslot=(tl - 4 + r) % 8, r=r) for r in range(5)]
                    attn_tile(pieces, tcol, oc_slot)
                    scr_row = tl - (p_tiles - NSCR + 1)
                    if scr_row >= 0:
                        S.dma("pool", (lambda e, oc_slot=oc_slot, scr_row=scr_row: e.dma_start(out=ocat_scr[scr_row * 128:(scr_row + 1) * 128, :], in_=ocat[:, oc_slot, :])),
                              reads=[ocat.r[oc_slot]])
                    if "ocat" in dbg and tl >= first_own:
                        S.dma("pool", (lambda e, oc_slot=oc_slot, tl=tl: e.dma_start(out=dbg["ocat"][(tl - first_own) * 128:(tl - first_own + 1) * 128, :], in_=ocat[:, oc_slot, :])),
                              reads=[ocat.r[oc_slot]])
            return oc_slot

        first_full_tile = first_own - 2
        first_kv_tile = first_full_tile - 4
        tl = p1_from
        while tl < p1_to:
            tls = list(range(tl, min(tl + NTL1, p1_to)))
            full = tls[0] >= first_full_tile
            gkv = tls[0] >= first_kv_tile
            gq = tls[0] >= first_full_tile - NTL1
            p1_group(tls, gq, gkv, full)
            tl += NTL1
        S.dma("pool", lambda e: e.dma_start(out=S_p[:, :], in_=flat(Sp[:])), reads=[Sp])
        S.dma("pool", lambda e: e.dma_start(out=qc_p[:, :].rearrange("p (a b) -> p a b", a=12), in_=qtail[:, :, 0, :]), reads=[qtail])

        if cfg.get("sample", True):
            load_small(cp, cps_d[:, :])
            load_BT(bts_d, None)
            for i, (a, b) in enumerate(((0, 1024), (1024, 2048), (2048, 3072), (3072, SMK_W))):
                S.dma("sp", (lambda e, i=i, a=a, b=b: e.dma_start(out=stage[:, i % 2, 0:b - a], in_=smk_d[:, a:b])), writes=[stage.r[i % 2]])
                S.op("dve", (lambda e, i=i, a=a, b=b: e.tensor_copy(out=smk[:, a:b], in_=stage[:, i % 2, 0:b - a])), reads=[stage.r[i % 2]], writes=[smk])
            load_small(qtail, qc0_d[:, :])
            for s in range(4):
                S.dma("sp", (lambda e, s=s: e.dma_start(out=Ss[:, s], in_=s0_d[s].rearrange("h k v -> k h v"))), writes=[Ss.r[s]])
                S.op("act", (lambda e, s=s: e.activation(out=Ssb[:, s], in_=Ss[:, s], func=AF.Copy)), reads=[Ss.r[s]], writes=[Ssb.r[s]])
            ocs = p1_group([0], True, True, True, sample=True)
            S.dma("pool", lambda e: e.dma_start(out=S_s[:, :], in_=flat(Ss[:])), reads=[Ss])
            S.dma("pool", lambda e: e.dma_start(out=qc_s[:, :], in_=flat(qtail[:])), reads=[qtail])

            def mk_prep(s, r, slot, sl):
                def prep():
                    S.dma("sp", (lambda e: e.dma_start(out=stage[:, sl, 0:512], in_=ck_d[s, r * 128:(r + 1) * 128, :])), writes=[stage.r[sl]])
                    S.dma("sp", (lambda e: e.dma_start(out=stage[:, sl, 512:1024], in_=cv_d[s, r * 128:(r + 1) * 128, :])), writes=[stage.r[sl]])
                    S.op("dve", (lambda e: e.tensor_copy(out=xn[:, 0:512], in_=stage[:, sl, 0:512])), reads=[stage.r[sl]], writes=[xn])
                    S.op("pe", [(lambda e, c=c: e.transpose(psT[:, c * 128:(c + 1) * 128], xn[:, c * 128:(c + 1) * 128], identb[:])) for c in range(4)],
                         reads=[xn, identb], writes=[psT])
                    S.op("act", (lambda e: e.activation(out=kbT[:, :, slot, :], in_=psT[:, 0:512].rearrange("p (a b) -> p a b", a=4), func=AF.Copy)),
                         reads=[psT], writes=[kbT.r[slot]])
                    S.op("act", (lambda e: e.activation(out=V1[:, slot], in_=stage[:, sl, 512:1024].rearrange("p (a b) -> p a b", a=8), func=AF.Copy)),
                         reads=[stage.r[sl]], writes=[V1.r[slot]])
                    S.op("pool", (lambda e: e.memset(vcol[:, slot:slot + 1], 1.0)), writes=[vcol.r[slot]])
                return prep

            pieces = []
            for s in range(4):
                for r in range(4):
                    i = s * 4 + r
                    slot = i % 7 + 1
                    pieces.append(dict(kslot=slot, vslot=slot, r=r, seq=s, prep=mk_prep(s, r, slot, i % 2)))
                pieces.append(dict(kslot=0, vslot=0, r=4, seq=s, newk=True))
            attn_tile(pieces, 0, ocs)
            S.dma("pool", lambda e: e.dma_start(out=ocat_scr[(NSCR - 1) * 128:NSCR * 128, :], in_=ocat[:, ocs, :]), reads=[ocat.r[ocs]])
            if "ocat_s" in dbg:
                S.dma("pool", lambda e: e.dma_start(out=dbg["ocat_s"][:, :], in_=ocat[:, ocs, :]), reads=[ocat.r[ocs]])

    if do_p2:
        S.barrier()
        aptr[0] = shared_end
        wgu_bf = SB("wgu_bf", [128, 8, 2 * D_FF], BF16)
        wd_bf = SB("wd_bf", [128, NFF, 1024], BF16)
        wout_bf = SB("wout_bf", [128, 8, 1024], BF16)
        nfpre = SB("nfpre", [128, 8])
        nmpost = SB("nmpost", [128, 1024])
        nfpost = SB("nfpost", [128, 1024])
        fw = SB("fw", [128, NFF, 3])
        fb = SB("fb", [128, NFF])
        gtail = SB("gtail", [128, NFF, 4, 2])
        oc2 = SB("oc2", [128, 2, 1024], BF16, nslots=2)
        oT = SB("oT", [128, 8, 128], BF16)
        x1 = SB("x1", [128, 1024])
        u2T = SB("u2T", [128, 8, 128], BF16)
        gxp = SB("gxp", [128, 2, 144], F32, nslots=2)
        cv2 = SB("cv2", [128, 2, 128], F32, nslots=2)
        t2 = SB("t2", [128, 2, 128], F32, nslots=2)
        sg2 = SB("sg2", [128, 2, 128], F32, nslots=2)
        hT = SB("hT", [128, NFF, 128], BF16)
        yb = SB("yb", [128, 2, 1024], F32, nslots=2)
        st2 = SB("st2", [128, 8])
        print("phase2 arena used", aptr[0], "of", ARENA_F32)

        load_small(nfpre, nfpre_d[:, :])
        load_small(nmpost, nmpost_d[:, :])
        load_small(nfpost, nfpost_d[:, :])
        load_small(fw, fw_d[:, :])
        load_small(fb, fb_d[:, :])
        S.op("pool", lambda e: e.memset(flat(gtail[:]), 0.0), writes=[gtail])
        if not do_p1:
            S.dma("sp", lambda e: e.dma_start(out=stage[:, 0, 0:128], in_=cpp_d[:, CP_IDENT:CP_IDENT + 128]), writes=[stage.r[0]])
            S.op("dve", lambda e: e.tensor_copy(out=identb[:], in_=stage[:, 0, 0:128]), reads=[stage.r[0]], writes=[identb])
        load_cast_weight(lambda k, c0, c1: wout_bf[:, k, c0:c1], lambda k, c0, c1: w_out_d[k * 128:(k + 1) * 128, c0:c1], 1024, 8, None, wout_bf)
        sfn2 = lambda k: nfpre[:, k:k + 1]
        sfn2.buf = nfpre
        load_cast_weight(lambda k, c0, c1: wgu_bf[:, k, c0:c1], lambda k, c0, c1: w_gu_d[k * 128:(k + 1) * 128, c0:c1], 2 * D_FF, 8, sfn2, wgu_bf)
        load_cast_weight(lambda k, c0, c1: wd_bf[:, k, c0:c1], lambda k, c0, c1: w_d_d[k * 128:(k + 1) * 128, c0:c1], 1024, NFF, None, wd_bf)

        o2ctr = [0]
        yctr = [0]
        fctr = [0]

        def rms_finish(ps_list, resid_ap, resid_res, gain, dst_ap, dst_res):
            for i, ps in enumerate(ps_list):
                S.op("act", (lambda e, i=i, ps=ps: e.activation(out=yjunk[:, i * 512:(i + 1) * 512], in_=ps[:], func=AF.Square, accum_out=st2[:, i:i + 1])),
                     reads=[ps], writes=[st2, yjunk_b])
            S.op("dve", lambda e: e.tensor_tensor(out=st2[:, 2:3], in0=st2[:, 0:1], in1=st2[:, 1:2], op=ALU.add), reads=[st2], writes=[st2])
            S.op("act", lambda e: e.activation(out=st2[:, 3:4], in_=st2[:, 2:3], func=AF.Ln, scale=1.0 / 1024, bias=epsb[:, 0:1]), reads=[st2, epsb], writes=[st2])
            S.op("act", lambda e: e.activation(out=st2[:, 4:5], in_=st2[:, 3:4], func=AF.Exp, scale=-0.5), reads=[st2], writes=[st2])
            for i, ps in enumerate(ps_list):
                S.op("dve", (lambda e, i=i, ps=ps: e.scalar_tensor_tensor(out=dst_ap[:, i * 512:(i + 1) * 512], in0=ps[:], scalar=st2[:, 4:5], in1=gain[:, i * 512:(i + 1) * 512],
                                                                         op0=ALU.mult, op1=ALU.mult)),
                     reads=[ps, st2, gain], writes=dst_res)
                S.op("pool", (lambda e, i=i: e.tensor_tensor(out=dst_ap[:, i * 512:(i + 1) * 512], in0=dst_ap[:, i * 512:(i + 1) * 512], in1=resid_ap[:, i * 512:(i + 1) * 512], op=ALU.add)),
                     reads=dst_res + resid_res, writes=dst_res)

        yjunk_b = xn
        yjunk = xn[:]

        def p2_tile(scr_row, x_src, y_dst, nseq, L):
            NT = 128
            osl = o2ctr[0] % 2
            o2ctr[0] += 1
            S.dma("sp", lambda e: e.dma_start(out=oc2[:, osl, :], in_=ocat_scr[scr_row * 128:(scr_row + 1) * 128, :]), writes=[oc2.r[osl]])
            xsl = xctr[0] % 2
            xctr[0] += 1
            S.dma("sp", lambda e: e.dma_start(out=xin[:, xsl, :], in_=x_src), writes=[xin.r[xsl]])
            S.op("pe", [(lambda e, k=k: e.transpose(psT[:, k * 128:(k + 1) * 128], oc2[:, osl, k * 128:(k + 1) * 128], identb[:])) for k in range(8)],
                 reads=[oc2.r[osl], identb], writes=[psT])
            S.op("act", lambda e: e.activation(out=oT[:], in_=psT[:].rearrange("p (a b) -> p a b", a=8), func=AF.Copy), reads=[psT], writes=[oT])
            proj_tm(wout_bf, 0, 512, oT, 0, psA)
            proj_tm(wout_bf, 512, 512, oT, 0, psB)
            rms_finish([psA, psB], xin[:, xsl, :], [xin.r[xsl]], nmpost, x1[:], [x1])
            rms_stats(x1[:], [x1], xn[:], [xn], 1024)
            S.op("dve", lambda e: e.tensor_scalar(out=xn[:], in0=x1[:], scalar1=st[:, 2:3], scalar2=None, op0=ALU.mult), reads=[x1, st], writes=[xn])
            S.op("pe", [(lambda e, k=k: e.transpose(psT[:, k * 128:(k + 1) * 128], xn[:, k * 128:(k + 1) * 128], identb[:])) for k in range(8)],
                 reads=[xn, identb], writes=[psT])
            S.op("act", lambda e: e.activation(out=u2T[:], in_=psT[:].rearrange("p (a b) -> p a b", a=8), func=AF.Copy), reads=[psT], writes=[u2T])
            for c in range(NFF):
                psg = next_psP()
                proj_fm(wgu_bf, c * 128, u2T, NT, psg)
                fsl = fctr[0] % 2
                fctr[0] += 1
                gx = gxp[:, fsl, 0:nseq * (L + 2)].rearrange("p (s t) -> p s t", s=nseq)
                cvv = cv2[:, fsl, :].rearrange("p (s t) -> p s t", s=nseq)
                S.op("pool", (lambda e, c=c, gx=gx: e.tensor_copy(out=gx[:, :, 0:2], in_=gtail[:, c, 0:nseq, :])), reads=[gtail], writes=[gxp.r[fsl]])
                S.op("act", (lambda e, psg=psg, gx=gx: e.activation(out=gx[:, :, 2:2 + L], in_=psg[:, 0:NT].rearrange("p (s t) -> p s t", s=nseq), func=AF.Copy)),
                     reads=[psg], writes=[gxp.r[fsl]])
                S.op("pool", (lambda e, c=c, gx=gx: e.tensor_copy(out=gtail[:, c, 0:nseq, :], in_=gx[:, :, L:L + 2])), reads=[gxp.r[fsl]], writes=[gtail])
                psu = next_psP()
                proj_fm(wgu_bf, D_FF + c * 128, u2T, NT, psu)
                S.op("dve", (lambda e, c=c, gx=gx, cvv=cvv: e.tensor_scalar(out=cvv, in0=gx[:, :, 0:L], scalar1=fw[:, c, 0:1], scalar2=fb[:, c:c + 1], op0=ALU.mult, op1=ALU.add)),
                     reads=[gxp.r[fsl], fw, fb], writes=[cv2.r[fsl]])
                for i in (1, 2):
                    S.op("dve", (lambda e, c=c, i=i, gx=gx, cvv=cvv: e.scalar_tensor_tensor(out=cvv, in0=gx[:, :, i:i + L], scalar=fw[:, c, i:i + 1], in1=cvv, op0=ALU.mult, op1=ALU.add)),
                         reads=[gxp.r[fsl], fw, cv2.r[fsl]], writes=[cv2.r[fsl]])
                S.op("act", (lambda e, fsl=fsl: e.activation(out=t2[:, fsl, :], in_=cv2[:, fsl, :], func=AF.Square)), reads=[cv2.r[fsl]], writes=[t2.r[fsl]])
                S.op("pool", (lambda e, fsl=fsl: e.tensor_scalar(out=t2[:, fsl, :], in0=t2[:, fsl, :], scalar1=0.044715, scalar2=1.0, op0=ALU.mult, op1=ALU.add)),
                     reads=[t2.r[fsl]], writes=[t2.r[fsl]])
                S.op("pool", (lambda e, fsl=fsl: e.tensor_tensor(out=t2[:, fsl, :], in0=t2[:, fsl, :], in1=cv2[:, fsl, :], op=ALU.mult)),
                     reads=[t2.r[fsl], cv2.r[fsl]], writes=[t2.r[fsl]])
                S.op("act", (lambda e, fsl=fsl: e.activation(out=sg2[:, fsl, :], in_=t2[:, fsl, :], func=AF.Sigmoid, scale=1.5957691216057308)), reads=[t2.r[fsl]], writes=[sg2.r[fsl]])
                S.op("pool", (lambda e, fsl=fsl: e.tensor_tensor(out=sg2[:, fsl, :], in0=sg2[:, fsl, :], in1=cv2[:, fsl, :], op=ALU.mult)),
                     reads=[sg2.r[fsl], cv2.r[fsl]], writes=[sg2.r[fsl]])
                S.op("dve", (lambda e, c=c, fsl=fsl, psu=psu: e.tensor_tensor(out=hT[:, c, :], in0=psu[:, 0:NT], in1=sg2[:, fsl, :], op=ALU.mult)),
                     reads=[psu, sg2.r[fsl]], writes=[hT])
            S.op("pe", [(lambda e, c=c: e.matmul(psA[:], lhsT=hT[:, c, :], rhs=wd_bf[:, c, 0:512], start=(c == 0), stop=(c == NFF - 1))) for c in range(NFF)],
                 reads=[hT, wd_bf], writes=[psA])
            S.op("pe", [(lambda e, c=c: e.matmul(psB[:], lhsT=hT[:, c, :], rhs=wd_bf[:, c, 512:1024], start=(c == 0), stop=(c == NFF - 1))) for c in range(NFF)],
                 reads=[hT, wd_bf], writes=[psB])
            ysl = yctr[0] % 2
            yctr[0] += 1
            rms_finish([psA, psB], x1[:], [x1], nfpost, yb[:, ysl, :], [yb.r[ysl]])
            if y_dst is not None:
                S.dma("pool", lambda e: e.dma_start(out=y_dst, in_=yb[:, ysl, :]), reads=[yb.r[ysl]])

        p2_from = cfg.get("p2_from", first_own - 1)
        p2_to = cfg.get("p2_to", p_tiles)
        for tl in range(p2_from, p2_to):
            own = tl - first_own
            p2_tile(tl - (p_tiles - NSCR + 1), xp_d[tl * 128:(tl + 1) * 128, :],
                    y_p[own * 128:(own + 1) * 128, :] if own >= 0 else None, 1, 128)
        S.dma("pool", lambda e: e.dma_start(out=fc_p[:, :].rearrange("p (a b) -> p a b", a=NFF), in_=gtail[:, :, 0, :]), reads=[gtail])
        if cfg.get("sample", True):
            load_small(gtail, fc0_d[:, :])
            p2_tile(NSCR - 1, xs_d[:, :], y_s[:, :], 4, 32)
            S.dma("pool", lambda e: e.dma_start(out=fc_s[:, :], in_=flat(gtail[:])), reads=[gtail])

    S.finish()
    with nc.Block() as block:
        @block.tensor
        def _(e):
            S.replay("pe", e)

        @block.vector
        def _(e):
            S.replay("dve", e)

        @block.scalar
        def _(e):
            S.replay("act", e)

        @block.gpsimd
        def _(e):
            S.replay("pool", e)

        @block.sync
        def _(e):
            S.replay("sp", e)
    es.close()
    print("instructions:", S.ninstr, {k: len(v) for k, v in S.ops.items()}, "sems", len(S.sems))
    return nc


def _cpack(bs):
    m = np.arange(128)
    same = (m[:, None] // bs) == (m[None, :] // bs)
    tri = same & (m[:, None] <= m[None, :])
    strict = same & (m[:, None] > m[None, :])
    incl = same & (m[:, None] >= m[None, :])
    ident = np.eye(128, dtype=bool)
    nb = 128 // bs
    bm = np.zeros((128, 4), np.float32)
    cm = np.zeros((128, 4, 128), np.float32)
    for b in range(nb):
        bm[b * bs:(b + 1) * bs, b] = 1.0
        cm[:, b, :] = bm[:, b:b + 1]
    parts = [tri, same, np.tile(strict, (1, 4)), np.tile(incl, (1, 4)), np.tile(ident, (1, 4)), bm, -bm, cm.reshape(128, 512)]
    out = np.concatenate([np.asarray(p, np.float32) for p in parts], axis=1)
    assert out.shape == (128, CP_W)
    return np.ascontiguousarray(out)


def _bias_tables(rel_bias):
    tab = np.asarray(rel_bias, np.float32)
    kj = np.arange(128)[:, None, None]
    r = np.arange(5)[None, :, None]
    qi = np.arange(128)[None, None, :]
    rel_p = (4 - r) * 128 + qi - kj
    idx_p = np.clip(rel_p, -128, 128) + 128
    btp = tab[:, idx_p].transpose(1, 2, 0, 3)
    mask = ((r == 4) & (kj >= 64) & (qi < 64)) | ((r == 0) & (kj < 64) & (qi >= 64))
    btm = np.where(mask, np.float32(NEG), np.float32(0.0)).astype(np.float32)
    btm = np.broadcast_to(btm[:, :, None, :], (128, 5, 8, 128))
    rel_s = np.where(r < 4, (4 - r) * 128 + (qi % 32) - kj, (qi % 32) - (kj % 32))
    idx_s = np.clip(rel_s, -128, 128) + 128
    bts = tab[:, idx_s].transpose(1, 2, 0, 3)
    f = lambda a: np.ascontiguousarray(np.asarray(a, np.float32).reshape(128, 5 * 8 * 128))
    return f(btp), f(btm), f(bts)


def _smk():
    out = np.zeros((128, SMK_W), np.float32)
    q = np.arange(128)
    for s in range(4):
        cm = np.where(q // 32 == s, 0.0, NEG).astype(np.float32)
        out[0, s * 512:(s + 1) * 512] = np.tile(cm, 4)
        out[0, 2048 + s * 128:2048 + (s + 1) * 128] = cm
    out[0, 2560:2688] = 1.0
    out[0, 2688:3200] = 1.0
    return out


def make_in_maps(inp):
    f32 = lambda a: np.ascontiguousarray(np.asarray(a, np.float32))
    xpr, xsm = f32(inp["x_prompt"]), f32(inp["x_sample"])
    ckf = f32(inp["cache_band_k"])[0].reshape(32, 512, 512)
    cvf = f32(inp["cache_band_v"])[0].reshape(32, 512, 512)
    sdl = f32(inp["state_delta"])[0]
    sqc = f32(inp["state_qkv_conv"])[0]
    sfc = f32(inp["state_ffn_conv"])[0]
    rep = lambda v, n: np.ascontiguousarray(np.broadcast_to(f32(v).reshape(1, -1), (128, n)))
    pk = lambda v: np.ascontiguousarray(f32(v).reshape(-1, 128).T)
    btp, btm, bts = _bias_tables(inp["rel_bias"][0])
    common = dict(
        w_in=f32(inp["w_in"][0]), w_out=f32(inp["w_out"][0]), w_gu=f32(inp["w_gate_up"][0]), w_d=f32(inp["w_down"][0]),
        nmpre=pk(inp["norm_mix_pre"][0]), nfpre=pk(inp["norm_ffn_pre"][0]),
        nmpost=rep(inp["norm_mix_post"][0], 1024), nfpost=rep(inp["norm_ffn_post"][0], 1024),
        cw=np.ascontiguousarray(f32(inp["qkv_conv_w"][0]).reshape(4, 12, 128).transpose(2, 1, 0).reshape(128, 48)),
        fw=np.ascontiguousarray(f32(inp["ffn_conv_w"][0]).reshape(3, NFF, 128).transpose(2, 1, 0).reshape(128, NFF * 3)),
        fb=pk(inp["ffn_conv_b"][0]),
        alog=rep(inp["a_log"][0], 4), dtb=rep(inp["dt_bias"][0], 4),
        gnw=rep(np.tile(f32(inp["gdn_norm_w"][0]), 4), 512), anw=rep(np.tile(f32(inp["attn_norm_w"][0]), 8), 512),
        btp=btp, btm=btm, bts=bts, cpp=_cpack(64), cps=_cpack(32), smk=_smk(),
    )
    maps = []
    for c in range(8):
        s, half = c // 2, c % 2
        T0 = half * 2048
        xw = np.zeros((4096, 1024), np.float32)
        if half == 0:
            xw[2048:] = xpr[s, 0:2048]
        else:
            xw[:] = xpr[s]
        pos = T0 - 2048 + np.arange(4096)
        kval = (pos >= 0).astype(np.float32).reshape(32, 128).T
        sq = slice(4 * c, 4 * c + 4)
        m = dict(common)
        m.update(
            xp=xw, xs=np.ascontiguousarray(xsm[sq].reshape(128, 1024)), kvalid=np.ascontiguousarray(kval),
            ck=np.ascontiguousarray(ckf[sq]), cv=np.ascontiguousarray(cvf[sq]), s0=np.ascontiguousarray(sdl[sq]),
            qc0=np.ascontiguousarray(sqc[sq].reshape(4, 3, 12, 128).transpose(3, 2, 0, 1).reshape(128, 144)),
            fc0=np.ascontiguousarray(sfc[sq].reshape(4, 2, NFF, 128).transpose(3, 2, 0, 1).reshape(128, NFF * 8)),
        )
        maps.append(m)
    return maps


def assemble(results):
    y_prompt = np.zeros((4, 4096, 1024), np.float32)
    y_sample = np.zeros((32, 32, 1024), np.float32)
    bkp = np.zeros((1, 4, 512, 8, 64), np.float32)
    bvp = np.zeros((1, 4, 512, 8, 64), np.float32)
    dlp = np.zeros((1, 4, 4, 128, 128), np.float32)
    qcp = np.zeros((1, 4, 3, 1536), np.float32)
    fcp = np.zeros((1, 4, 2, D_FF), np.float32)
    bks = np.zeros((1, 32, 32, 8, 64), np.float32)
    bvs = np.zeros((1, 32, 32, 8, 64), np.float32)
    dls = np.zeros((1, 32, 4, 128, 128), np.float32)
    qcs = np.zeros((1, 32, 3, 1536), np.float32)
    fcs = np.zeros((1, 32, 2, D_FF), np.float32)
    for c, r in enumerate(results):
        s, half = c // 2, c % 2
        y_prompt[s, half * 2048:(half + 1) * 2048] = r["y_p"]
        if half == 1:
            bkp[0, s] = r["bk_p"].reshape(512, 8, 64)
            bvp[0, s] = r["bv_p"].reshape(512, 8, 64)
            dlp[0, s] = r["S_p"].reshape(128, 4, 128).transpose(1, 0, 2)
            qcp[0, s] = r["qc_p"].reshape(128, 12, 3).transpose(2, 1, 0).reshape(3, 1536)
            fcp[0, s] = r["fc_p"].reshape(128, NFF, 2).transpose(2, 1, 0).reshape(2, D_FF)
        sq = slice(4 * c, 4 * c + 4)
        y_sample[sq] = r["y_s"].reshape(4, 32, 1024)
        bks[0, sq] = r["bk_s"].reshape(4, 32, 8, 64)
        bvs[0, sq] = r["bv_s"].reshape(4, 32, 8, 64)
        dls[0, sq] = r["S_s"].reshape(128, 4, 4, 128).transpose(1, 2, 0, 3)
        qcs[0, sq] = r["qc_s"].reshape(128, 12, 4, 3).transpose(2, 3, 1, 0).reshape(4, 3, 1536)
        fcs[0, sq] = r["fc_s"].reshape(128, NFF, 4, 2).transpose(2, 3, 1, 0).reshape(4, 2, D_FF)
    return (y_prompt, y_sample, bkp, bvp, dlp, qcp, fcp, bks, bvs, dls, qcs, fcs)


def kernel(**inputs):
    nc = build_program()
    maps = make_in_maps(inputs)
    res = run_bass_kernel_spmd(nc, maps, core_ids=list(range(8)))
    return assemble(res.results)
```

```python
import os
import numpy as np
from contextlib import ExitStack
import ml_dtypes
import concourse.bass as bass
import concourse.mybir as mybir
from concourse.bass_utils import run_bass_kernel_spmd

F32 = mybir.dt.float32
BF16 = mybir.dt.bfloat16
AF = mybir.ActivationFunctionType
ALU = mybir.AluOpType

D_MODEL = 1024
SEQ = 4096
GDN_H = 4
ATT_H = 8
D_FF = 2816
NFF = 22
IN_COLS = 3592
EPS = 1e-6
NEG = -30000.0
C_Q, C_K, C_V, C_Z, C_B, C_A, C_QB, C_KB, C_VB = 0, 512, 1024, 1536, 2048, 2052, 2056, 2568, 3080
CP_TRI, CP_SAME, CP_STRICT, CP_INCL, CP_IDENT, CP_BM, CP_NBM, CP_CM = 0, 128, 256, 768, 1280, 1792, 1796, 1800
CP_W = 1800 + 512


class Res:
    __slots__ = ("name", "lw", "rd")

    def __init__(self, name):
        self.name = name
        self.lw = None
        self.rd = {}


class Buf:
    def __init__(self, h, name, nslots=1):
        self.h = h
        self.r = [Res(f"{name}.{i}") for i in range(nslots)]

    def __getitem__(self, idx):
        return self.h[idx]


def _res(x):
    out = []
    for a in x:
        if isinstance(a, Buf):
            out.extend(a.r)
        else:
            out.append(a)
    return out


class Sched:
    CE = ("pe", "dve", "act", "pool")
    ROT = 20000

    def __init__(self, nc, es):
        self.nc, self.es = nc, es
        self.ops = {e: [] for e in ("pe", "dve", "act", "pool", "sp")}
        self.sems = []
        self.cur = {}
        self.cnt = {}
        for e in self.CE:
            self._newsem(e)
        self.waited = {e: {} for e in self.ops}
        self.dpool = {q: [] for q in ("sp", "pool", "act")}
        self.dnext = {q: 0 for q in self.dpool}
        for q, n in (("sp", 12), ("pool", 12), ("act", 4)):
            for i in range(n):
                s = es.enter_context(nc.semaphore(f"d_{q}{i}"))
                self.sems.append(s)
                self.dpool[q].append([len(self.sems) - 1, 0])
        self.ninstr = 0

    def _newsem(self, e):
        s = self.es.enter_context(self.nc.semaphore(f"s_{e}{len(self.sems)}"))
        self.sems.append(s)
        self.cur[e] = len(self.sems) - 1
        self.cnt[e] = 0

    def _deps(self, eng, reads, writes, strict=True):
        deps = {}

        def add(t):
            if t is None:
                return
            if deps.get(t[0], -1) < t[1]:
                deps[t[0]] = t[1]
        for r in reads:
            add(r.lw)
        for w in writes:
            add(w.lw)
            for k, v in w.rd.items():
                add((k, v))
        waits = []
        wd = self.waited[eng]
        for k, v in deps.items():
            if not strict and eng in self.cur and k == self.cur[eng]:
                continue
            if wd.get(k, -1) >= v:
                continue
            wd[k] = v
            waits.append((k, v))
        return waits

    def _mark(self, ticket, reads, writes):
        for w in writes:
            w.lw = ticket
            w.rd = {}
        for r in reads:
            if r.rd.get(ticket[0], -1) < ticket[1]:
                r.rd[ticket[0]] = ticket[1]

    def op(self, eng, fns, reads=(), writes=()):
        reads, writes = _res(reads), _res(writes)
        if not isinstance(fns, (list, tuple)):
            fns = [fns]
        waits = self._deps(eng, reads, writes, strict=(eng != "pe"))
        if self.cnt[eng] >= self.ROT:
            self._newsem(eng)
        self.cnt[eng] += 1
        ticket = (self.cur[eng], self.cnt[eng])
        n = len(fns)
        for i, f in enumerate(fns):
            self.ops[eng].append((waits if i == 0 else (), f, ticket[0] if i == n - 1 else None, 1))
        self.ninstr += n
        self._mark(ticket, reads, writes)

    def dma(self, q, fn, reads=(), writes=()):
        reads, writes = _res(reads), _res(writes)
        waits = self._deps(q, reads, writes)
        pool = self.dpool[q]
        slot = pool[self.dnext[q] % len(pool)]
        self.dnext[q] += 1
        wd = self.waited[q]
        if slot[1] > 0 and wd.get(slot[0], -1) < slot[1]:
            wd[slot[0]] = slot[1]
            waits.append((slot[0], slot[1]))
        slot[1] += 16
        ticket = (slot[0], slot[1])
        self.ops[q].append((waits, fn, slot[0], 16))
        self.ninstr += 1
        self._mark(ticket, reads, writes)

    def barrier(self):
        allw = []
        for q in self.dpool:
            for si, tgt in self.dpool[q]:
                if tgt > 0:
                    allw.append((si, tgt))
        for e in self.CE:
            if self.cnt[e] > 0:
                allw.append((self.cur[e], self.cnt[e]))
        for eng in self.ops:
            wd = self.waited[eng]
            waits = [(k, v) for k, v in allw if wd.get(k, -1) < v]
            for k, v in waits:
                wd[k] = v
            self.ops[eng].append((waits, None, None, 0))

    def finish(self):
        waits = []
        for q in self.dpool:
            for si, tgt in self.dpool[q]:
                if tgt > 0:
                    waits.append((si, tgt))
        for e in self.CE:
            if self.cnt[e] > 0:
                waits.append((self.cur[e], self.cnt[e]))
        self.ops["sp"].append((waits, None, None, 0))

    def replay(self, eng, e):
        sems = self.sems
        for waits, fn, inc, amt in self.ops[eng]:
            for k, v in waits:
                e.wait_ge(sems[k], v)
            if fn is None:
                continue
            ins = fn(e)
            if inc is not None:
                ins.then_inc(sems[inc], amt)


ARENA_F32 = 53000
SMK_W = 4 * 512 + 4 * 128 + 128 + 512
NSCR = 19


def build_program(cfg=None):
    cfg = cfg or {}
    nc = bass.Bass("TRN2", target_bir_lowering=False)
    es = ExitStack()

    def DI(name, shape, dt=F32):
        return nc.dram_tensor(name, list(shape), dt, kind="ExternalInput").ap()

    def DO(name, shape, dt=F32):
        return nc.dram_tensor(name, list(shape), dt, kind="ExternalOutput").ap()

    xp_d = DI("xp", [4096, 1024])
    xs_d = DI("xs", [128, 1024])
    kvalid_d = DI("kvalid", [128, 32])
    ck_d = DI("ck", [4, 512, 512])
    cv_d = DI("cv", [4, 512, 512])
    s0_d = DI("s0", [4, 4, 128, 128])
    qc0_d = DI("qc0", [128, 12 * 4 * 3])
    fc0_d = DI("fc0", [128, NFF * 4 * 2])
    w_in_d = DI("w_in", [1024, IN_COLS])
    w_out_d = DI("w_out", [1024, 1024])
    w_gu_d = DI("w_gu", [1024, 2 * D_FF])
    w_d_d = DI("w_d", [D_FF, 1024])
    nmpre_d = DI("nmpre", [128, 8])
    nfpre_d = DI("nfpre", [128, 8])
    nmpost_d = DI("nmpost", [128, 1024])
    nfpost_d = DI("nfpost", [128, 1024])
    cw_d = DI("cw", [128, 48])
    fw_d = DI("fw", [128, NFF * 3])
    fb_d = DI("fb", [128, NFF])
    alog_d = DI("alog", [128, 4])
    dtb_d = DI("dtb", [128, 4])
    gnw_d = DI("gnw", [128, 512])
    anw_d = DI("anw", [128, 512])
    btp_d = DI("btp", [128, 5 * 8 * 128])
    btm_d = DI("btm", [128, 5 * 8 * 128])
    bts_d = DI("bts", [128, 5 * 8 * 128])
    cpp_d = DI("cpp", [128, CP_W])
    cps_d = DI("cps", [128, CP_W])
    smk_d = DI("smk", [128, SMK_W])

    y_p = DO("y_p", [2048, 1024])
    bk_p = DO("bk_p", [512, 512])
    bv_p = DO("bv_p", [512, 512])
    S_p = DO("S_p", [128, 512])
    qc_p = DO("qc_p", [128, 36])
    fc_p = DO("fc_p", [128, NFF * 2])
    y_s = DO("y_s", [128, 1024])
    bk_s = DO("bk_s", [128, 512])
    bv_s = DO("bv_s", [128, 512])
    S_s = DO("S_s", [128, 4 * 512])
    qc_s = DO("qc_s", [128, 12 * 4 * 3])
    fc_s = DO("fc_s", [128, NFF * 4 * 2])
    ocat_scr = nc.dram_tensor("ocat_scr", [NSCR * 128, 1024], BF16).ap()
    dbg = {}
    for name, shape in (cfg.get("dbg") or {}).items():
        dbg[name] = DO("dbg_" + name, shape, BF16 if name.startswith(("ocat", "bf_")) else F32)

    S = Sched(nc, es)
    dump_tile = cfg.get("dump_tile", -1)

    def DUMP(name, ap, res, tag):
        if name in dbg and tag == dump_tile:
            S.dma("pool", lambda e: e.dma_start(out=dbg[name][:, :], in_=ap), reads=res)

    arena = es.enter_context(nc.sbuf_tensor("arena", [128, ARENA_F32], F32))
    aptr = [0]

    def SB(name, shape, dt=F32, nslots=1):
        n = int(np.prod(shape[1:]))
        nf = n if dt == F32 else (n + 1) // 2
        nf = (nf + 1) // 2 * 2
        off = aptr[0]
        aptr[0] += nf
        assert aptr[0] <= ARENA_F32, f"SBUF arena overflow at {name}: {aptr[0]}"
        v = arena[:, off:off + nf]
        if dt != F32:
            v = v.bitcast(dt)[:, 0:n]
        if len(shape) == 3:
            v = v.rearrange("p (a b) -> p a b", a=shape[1])
        elif len(shape) == 4:
            v = v.rearrange("p (a b c) -> p a b c", a=shape[1], b=shape[2])
        return Buf(v, name, nslots)

    def flat(ap):
        nd = len(ap.shape)
        if nd == 2:
            return ap
        if nd == 3:
            return ap.rearrange("p a b -> p (a b)")
        return ap.rearrange("p a b c -> p (a b c)")

    def PS(name, shape, dt=F32):
        h = es.enter_context(nc.psum_tensor(name, list(shape), dt))
        return Buf(h, name, 1)

    psT = PS("psT", [128, 1024], BF16)
    psP = [PS("psP0", [128, 512]), PS("psP1", [128, 512])]
    psA = PS("psA", [128, 512])
    psB = PS("psB", [128, 512])
    psS = PS("psS", [128, 512])
    psO = PS("psO", [128, 512])
    psM = PS("psM", [128, 512])
    pctr = [0]

    def next_psP():
        pctr[0] += 1
        return psP[pctr[0] % 2]

    identb = SB("identb", [128, 128], BF16)
    stage = SB("stage", [128, 2, 1024], F32, nslots=2)
    xin = SB("xin", [128, 2, 1024], F32, nslots=2)
    xn = SB("xn", [128, 1024], BF16)
    st = SB("st", [128, 8])
    epsb = SB("epsb", [128, 2])
    shared_end = aptr[0]

    stctr = [0]

    def load_cast_weight(dst_fn, src_ap_fn, ncols, nk, scale_ap_fn, dstbuf):
        for k in range(nk):
            for c0 in range(0, ncols, 1024):
                c1 = min(ncols, c0 + 1024)
                sl = stctr[0] % 2
                stctr[0] += 1
                w = c1 - c0
                S.dma("sp", (lambda e, sl=sl, k=k, c0=c0, c1=c1, w=w: e.dma_start(out=stage[:, sl, 0:w], in_=src_ap_fn(k, c0, c1))),
                      writes=[stage.r[sl]])
                sc = scale_ap_fn(k) if scale_ap_fn else None
                rds = [stage.r[sl]] + ([scale_ap_fn.buf] if scale_ap_fn else [])
                if stctr[0] % 2:
                    if sc is None:
                        S.op("act", (lambda e, sl=sl, k=k, c0=c0, c1=c1, w=w: e.activation(out=dst_fn(k, c0, c1), in_=stage[:, sl, 0:w], func=AF.Copy)),
                             reads=rds, writes=[dstbuf])
                    else:
                        S.op("act", (lambda e, sl=sl, k=k, c0=c0, c1=c1, w=w, sc=sc: e.activation(out=dst_fn(k, c0, c1), in_=stage[:, sl, 0:w], func=AF.Copy, scale=sc)),
                             reads=rds, writes=[dstbuf])
                else:
                    if sc is None:
                        S.op("dve", (lambda e, sl=sl, k=k, c0=c0, c1=c1, w=w: e.tensor_copy(out=dst_fn(k, c0, c1), in_=stage[:, sl, 0:w])),
                             reads=rds, writes=[dstbuf])
                    else:
                        S.op("dve", (lambda e, sl=sl, k=k, c0=c0, c1=c1, w=w, sc=sc: e.tensor_scalar(out=dst_fn(k, c0, c1), in0=stage[:, sl, 0:w], scalar1=sc, scalar2=None, op0=ALU.mult)),
                             reads=rds, writes=[dstbuf])

    def load_small(buf, dram_ap):
        S.dma("sp", (lambda e: e.dma_start(out=flat(buf[:]), in_=dram_ap)), writes=[buf])

    def rms_stats(src_ap, src_res, junk_ap, junk_res, ncols):
        S.op("act", lambda e: e.activation(out=junk_ap, in_=src_ap, func=AF.Square, accum_out=st[:, 0:1]),
             reads=src_res, writes=[st] + junk_res)
        S.op("act", lambda e: e.activation(out=st[:, 1:2], in_=st[:, 0:1], func=AF.Ln, scale=1.0 / ncols, bias=epsb[:, 0:1]),
             reads=[st, epsb], writes=[st])
        S.op("act", lambda e: e.activation(out=st[:, 2:3], in_=st[:, 1:2], func=AF.Exp, scale=-0.5),
             reads=[st], writes=[st])

    S.op("pool", lambda e: e.memset(epsb[:, 0:1], EPS), writes=[epsb])
    S.op("pool", lambda e: e.memset(epsb[:, 1:2], 1.0), writes=[epsb])

    do_p1 = cfg.get("p1", True)
    do_p2 = cfg.get("p2", True)
    NTL1 = cfg.get("ntl1", 2)
    p_tiles = 32
    first_own = 16
    p1_from = cfg.get("p1_from", 0)
    p1_to = cfg.get("p1_to", p_tiles)
    xctr = [0]

    def norm_transpose(src_dram_ap, dstT, col0, junk=None):
        sl = xctr[0] % 2
        xctr[0] += 1
        S.dma("sp", lambda e: e.dma_start(out=xin[:, sl, :], in_=src_dram_ap), writes=[xin.r[sl]])
        rms_stats(xin[:, sl, :], [xin.r[sl]], xn[:], [xn], 1024)
        S.op("dve", lambda e: e.tensor_scalar(out=xn[:], in0=xin[:, sl, :], scalar1=st[:, 2:3], scalar2=None, op0=ALU.mult),
             reads=[xin.r[sl], st], writes=[xn])
        S.op("pe", [(lambda e, k=k: e.transpose(psT[:, k * 128:(k + 1) * 128], xn[:, k * 128:(k + 1) * 128], identb[:])) for k in range(8)],
             reads=[xn, identb], writes=[psT])
        S.op("act", lambda e: e.activation(out=dstT[:, :, col0:col0 + 128], in_=psT[:].rearrange("p (a b) -> p a b", a=8), func=AF.Copy),
             reads=[psT], writes=[dstT])
        return sl

    def proj_fm(wbuf, wcol0, rhsT, NT, ps, nk=8):
        S.op("pe", [(lambda e, k=k: e.matmul(ps[:, 0:NT], lhsT=wbuf[:, k, wcol0:wcol0 + 128], rhs=rhsT[:, k, 0:NT], start=(k == 0), stop=(k == nk - 1))) for k in range(nk)],
             reads=[wbuf, rhsT], writes=[ps])

    def proj_tm(wbuf, wcol0, ncols, lhsT_buf, col0, ps, pcol0=0, nk=8):
        S.op("pe", [(lambda e, k=k: e.matmul(ps[:, pcol0:pcol0 + ncols], lhsT=lhsT_buf[:, k, col0:col0 + 128], rhs=wbuf[:, k, wcol0:wcol0 + ncols], start=(k == 0), stop=(k == nk - 1))) for k in range(nk)],
             reads=[wbuf, lhsT_buf], writes=[ps])

    if do_p1:
        aptr[0] = shared_end
        NT1 = NTL1 * 128
        cp = SB("cp", [128, CP_W])
        tri2 = cp[:, CP_TRI:CP_TRI + 128]
        same2 = cp[:, CP_SAME:CP_SAME + 128]
        identf = cp[:, CP_IDENT:CP_IDENT + 128]

        def c4(off):
            return cp[:, off:off + 512].rearrange("p (a b) -> p a b", a=4)

        strict4, incl4, ident4, cm4 = c4(CP_STRICT), c4(CP_INCL), c4(CP_IDENT), c4(CP_CM)
        w_in_bf = SB("w_in_bf", [128, 8, IN_COLS], BF16)
        nmpre = SB("nmpre", [128, 8])
        cw = SB("cw", [128, 12, 4])
        alog = SB("alog", [128, 4])
        nea = SB("nea", [128, 4])
        dtb = SB("dtb", [128, 4])
        gnw4 = SB("gnw4", [128, 4, 128])
        anw8 = SB("anw8", [128, 8, 64])
        kvalid = SB("kvalid", [128, 32])
        BT = SB("BT", [128, 5, 8, 128], BF16)
        uT = SB("uT", [128, 8, NT1], BF16)
        xpb = SB("xpb", [128, 2, NT1 + 16], F32, nslots=2)
        cvb = SB("cvb", [128, 2, NT1], F32, nslots=2)
        sqb = SB("sqb", [128, NT1], F32)
        rnb = SB("rnb", [128, NT1], F32)
        knT = SB("knT", [128, 4, NT1], BF16)
        qnT = SB("qnT", [128, 4, NT1], BF16)
        vT = SB("vT", [128, 4, NT1], BF16)
        zs = SB("zs", [128, NTL1, 512], BF16, nslots=NTL1)
        kbT = SB("kbT", [128, 4, 8, 128], BF16, nslots=8)
        V1 = SB("V1", [128, 8, 8, 64], BF16, nslots=8)
        vcol = SB("vcol", [128, 8], BF16, nslots=8)
        qbz = SB("qbz", [128, 4, 2, NT1], BF16)
        qtail = SB("qtail", [128, 12, 4, 3])
        ones128 = SB("ones128", [128, 128])
        gt = SB("gt", [128, 80])
        Rb = SB("Rb", [128, 4, 128])
        Eb = SB("Eb", [128, 4, 128])
        tmpb = SB("tmpb", [128, 4, 128])
        Qb = [SB("Qb0", [128, 4, 128]), SB("Qb1", [128, 4, 128])]
        Pb = [SB("Pb0", [128, 4, 128]), SB("Pb1", [128, 4, 128])]
        Xb = SB("Xb", [128, 4, 128])
        Xbf = SB("Xbf", [128, 4, 128], BF16)
        qkb = SB("qkb", [128, 4, 128], BF16)
        qkT = SB("qkT", [128, 4, 128], BF16)
        kbg = SB("kbg", [128, 4, 128], BF16)
        kgm = SB("kgm", [128, 4, 4, 128], BF16)
        vbb = SB("vbb", [128, 4, 128], BF16)
        vn32 = SB("vn32", [128, 4, 128])
        vnb = SB("vnb", [128, 4, 128], BF16)
        wTb = SB("wTb", [128, 4, 128], BF16)
        oacc = SB("oacc", [128, 4, 128])
        Sp = SB("Sp", [128, 4, 128])
        Spb = SB("Spb", [128, 4, 128], BF16)
        Ss = SB("Ss", [128, 4, 4, 128], F32, nslots=4)
        Ssb = SB("Ssb", [128, 4, 4, 128], BF16, nslots=4)
        scb = SB("scb", [128, 4, 128])
        PT = SB("PT", [128, 2, 4, 128], BF16, nslots=2)
        ob32 = SB("ob32", [128, 8, 64])
        ocat = SB("ocat", [128, 2, 1024], BF16, nslots=2)
        kvout = SB("kvout", [128, 2, 512], F32, nslots=2)
        smk = SB("smk", [128, SMK_W], BF16)
        print("phase1 arena used", aptr[0], "of", ARENA_F32)

        load_small(cp, cpp_d[:, :])
        S.op("dve", lambda e: e.tensor_copy(out=identb[:], in_=identf), reads=[cp], writes=[identb])
        load_small(nmpre, nmpre_d[:, :])
        load_small(cw, cw_d[:, :])
        load_small(alog, alog_d[:, :])
        load_small(dtb, dtb_d[:, :])
        load_small(gnw4, gnw_d[:, :])
        load_small(anw8, anw_d[:, :])
        load_small(kvalid, kvalid_d[:, :])
        S.op("act", lambda e: e.activation(out=nea[:], in_=alog[:], func=AF.Exp), reads=[alog], writes=[nea])
        S.op("dve", lambda e: e.tensor_scalar(out=nea[:], in0=nea[:], scalar1=-1.0, scalar2=None, op0=ALU.mult), reads=[nea], writes=[nea])
        S.op("pool", lambda e: e.memset(ones128[:], 1.0), writes=[ones128])
        S.op("pool", lambda e: e.memset(flat(qbz[:]), 0.0), writes=[qbz])
        S.op("pool", lambda e: e.memset(flat(qtail[:]), 0.0), writes=[qtail])
        S.op("pool", lambda e: e.memset(flat(Sp[:]), 0.0), writes=[Sp])
        S.op("pool", lambda e: e.memset(flat(Spb[:]), 0.0), writes=[Spb])
        S.op("pool", lambda e: e.tensor_copy(out=vcol[:], in_=kvalid[:, 0:8]), reads=[kvalid], writes=[vcol])

        def load_BT(tab_d, mask_d):
            BTf = flat(BT[:])
            for i in range(5):
                S.dma("sp", (lambda e, i=i: e.dma_start(out=stage[:, 0, :], in_=tab_d[:, i * 1024:(i + 1) * 1024])), writes=[stage.r[0]])
                if mask_d is not None:
                    S.dma("sp", (lambda e, i=i: e.dma_start(out=stage[:, 1, :], in_=mask_d[:, i * 1024:(i + 1) * 1024])), writes=[stage.r[1]])
                    S.op("dve", (lambda e, i=i: e.tensor_tensor(out=BTf[:, i * 1024:(i + 1) * 1024], in0=stage[:, 0, :], in1=stage[:, 1, :], op=ALU.add)),
                         reads=[stage], writes=[BT])
                else:
                    S.op("dve", (lambda e, i=i: e.tensor_copy(out=BTf[:, i * 1024:(i + 1) * 1024], in_=stage[:, 0, :])),
                         reads=[stage.r[0]], writes=[BT])

        load_BT(btp_d, btm_d)
        sfn = lambda k: nmpre[:, k:k + 1]
        sfn.buf = nmpre
        load_cast_weight(lambda k, c0, c1: w_in_bf[:, k, c0:c1],
                         lambda k, c0, c1: w_in_d[k * 128:(k + 1) * 128, c0:c1],
                         IN_COLS, 8, sfn, w_in_bf)

        cvctr = [0]

        def conv_silu(ps, c, nseq, L, NT, dst_ap, dst_res, eng, sl):
            xpv = xpb[:, sl, 0:nseq * (L + 3)].rearrange("p (s t) -> p s t", s=nseq)
            cvv = cvb[:, sl, 0:NT].rearrange("p (s t) -> p s t", s=nseq)
            S.op("pool", lambda e: e.tensor_copy(out=xpv[:, :, 0:3], in_=qtail[:, c, 0:nseq, :]), reads=[qtail], writes=[xpb.r[sl]])
            S.op("act", lambda e: e.activation(out=xpv[:, :, 3:3 + L], in_=ps[:, 0:NT].rearrange("p (s t) -> p s t", s=nseq), func=AF.Copy),
                 reads=[ps], writes=[xpb.r[sl]])
            S.op("pool", lambda e: e.tensor_copy(out=qtail[:, c, 0:nseq, :], in_=xpv[:, :, L:L + 3]), reads=[xpb.r[sl]], writes=[qtail])
            S.op(eng, lambda e: e.tensor_scalar(out=cvv, in0=xpv[:, :, 0:L], scalar1=cw[:, c, 0:1], scalar2=None, op0=ALU.mult),
                 reads=[xpb.r[sl], cw], writes=[cvb.r[sl]])
            for i in range(1, 4):
                S.op(eng, (lambda e, i=i: e.scalar_tensor_tensor(out=cvv, in0=xpv[:, :, i:i + L], scalar=cw[:, c, i:i + 1], in1=cvv, op0=ALU.mult, op1=ALU.add)),
                     reads=[xpb.r[sl], cw, cvb.r[sl]], writes=[cvb.r[sl]])
            S.op("act", lambda e: e.activation(out=dst_ap, in_=cvb[:, sl, 0:NT], func=AF.Silu), reads=[cvb.r[sl]], writes=dst_res)

        def l2norm_chunk(src_ap, src_res, dstT, h, NT, scale):
            S.op("pool", lambda e: e.tensor_tensor(out=sqb[:, 0:NT], in0=src_ap, in1=src_ap, op=ALU.mult), reads=src_res, writes=[sqb])
            S.op("pe", lambda e: e.matmul(psM[:, 0:NT], lhsT=ones128[:], rhs=sqb[:, 0:NT], start=True, stop=True), reads=[ones128, sqb], writes=[psM])
            S.op("act", lambda e: e.activation(out=rnb[:, 0:NT], in_=psM[:, 0:NT], func=AF.Ln, bias=epsb[:, 0:1]), reads=[psM, epsb], writes=[rnb])
            S.op("act", lambda e: e.activation(out=rnb[:, 0:NT], in_=rnb[:, 0:NT], func=AF.Exp, scale=-0.5), reads=[rnb], writes=[rnb])
            S.op("dve", lambda e: e.scalar_tensor_tensor(out=dstT[:, h, 0:NT], in0=src_ap, scalar=float(scale), in1=rnb[:, 0:NT], op0=ALU.mult, op1=ALU.mult),
                 reads=src_res + [rnb], writes=[dstT])

        def bc(ap2, n, w):
            return ap2.unsqueeze(2).to_broadcast([128, n, w])

        def gdn_tile(tcol, nb, full, S_in, Sb_in, S_out, Sb_out, zslot, oc_slot, tag=-1):
            tc = slice(tcol, tcol + 128)
            beta, t1, g, Gs, eG, ekg, ckbg, nbeta = (gt[:, 0:4], gt[:, 4:8], gt[:, 8:12], gt[:, 12:20], gt[:, 20:24], gt[:, 24:28], gt[:, 28:32], gt[:, 32:36])
            dlb = gt[:, 36:36 + 4 * nb]
            ssq = gt[:, 52:56]
            cog = gt[:, 56:60]
            S.op("act", lambda e: e.activation(out=beta, in_=psM[:, 0:4], func=AF.Sigmoid), reads=[psM], writes=[gt])
            S.op("dve", lambda e: e.tensor_tensor(out=t1, in0=psM[:, 4:8], in1=dtb[:], op=ALU.add), reads=[psM, dtb], writes=[gt])
            S.op("act", lambda e: e.activation(out=t1, in_=t1, func=AF.Exp), reads=[gt], writes=[gt])
            S.op("act", lambda e: e.activation(out=t1, in_=t1, func=AF.Ln, bias=epsb[:, 1:2]), reads=[gt, epsb], writes=[gt])
            S.op("dve", lambda e: e.tensor_tensor(out=g, in0=t1, in1=nea[:], op=ALU.mult), reads=[gt, nea], writes=[gt])
            fns = [lambda e: e.matmul(psM[:, 16:20], lhsT=tri2, rhs=g, start=True, stop=True),
                   lambda e: e.matmul(psM[:, 20:24], lhsT=same2, rhs=g, start=True, stop=True)]
            for b in range(nb):
                fns.append(lambda e, b=b: e.matmul(psM[:, 24 + 4 * b:28 + 4 * b], lhsT=cm4[:, b, :], rhs=g, start=True, stop=True))
            S.op("pe", fns, reads=[cp, gt], writes=[psM])
            S.op("act", lambda e: e.activation(out=Gs, in_=psM[:, 16:24], func=AF.Copy), reads=[psM], writes=[gt])
            S.op("act", lambda e: e.activation(out=dlb, in_=psM[:, 24:24 + 4 * nb], func=AF.Exp), reads=[psM], writes=[gt])
            S.op("dve", lambda e: e.tensor_tensor(out=ekg, in0=Gs[:, 4:8], in1=Gs[:, 0:4], op=ALU.subtract), reads=[gt], writes=[gt])
            S.op("act", lambda e: e.activation(out=ekg, in_=ekg, func=AF.Exp), reads=[gt], writes=[gt])
            S.op("act", lambda e: e.activation(out=eG, in_=Gs[:, 0:4], func=AF.Exp), reads=[gt], writes=[gt])
            S.op("dve", lambda e: e.tensor_tensor(out=ckbg, in0=beta, in1=eG, op=ALU.mult), reads=[gt], writes=[gt])
            S.op("dve", lambda e: e.tensor_scalar(out=nbeta, in0=beta, scalar1=-1.0, scalar2=None, op0=ALU.mult), reads=[gt], writes=[gt])
            DUMP("gt", gt[:, 0:64], [gt], tag)
            DUMP("bf_knT", flat(knT[:]), [knT], tag)
            DUMP("bf_vT", flat(vT[:]), [vT], tag)
            S.op("pool", lambda e: e.tensor_tensor(out=Rb[:], in0=strict4, in1=bc(g, 4, 128), op=ALU.mult), reads=[cp, gt], writes=[Rb])
            S.op("pe", [(lambda e, h=h: e.matmul(psA[:, h * 128:(h + 1) * 128], lhsT=tri2, rhs=Rb[:, h, :], start=True, stop=True)) for h in range(4)],
                 reads=[cp, Rb], writes=[psA])
            S.op("act", lambda e: e.activation(out=flat(Eb[:]), in_=psA[:], func=AF.Exp), reads=[psA], writes=[Eb])
            S.op("pe", [(lambda e, h=h: e.matmul(psB[:, h * 128:(h + 1) * 128], lhsT=knT[:, h, tc], rhs=knT[:, h, tc], start=True, stop=True)) for h in range(4)],
                 reads=[knT], writes=[psB])
            S.op("dve", lambda e: e.tensor_tensor(out=flat(tmpb[:]), in0=psB[:], in1=flat(Eb[:]), op=ALU.mult), reads=[psB, Eb], writes=[tmpb])
            S.op("dve", lambda e: e.tensor_tensor(out=tmpb[:], in0=tmpb[:], in1=bc(nbeta, 4, 128), op=ALU.mult), reads=[tmpb, gt], writes=[tmpb])
            Q, Q2, P, P2 = Qb[0], Qb[1], Pb[0], Pb[1]
            DUMP("E", flat(Eb[:]), [Eb], tag)
            S.op("dve", lambda e, Q=Q: e.tensor_tensor(out=Q[:], in0=tmpb[:], in1=strict4, op=ALU.mult), reads=[tmpb, cp], writes=[Q])
            DUMP("Q0", flat(Q[:]), [Q], tag)
            S.op("pe", [(lambda e, h=h, Q=Q: e.transpose(psA[:, h * 128:(h + 1) * 128], Q[:, h, :], identf)) for h in range(4)],
                 reads=[Q, cp], writes=[psA])
            S.op("act", lambda e, P=P: e.activation(out=flat(P[:]), in_=psA[:], func=AF.Copy), reads=[psA], writes=[P])
            if full:
                S.op("pe", [(lambda e, h=h: e.matmul(psB[:, h * 128:(h + 1) * 128], lhsT=qnT[:, h, tc], rhs=knT[:, h, tc], start=True, stop=True)) for h in range(4)],
                     reads=[qnT, knT], writes=[psB])
                S.op("dve", lambda e: e.tensor_tensor(out=flat(tmpb[:]), in0=psB[:], in1=flat(Eb[:]), op=ALU.mult), reads=[psB, Eb], writes=[tmpb])
                S.op("pool", lambda e: e.tensor_tensor(out=qkb[:], in0=tmpb[:], in1=incl4, op=ALU.mult), reads=[tmpb, cp], writes=[qkb])
                S.op("pe", [(lambda e, h=h: e.transpose(psT[:, h * 128:(h + 1) * 128], qkb[:, h, :], identb[:])) for h in range(4)],
                     reads=[qkb, identb], writes=[psT])
                S.op("act", lambda e: e.activation(out=flat(qkT[:]), in_=psT[:, 0:512], func=AF.Copy), reads=[psT], writes=[qkT])
            DUMP("P0", flat(P[:]), [P], tag)
            S.op("dve", lambda e, P=P: e.tensor_tensor(out=Xb[:], in0=P[:], in1=ident4, op=ALU.add), reads=[P, cp], writes=[Xb])
            nlev = 5
            for lv in range(nlev):
                S.op("pe", [(lambda e, h=h, P=P, Q=Q: e.matmul(psB[:, h * 128:(h + 1) * 128], lhsT=P[:, h, :], rhs=Q[:, h, :], start=True, stop=True)) for h in range(4)],
                     reads=[P, Q], writes=[psB])
                S.op("act", (lambda e, Q2=Q2: e.activation(out=flat(Q2[:]), in_=psB[:], func=AF.Copy)), reads=[psB], writes=[Q2])
                S.op("pe", [(lambda e, h=h, Q2=Q2: e.matmul(psA[:, h * 128:(h + 1) * 128], lhsT=Q2[:, h, :], rhs=Xb[:, h, :], start=True, stop=True)) for h in range(4)],
                     reads=[Q2, Xb], writes=[psA])
                S.op("dve", lambda e: e.tensor_tensor(out=flat(Xb[:]), in0=flat(Xb[:]), in1=psA[:], op=ALU.add), reads=[Xb, psA], writes=[Xb])
                if lv < nlev - 1:
                    S.op("pe", [(lambda e, h=h, P=P, Q=Q: e.matmul(psB[:, h * 128:(h + 1) * 128], lhsT=Q[:, h, :], rhs=P[:, h, :], start=True, stop=True)) for h in range(4)],
                         reads=[P, Q], writes=[psB])
                    S.op("act", (lambda e, P2=P2: e.activation(out=flat(P2[:]), in_=psB[:], func=AF.Copy)), reads=[psB], writes=[P2])
                    P, P2 = P2, P
                Q, Q2 = Q2, Q
            DUMP("X", flat(Xb[:]), [Xb], tag)
            S.op("act", lambda e: e.activation(out=Xbf[:], in_=Xb[:], func=AF.Copy), reads=[Xb], writes=[Xbf])
            S.op("pe", [(lambda e, h=h: e.transpose(psT[:, h * 128:(h + 1) * 128], knT[:, h, tc], identb[:])) for h in range(4)] +
                 [(lambda e, h=h: e.transpose(psT[:, 512 + h * 128:512 + (h + 1) * 128], vT[:, h, tc], identb[:])) for h in range(4)],
                 reads=[knT, vT, identb], writes=[psT])
            kTv = psT[:, 0:512].rearrange("p (a b) -> p a b", a=4)
            vTv = psT[:, 512:1024].rearrange("p (a b) -> p a b", a=4)
            S.op("dve", lambda e: e.tensor_tensor(out=kbg[:], in0=kTv, in1=bc(ckbg, 4, 128), op=ALU.mult), reads=[psT, gt], writes=[kbg])
            S.op("dve", lambda e: e.tensor_tensor(out=vbb[:], in0=vTv, in1=bc(beta, 4, 128), op=ALU.mult), reads=[psT, gt], writes=[vbb])
            for b in range(nb):
                cof = gt[:, 60 + 4 * b:64 + 4 * b]
                S.op("dve", (lambda e, b=b, cof=cof: e.tensor_scalar(out=cof, in0=ekg, scalar1=cp[:, CP_BM + b:CP_BM + b + 1], scalar2=None, op0=ALU.mult)),
                     reads=[gt, cp], writes=[gt])
                S.op("dve", (lambda e, b=b, cof=cof: e.tensor_tensor(out=kgm[:, b], in0=kTv, in1=bc(cof, 4, 128), op=ALU.mult)),
                     reads=[psT, gt], writes=[kgm])
            S.op("pe", [(lambda e, h=h: e.matmul(psA[:, h * 128:(h + 1) * 128], lhsT=Xbf[:, h, :], rhs=vbb[:, h, :], start=True, stop=True)) for h in range(4)],
                 reads=[Xbf, vbb], writes=[psA])
            S.op("act", lambda e: e.activation(out=flat(vn32[:]), in_=psA[:], func=AF.Copy), reads=[psA], writes=[vn32])
            S.op("pe", [(lambda e, h=h: e.matmul(psB[:, h * 128:(h + 1) * 128], lhsT=kbg[:, h, :], rhs=Xbf[:, h, :], start=True, stop=True)) for h in range(4)],
                 reads=[kbg, Xbf], writes=[psB])
            S.op("act", lambda e: e.activation(out=flat(wTb[:]), in_=psB[:], func=AF.Copy), reads=[psB], writes=[wTb])
            DUMP("u", flat(vn32[:]), [vn32], tag)
            DUMP("bf_wT", flat(wTb[:]), [wTb], tag)
            DUMP("bf_kbg", flat(kbg[:]), [kbg], tag)
            DUMP("bf_vb", flat(vbb[:]), [vbb], tag)
            for b in range(nb):
                Si, Sbi, So, Sbo = S_in(b), Sb_in(b), S_out(b), Sb_out(b)
                S.op("pe", [(lambda e, h=h, Sbi=Sbi: e.matmul(psA[:, h * 128:(h + 1) * 128], lhsT=wTb[:, h, :], rhs=Sbi[0][:, h, :], start=True, stop=True)) for h in range(4)],
                     reads=[wTb, Sbi[1]], writes=[psA])
                S.op("dve", (lambda e, b=b: e.scalar_tensor_tensor(out=flat(vn32[:]), in0=psA[:], scalar=cp[:, CP_NBM + b:CP_NBM + b + 1],
                                                                    in1=flat(vn32[:]), op0=ALU.mult, op1=ALU.add)),
                     reads=[psA, cp, vn32], writes=[vn32])
                S.op("act", lambda e: e.activation(out=vnb[:], in_=vn32[:], func=AF.Copy), reads=[vn32], writes=[vnb])
                if full:
                    S.op("dve", (lambda e, b=b: e.tensor_scalar(out=cog, in0=eG, scalar1=cp[:, CP_BM + b:CP_BM + b + 1], scalar2=None, op0=ALU.mult)),
                         reads=[gt, cp], writes=[gt])
                    S.op("pe", [(lambda e, h=h, Sbi=Sbi: e.matmul(psB[:, h * 128:(h + 1) * 128], lhsT=qnT[:, h, tc], rhs=Sbi[0][:, h, :], start=True, stop=True)) for h in range(4)],
                         reads=[qnT, Sbi[1]], writes=[psB])
                    psBv = psB[:].rearrange("p (a b) -> p a b", a=4)
                    if b == 0:
                        S.op("dve", lambda e: e.tensor_tensor(out=oacc[:], in0=psBv, in1=bc(cog, 4, 128), op=ALU.mult), reads=[psB, gt], writes=[oacc])
                    else:
                        S.op("dve", lambda e: e.tensor_tensor(out=tmpb[:], in0=psBv, in1=bc(cog, 4, 128), op=ALU.mult), reads=[psB, gt], writes=[tmpb])
                        S.op("pool", lambda e: e.tensor_tensor(out=oacc[:], in0=oacc[:], in1=tmpb[:], op=ALU.add), reads=[oacc, tmpb], writes=[oacc])
                S.op("pe", [(lambda e, h=h, b=b: e.matmul(psA[:, h * 128:(h + 1) * 128], lhsT=kgm[:, b, h, :], rhs=vnb[:, h, :], start=True, stop=True)) for h in range(4)],
                     reads=[kgm, vnb], writes=[psA])
                for h in range(4):
                    S.op("dve", (lambda e, h=h, b=b, Si=Si, So=So: e.scalar_tensor_tensor(out=So[0][:, h, :], in0=Si[0][:, h, :], scalar=dlb[:, 4 * b + h:4 * b + h + 1],
                                                                                      in1=psA[:, h * 128:(h + 1) * 128], op0=ALU.mult, op1=ALU.add)),
                         reads=[Si[1], gt, psA], writes=[So[1]])
                if Sbo is not None:
                    S.op("act", (lambda e, So=So, Sbo=Sbo: e.activation(out=Sbo[0], in_=So[0], func=AF.Copy)), reads=[So[1]], writes=[Sbo[1]])
            DUMP("vn", flat(vn32[:]), [vn32], tag)
            DUMP("Safter", flat(S_out(nb - 1)[0]), [S_out(nb - 1)[1]], tag)
            if full:
                S.op("pe", [(lambda e, h=h: e.matmul(psB[:, h * 128:(h + 1) * 128], lhsT=qkT[:, h, :], rhs=vnb[:, h, :], start=True, stop=True)) for h in range(4)],
                     reads=[qkT, vnb], writes=[psB])
                S.op("dve", lambda e: e.tensor_tensor(out=flat(oacc[:]), in0=flat(oacc[:]), in1=psB[:], op=ALU.add), reads=[oacc, psB], writes=[oacc])
                for h in range(4):
                    S.op("act", (lambda e, h=h: e.activation(out=tmpb[:, h, :], in_=oacc[:, h, :], func=AF.Square, accum_out=ssq[:, h:h + 1])),
                         reads=[oacc], writes=[tmpb, gt])
                S.op("act", lambda e: e.activation(out=ssq, in_=ssq, func=AF.Ln, scale=1.0 / 128, bias=epsb[:, 0:1]), reads=[gt, epsb], writes=[gt])
                S.op("act", lambda e: e.activation(out=ssq, in_=ssq, func=AF.Exp, scale=-0.5), reads=[gt], writes=[gt])
                DUMP("o", flat(oacc[:]), [oacc], tag)
                S.op("dve", lambda e: e.tensor_tensor(out=tmpb[:], in0=oacc[:], in1=bc(ssq, 4, 128), op=ALU.mult), reads=[oacc, gt], writes=[tmpb])
                S.op("pool", lambda e: e.tensor_tensor(out=tmpb[:], in0=tmpb[:], in1=gnw4[:], op=ALU.mult), reads=[tmpb, gnw4], writes=[tmpb])
                S.op("dve", lambda e: e.tensor_tensor(out=ocat[:, oc_slot, 0:512], in0=flat(tmpb[:]), in1=zs[:, zslot, :], op=ALU.mult),
                     reads=[tmpb, zs.r[zslot]], writes=[ocat.r[oc_slot]])

        def attn_tile(pieces, qcol, oc_slot):
            qc = slice(qcol, qcol + 128)
            npc = len(pieces)
            first_pv = [True]
            pctr2 = [0]
            for pi, pc in enumerate(pieces):
                if pc.get("prep"):
                    pc["prep"]()
                for hh in range(2):
                    fns = []
                    first = True
                    if pc.get("seq") is not None:
                        s = pc["seq"]
                        fns.append(lambda e, s=s: e.matmul(psS[:], lhsT=smk[:, 2560:2688], rhs=smk[:, s * 512:(s + 1) * 512], start=True, stop=False, skip_group_check=True))
                        first = False
                        if pc.get("newk"):
                            fns.append(lambda e, s=s: e.matmul(psS[:], lhsT=smk[:, 2048 + s * 128:2048 + (s + 1) * 128], rhs=smk[:, 2688:3200], start=False, stop=False, skip_group_check=True))
                    for j in range(4):
                        h = 4 * hh + j
                        fns.append(lambda e, h=h, j=j, pc=pc, first=first: e.matmul(psS[:, j * 128:(j + 1) * 128], lhsT=kbT[:, h // 2, pc["kslot"], :],
                                                                                   rhs=qbz[:, h // 2, h % 2, qc], start=first, stop=True, skip_group_check=True))
                    S.op("pe", fns, reads=[kbT.r[pc["kslot"]], qbz, smk], writes=[psS])
                    S.op("dve", (lambda e, pc=pc, hh=hh: e.tensor_tensor(out=scb[:], in0=psS[:].rearrange("p (a b) -> p a b", a=4), in1=BT[:, pc["r"], 4 * hh:4 * hh + 4, :], op=ALU.add)),
                         reads=[psS, BT], writes=[scb])
                    psl = pctr2[0] % 2
                    pctr2[0] += 1
                    S.op("act", (lambda e, psl=psl: e.activation(out=PT[:, psl], in_=scb[:], func=AF.Exp)), reads=[scb], writes=[PT.r[psl]])
                    fns = []
                    for j in range(4):
                        h = 4 * hh + j
                        stt = first_pv[0]
                        first_pv[0] = False
                        fns.append(lambda e, j=j, h=h, pc=pc, psl=psl, stt=stt: e.matmul(psO[:, h * 64:(h + 1) * 64], lhsT=PT[:, psl, j, :], rhs=V1[:, pc["vslot"], h, :],
                                                                                        start=stt, stop=False, skip_group_check=True))
                        fns.append(lambda e, j=j, h=h, pc=pc, psl=psl, pi=pi: e.matmul(psM[:, 64 + pi * 8 + h:64 + pi * 8 + h + 1], lhsT=PT[:, psl, j, :], rhs=vcol[:, pc["vslot"]:pc["vslot"] + 1],
                                                                                      start=True, stop=True, skip_group_check=True))
                    S.op("pe", fns, reads=[PT.r[psl], V1.r[pc["vslot"]], vcol.r[pc["vslot"]]], writes=[psO, psM])
            rden = gt[:, 64:72]
            ss8 = gt[:, 72:80]
            S.op("dve", lambda e: e.tensor_reduce(out=rden, in_=psM[:, 64:64 + npc * 8].rearrange("p (a b) -> p b a", b=8), axis=mybir.AxisListType.X, op=ALU.add),
                 reads=[psM], writes=[gt])
            S.op("dve", lambda e: e.tensor_scalar(out=rden, in0=rden, scalar1=1e-30, scalar2=None, op0=ALU.max), reads=[gt], writes=[gt])
            S.op("dve", lambda e: e.reciprocal(out=rden, in_=rden), reads=[gt], writes=[gt])
            S.op("dve", lambda e: e.tensor_tensor(out=ob32[:], in0=psO[:].rearrange("p (a b) -> p a b", a=8), in1=bc(rden, 8, 64), op=ALU.mult),
                 reads=[psO, gt], writes=[ob32])
            for h in range(8):
                S.op("act", (lambda e, h=h: e.activation(out=scb[:, 0, 0:64], in_=ob32[:, h, :], func=AF.Square, accum_out=ss8[:, h:h + 1])),
                     reads=[ob32], writes=[scb, gt])
            S.op("act", lambda e: e.activation(out=ss8, in_=ss8, func=AF.Ln, scale=1.0 / 64, bias=epsb[:, 0:1]), reads=[gt, epsb], writes=[gt])
            S.op("act", lambda e: e.activation(out=ss8, in_=ss8, func=AF.Exp, scale=-0.5), reads=[gt], writes=[gt])
            S.op("dve", lambda e: e.tensor_tensor(out=ob32[:], in0=ob32[:], in1=bc(ss8, 8, 64), op=ALU.mult), reads=[ob32, gt], writes=[ob32])
            S.op("pool", lambda e: e.tensor_tensor(out=ocat[:, oc_slot, 512:1024].rearrange("p (a b) -> p a b", a=8), in0=ob32[:], in1=anw8[:], op=ALU.mult),
                 reads=[ob32, anw8], writes=[ocat.r[oc_slot]])

        occtr = [0]
        kvctr = [0]

        def p1_group(tls, gq, gkv, full, sample=False):
            ntl = len(tls)
            NT = ntl * 128
            nseq, L = (4, 32) if sample else (1, NT)
            for ti, tl in enumerate(tls):
                src = xs_d[:, :] if sample else xp_d[tl * 128:(tl + 1) * 128, :]
                norm_transpose(src, uT, ti * 128)
            chunks = list(range(12)) if gq else list(range(4, 12))
            for ci, c in enumerate(chunks):
                ps = next_psP()
                proj_fm(w_in_bf, c * 128, uT, NT, ps)
                eng = "dve"
                sl = cvctr[0] % 2
                cvctr[0] += 1
                if c < 8:
                    conv_silu(ps, c, nseq, L, NT, cvb[:, sl, 0:NT], [cvb.r[sl]], eng, sl)
                    if c < 4:
                        if full:
                            l2norm_chunk(cvb[:, sl, 0:NT], [cvb.r[sl]], qnT, c, NT, 128.0 ** -0.5)
                    else:
                        l2norm_chunk(cvb[:, sl, 0:NT], [cvb.r[sl]], knT, c - 4, NT, 1.0)
                else:
                    conv_silu(ps, c, nseq, L, NT, vT[:, c - 8, 0:NT], [vT], eng, sl)
            if full:
                for c in range(4):
                    ps = next_psP()
                    proj_fm(w_in_bf, C_QB + c * 128, uT, NT, ps)
                    S.op("act", (lambda e, c=c, ps=ps: e.activation(out=qbz[0:64, c, 0, 0:NT], in_=ps[0:64, 0:NT], func=AF.Copy, scale=0.125)), reads=[ps], writes=[qbz])
                    S.op("act", (lambda e, c=c, ps=ps: e.activation(out=qbz[64:128, c, 1, 0:NT], in_=ps[64:128, 0:NT], func=AF.Copy, scale=0.125)), reads=[ps], writes=[qbz])
            if gkv:
                for c in range(4):
                    ps = next_psP()
                    proj_fm(w_in_bf, C_KB + c * 128, uT, NT, ps)
                    for ti, tl in enumerate(tls):
                        slot = tl % 8
                        S.op("act", (lambda e, c=c, ps=ps, ti=ti, slot=slot: e.activation(out=kbT[:, c, slot, :], in_=ps[:, ti * 128:(ti + 1) * 128], func=AF.Copy)),
                             reads=[ps], writes=[kbT.r[slot]])
            for ti, tl in enumerate(tls):
                tcol = ti * 128
                if gkv:
                    slot = tl % 8
                    ps = next_psP()
                    proj_tm(w_in_bf, C_VB, 512, uT, tcol, ps)
                    S.op("act", (lambda e, ps=ps, slot=slot: e.activation(out=V1[:, slot], in_=ps[:].rearrange("p (a b) -> p a b", a=8), func=AF.Copy)),
                         reads=[ps], writes=[V1.r[slot]])
                    if sample:
                        S.op("pool", (lambda e, slot=slot: e.memset(vcol[:, slot:slot + 1], 1.0)), writes=[vcol.r[slot]])
                    else:
                        S.op("pool", (lambda e, slot=slot, tl=tl: e.tensor_copy(out=vcol[:, slot:slot + 1], in_=kvalid[:, tl:tl + 1])),
                             reads=[kvalid], writes=[vcol.r[slot]])
                    want_out = sample or (tl >= p_tiles - 4)
                    if want_out:
                        orow = 0 if sample else (tl - (p_tiles - 4)) * 128
                        kd, vd = (bk_s, bv_s) if sample else (bk_p, bv_p)
                        ks = kvctr[0] % 2
                        kvctr[0] += 1
                        S.op("act", (lambda e, ps=ps, ks=ks: e.activation(out=kvout[:, ks, :], in_=ps[:], func=AF.Copy)), reads=[ps], writes=[kvout.r[ks]])
                        S.dma("pool", (lambda e, ks=ks, vd=vd, orow=orow: e.dma_start(out=vd[orow:orow + 128, :], in_=kvout[:, ks, :])), reads=[kvout.r[ks]])
                        ps2 = next_psP()
                        proj_tm(w_in_bf, C_KB, 512, uT, tcol, ps2)
                        ks = kvctr[0] % 2
                        kvctr[0] += 1
                        S.op("act", (lambda e, ps2=ps2, ks=ks: e.activation(out=kvout[:, ks, :], in_=ps2[:], func=AF.Copy)), reads=[ps2], writes=[kvout.r[ks]])
                        S.dma("pool", (lambda e, ks=ks, kd=kd, orow=orow: e.dma_start(out=kd[orow:orow + 128, :], in_=kvout[:, ks, :])), reads=[kvout.r[ks]])
                if full:
                    ps = next_psP()
                    proj_tm(w_in_bf, C_Z, 512, uT, tcol, ps)
                    S.op("act", (lambda e, ps=ps, ti=ti: e.activation(out=zs[:, ti, :], in_=ps[:], func=AF.Silu)), reads=[ps], writes=[zs.r[ti]])
                proj_tm(w_in_bf, C_B, 8, uT, tcol, psM, 0)
                oc_slot = occtr[0] % 2
                if full:
                    occtr[0] += 1
                if sample:
                    gdn_tile(tcol, 4, True,
                             lambda b: (Ss[:, b], Ss.r[b]), lambda b: (Ssb[:, b], Ssb.r[b]),
                             lambda b: (Ss[:, b], Ss.r[b]), lambda b: None, ti, oc_slot)
                else:
                    gdn_tile(tcol, 2, full,
                             lambda b: (Sp[:], Sp.r[0]), lambda b: (Spb[:], Spb.r[0]),
                             lambda b: (Sp[:], Sp.r[0]), lambda b: (Spb[:], Spb.r[0]), ti, oc_slot, tag=tl)
                if full and not sample:
                    pieces = [dict(kslot=(tl - 4 + r) % 8, vslot=(tl - 4 + r) % 8, r=r) for r in range(5)]
                    attn_tile(pieces, tcol, oc_slot)
                    scr_row = tl - (p_tiles - NSCR + 1)
                    if scr_row >= 0:
                        S.dma("pool", (lambda e, oc_slot=oc_slot, scr_row=scr_row: e.dma_start(out=ocat_scr[scr_row * 128:(scr_row + 1) * 128, :], in_=ocat[:, oc_slot, :])),
                              reads=[ocat.r[oc_slot]])
                    if "ocat" in dbg and tl >= first_own:
                        S.dma("pool", (lambda e, oc_slot=oc_slot, tl=tl: e.dma_start(out=dbg["ocat"][(tl - first_own) * 128:(tl - first_own + 1) * 128, :], in_=ocat[:, oc_slot, :])),
                              reads=[ocat.r[oc_slot]])
            return oc_slot

        first_full_tile = first_own - 2
        first_kv_tile = first_full_tile - 4
        tl = p1_from
        while tl < p1_to:
            tls = list(range(tl, min(tl + NTL1, p1_to)))
            full = tls[0] >= first_full_tile
            gkv = tls[0] >= first_kv_tile
            gq = tls[0] >= first_full_tile - NTL1
            p1_group(tls, gq, gkv, full)
            tl += NTL1
        S.dma("pool", lambda e: e.dma_start(out=S_p[:, :], in_=flat(Sp[:])), reads=[Sp])
        S.dma("pool", lambda e: e.dma_start(out=qc_p[:, :].rearrange("p (a b) -> p a b", a=12), in_=qtail[:, :, 0, :]), reads=[qtail])

        if cfg.get("sample", True):
            load_small(cp, cps_d[:, :])
            load_BT(bts_d, None)
            for i, (a, b) in enumerate(((0, 1024), (1024, 2048), (2048, 3072), (3072, SMK_W))):
                S.dma("sp", (lambda e, i=i, a=a, b=b: e.dma_start(out=stage[:, i % 2, 0:b - a], in_=smk_d[:, a:b])), writes=[stage.r[i % 2]])
                S.op("dve", (lambda e, i=i, a=a, b=b: e.tensor_copy(out=smk[:, a:b], in_=stage[:, i % 2, 0:b - a])), reads=[stage.r[i % 2]], writes=[smk])
            load_small(qtail, qc0_d[:, :])
            for s in range(4):
                S.dma("sp", (lambda e, s=s: e.dma_start(out=Ss[:, s], in_=s0_d[s].rearrange("h k v -> k h v"))), writes=[Ss.r[s]])
                S.op("act", (lambda e, s=s: e.activation(out=Ssb[:, s], in_=Ss[:, s], func=AF.Copy)), reads=[Ss.r[s]], writes=[Ssb.r[s]])
            ocs = p1_group([0], True, True, True, sample=True)
            S.dma("pool", lambda e: e.dma_start(out=S_s[:, :], in_=flat(Ss[:])), reads=[Ss])
            S.dma("pool", lambda e: e.dma_start(out=qc_s[:, :], in_=flat(qtail[:])), reads=[qtail])

            def mk_prep(s, r, slot, sl):
                def prep():
                    S.dma("sp", (lambda e: e.dma_start(out=stage[:, sl, 0:512], in_=ck_d[s, r * 128:(r + 1) * 128, :])), writes=[stage.r[sl]])
                    S.dma("sp", (lambda e: e.dma_start(out=stage[:, sl, 512:1024], in_=cv_d[s, r * 128:(r + 1) * 128, :])), writes=[stage.r[sl]])
                    S.op("dve", (lambda e: e.tensor_copy(out=xn[:, 0:512], in_=stage[:, sl, 0:512])), reads=[stage.r[sl]], writes=[xn])
                    S.op("pe", [(lambda e, c=c: e.transpose(psT[:, c * 128:(c + 1) * 128], xn[:, c * 128:(c + 1) * 128], identb[:])) for c in range(4)],
                         reads=[xn, identb], writes=[psT])
                    S.op("act", (lambda e: e.activation(out=kbT[:, :, slot, :], in_=psT[:, 0:512].rearrange("p (a b) -> p a b", a=4), func=AF.Copy)),
                         reads=[psT], writes=[kbT.r[slot]])
                    S.op("act", (lambda e: e.activation(out=V1[:, slot], in_=stage[:, sl, 512:1024].rearrange("p (a b) -> p a b", a=8), func=AF.Copy)),
                         reads=[stage.r[sl]], writes=[V1.r[slot]])
                    S.op("pool", (lambda e: e.memset(vcol[:, slot:slot + 1], 1.0)), writes=[vcol.r[slot]])
                return prep

            pieces = []
            for s in range(4):
                for r in range(4):
                    i = s * 4 + r
                    slot = i % 7 + 1
                    pieces.append(dict(kslot=slot, vslot=slot, r=r, seq=s, prep=mk_prep(s, r, slot, i % 2)))
                pieces.append(dict(kslot=0, vslot=0, r=4, seq=s, newk=True))
            attn_tile(pieces, 0, ocs)
            S.dma("pool", lambda e: e.dma_start(out=ocat_scr[(NSCR - 1) * 128:NSCR * 128, :], in_=ocat[:, ocs, :]), reads=[ocat.r[ocs]])
            if "ocat_s" in dbg:
                S.dma("pool", lambda e: e.dma_start(out=dbg["ocat_s"][:, :], in_=ocat[:, ocs, :]), reads=[ocat.r[ocs]])

    if do_p2:
        S.barrier()
        aptr[0] = shared_end
        wgu_bf = SB("wgu_bf", [128, 8, 2 * D_FF], BF16)
        wd_bf = SB("wd_bf", [128, NFF, 1024], BF16)
        wout_bf = SB("wout_bf", [128, 8, 1024], BF16)
        nfpre = SB("nfpre", [128, 8])
        nmpost = SB("nmpost", [128, 1024])
        nfpost = SB("nfpost", [128, 1024])
        fw = SB("fw", [128, NFF, 3])
        fb = SB("fb", [128, NFF])
        gtail = SB("gtail", [128, NFF, 4, 2])
        oc2 = SB("oc2", [128, 1, 1024], BF16, nslots=1)
        upb = SB("upb", [128, 4, 128], F32, nslots=4)
        oT = SB("oT", [128, 8, 128], BF16)
        x1 = SB("x1", [128, 1024])
        u2T = SB("u2T", [128, 8, 128], BF16)
        gxp = SB("gxp", [128, 4, 144], F32, nslots=4)
        cv2 = SB("cv2", [128, 4, 128], F32, nslots=4)
        t2 = SB("t2", [128, 4, 128], F32, nslots=4)
        sg2 = SB("sg2", [128, 4, 128], F32, nslots=4)
        junk2 = SB("junk2", [128, 512], BF16)
        hT = SB("hT", [128, NFF, 128], BF16)
        yb = SB("yb", [128, 1, 1024], F32, nslots=1)
        st2 = SB("st2", [128, 8])
        print("phase2 arena used", aptr[0], "of", ARENA_F32)

        load_small(nfpre, nfpre_d[:, :])
        load_small(nmpost, nmpost_d[:, :])
        load_small(nfpost, nfpost_d[:, :])
        load_small(fw, fw_d[:, :])
        load_small(fb, fb_d[:, :])
        S.op("pool", lambda e: e.memset(flat(gtail[:]), 0.0), writes=[gtail])
        if not do_p1:
            S.dma("sp", lambda e: e.dma_start(out=stage[:, 0, 0:128], in_=cpp_d[:, CP_IDENT:CP_IDENT + 128]), writes=[stage.r[0]])
            S.op("dve", lambda e: e.tensor_copy(out=identb[:], in_=stage[:, 0, 0:128]), reads=[stage.r[0]], writes=[identb])
        load_cast_weight(lambda k, c0, c1: wout_bf[:, k, c0:c1], lambda k, c0, c1: w_out_d[k * 128:(k + 1) * 128, c0:c1], 1024, 8, None, wout_bf)
        sfn2 = lambda k: nfpre[:, k:k + 1]
        sfn2.buf = nfpre
        load_cast_weight(lambda k, c0, c1: wgu_bf[:, k, c0:c1], lambda k, c0, c1: w_gu_d[k * 128:(k + 1) * 128, c0:c1], 2 * D_FF, 8, sfn2, wgu_bf)
        load_cast_weight(lambda k, c0, c1: wd_bf[:, k, c0:c1], lambda k, c0, c1: w_d_d[k * 128:(k + 1) * 128, c0:c1], 1024, NFF, None, wd_bf)

        o2ctr = [0]
        yctr = [0]
        fctr = [0]

        def rms_finish(ps_list, resid_ap, resid_res, gain, dst_ap, dst_res):
            for i, ps in enumerate(ps_list):
                S.op("act", (lambda e, i=i, ps=ps: e.activation(out=junk2[:], in_=ps[:], func=AF.Square, accum_out=st2[:, i:i + 1])),
                     reads=[ps], writes=[st2, junk2])
            S.op("dve", lambda e: e.tensor_tensor(out=st2[:, 2:3], in0=st2[:, 0:1], in1=st2[:, 1:2], op=ALU.add), reads=[st2], writes=[st2])
            S.op("act", lambda e: e.activation(out=st2[:, 3:4], in_=st2[:, 2:3], func=AF.Ln, scale=1.0 / 1024, bias=epsb[:, 0:1]), reads=[st2, epsb], writes=[st2])
            S.op("act", lambda e: e.activation(out=st2[:, 4:5], in_=st2[:, 3:4], func=AF.Exp, scale=-0.5), reads=[st2], writes=[st2])
            for i, ps in enumerate(ps_list):
                S.op("dve", (lambda e, i=i, ps=ps: e.scalar_tensor_tensor(out=dst_ap[:, i * 512:(i + 1) * 512], in0=ps[:], scalar=st2[:, 4:5], in1=gain[:, i * 512:(i + 1) * 512],
                                                                         op0=ALU.mult, op1=ALU.mult)),
                     reads=[ps, st2, gain], writes=dst_res)
                S.op("pool", (lambda e, i=i: e.tensor_tensor(out=dst_ap[:, i * 512:(i + 1) * 512], in0=dst_ap[:, i * 512:(i + 1) * 512], in1=resid_ap[:, i * 512:(i + 1) * 512], op=ALU.add)),
                     reads=dst_res + resid_res, writes=dst_res)

        yjunk_b = xn
        yjunk = xn[:]

        ring = [psP[0], psP[1], psS, psO, psM]
        rctr = [0]

        def next_ring():
            bnk = ring[rctr[0] % len(ring)]
            rctr[0] += 1
            return bnk

        u2T_b = Buf(stage[:, 1, 0:512].bitcast(BF16).rearrange("p (a b) -> p a b", a=8), "u2T_b")
        u2T_b.r = [stage.r[1]]
        x1s = [(x1[:], x1.r[0]), (stage[:, 0, :], stage.r[0])]
        u2Ts = [u2T, u2T_b]

        def p2_front(scr_row, x_src, slot):
            osl = 0
            S.dma("sp", lambda e: e.dma_start(out=oc2[:, osl, :], in_=ocat_scr[scr_row * 128:(scr_row + 1) * 128, :]), writes=[oc2.r[osl]])
            xsl = xctr[0] % 2
            xctr[0] += 1
            S.dma("sp", lambda e: e.dma_start(out=xin[:, xsl, :], in_=x_src), writes=[xin.r[xsl]])
            S.op("pe", [(lambda e, k=k: e.transpose(psT[:, k * 128:(k + 1) * 128], oc2[:, osl, k * 128:(k + 1) * 128], identb[:])) for k in range(8)],
                 reads=[oc2.r[osl], identb], writes=[psT])
            S.op("act", lambda e: e.activation(out=oT[:], in_=psT[:].rearrange("p (a b) -> p a b", a=8), func=AF.Copy), reads=[psT], writes=[oT])
            proj_tm(wout_bf, 0, 512, oT, 0, psA)
            proj_tm(wout_bf, 512, 512, oT, 0, psB)
            x1a, x1r = x1s[slot]
            rms_finish([psA, psB], xin[:, xsl, :], [xin.r[xsl]], nmpost, x1a, [x1r])
            rms_stats(x1a, [x1r], xn[:], [xn], 1024)
            S.op("dve", lambda e: e.tensor_scalar(out=xn[:], in0=x1a, scalar1=st[:, 2:3], scalar2=None, op0=ALU.mult), reads=[x1r, st], writes=[xn])
            S.op("pe", [(lambda e, k=k: e.transpose(psT[:, k * 128:(k + 1) * 128], xn[:, k * 128:(k + 1) * 128], identb[:])) for k in range(8)],
                 reads=[xn, identb], writes=[psT])
            uu = u2Ts[slot]
            S.op("act", lambda e: e.activation(out=uu[:], in_=psT[:].rearrange("p (a b) -> p a b", a=8), func=AF.Copy), reads=[psT], writes=[uu])

        def p2_body(slot, y_dst, nseq, L, hook):
            NT = 128
            uu = u2Ts[slot]
            x1a, x1r = x1s[slot]
            st_ = {}

            def stageA(c):
                psg = next_ring()
                proj_fm(wgu_bf, c * 128, uu, NT, psg)
                fsl = c % 4
                gx = gxp[:, fsl, 0:nseq * (L + 2)].rearrange("p (s t) -> p s t", s=nseq)
                cvv = cv2[:, fsl, :].rearrange("p (s t) -> p s t", s=nseq)
                S.op("pool", (lambda e: e.tensor_copy(out=gx[:, :, 0:2], in_=gtail[:, c, 0:nseq, :])), reads=[gtail], writes=[gxp.r[fsl]])
                S.op("act", (lambda e: e.activation(out=gx[:, :, 2:2 + L], in_=psg[:, 0:NT].rearrange("p (s t) -> p s t", s=nseq), func=AF.Copy)),
                     reads=[psg], writes=[gxp.r[fsl]])
                S.op("pool", (lambda e: e.tensor_copy(out=gtail[:, c, 0:nseq, :], in_=gx[:, :, L:L + 2])), reads=[gxp.r[fsl]], writes=[gtail])
                psu = next_ring()
                proj_fm(wgu_bf, D_FF + c * 128, uu, NT, psu)
                S.op("act", (lambda e: e.activation(out=upb[:, fsl, :], in_=psu[:, 0:NT], func=AF.Copy)), reads=[psu], writes=[upb.r[fsl]])
                S.op("dve", (lambda e: e.tensor_scalar(out=cvv, in0=gx[:, :, 0:L], scalar1=fw[:, c, 0:1], scalar2=fb[:, c:c + 1], op0=ALU.mult, op1=ALU.add)),
                     reads=[gxp.r[fsl], fw, fb], writes=[cv2.r[fsl]])
                for i in (1, 2):
                    S.op("dve", (lambda e, i=i: e.scalar_tensor_tensor(out=cvv, in0=gx[:, :, i:i + L], scalar=fw[:, c, i:i + 1], in1=cvv, op0=ALU.mult, op1=ALU.add)),
                         reads=[gxp.r[fsl], fw, cv2.r[fsl]], writes=[cv2.r[fsl]])

            def stageB(c):
                fsl = c % 4
                S.op("act", (lambda e: e.activation(out=t2[:, fsl, :], in_=cv2[:, fsl, :], func=AF.Square, scale=0.044715 ** 0.5)), reads=[cv2.r[fsl]], writes=[t2.r[fsl]])
                S.op("dve", (lambda e: e.scalar_tensor_tensor(out=t2[:, fsl, :], in0=t2[:, fsl, :], scalar=1.0, in1=cv2[:, fsl, :], op0=ALU.add, op1=ALU.mult)),
                     reads=[t2.r[fsl], cv2.r[fsl]], writes=[t2.r[fsl]])

            def stageC(c):
                fsl = c % 4
                S.op("act", (lambda e: e.activation(out=sg2[:, fsl, :], in_=t2[:, fsl, :], func=AF.Sigmoid, scale=1.5957691216057308)), reads=[t2.r[fsl]], writes=[sg2.r[fsl]])
                S.op("pool", (lambda e: e.tensor_tensor(out=sg2[:, fsl, :], in0=sg2[:, fsl, :], in1=cv2[:, fsl, :], op=ALU.mult)),
                     reads=[sg2.r[fsl], cv2.r[fsl]], writes=[sg2.r[fsl]])
                S.op("dve", (lambda e: e.tensor_tensor(out=hT[:, c, :], in0=upb[:, fsl, :], in1=sg2[:, fsl, :], op=ALU.mult)),
                     reads=[upb.r[fsl], sg2.r[fsl]], writes=[hT])

            for i in range(NFF + 2):
                if i < NFF:
                    stageA(i)
                if 0 <= i - 1 < NFF:
                    stageB(i - 1)
                if 0 <= i - 2 < NFF:
                    stageC(i - 2)
                if i == 10 and hook is not None:
                    hook()
            S.op("pe", [(lambda e, c=c: e.matmul(psA[:], lhsT=hT[:, c, :], rhs=wd_bf[:, c, 0:512], start=(c == 0), stop=(c == NFF - 1))) for c in range(NFF)],
                 reads=[hT, wd_bf], writes=[psA])
            S.op("pe", [(lambda e, c=c: e.matmul(psB[:], lhsT=hT[:, c, :], rhs=wd_bf[:, c, 512:1024], start=(c == 0), stop=(c == NFF - 1))) for c in range(NFF)],
                 reads=[hT, wd_bf], writes=[psB])
            rms_finish([psA, psB], x1a, [x1r], nfpost, yb[:, 0, :], [yb.r[0]])
            if y_dst is not None:
                S.dma("pool", lambda e: e.dma_start(out=y_dst, in_=yb[:, 0, :]), reads=[yb.r[0]])

        p2_from = cfg.get("p2_from", first_own - 1)
        p2_to = cfg.get("p2_to", p_tiles)
        jobs = []
        for tl in range(p2_from, p2_to):
            own = tl - first_own
            jobs.append(dict(scr=tl - (p_tiles - NSCR + 1), x=xp_d[tl * 128:(tl + 1) * 128, :],
                             y=(y_p[own * 128:(own + 1) * 128, :] if own >= 0 else None), nseq=1, L=128, sample=False))
        if cfg.get("sample", True):
            jobs.append(dict(scr=NSCR - 1, x=xs_d[:, :], y=y_s[:, :], nseq=4, L=32, sample=True))
        p2_front(jobs[0]["scr"], jobs[0]["x"], 0)
        for ji, jb in enumerate(jobs):
            nxt = jobs[ji + 1] if ji + 1 < len(jobs) else None
            hook = (lambda nxt=nxt, ji=ji: p2_front(nxt["scr"], nxt["x"], (ji + 1) % 2)) if nxt is not None else None
            if jb["sample"]:
                S.dma("pool", lambda e: e.dma_start(out=fc_p[:, :].rearrange("p (a b) -> p a b", a=NFF), in_=gtail[:, :, 0, :]), reads=[gtail])
                load_small(gtail, fc0_d[:, :])
            p2_body(ji % 2, jb["y"], jb["nseq"], jb["L"], hook)
        if jobs[-1]["sample"]:
            S.dma("pool", lambda e: e.dma_start(out=fc_s[:, :], in_=flat(gtail[:])), reads=[gtail])
        else:
            S.dma("pool", lambda e: e.dma_start(out=fc_p[:, :].rearrange("p (a b) -> p a b", a=NFF), in_=gtail[:, :, 0, :]), reads=[gtail])

    S.finish()
    with nc.Block() as block:
        @block.tensor
        def _(e):
            S.replay("pe", e)

        @block.vector
        def _(e):
            S.replay("dve", e)

        @block.scalar
        def _(e):
            S.replay("act", e)

        @block.gpsimd
        def _(e):
            S.replay("pool", e)

        @block.sync
        def _(e):
            S.replay("sp", e)
    es.close()
    print("instructions:", S.ninstr, {k: len(v) for k, v in S.ops.items()}, "sems", len(S.sems))
    return nc


def _cpack(bs):
    m = np.arange(128)
    same = (m[:, None] // bs) == (m[None, :] // bs)
    tri = same & (m[:, None] <= m[None, :])
    strict = same & (m[:, None] > m[None, :])
    incl = same & (m[:, None] >= m[None, :])
    ident = np.eye(128, dtype=bool)
    nb = 128 // bs
    bm = np.zeros((128, 4), np.float32)
    cm = np.zeros((128, 4, 128), np.float32)
    for b in range(nb):
        bm[b * bs:(b + 1) * bs, b] = 1.0
        cm[:, b, :] = bm[:, b:b + 1]
    parts = [tri, same, np.tile(strict, (1, 4)), np.tile(incl, (1, 4)), np.tile(ident, (1, 4)), bm, -bm, cm.reshape(128, 512)]
    out = np.concatenate([np.asarray(p, np.float32) for p in parts], axis=1)
    assert out.shape == (128, CP_W)
    return np.ascontiguousarray(out)


def _bias_tables(rel_bias):
    tab = np.asarray(rel_bias, np.float32)
    kj = np.arange(128)[:, None, None]
    r = np.arange(5)[None, :, None]
    qi = np.arange(128)[None, None, :]
    rel_p = (4 - r) * 128 + qi - kj
    idx_p = np.clip(rel_p, -128, 128) + 128
    btp = tab[:, idx_p].transpose(1, 2, 0, 3)
    mask = ((r == 4) & (kj >= 64) & (qi < 64)) | ((r == 0) & (kj < 64) & (qi >= 64))
    btm = np.where(mask, np.float32(NEG), np.float32(0.0)).astype(np.float32)
    btm = np.broadcast_to(btm[:, :, None, :], (128, 5, 8, 128))
    rel_s = np.where(r < 4, (4 - r) * 128 + (qi % 32) - kj, (qi % 32) - (kj % 32))
    idx_s = np.clip(rel_s, -128, 128) + 128
    bts = tab[:, idx_s].transpose(1, 2, 0, 3)
    f = lambda a: np.ascontiguousarray(np.asarray(a, np.float32).reshape(128, 5 * 8 * 128))
    return f(btp), f(btm), f(bts)


def _smk():
    out = np.zeros((128, SMK_W), np.float32)
    q = np.arange(128)
    for s in range(4):
        cm = np.where(q // 32 == s, 0.0, NEG).astype(np.float32)
        out[0, s * 512:(s + 1) * 512] = np.tile(cm, 4)
        out[0, 2048 + s * 128:2048 + (s + 1) * 128] = cm
    out[0, 2560:2688] = 1.0
    out[0, 2688:3200] = 1.0
    return out


def make_in_maps(inp):
    f32 = lambda a: np.ascontiguousarray(np.asarray(a, np.float32))
    xpr, xsm = f32(inp["x_prompt"]), f32(inp["x_sample"])
    ckf = f32(inp["cache_band_k"])[0].reshape(32, 512, 512)
    cvf = f32(inp["cache_band_v"])[0].reshape(32, 512, 512)
    sdl = f32(inp["state_delta"])[0]
    sqc = f32(inp["state_qkv_conv"])[0]
    sfc = f32(inp["state_ffn_conv"])[0]
    rep = lambda v, n: np.ascontiguousarray(np.broadcast_to(f32(v).reshape(1, -1), (128, n)))
    pk = lambda v: np.ascontiguousarray(f32(v).reshape(-1, 128).T)
    btp, btm, bts = _bias_tables(inp["rel_bias"][0])
    common = dict(
        w_in=f32(inp["w_in"][0]), w_out=f32(inp["w_out"][0]), w_gu=f32(inp["w_gate_up"][0]), w_d=f32(inp["w_down"][0]),
        nmpre=pk(inp["norm_mix_pre"][0]), nfpre=pk(inp["norm_ffn_pre"][0]),
        nmpost=rep(inp["norm_mix_post"][0], 1024), nfpost=rep(inp["norm_ffn_post"][0], 1024),
        cw=np.ascontiguousarray(f32(inp["qkv_conv_w"][0]).reshape(4, 12, 128).transpose(2, 1, 0).reshape(128, 48)),
        fw=np.ascontiguousarray(f32(inp["ffn_conv_w"][0]).reshape(3, NFF, 128).transpose(2, 1, 0).reshape(128, NFF * 3)),
        fb=pk(inp["ffn_conv_b"][0]),
        alog=rep(inp["a_log"][0], 4), dtb=rep(inp["dt_bias"][0], 4),
        gnw=rep(np.tile(f32(inp["gdn_norm_w"][0]), 4), 512), anw=rep(np.tile(f32(inp["attn_norm_w"][0]), 8), 512),
        btp=btp, btm=btm, bts=bts, cpp=_cpack(64), cps=_cpack(32), smk=_smk(),
    )
    maps = []
    for c in range(8):
        s, half = c // 2, c % 2
        T0 = half * 2048
        xw = np.zeros((4096, 1024), np.float32)
        if half == 0:
            xw[2048:] = xpr[s, 0:2048]
        else:
            xw[:] = xpr[s]
        pos = T0 - 2048 + np.arange(4096)
        kval = (pos >= 0).astype(np.float32).reshape(32, 128).T
        sq = slice(4 * c, 4 * c + 4)
        m = dict(common)
        m.update(
            xp=xw, xs=np.ascontiguousarray(xsm[sq].reshape(128, 1024)), kvalid=np.ascontiguousarray(kval),
            ck=np.ascontiguousarray(ckf[sq]), cv=np.ascontiguousarray(cvf[sq]), s0=np.ascontiguousarray(sdl[sq]),
            qc0=np.ascontiguousarray(sqc[sq].reshape(4, 3, 12, 128).transpose(3, 2, 0, 1).reshape(128, 144)),
            fc0=np.ascontiguousarray(sfc[sq].reshape(4, 2, NFF, 128).transpose(3, 2, 0, 1).reshape(128, NFF * 8)),
        )
        maps.append(m)
    return maps


def assemble(results):
    y_prompt = np.zeros((4, 4096, 1024), np.float32)
    y_sample = np.zeros((32, 32, 1024), np.float32)
    bkp = np.zeros((1, 4, 512, 8, 64), np.float32)
    bvp = np.zeros((1, 4, 512, 8, 64), np.float32)
    dlp = np.zeros((1, 4, 4, 128, 128), np.float32)
    qcp = np.zeros((1, 4, 3, 1536), np.float32)
    fcp = np.zeros((1, 4, 2, D_FF), np.float32)
    bks = np.zeros((1, 32, 32, 8, 64), np.float32)
    bvs = np.zeros((1, 32, 32, 8, 64), np.float32)
    dls = np.zeros((1, 32, 4, 128, 128), np.float32)
    qcs = np.zeros((1, 32, 3, 1536), np.float32)
    fcs = np.zeros((1, 32, 2, D_FF), np.float32)
    for c, r in enumerate(results):
        s, half = c // 2, c % 2
        y_prompt[s, half * 2048:(half + 1) * 2048] = r["y_p"]
        if half == 1:
            bkp[0, s] = r["bk_p"].reshape(512, 8, 64)
            bvp[0, s] = r["bv_p"].reshape(512, 8, 64)
            dlp[0, s] = r["S_p"].reshape(128, 4, 128).transpose(1, 0, 2)
            qcp[0, s] = r["qc_p"].reshape(128, 12, 3).transpose(2, 1, 0).reshape(3, 1536)
            fcp[0, s] = r["fc_p"].reshape(128, NFF, 2).transpose(2, 1, 0).reshape(2, D_FF)
        sq = slice(4 * c, 4 * c + 4)
        y_sample[sq] = r["y_s"].reshape(4, 32, 1024)
        bks[0, sq] = r["bk_s"].reshape(4, 32, 8, 64)
        bvs[0, sq] = r["bv_s"].reshape(4, 32, 8, 64)
        dls[0, sq] = r["S_s"].reshape(128, 4, 4, 128).transpose(1, 2, 0, 3)
        qcs[0, sq] = r["qc_s"].reshape(128, 12, 4, 3).transpose(2, 3, 1, 0).reshape(4, 3, 1536)
        fcs[0, sq] = r["fc_s"].reshape(128, NFF, 4, 2).transpose(2, 3, 1, 0).reshape(4, 2, D_FF)
    return (y_prompt, y_sample, bkp, bvp, dlp, qcp, fcp, bks, bvs, dls, qcs, fcs)


def kernel(**inputs):
    nc = build_program()
    maps = make_in_maps(inputs)
    res = run_bass_kernel_spmd(nc, maps, core_ids=list(range(8)))
    return assemble(res.results)
```

```python
import os
import numpy as np
from contextlib import ExitStack
import ml_dtypes
import concourse.bass as bass
import concourse.mybir as mybir
from concourse.bass_utils import run_bass_kernel_spmd

F32 = mybir.dt.float32
BF16 = mybir.dt.bfloat16
F32R = mybir.dt.float32r
AF = mybir.ActivationFunctionType
ALU = mybir.AluOpType

D_MODEL = 1024
SEQ = 4096
GDN_H = 4
ATT_H = 8
D_FF = 2816
NFF = 22
IN_COLS = 3592
EPS = 1e-6
NEG = -30000.0
C_Q, C_K, C_V, C_Z, C_B, C_A, C_QB, C_KB, C_VB = 0, 512, 1024, 1536, 2048, 2052, 2056, 2568, 3080
CP_TRI, CP_SAME, CP_STRICT, CP_INCL, CP_IDENT, CP_BM, CP_NBM, CP_CM = 0, 128, 256, 768, 1280, 1792, 1796, 1800
CP_W = 1800 + 512


class Res:
    __slots__ = ("name", "lw", "rd")

    def __init__(self, name):
        self.name = name
        self.lw = None
        self.rd = {}


class Buf:
    def __init__(self, h, name, nslots=1):
        self.h = h
        self.r = [Res(f"{name}.{i}") for i in range(nslots)]

    def __getitem__(self, idx):
        return self.h[idx]


def _res(x):
    out = []
    for a in x:
        if isinstance(a, Buf):
            out.extend(a.r)
        else:
            out.append(a)
    return out


class Sched:
    CE = ("pe", "dve", "act", "pool")
    ROT = 20000

    def __init__(self, nc, es):
        self.nc, self.es = nc, es
        self.ops = {e: [] for e in ("pe", "dve", "act", "pool", "sp")}
        self.sems = []
        self.cur = {}
        self.cnt = {}
        for e in self.CE:
            self._newsem(e)
        self.waited = {e: {} for e in self.ops}
        self.dpool = {q: [] for q in ("sp", "pool", "act")}
        self.dnext = {q: 0 for q in self.dpool}
        for q, n in (("sp", 12), ("pool", 12), ("act", 4)):
            for i in range(n):
                s = es.enter_context(nc.semaphore(f"d_{q}{i}"))
                self.sems.append(s)
                self.dpool[q].append([len(self.sems) - 1, 0])
        self.ninstr = 0

    def _newsem(self, e):
        s = self.es.enter_context(self.nc.semaphore(f"s_{e}{len(self.sems)}"))
        self.sems.append(s)
        self.cur[e] = len(self.sems) - 1
        self.cnt[e] = 0

    def _deps(self, eng, reads, writes, strict=True):
        deps = {}

        def add(t):
            if t is None:
                return
            if deps.get(t[0], -1) < t[1]:
                deps[t[0]] = t[1]
        for r in reads:
            add(r.lw)
        for w in writes:
            add(w.lw)
            for k, v in w.rd.items():
                add((k, v))
        waits = []
        wd = self.waited[eng]
        for k, v in deps.items():
            if not strict and eng in self.cur and k == self.cur[eng]:
                continue
            if wd.get(k, -1) >= v:
                continue
            wd[k] = v
            waits.append((k, v))
        return waits

    def _mark(self, ticket, reads, writes):
        for w in writes:
            w.lw = ticket
            w.rd = {}
        for r in reads:
            if r.rd.get(ticket[0], -1) < ticket[1]:
                r.rd[ticket[0]] = ticket[1]

    def op(self, eng, fns, reads=(), writes=()):
        reads, writes = _res(reads), _res(writes)
        if not isinstance(fns, (list, tuple)):
            fns = [fns]
        waits = self._deps(eng, reads, writes, strict=(eng != "pe"))
        if self.cnt[eng] >= self.ROT:
            self._newsem(eng)
        self.cnt[eng] += 1
        ticket = (self.cur[eng], self.cnt[eng])
        n = len(fns)
        for i, f in enumerate(fns):
            self.ops[eng].append((waits if i == 0 else (), f, ticket[0] if i == n - 1 else None, 1))
        self.ninstr += n
        self._mark(ticket, reads, writes)

    def dma(self, q, fn, reads=(), writes=()):
        reads, writes = _res(reads), _res(writes)
        waits = self._deps(q, reads, writes)
        pool = self.dpool[q]
        slot = pool[self.dnext[q] % len(pool)]
        self.dnext[q] += 1
        wd = self.waited[q]
        if slot[1] > 0 and wd.get(slot[0], -1) < slot[1]:
            wd[slot[0]] = slot[1]
            waits.append((slot[0], slot[1]))
        slot[1] += 16
        ticket = (slot[0], slot[1])
        self.ops[q].append((waits, fn, slot[0], 16))
        self.ninstr += 1
        self._mark(ticket, reads, writes)

    def barrier(self):
        allw = []
        for q in self.dpool:
            for si, tgt in self.dpool[q]:
                if tgt > 0:
                    allw.append((si, tgt))
        for e in self.CE:
            if self.cnt[e] > 0:
                allw.append((self.cur[e], self.cnt[e]))
        for eng in self.ops:
            wd = self.waited[eng]
            waits = [(k, v) for k, v in allw if wd.get(k, -1) < v]
            for k, v in waits:
                wd[k] = v
            self.ops[eng].append((waits, None, None, 0))

    def finish(self):
        waits = []
        for q in self.dpool:
            for si, tgt in self.dpool[q]:
                if tgt > 0:
                    waits.append((si, tgt))
        for e in self.CE:
            if self.cnt[e] > 0:
                waits.append((self.cur[e], self.cnt[e]))
        self.ops["sp"].append((waits, None, None, 0))

    def replay(self, eng, e):
        sems = self.sems
        for waits, fn, inc, amt in self.ops[eng]:
            for k, v in waits:
                e.wait_ge(sems[k], v)
            if fn is None:
                continue
            ins = fn(e)
            if inc is not None:
                ins.then_inc(sems[inc], amt)


ARENA_F32 = 53000
SMK_W = 4 * 512 + 4 * 128 + 128 + 512
NSCR = 19


def build_program(cfg=None):
    cfg = cfg or {}
    nc = bass.Bass("TRN2", target_bir_lowering=False)
    es = ExitStack()

    def DI(name, shape, dt=F32):
        return nc.dram_tensor(name, list(shape), dt, kind="ExternalInput").ap()

    def DO(name, shape, dt=F32):
        return nc.dram_tensor(name, list(shape), dt, kind="ExternalOutput").ap()

    xp_d = DI("xp", [4096, 1024])
    xs_d = DI("xs", [128, 1024])
    kvalid_d = DI("kvalid", [128, 32])
    ck_d = DI("ck", [4, 512, 512])
    cv_d = DI("cv", [4, 512, 512])
    s0_d = DI("s0", [4, 4, 128, 128])
    qc0_d = DI("qc0", [128, 12 * 4 * 3])
    fc0_d = DI("fc0", [128, NFF * 4 * 2])
    w_in_d = DI("w_in", [1024, IN_COLS])
    w_out_d = DI("w_out", [1024, 1024])
    w_gu_d = DI("w_gu", [1024, 2 * D_FF])
    w_d_d = DI("w_d", [D_FF, 1024])
    nmpre_d = DI("nmpre", [128, 8])
    nfpre_d = DI("nfpre", [128, 8])
    nmpost_d = DI("nmpost", [128, 1024])
    nfpost_d = DI("nfpost", [128, 1024])
    cw_d = DI("cw", [128, 48])
    fw_d = DI("fw", [128, NFF * 3])
    fb_d = DI("fb", [128, NFF])
    alog_d = DI("alog", [128, 4])
    dtb_d = DI("dtb", [128, 4])
    gnw_d = DI("gnw", [128, 512])
    anw_d = DI("anw", [128, 512])
    btp_d = DI("btp", [128, 5 * 8 * 128])
    btm_d = DI("btm", [128, 5 * 8 * 128])
    bts_d = DI("bts", [128, 5 * 8 * 128])
    cpp_d = DI("cpp", [128, CP_W])
    cps_d = DI("cps", [128, CP_W])
    smk_d = DI("smk", [128, SMK_W])

    y_p = DO("y_p", [2048, 1024])
    bk_p = DO("bk_p", [512, 512])
    bv_p = DO("bv_p", [512, 512])
    S_p = DO("S_p", [128, 512])
    qc_p = DO("qc_p", [128, 36])
    fc_p = DO("fc_p", [128, NFF * 2])
    y_s = DO("y_s", [128, 1024])
    bk_s = DO("bk_s", [128, 512])
    bv_s = DO("bv_s", [128, 512])
    S_s = DO("S_s", [128, 4 * 512])
    qc_s = DO("qc_s", [128, 12 * 4 * 3])
    fc_s = DO("fc_s", [128, NFF * 4 * 2])
    ocat_scr = nc.dram_tensor("ocat_scr", [NSCR * 128, 1024], BF16).ap()
    dbg = {}
    for name, shape in (cfg.get("dbg") or {}).items():
        dbg[name] = DO("dbg_" + name, shape, BF16 if name.startswith(("ocat", "bf_")) else F32)

    S = Sched(nc, es)
    dump_tile = cfg.get("dump_tile", -1)

    def DUMP(name, ap, res, tag):
        if name in dbg and tag == dump_tile:
            S.dma("pool", lambda e: e.dma_start(out=dbg[name][:, :], in_=ap), reads=res)

    arena = es.enter_context(nc.sbuf_tensor("arena", [128, ARENA_F32], F32))
    aptr = [0]

    def SB(name, shape, dt=F32, nslots=1, at=None):
        n = int(np.prod(shape[1:]))
        nf = n if dt == F32 else (n + 1) // 2
        nf = (nf + 1) // 2 * 2
        if at is not None:
            off = at[0]
            at[0] += nf
            assert at[0] <= at[1], f"alias region overflow at {name}"
        else:
            off = aptr[0]
            aptr[0] += nf
        assert aptr[0] <= ARENA_F32, f"SBUF arena overflow at {name}: {aptr[0]}"
        v = arena[:, off:off + nf]
        if dt != F32:
            v = v.bitcast(dt)[:, 0:n]
        if len(shape) == 3:
            v = v.rearrange("p (a b) -> p a b", a=shape[1])
        elif len(shape) == 4:
            v = v.rearrange("p (a b c) -> p a b c", a=shape[1], b=shape[2])
        return Buf(v, name, nslots)

    def flat(ap):
        nd = len(ap.shape)
        if nd == 2:
            return ap
        if nd == 3:
            return ap.rearrange("p a b -> p (a b)")
        return ap.rearrange("p a b c -> p (a b c)")

    def PS(name, shape, dt=F32):
        h = es.enter_context(nc.psum_tensor(name, list(shape), dt))
        return Buf(h, name, 1)

    psT = PS("psT", [128, 1024], BF16)
    psP = [PS("psP0", [128, 512]), PS("psP1", [128, 512])]
    psA = PS("psA", [128, 512])
    psB = PS("psB", [128, 512])
    psS = PS("psS", [128, 512])
    psO = PS("psO", [128, 512])
    psM = PS("psM", [128, 512])
    pctr = [0]

    def next_psP():
        pctr[0] += 1
        return psP[pctr[0] % 2]

    identb = SB("identb", [128, 128], BF16)
    stage = SB("stage", [128, 2, 1024], F32, nslots=2)
    xin = SB("xin", [128, 2, 1024], F32, nslots=2)
    xn = SB("xn", [128, 1024], BF16)
    st = SB("st", [128, 8])
    epsb = SB("epsb", [128, 2])
    shared_end = aptr[0]

    stctr = [0]

    def load_cast_weight(dst_fn, src_ap_fn, ncols, nk, scale_ap_fn, dstbuf):
        for k in range(nk):
            for c0 in range(0, ncols, 1024):
                c1 = min(ncols, c0 + 1024)
                sl = stctr[0] % 2
                stctr[0] += 1
                w = c1 - c0
                S.dma("sp", (lambda e, sl=sl, k=k, c0=c0, c1=c1, w=w: e.dma_start(out=stage[:, sl, 0:w], in_=src_ap_fn(k, c0, c1))),
                      writes=[stage.r[sl]])
                sc = scale_ap_fn(k) if scale_ap_fn else None
                rds = [stage.r[sl]] + ([scale_ap_fn.buf] if scale_ap_fn else [])
                if stctr[0] % 2:
                    if sc is None:
                        S.op("act", (lambda e, sl=sl, k=k, c0=c0, c1=c1, w=w: e.activation(out=dst_fn(k, c0, c1), in_=stage[:, sl, 0:w], func=AF.Copy)),
                             reads=rds, writes=[dstbuf])
                    else:
                        S.op("act", (lambda e, sl=sl, k=k, c0=c0, c1=c1, w=w, sc=sc: e.activation(out=dst_fn(k, c0, c1), in_=stage[:, sl, 0:w], func=AF.Copy, scale=sc)),
                             reads=rds, writes=[dstbuf])
                else:
                    if sc is None:
                        S.op("dve", (lambda e, sl=sl, k=k, c0=c0, c1=c1, w=w: e.tensor_copy(out=dst_fn(k, c0, c1), in_=stage[:, sl, 0:w])),
                             reads=rds, writes=[dstbuf])
                    else:
                        S.op("dve", (lambda e, sl=sl, k=k, c0=c0, c1=c1, w=w, sc=sc: e.tensor_scalar(out=dst_fn(k, c0, c1), in0=stage[:, sl, 0:w], scalar1=sc, scalar2=None, op0=ALU.mult)),
                             reads=rds, writes=[dstbuf])

    def load_small(buf, dram_ap):
        S.dma("sp", (lambda e: e.dma_start(out=flat(buf[:]), in_=dram_ap)), writes=[buf])

    def rms_stats(src_ap, src_res, junk_ap, junk_res, ncols):
        S.op("act", lambda e: e.activation(out=junk_ap, in_=src_ap, func=AF.Square, accum_out=st[:, 0:1]),
             reads=src_res, writes=[st] + junk_res)
        S.op("act", lambda e: e.activation(out=st[:, 1:2], in_=st[:, 0:1], func=AF.Ln, scale=1.0 / ncols, bias=epsb[:, 0:1]),
             reads=[st, epsb], writes=[st])
        S.op("act", lambda e: e.activation(out=st[:, 2:3], in_=st[:, 1:2], func=AF.Exp, scale=-0.5),
             reads=[st], writes=[st])

    S.op("pool", lambda e: e.memset(epsb[:, 0:1], EPS), writes=[epsb])
    S.op("pool", lambda e: e.memset(epsb[:, 1:2], 1.0), writes=[epsb])

    do_p1 = cfg.get("p1", True)
    do_p2 = cfg.get("p2", True)
    NTL1 = cfg.get("ntl1", 2)
    p_tiles = 32
    first_own = 16
    p1_from = cfg.get("p1_from", 0)
    p1_to = cfg.get("p1_to", p_tiles)
    xctr = [0]

    def norm_transpose(src_dram_ap, dstT, col0, junk=None):
        sl = xctr[0] % 2
        xctr[0] += 1
        S.dma("sp", lambda e: e.dma_start(out=xin[:, sl, :], in_=src_dram_ap), writes=[xin.r[sl]])
        rms_stats(xin[:, sl, :], [xin.r[sl]], xn[:], [xn], 1024)
        S.op("dve", lambda e: e.tensor_scalar(out=xn[:], in0=xin[:, sl, :], scalar1=st[:, 2:3], scalar2=None, op0=ALU.mult),
             reads=[xin.r[sl], st], writes=[xn])
        S.op("pe", [(lambda e, k=k: e.transpose(psT[:, k * 128:(k + 1) * 128], xn[:, k * 128:(k + 1) * 128], identb[:])) for k in range(8)],
             reads=[xn, identb], writes=[psT])
        S.op("act", lambda e: e.activation(out=dstT[:, :, col0:col0 + 128], in_=psT[:].rearrange("p (a b) -> p a b", a=8), func=AF.Copy),
             reads=[psT], writes=[dstT])
        return sl

    def proj_fm(wbuf, wcol0, rhsT, NT, ps, nk=8):
        S.op("pe", [(lambda e, k=k: e.matmul(ps[:, 0:NT], lhsT=wbuf[:, k, wcol0:wcol0 + 128], rhs=rhsT[:, k, 0:NT], start=(k == 0), stop=(k == nk - 1))) for k in range(nk)],
             reads=[wbuf, rhsT], writes=[ps])

    def proj_tm(wbuf, wcol0, ncols, lhsT_buf, col0, ps, pcol0=0, nk=8):
        S.op("pe", [(lambda e, k=k: e.matmul(ps[:, pcol0:pcol0 + ncols], lhsT=lhsT_buf[:, k, col0:col0 + 128], rhs=wbuf[:, k, wcol0:wcol0 + ncols], start=(k == 0), stop=(k == nk - 1))) for k in range(nk)],
             reads=[wbuf, lhsT_buf], writes=[ps])

    if do_p1:
        aptr[0] = shared_end
        NT1 = NTL1 * 128
        cp = SB("cp", [128, CP_W])
        tri2 = cp[:, CP_TRI:CP_TRI + 128]
        same2 = cp[:, CP_SAME:CP_SAME + 128]
        identf = cp[:, CP_IDENT:CP_IDENT + 128]

        def c4(off):
            return cp[:, off:off + 512].rearrange("p (a b) -> p a b", a=4)

        strict4, incl4, ident4, cm4 = c4(CP_STRICT), c4(CP_INCL), c4(CP_IDENT), c4(CP_CM)
        w_in_bf = SB("w_in_bf", [128, 8, IN_COLS], BF16)
        nmpre = SB("nmpre", [128, 8])
        cw = SB("cw", [128, 12, 4])
        alog = SB("alog", [128, 4])
        nea = SB("nea", [128, 4])
        dtb = SB("dtb", [128, 4])
        gnw4 = SB("gnw4", [128, 4, 128])
        anw8 = SB("anw8", [128, 8, 64])
        kvalid = SB("kvalid", [128, 32])
        BT = SB("BT", [128, 5, 8, 128], BF16)
        uT0 = SB("uT", [128, 8, NT1], BF16)
        xpb = SB("xpb", [128, 2, NT1 + 16], F32, nslots=2)
        cvb = SB("cvb", [128, 4, NT1], F32, nslots=4)
        sqb = SB("sqb", [128, 2, NT1], F32, nslots=2)
        rnb = SB("rnb", [128, 2, NT1], F32, nslots=2)
        gta = SB("gta", [128, 16])
        knT0 = SB("knT", [128, 4, NT1], BF16)
        qnT0 = SB("qnT", [128, 4, NT1], BF16)
        vT0 = SB("vT", [128, 4, NT1], BF16)
        zs0 = SB("zs", [128, NTL1, 512], BF16, nslots=NTL1)
        graw = SB("graw", [128, 2, NTL1, 8], F32, nslots=2 * NTL1)
        kbT = SB("kbT", [128, 4, 8, 128], BF16, nslots=8)
        V1 = SB("V1", [128, 8, 8, 64], BF16, nslots=8)
        vcol = SB("vcol", [128, 8], BF16, nslots=8)
        qbz0 = SB("qbz", [128, 4, 2, NT1], BF16)
        qtail = SB("qtail", [128, 12, 4, 3])
        ones128 = SB("ones128", [128, 128])
        gt = SB("gt", [128, 80])
        Rb = SB("Rb", [128, 4, 128])
        Eb = SB("Eb", [128, 4, 128])
        tmpb = SB("tmpb", [128, 4, 128])
        Qb = [SB("Qb0", [128, 4, 128]), SB("Qb1", [128, 4, 128])]
        Pb = [SB("Pb0", [128, 4, 128]), SB("Pb1", [128, 4, 128])]
        Xb = SB("Xb", [128, 4, 128])
        Xbf = SB("Xbf", [128, 4, 128], BF16)
        qkb = SB("qkb", [128, 4, 128], BF16)
        qkT = SB("qkT", [128, 4, 128], BF16)
        kbg = SB("kbg", [128, 4, 128], BF16)
        kgm = SB("kgm", [128, 4, 4, 128], BF16)
        vbb = SB("vbb", [128, 4, 128], BF16)
        vn32 = SB("vn32", [128, 4, 128])
        vnb = SB("vnb", [128, 4, 128], BF16)
        wTb = SB("wTb", [128, 4, 128], BF16)
        oacc = SB("oacc", [128, 4, 128])
        Sp = SB("Sp", [128, 4, 128])
        Spb = SB("Spb", [128, 4, 128], BF16)
        ureg0 = aptr[0]
        Ss = SB("Ss", [128, 4, 4, 128], F32, nslots=4)
        Ssb = SB("Ssb", [128, 4, 4, 128], BF16, nslots=4)
        smk = SB("smk", [128, SMK_W], BF16)
        ureg = [ureg0, aptr[0]]
        uT1 = SB("uT1", [128, 8, NT1], BF16, at=ureg)
        knT1 = SB("knT1", [128, 4, NT1], BF16, at=ureg)
        qnT1 = SB("qnT1", [128, 4, NT1], BF16, at=ureg)
        vT1 = SB("vT1", [128, 4, NT1], BF16, at=ureg)
        zs1 = SB("zs1", [128, NTL1, 512], BF16, nslots=NTL1, at=ureg)
        qbz1 = SB("qbz1", [128, 4, 2, NT1], BF16, at=ureg)
        uT, knT, qnT, vT, zs, qbz = [uT0, uT1], [knT0, knT1], [qnT0, qnT1], [vT0, vT1], [zs0, zs1], [qbz0, qbz1]
        scb = SB("scb", [128, 4, 128])
        PT = SB("PT", [128, 2, 4, 128], BF16, nslots=2)
        ob32 = SB("ob32", [128, 8, 64])
        ocat = SB("ocat", [128, 2, 1024], BF16, nslots=2)
        kvout = SB("kvout", [128, 1, 512], F32, nslots=1)
        print("phase1 arena used", aptr[0], "of", ARENA_F32)

        load_small(cp, cpp_d[:, :])
        S.op("dve", lambda e: e.tensor_copy(out=identb[:], in_=identf), reads=[cp], writes=[identb])
        load_small(nmpre, nmpre_d[:, :])
        load_small(cw, cw_d[:, :])
        load_small(alog, alog_d[:, :])
        load_small(dtb, dtb_d[:, :])
        load_small(gnw4, gnw_d[:, :])
        load_small(anw8, anw_d[:, :])
        load_small(kvalid, kvalid_d[:, :])
        S.op("act", lambda e: e.activation(out=nea[:], in_=alog[:], func=AF.Exp), reads=[alog], writes=[nea])
        S.op("dve", lambda e: e.tensor_scalar(out=nea[:], in0=nea[:], scalar1=-1.0, scalar2=None, op0=ALU.mult), reads=[nea], writes=[nea])
        S.op("pool", lambda e: e.memset(ones128[:], 1.0), writes=[ones128])
        S.op("pool", lambda e: e.memset(flat(qbz0[:]), 0.0), writes=[qbz0])
        S.op("pool", lambda e: e.memset(flat(qbz1[:]), 0.0), writes=[qbz1])
        S.op("pool", lambda e: e.memset(flat(qtail[:]), 0.0), writes=[qtail])
        S.op("pool", lambda e: e.memset(flat(Sp[:]), 0.0), writes=[Sp])
        S.op("pool", lambda e: e.memset(flat(Spb[:]), 0.0), writes=[Spb])
        S.op("pool", lambda e: e.tensor_copy(out=vcol[:], in_=kvalid[:, 0:8]), reads=[kvalid], writes=[vcol])

        def load_BT(tab_d, mask_d):
            BTf = flat(BT[:])
            for i in range(5):
                S.dma("sp", (lambda e, i=i: e.dma_start(out=stage[:, 0, :], in_=tab_d[:, i * 1024:(i + 1) * 1024])), writes=[stage.r[0]])
                if mask_d is not None:
                    S.dma("sp", (lambda e, i=i: e.dma_start(out=stage[:, 1, :], in_=mask_d[:, i * 1024:(i + 1) * 1024])), writes=[stage.r[1]])
                    S.op("dve", (lambda e, i=i: e.tensor_tensor(out=BTf[:, i * 1024:(i + 1) * 1024], in0=stage[:, 0, :], in1=stage[:, 1, :], op=ALU.add)),
                         reads=[stage], writes=[BT])
                else:
                    S.op("dve", (lambda e, i=i: e.tensor_copy(out=BTf[:, i * 1024:(i + 1) * 1024], in_=stage[:, 0, :])),
                         reads=[stage.r[0]], writes=[BT])

        load_BT(btp_d, btm_d)
        sfn = lambda k: nmpre[:, k:k + 1]
        sfn.buf = nmpre
        load_cast_weight(lambda k, c0, c1: w_in_bf[:, k, c0:c1],
                         lambda k, c0, c1: w_in_d[k * 128:(k + 1) * 128, c0:c1],
                         IN_COLS, 8, sfn, w_in_bf)

        cvctr = [0]

        def stA(c, nseq, L, NT, xsl, csl, need_conv, par):
            ps = next_psP()
            proj_fm(w_in_bf, c * 128, uT[par], NT, ps)
            xpv = xpb[:, xsl, 0:nseq * (L + 3)].rearrange("p (s t) -> p s t", s=nseq)
            cvv = cvb[:, csl, 0:NT].rearrange("p (s t) -> p s t", s=nseq)
            S.op("pool", lambda e: e.tensor_copy(out=xpv[:, :, 0:3], in_=qtail[:, c, 0:nseq, :]), reads=[qtail], writes=[xpb.r[xsl]])
            S.op("act", lambda e: e.activation(out=xpv[:, :, 3:3 + L], in_=ps[:, 0:NT].rearrange("p (s t) -> p s t", s=nseq), func=AF.Copy),
                 reads=[ps], writes=[xpb.r[xsl]])
            S.op("pool", lambda e: e.tensor_copy(out=qtail[:, c, 0:nseq, :], in_=xpv[:, :, L:L + 3]), reads=[xpb.r[xsl]], writes=[qtail])
            if not need_conv:
                return
            S.op("dve", lambda e: e.tensor_scalar(out=cvv, in0=xpv[:, :, 0:L], scalar1=cw[:, c, 0:1], scalar2=None, op0=ALU.mult),
                 reads=[xpb.r[xsl], cw], writes=[cvb.r[csl]])
            for i in range(1, 4):
                S.op("dve", (lambda e, i=i: e.scalar_tensor_tensor(out=cvv, in0=xpv[:, :, i:i + L], scalar=cw[:, c, i:i + 1], in1=cvv, op0=ALU.mult, op1=ALU.add)),
                     reads=[xpb.r[xsl], cw, cvb.r[csl]], writes=[cvb.r[csl]])

        def stB(c, NT, csl, qsl, par):
            sg = sqb[:, qsl, 0:NT]
            cvs = cvb[:, csl, 0:NT]
            S.op("act", lambda e: e.activation(out=sg, in_=cvs, func=AF.Exp, scale=-1.0), reads=[cvb.r[csl]], writes=[sqb.r[qsl]])
            S.op("act", lambda e: e.activation(out=sg, in_=sg, func=AF.Ln, bias=epsb[:, 1:2]), reads=[sqb.r[qsl], epsb], writes=[sqb.r[qsl]])
            S.op("act", lambda e: e.activation(out=sg, in_=sg, func=AF.Exp, scale=-1.0), reads=[sqb.r[qsl]], writes=[sqb.r[qsl]])
            if c >= 8:
                S.op("pool", lambda e: e.tensor_tensor(out=vT[par][:, c - 8, 0:NT], in0=cvs, in1=sg, op=ALU.mult), reads=[cvb.r[csl], sqb.r[qsl]], writes=[vT[par]])
                return
            S.op("pool", lambda e: e.tensor_tensor(out=cvs, in0=cvs, in1=sg, op=ALU.mult), reads=[cvb.r[csl], sqb.r[qsl]], writes=[cvb.r[csl]])
            S.op("pool", lambda e: e.tensor_tensor(out=sg, in0=cvs, in1=cvs, op=ALU.mult), reads=[cvb.r[csl]], writes=[sqb.r[qsl]])

        def stC(c, NT, csl, qsl, par):
            if c >= 8:
                return
            psn = next_psP()
            S.op("pe", lambda e: e.matmul(psn[:, 0:NT], lhsT=ones128[:], rhs=sqb[:, qsl, 0:NT], start=True, stop=True), reads=[ones128, sqb.r[qsl]], writes=[psn])
            S.op("act", lambda e: e.activation(out=rnb[:, qsl, 0:NT], in_=psn[:, 0:NT], func=AF.Ln, bias=epsb[:, 0:1]), reads=[psn, epsb], writes=[rnb.r[qsl]])

        def stD(c, NT, csl, qsl, par):
            if c >= 8:
                return
            dstT, h, scale = (qnT[par], c, 128.0 ** -0.5) if c < 4 else (knT[par], c - 4, 1.0)
            S.op("act", lambda e: e.activation(out=rnb[:, qsl, 0:NT], in_=rnb[:, qsl, 0:NT], func=AF.Exp, scale=-0.5), reads=[rnb.r[qsl]], writes=[rnb.r[qsl]])
            S.op("dve", lambda e: e.scalar_tensor_tensor(out=dstT[:, h, 0:NT], in0=cvb[:, csl, 0:NT], scalar=float(scale), in1=rnb[:, qsl, 0:NT], op0=ALU.mult, op1=ALU.mult),
                 reads=[cvb.r[csl], rnb.r[qsl]], writes=[dstT])

        def bc(ap2, n, w):
            return ap2.unsqueeze(2).to_broadcast([128, n, w])

        def gdn_tile(tcol, nb, full, S_in, Sb_in, S_out, Sb_out, zslot, oc_slot, par, tag=-1):
            tc = slice(tcol, tcol + 128)
            beta, t1, g, Gs, eG, ekg, ckbg, nbeta = (gt[:, 0:4], gt[:, 4:8], gt[:, 8:12], gt[:, 12:20], gt[:, 20:24], gt[:, 24:28], gt[:, 28:32], gt[:, 32:36])
            dlb = gt[:, 36:36 + 4 * nb]
            ssq = gt[:, 52:56]
            cog = gt[:, 56:60]
            gr = graw[:, par, zslot, :]
            grr = graw.r[par * NTL1 + zslot]
            S.op("act", lambda e: e.activation(out=beta, in_=gr[:, 0:4], func=AF.Exp, scale=-1.0), reads=[grr], writes=[gt])
            S.op("dve", lambda e: e.tensor_scalar(out=beta, in0=beta, scalar1=1.0, scalar2=None, op0=ALU.add), reads=[gt], writes=[gt])
            S.op("dve", lambda e: e.reciprocal(out=beta, in_=beta), reads=[gt], writes=[gt])
            S.op("dve", lambda e: e.tensor_tensor(out=t1, in0=gr[:, 4:8], in1=dtb[:], op=ALU.add), reads=[grr, dtb], writes=[gt])
            S.op("act", lambda e: e.activation(out=t1, in_=t1, func=AF.Exp), reads=[gt], writes=[gt])
            S.op("act", lambda e: e.activation(out=t1, in_=t1, func=AF.Ln, bias=epsb[:, 1:2]), reads=[gt, epsb], writes=[gt])
            S.op("dve", lambda e: e.tensor_tensor(out=g, in0=t1, in1=nea[:], op=ALU.mult), reads=[gt, nea], writes=[gt])
            fns = [lambda e: e.matmul(psM[:, 16:20], lhsT=tri2, rhs=g, start=True, stop=True),
                   lambda e: e.matmul(psM[:, 20:24], lhsT=same2, rhs=g, start=True, stop=True)]
            for b in range(nb):
                fns.append(lambda e, b=b: e.matmul(psM[:, 24 + 4 * b:28 + 4 * b], lhsT=cm4[:, b, :], rhs=g, start=True, stop=True))
            S.op("pe", fns, reads=[cp, gt], writes=[psM])
            yield
            S.op("act", lambda e: e.activation(out=Gs, in_=psM[:, 16:24], func=AF.Copy), reads=[psM], writes=[gt])
            S.op("act", lambda e: e.activation(out=dlb, in_=psM[:, 24:24 + 4 * nb], func=AF.Exp), reads=[psM], writes=[gt])
            S.op("dve", lambda e: e.tensor_tensor(out=ekg, in0=Gs[:, 4:8], in1=Gs[:, 0:4], op=ALU.subtract), reads=[gt], writes=[gt])
            S.op("act", lambda e: e.activation(out=ekg, in_=ekg, func=AF.Exp), reads=[gt], writes=[gt])
            S.op("act", lambda e: e.activation(out=eG, in_=Gs[:, 0:4], func=AF.Exp), reads=[gt], writes=[gt])
            S.op("dve", lambda e: e.tensor_tensor(out=ckbg, in0=beta, in1=eG, op=ALU.mult), reads=[gt], writes=[gt])
            S.op("dve", lambda e: e.tensor_scalar(out=nbeta, in0=beta, scalar1=-1.0, scalar2=None, op0=ALU.mult), reads=[gt], writes=[gt])
            DUMP("gt", gt[:, 0:64], [gt], tag)
            DUMP("bf_knT", flat(knT[par][:]), [knT[par]], tag)
            DUMP("bf_vT", flat(vT[par][:]), [vT[par]], tag)
            S.op("pool", lambda e: e.tensor_tensor(out=Rb[:], in0=strict4, in1=bc(g, 4, 128), op=ALU.mult), reads=[cp, gt], writes=[Rb])
            S.op("pe", [(lambda e, h=h: e.matmul(psA[:, h * 128:(h + 1) * 128], lhsT=tri2, rhs=Rb[:, h, :], start=True, stop=True)) for h in range(4)],
                 reads=[cp, Rb], writes=[psA])
            yield
            S.op("act", lambda e: e.activation(out=flat(Eb[:]), in_=psA[:], func=AF.Exp), reads=[psA], writes=[Eb])
            S.op("pe", [(lambda e, h=h: e.matmul(psB[:, h * 128:(h + 1) * 128], lhsT=knT[par][:, h, tc], rhs=knT[par][:, h, tc], start=True, stop=True)) for h in range(4)],
                 reads=[knT[par]], writes=[psB])
            yield
            S.op("dve", lambda e: e.tensor_tensor(out=flat(tmpb[:]), in0=psB[:], in1=flat(Eb[:]), op=ALU.mult), reads=[psB, Eb], writes=[tmpb])
            S.op("dve", lambda e: e.tensor_tensor(out=tmpb[:], in0=tmpb[:], in1=bc(nbeta, 4, 128), op=ALU.mult), reads=[tmpb, gt], writes=[tmpb])
            Q, Q2, P, P2 = Qb[0], Qb[1], Pb[0], Pb[1]
            DUMP("E", flat(Eb[:]), [Eb], tag)
            S.op("dve", lambda e, Q=Q: e.tensor_tensor(out=Q[:], in0=tmpb[:], in1=strict4, op=ALU.mult), reads=[tmpb, cp], writes=[Q])
            DUMP("Q0", flat(Q[:]), [Q], tag)
            S.op("pe", [(lambda e, h=h, Q=Q: e.transpose(psA[:, h * 128:(h + 1) * 128], Q[:, h, :], identf)) for h in range(4)],
                 reads=[Q, cp], writes=[psA])
            yield
            S.op("act", lambda e, P=P: e.activation(out=flat(P[:]), in_=psA[:], func=AF.Copy), reads=[psA], writes=[P])
            if full:
                S.op("pe", [(lambda e, h=h: e.matmul(psB[:, h * 128:(h + 1) * 128], lhsT=qnT[par][:, h, tc], rhs=knT[par][:, h, tc], start=True, stop=True)) for h in range(4)],
                     reads=[qnT[par], knT[par]], writes=[psB])
                yield
                S.op("dve", lambda e: e.tensor_tensor(out=flat(tmpb[:]), in0=psB[:], in1=flat(Eb[:]), op=ALU.mult), reads=[psB, Eb], writes=[tmpb])
                S.op("pool", lambda e: e.tensor_tensor(out=qkb[:], in0=tmpb[:], in1=incl4, op=ALU.mult), reads=[tmpb, cp], writes=[qkb])
                S.op("pe", [(lambda e, h=h: e.transpose(psT[:, h * 128:(h + 1) * 128], qkb[:, h, :], identb[:])) for h in range(4)],
                     reads=[qkb, identb], writes=[psT])
                yield
                S.op("act", lambda e: e.activation(out=flat(qkT[:]), in_=psT[:, 0:512], func=AF.Copy), reads=[psT], writes=[qkT])
            DUMP("P0", flat(P[:]), [P], tag)
            S.op("dve", lambda e, P=P: e.tensor_tensor(out=Xb[:], in0=P[:], in1=ident4, op=ALU.add), reads=[P, cp], writes=[Xb])
            nlev = 5
            for lv in range(nlev):
                S.op("pe", [(lambda e, h=h, P=P, Q=Q: e.matmul(psB[:, h * 128:(h + 1) * 128], lhsT=P[:, h, :], rhs=Q[:, h, :], start=True, stop=True)) for h in range(4)],
                     reads=[P, Q], writes=[psB])
                yield
                S.op("act", (lambda e, Q2=Q2: e.activation(out=flat(Q2[:]), in_=psB[:], func=AF.Copy)), reads=[psB], writes=[Q2])
                S.op("pe", [(lambda e, h=h, Q2=Q2: e.matmul(psA[:, h * 128:(h + 1) * 128], lhsT=Q2[:, h, :], rhs=Xb[:, h, :], start=True, stop=True)) for h in range(4)],
                     reads=[Q2, Xb], writes=[psA])
                yield
                S.op("dve", lambda e: e.tensor_tensor(out=flat(Xb[:]), in0=flat(Xb[:]), in1=psA[:], op=ALU.add), reads=[Xb, psA], writes=[Xb])
                if lv < nlev - 1:
                    S.op("pe", [(lambda e, h=h, P=P, Q=Q: e.matmul(psB[:, h * 128:(h + 1) * 128], lhsT=Q[:, h, :], rhs=P[:, h, :], start=True, stop=True)) for h in range(4)],
                         reads=[P, Q], writes=[psB])
                    yield
                    S.op("act", (lambda e, P2=P2: e.activation(out=flat(P2[:]), in_=psB[:], func=AF.Copy)), reads=[psB], writes=[P2])
                    P, P2 = P2, P
                Q, Q2 = Q2, Q
            DUMP("X", flat(Xb[:]), [Xb], tag)
            S.op("act", lambda e: e.activation(out=Xbf[:], in_=Xb[:], func=AF.Copy), reads=[Xb], writes=[Xbf])
            S.op("pe", [(lambda e, h=h: e.transpose(psT[:, h * 128:(h + 1) * 128], knT[par][:, h, tc], identb[:])) for h in range(4)] +
                 [(lambda e, h=h: e.transpose(psT[:, 512 + h * 128:512 + (h + 1) * 128], vT[par][:, h, tc], identb[:])) for h in range(4)],
                 reads=[knT[par], vT[par], identb], writes=[psT])
            yield
            kTv = psT[:, 0:512].rearrange("p (a b) -> p a b", a=4)
            vTv = psT[:, 512:1024].rearrange("p (a b) -> p a b", a=4)
            S.op("dve", lambda e: e.tensor_tensor(out=kbg[:], in0=kTv, in1=bc(ckbg, 4, 128), op=ALU.mult), reads=[psT, gt], writes=[kbg])
            S.op("dve", lambda e: e.tensor_tensor(out=vbb[:], in0=vTv, in1=bc(beta, 4, 128), op=ALU.mult), reads=[psT, gt], writes=[vbb])
            for b in range(nb):
                cof = gt[:, 60 + 4 * b:64 + 4 * b]
                S.op("dve", (lambda e, b=b, cof=cof: e.tensor_scalar(out=cof, in0=ekg, scalar1=cp[:, CP_BM + b:CP_BM + b + 1], scalar2=None, op0=ALU.mult)),
                     reads=[gt, cp], writes=[gt])
                S.op("dve", (lambda e, b=b, cof=cof: e.tensor_tensor(out=kgm[:, b], in0=kTv, in1=bc(cof, 4, 128), op=ALU.mult)),
                     reads=[psT, gt], writes=[kgm])
            S.op("pe", [(lambda e, h=h: e.matmul(psA[:, h * 128:(h + 1) * 128], lhsT=Xbf[:, h, :], rhs=vbb[:, h, :], start=True, stop=True)) for h in range(4)],
                 reads=[Xbf, vbb], writes=[psA])
            yield
            S.op("act", lambda e: e.activation(out=flat(vn32[:]), in_=psA[:], func=AF.Copy), reads=[psA], writes=[vn32])
            S.op("pe", [(lambda e, h=h: e.matmul(psB[:, h * 128:(h + 1) * 128], lhsT=kbg[:, h, :], rhs=Xbf[:, h, :], start=True, stop=True)) for h in range(4)],
                 reads=[kbg, Xbf], writes=[psB])
            yield
            S.op("act", lambda e: e.activation(out=flat(wTb[:]), in_=psB[:], func=AF.Copy), reads=[psB], writes=[wTb])
            DUMP("u", flat(vn32[:]), [vn32], tag)
            DUMP("bf_wT", flat(wTb[:]), [wTb], tag)
            DUMP("bf_kbg", flat(kbg[:]), [kbg], tag)
            DUMP("bf_vb", flat(vbb[:]), [vbb], tag)
            for b in range(nb):
                Si, Sbi, So, Sbo = S_in(b), Sb_in(b), S_out(b), Sb_out(b)
                S.op("pe", [(lambda e, h=h, Sbi=Sbi: e.matmul(psA[:, h * 128:(h + 1) * 128], lhsT=wTb[:, h, :], rhs=Sbi[0][:, h, :], start=True, stop=True)) for h in range(4)],
                     reads=[wTb, Sbi[1]], writes=[psA])
                yield
                S.op("dve", (lambda e, b=b: e.scalar_tensor_tensor(out=flat(vn32[:]), in0=psA[:], scalar=cp[:, CP_NBM + b:CP_NBM + b + 1],
                                                                    in1=flat(vn32[:]), op0=ALU.mult, op1=ALU.add)),
                     reads=[psA, cp, vn32], writes=[vn32])
                S.op("act", lambda e: e.activation(out=vnb[:], in_=vn32[:], func=AF.Copy), reads=[vn32], writes=[vnb])
                if full:
                    S.op("dve", (lambda e, b=b: e.tensor_scalar(out=cog, in0=eG, scalar1=cp[:, CP_BM + b:CP_BM + b + 1], scalar2=None, op0=ALU.mult)),
                         reads=[gt, cp], writes=[gt])
                    S.op("pe", [(lambda e, h=h, Sbi=Sbi: e.matmul(psB[:, h * 128:(h + 1) * 128], lhsT=qnT[par][:, h, tc], rhs=Sbi[0][:, h, :], start=True, stop=True)) for h in range(4)],
                         reads=[qnT[par], Sbi[1]], writes=[psB])
                    yield
                    psBv = psB[:].rearrange("p (a b) -> p a b", a=4)
                    if b == 0:
                        S.op("dve", lambda e: e.tensor_tensor(out=oacc[:], in0=psBv, in1=bc(cog, 4, 128), op=ALU.mult), reads=[psB, gt], writes=[oacc])
                    else:
                        S.op("dve", lambda e: e.tensor_tensor(out=tmpb[:], in0=psBv, in1=bc(cog, 4, 128), op=ALU.mult), reads=[psB, gt], writes=[tmpb])
                        S.op("pool", lambda e: e.tensor_tensor(out=oacc[:], in0=oacc[:], in1=tmpb[:], op=ALU.add), reads=[oacc, tmpb], writes=[oacc])
                S.op("pe", [(lambda e, h=h, b=b: e.matmul(psA[:, h * 128:(h + 1) * 128], lhsT=kgm[:, b, h, :], rhs=vnb[:, h, :], start=True, stop=True)) for h in range(4)],
                     reads=[kgm, vnb], writes=[psA])
                yield
                for h in range(4):
                    S.op("dve", (lambda e, h=h, b=b, Si=Si, So=So: e.scalar_tensor_tensor(out=So[0][:, h, :], in0=Si[0][:, h, :], scalar=dlb[:, 4 * b + h:4 * b + h + 1],
                                                                                      in1=psA[:, h * 128:(h + 1) * 128], op0=ALU.mult, op1=ALU.add)),
                         reads=[Si[1], gt, psA], writes=[So[1]])
                if Sbo is not None:
                    S.op("act", (lambda e, So=So, Sbo=Sbo: e.activation(out=Sbo[0], in_=So[0], func=AF.Copy)), reads=[So[1]], writes=[Sbo[1]])
            DUMP("vn", flat(vn32[:]), [vn32], tag)
            DUMP("Safter", flat(S_out(nb - 1)[0]), [S_out(nb - 1)[1]], tag)
            if full:
                S.op("pe", [(lambda e, h=h: e.matmul(psB[:, h * 128:(h + 1) * 128], lhsT=qkT[:, h, :], rhs=vnb[:, h, :], start=True, stop=True)) for h in range(4)],
                     reads=[qkT, vnb], writes=[psB])
                yield
                S.op("dve", lambda e: e.tensor_tensor(out=flat(oacc[:]), in0=flat(oacc[:]), in1=psB[:], op=ALU.add), reads=[oacc, psB], writes=[oacc])
                for h in range(4):
                    S.op("act", (lambda e, h=h: e.activation(out=tmpb[:, h, :], in_=oacc[:, h, :], func=AF.Square, accum_out=ssq[:, h:h + 1])),
                         reads=[oacc], writes=[tmpb, gt])
                S.op("act", lambda e: e.activation(out=ssq, in_=ssq, func=AF.Ln, scale=1.0 / 128, bias=epsb[:, 0:1]), reads=[gt, epsb], writes=[gt])
                S.op("act", lambda e: e.activation(out=ssq, in_=ssq, func=AF.Exp, scale=-0.5), reads=[gt], writes=[gt])
                DUMP("o", flat(oacc[:]), [oacc], tag)
                S.op("dve", lambda e: e.tensor_tensor(out=tmpb[:], in0=oacc[:], in1=bc(ssq, 4, 128), op=ALU.mult), reads=[oacc, gt], writes=[tmpb])
                S.op("pool", lambda e: e.tensor_tensor(out=tmpb[:], in0=tmpb[:], in1=gnw4[:], op=ALU.mult), reads=[tmpb, gnw4], writes=[tmpb])
                S.op("dve", lambda e: e.tensor_tensor(out=ocat[:, oc_slot, 0:512], in0=flat(tmpb[:]), in1=zs[par][:, zslot, :], op=ALU.mult),
                     reads=[tmpb, zs[par].r[zslot]], writes=[ocat.r[oc_slot]])

        def attn_tile(pieces, qcol, oc_slot, par):
            qc = slice(qcol, qcol + 128)
            npc = len(pieces)
            first_pv = [True]
            pctr2 = [0]
            for pi, pc in enumerate(pieces):
                if pc.get("prep"):
                    pc["prep"]()
                for hh in range(2):
                    fns = []
                    first = True
                    if pc.get("seq") is not None:
                        s = pc["seq"]
                        fns.append(lambda e, s=s: e.matmul(psS[:], lhsT=smk[:, 2560:2688], rhs=smk[:, s * 512:(s + 1) * 512], start=True, stop=False, skip_group_check=True))
                        first = False
                        if pc.get("newk"):
                            fns.append(lambda e, s=s: e.matmul(psS[:], lhsT=smk[:, 2048 + s * 128:2048 + (s + 1) * 128], rhs=smk[:, 2688:3200], start=False, stop=False, skip_group_check=True))
                    for j in range(4):
                        h = 4 * hh + j
                        fns.append(lambda e, h=h, j=j, pc=pc, first=first: e.matmul(psS[:, j * 128:(j + 1) * 128], lhsT=kbT[:, h // 2, pc["kslot"], :],
                                                                                   rhs=qbz[par][:, h // 2, h % 2, qc], start=first, stop=True, skip_group_check=True))
                    S.op("pe", fns, reads=[kbT.r[pc["kslot"]], qbz[par], smk], writes=[psS])
                    yield
                    S.op("dve", (lambda e, pc=pc, hh=hh: e.tensor_tensor(out=scb[:], in0=psS[:].rearrange("p (a b) -> p a b", a=4), in1=BT[:, pc["r"], 4 * hh:4 * hh + 4, :], op=ALU.add)),
                         reads=[psS, BT], writes=[scb])
                    psl = pctr2[0] % 2
                    pctr2[0] += 1
                    S.op("act", (lambda e, psl=psl: e.activation(out=PT[:, psl], in_=scb[:], func=AF.Exp)), reads=[scb], writes=[PT.r[psl]])
                    fns = []
                    for j in range(4):
                        h = 4 * hh + j
                        stt = first_pv[0]
                        first_pv[0] = False
                        fns.append(lambda e, j=j, h=h, pc=pc, psl=psl, stt=stt: e.matmul(psO[:, h * 64:(h + 1) * 64], lhsT=PT[:, psl, j, :], rhs=V1[:, pc["vslot"], h, :],
                                                                                        start=stt, stop=False, skip_group_check=True))
                        fns.append(lambda e, j=j, h=h, pc=pc, psl=psl, pi=pi: e.matmul(psM[:, 64 + pi * 8 + h:64 + pi * 8 + h + 1], lhsT=PT[:, psl, j, :], rhs=vcol[:, pc["vslot"]:pc["vslot"] + 1],
                                                                                      start=True, stop=True, skip_group_check=True))
                    S.op("pe", fns, reads=[PT.r[psl], V1.r[pc["vslot"]], vcol.r[pc["vslot"]]], writes=[psO, psM])
                    yield
            rden = gta[:, 0:8]
            ss8 = gta[:, 8:16]
            S.op("dve", lambda e: e.tensor_reduce(out=rden, in_=psM[:, 64:64 + npc * 8].rearrange("p (a b) -> p b a", b=8), axis=mybir.AxisListType.X, op=ALU.add),
                 reads=[psM], writes=[gta])
            S.op("dve", lambda e: e.tensor_scalar(out=rden, in0=rden, scalar1=1e-30, scalar2=None, op0=ALU.max), reads=[gta], writes=[gta])
            S.op("dve", lambda e: e.reciprocal(out=rden, in_=rden), reads=[gta], writes=[gta])
            S.op("dve", lambda e: e.tensor_tensor(out=ob32[:], in0=psO[:].rearrange("p (a b) -> p a b", a=8), in1=bc(rden, 8, 64), op=ALU.mult),
                 reads=[psO, gta], writes=[ob32])
            for h in range(8):
                S.op("act", (lambda e, h=h: e.activation(out=scb[:, 0, 0:64], in_=ob32[:, h, :], func=AF.Square, accum_out=ss8[:, h:h + 1])),
                     reads=[ob32], writes=[scb, gta])
            S.op("act", lambda e: e.activation(out=ss8, in_=ss8, func=AF.Ln, scale=1.0 / 64, bias=epsb[:, 0:1]), reads=[gta, epsb], writes=[gta])
            S.op("act", lambda e: e.activation(out=ss8, in_=ss8, func=AF.Exp, scale=-0.5), reads=[gta], writes=[gta])
            S.op("dve", lambda e: e.tensor_tensor(out=ob32[:], in0=ob32[:], in1=bc(ss8, 8, 64), op=ALU.mult), reads=[ob32, gta], writes=[ob32])
            S.op("pool", lambda e: e.tensor_tensor(out=ocat[:, oc_slot, 512:1024].rearrange("p (a b) -> p a b", a=8), in0=ob32[:], in1=anw8[:], op=ALU.mult),
                 reads=[ob32, anw8], writes=[ocat.r[oc_slot]])

        occtr = [0]
        kvctr = [0]

        def run_all(*gens):
            gens = [g for g in gens if g is not None]
            while gens:
                for g in list(gens):
                    try:
                        next(g)
                    except StopIteration:
                        gens.remove(g)

        def front_gen(tls, gq, gkv, full, sample, par):
            ntl = len(tls)
            NT = ntl * 128
            nseq, L = (4, 32) if sample else (1, NT)
            for ti, tl in enumerate(tls):
                src = xs_d[:, :] if sample else xp_d[tl * 128:(tl + 1) * 128, :]
                norm_transpose(src, uT[par], ti * 128)
                yield
            chunks = list(range(12)) if gq else list(range(4, 12))
            nch = len(chunks)
            ok = lambda c: (c >= 4 or full)
            for i in range(nch + 3):
                if i < nch:
                    c = chunks[i]
                    stA(c, nseq, L, NT, i % 2, i % 4, ok(c), par)
                if 0 <= i - 1 < nch and ok(chunks[i - 1]):
                    stB(chunks[i - 1], NT, (i - 1) % 4, (i - 1) % 2, par)
                if 0 <= i - 2 < nch and ok(chunks[i - 2]):
                    stC(chunks[i - 2], NT, (i - 2) % 4, (i - 2) % 2, par)
                if 0 <= i - 3 < nch and ok(chunks[i - 3]):
                    stD(chunks[i - 3], NT, (i - 3) % 4, (i - 3) % 2, par)
                yield
            if full:
                for c in range(4):
                    ps = next_psP()
                    proj_fm(w_in_bf, C_QB + c * 128, uT[par], NT, ps)
                    S.op("act", (lambda e, c=c, ps=ps: e.activation(out=qbz[par][0:64, c, 0, 0:NT], in_=ps[0:64, 0:NT], func=AF.Copy, scale=0.125)), reads=[ps], writes=[qbz[par]])
                    S.op("act", (lambda e, c=c, ps=ps: e.activation(out=qbz[par][64:128, c, 1, 0:NT], in_=ps[64:128, 0:NT], func=AF.Copy, scale=0.125)), reads=[ps], writes=[qbz[par]])
                    yield
            if gkv:
                for c in range(4):
                    ps = next_psP()
                    proj_fm(w_in_bf, C_KB + c * 128, uT[par], NT, ps)
                    for ti, tl in enumerate(tls):
                        slot = tl % 8
                        S.op("act", (lambda e, c=c, ps=ps, ti=ti, slot=slot: e.activation(out=kbT[:, c, slot, :], in_=ps[:, ti * 128:(ti + 1) * 128], func=AF.Copy)),
                             reads=[ps], writes=[kbT.r[slot]])
                    yield
            for ti, tl in enumerate(tls):
                tcol = ti * 128
                if gkv:
                    slot = tl % 8
                    ps = next_psP()
                    proj_tm(w_in_bf, C_VB, 512, uT[par], tcol, ps)
                    S.op("act", (lambda e, ps=ps, slot=slot: e.activation(out=V1[:, slot], in_=ps[:].rearrange("p (a b) -> p a b", a=8), func=AF.Copy)),
                         reads=[ps], writes=[V1.r[slot]])
                    if sample:
                        S.op("pool", (lambda e, slot=slot: e.memset(vcol[:, slot:slot + 1], 1.0)), writes=[vcol.r[slot]])
                    else:
                        S.op("pool", (lambda e, slot=slot, tl=tl: e.tensor_copy(out=vcol[:, slot:slot + 1], in_=kvalid[:, tl:tl + 1])),
                             reads=[kvalid], writes=[vcol.r[slot]])
                    yield
                    want_out = sample or (tl >= p_tiles - 4)
                    if want_out:
                        orow = 0 if sample else (tl - (p_tiles - 4)) * 128
                        kd, vd = (bk_s, bv_s) if sample else (bk_p, bv_p)
                        S.op("act", (lambda e, ps=ps: e.activation(out=kvout[:, 0, :], in_=ps[:], func=AF.Copy)), reads=[ps], writes=[kvout.r[0]])
                        S.dma("pool", (lambda e, vd=vd, orow=orow: e.dma_start(out=vd[orow:orow + 128, :], in_=kvout[:, 0, :])), reads=[kvout.r[0]])
                        ps2 = next_psP()
                        proj_tm(w_in_bf, C_KB, 512, uT[par], tcol, ps2)
                        S.op("act", (lambda e, ps2=ps2: e.activation(out=kvout[:, 0, :], in_=ps2[:], func=AF.Copy)), reads=[ps2], writes=[kvout.r[0]])
                        S.dma("pool", (lambda e, kd=kd, orow=orow: e.dma_start(out=kd[orow:orow + 128, :], in_=kvout[:, 0, :])), reads=[kvout.r[0]])
                        yield
                if full:
                    ps = next_psP()
                    proj_tm(w_in_bf, C_Z, 512, uT[par], tcol, ps)
                    zt = kvout[:, 0, :]
                    S.op("act", (lambda e, ps=ps: e.activation(out=zt, in_=ps[:], func=AF.Exp, scale=-1.0)), reads=[ps], writes=[kvout.r[0]])
                    S.op("act", (lambda e: e.activation(out=zt, in_=zt, func=AF.Ln, bias=epsb[:, 1:2])), reads=[kvout.r[0], epsb], writes=[kvout.r[0]])
                    S.op("act", (lambda e: e.activation(out=zt, in_=zt, func=AF.Exp, scale=-1.0)), reads=[kvout.r[0]], writes=[kvout.r[0]])
                    S.op("dve", (lambda e, ps=ps, ti=ti: e.tensor_tensor(out=zs[par][:, ti, :], in0=ps[:], in1=zt, op=ALU.mult)), reads=[ps, kvout.r[0]], writes=[zs[par].r[ti]])
                    yield
                proj_tm(w_in_bf, C_B, 8, uT[par], tcol, psM, 0)
                S.op("act", (lambda e, ti=ti: e.activation(out=graw[:, par, ti, :], in_=psM[:, 0:8], func=AF.Copy)), reads=[psM], writes=[graw.r[par * NTL1 + ti]])
                yield

        def work_gen(tls, full, sample, par, res):
            for ti, tl in enumerate(tls):
                tcol = ti * 128
                oc_slot = occtr[0] % 2
                if full:
                    occtr[0] += 1
                res["oc_slot"] = oc_slot
                if sample:
                    gens = [gdn_tile(tcol, 4, True,
                                     lambda b: (Ss[:, b], Ss.r[b]), lambda b: (Ssb[:, b], Ssb.r[b]),
                                     lambda b: (Ss[:, b], Ss.r[b]), lambda b: None, ti, oc_slot, par)]
                else:
                    gens = [gdn_tile(tcol, 2, full,
                                     lambda b: (Sp[:], Sp.r[0]), lambda b: (Spb[:], Spb.r[0]),
                                     lambda b: (Sp[:], Sp.r[0]), lambda b: (Spb[:], Spb.r[0]), ti, oc_slot, par, tag=tl)]
                    if full:
                        pieces = [dict(kslot=(tl - 4 + r) % 8, vslot=(tl - 4 + r) % 8, r=r) for r in range(5)]
                        gens.append(attn_tile(pieces, tcol, oc_slot, par))
                while gens:
                    for g in list(gens):
                        try:
                            next(g)
                        except StopIteration:
                            gens.remove(g)
                    yield
                if full and not sample:
                    scr_row = tl - (p_tiles - NSCR + 1)
                    if scr_row >= 0:
                        S.dma("pool", (lambda e, oc_slot=oc_slot, scr_row=scr_row: e.dma_start(out=ocat_scr[scr_row * 128:(scr_row + 1) * 128, :], in_=ocat[:, oc_slot, :])),
                              reads=[ocat.r[oc_slot]])
                    if "ocat" in dbg and tl >= first_own:
                        S.dma("pool", (lambda e, oc_slot=oc_slot, tl=tl: e.dma_start(out=dbg["ocat"][(tl - first_own) * 128:(tl - first_own + 1) * 128, :], in_=ocat[:, oc_slot, :])),
                              reads=[ocat.r[oc_slot]])

        first_full_tile = first_own - 2
        first_kv_tile = first_full_tile - 4
        tl = p1_from
        pending = None
        gi = 0
        while tl < p1_to:
            tls = list(range(tl, min(tl + NTL1, p1_to)))
            full = tls[0] >= first_full_tile
            gkv = tls[0] >= first_kv_tile
            gq = tls[0] >= first_full_tile - NTL1
            par = gi % 2
            run_all(front_gen(tls, gq, gkv, full, False, par), pending)
            pending = work_gen(tls, full, False, par, {})
            tl += NTL1
            gi += 1
        run_all(pending)
        S.dma("pool", lambda e: e.dma_start(out=S_p[:, :], in_=flat(Sp[:])), reads=[Sp])
        S.dma("pool", lambda e: e.dma_start(out=qc_p[:, :].rearrange("p (a b) -> p a b", a=12), in_=qtail[:, :, 0, :]), reads=[qtail])

        if cfg.get("sample", True):
            S.barrier()
            load_small(cp, cps_d[:, :])
            load_BT(bts_d, None)
            for i, (a, b) in enumerate(((0, 1024), (1024, 2048), (2048, 3072), (3072, SMK_W))):
                S.dma("sp", (lambda e, i=i, a=a, b=b: e.dma_start(out=stage[:, i % 2, 0:b - a], in_=smk_d[:, a:b])), writes=[stage.r[i % 2]])
                S.op("dve", (lambda e, i=i, a=a, b=b: e.tensor_copy(out=smk[:, a:b], in_=stage[:, i % 2, 0:b - a])), reads=[stage.r[i % 2]], writes=[smk])
            load_small(qtail, qc0_d[:, :])
            for s in range(4):
                S.dma("sp", (lambda e, s=s: e.dma_start(out=Ss[:, s], in_=s0_d[s].rearrange("h k v -> k h v"))), writes=[Ss.r[s]])
                S.op("act", (lambda e, s=s: e.activation(out=Ssb[:, s], in_=Ss[:, s], func=AF.Copy)), reads=[Ss.r[s]], writes=[Ssb.r[s]])
            run_all(front_gen([0], True, True, True, True, 0))
            sres = {}
            run_all(work_gen([0], True, True, 0, sres))
            ocs = sres["oc_slot"]
            S.dma("pool", lambda e: e.dma_start(out=S_s[:, :], in_=flat(Ss[:])), reads=[Ss])
            S.dma("pool", lambda e: e.dma_start(out=qc_s[:, :], in_=flat(qtail[:])), reads=[qtail])

            def mk_prep(s, r, slot, sl):
                def prep():
                    S.dma("sp", (lambda e: e.dma_start(out=stage[:, sl, 0:512], in_=ck_d[s, r * 128:(r + 1) * 128, :])), writes=[stage.r[sl]])
                    S.dma("sp", (lambda e: e.dma_start(out=stage[:, sl, 512:1024], in_=cv_d[s, r * 128:(r + 1) * 128, :])), writes=[stage.r[sl]])
                    S.op("dve", (lambda e: e.tensor_copy(out=xn[:, 0:512], in_=stage[:, sl, 0:512])), reads=[stage.r[sl]], writes=[xn])
                    S.op("pe", [(lambda e, c=c: e.transpose(psT[:, c * 128:(c + 1) * 128], xn[:, c * 128:(c + 1) * 128], identb[:])) for c in range(4)],
                         reads=[xn, identb], writes=[psT])
                    S.op("act", (lambda e: e.activation(out=kbT[:, :, slot, :], in_=psT[:, 0:512].rearrange("p (a b) -> p a b", a=4), func=AF.Copy)),
                         reads=[psT], writes=[kbT.r[slot]])
                    S.op("act", (lambda e: e.activation(out=V1[:, slot], in_=stage[:, sl, 512:1024].rearrange("p (a b) -> p a b", a=8), func=AF.Copy)),
                         reads=[stage.r[sl]], writes=[V1.r[slot]])
                    S.op("pool", (lambda e: e.memset(vcol[:, slot:slot + 1], 1.0)), writes=[vcol.r[slot]])
                return prep

            pieces = []
            for s in range(4):
                for r in range(4):
                    i = s * 4 + r
                    slot = i % 7 + 1
                    pieces.append(dict(kslot=slot, vslot=slot, r=r, seq=s, prep=mk_prep(s, r, slot, i % 2)))
                pieces.append(dict(kslot=0, vslot=0, r=4, seq=s, newk=True))
            run_all(attn_tile(pieces, 0, ocs, 0))
            S.dma("pool", lambda e: e.dma_start(out=ocat_scr[(NSCR - 1) * 128:NSCR * 128, :], in_=ocat[:, ocs, :]), reads=[ocat.r[ocs]])
            if "ocat_s" in dbg:
                S.dma("pool", lambda e: e.dma_start(out=dbg["ocat_s"][:, :], in_=ocat[:, ocs, :]), reads=[ocat.r[ocs]])

    if do_p2:
        S.barrier()
        aptr[0] = shared_end
        wgu_bf = SB("wgu_bf", [128, 8, 2 * D_FF], BF16)
        wd_bf = SB("wd_bf", [128, NFF, 1024], BF16)
        wout_bf = SB("wout_bf", [128, 8, 1024], BF16)
        nfpre = SB("nfpre", [128, 8])
        nmpost = SB("nmpost", [128, 1024])
        nfpost = SB("nfpost", [128, 1024])
        fw = SB("fw", [128, NFF, 3])
        fb = SB("fb", [128, NFF])
        gtail = SB("gtail", [128, NFF, 4, 2])
        oc2 = SB("oc2", [128, 1, 1024], BF16, nslots=1)
        upb = SB("upb", [128, 4, 128], F32, nslots=4)
        oT = SB("oT", [128, 8, 128], BF16)
        x1 = SB("x1", [128, 1024])
        u2T = SB("u2T", [128, 8, 128], BF16)
        gxp = SB("gxp", [128, 4, 144], F32, nslots=4)
        cv2 = SB("cv2", [128, 4, 128], F32, nslots=4)
        t2 = SB("t2", [128, 4, 128], F32, nslots=4)
        sg2 = SB("sg2", [128, 4, 128], F32, nslots=4)
        junk2 = SB("junk2", [128, 512], BF16)
        hT = SB("hT", [128, NFF, 128], BF16)
        yb = SB("yb", [128, 1, 1024], F32, nslots=1)
        st2 = SB("st2", [128, 8])
        print("phase2 arena used", aptr[0], "of", ARENA_F32)

        load_small(nfpre, nfpre_d[:, :])
        load_small(nmpost, nmpost_d[:, :])
        load_small(nfpost, nfpost_d[:, :])
        load_small(fw, fw_d[:, :])
        load_small(fb, fb_d[:, :])
        S.op("pool", lambda e: e.memset(flat(gtail[:]), 0.0), writes=[gtail])
        if not do_p1:
            S.dma("sp", lambda e: e.dma_start(out=stage[:, 0, 0:128], in_=cpp_d[:, CP_IDENT:CP_IDENT + 128]), writes=[stage.r[0]])
            S.op("dve", lambda e: e.tensor_copy(out=identb[:], in_=stage[:, 0, 0:128]), reads=[stage.r[0]], writes=[identb])
        load_cast_weight(lambda k, c0, c1: wout_bf[:, k, c0:c1], lambda k, c0, c1: w_out_d[k * 128:(k + 1) * 128, c0:c1], 1024, 8, None, wout_bf)
        sfn2 = lambda k: nfpre[:, k:k + 1]
        sfn2.buf = nfpre
        load_cast_weight(lambda k, c0, c1: wgu_bf[:, k, c0:c1], lambda k, c0, c1: w_gu_d[k * 128:(k + 1) * 128, c0:c1], 2 * D_FF, 8, sfn2, wgu_bf)
        load_cast_weight(lambda k, c0, c1: wd_bf[:, k, c0:c1], lambda k, c0, c1: w_d_d[k * 128:(k + 1) * 128, c0:c1], 1024, NFF, None, wd_bf)

        o2ctr = [0]
        yctr = [0]
        fctr = [0]

        def rms_finish(ps_list, resid_ap, resid_res, gain, dst_ap, dst_res):
            for i, ps in enumerate(ps_list):
                S.op("act", (lambda e, i=i, ps=ps: e.activation(out=junk2[:], in_=ps[:], func=AF.Square, accum_out=st2[:, i:i + 1])),
                     reads=[ps], writes=[st2, junk2])
            S.op("dve", lambda e: e.tensor_tensor(out=st2[:, 2:3], in0=st2[:, 0:1], in1=st2[:, 1:2], op=ALU.add), reads=[st2], writes=[st2])
            S.op("act", lambda e: e.activation(out=st2[:, 3:4], in_=st2[:, 2:3], func=AF.Ln, scale=1.0 / 1024, bias=epsb[:, 0:1]), reads=[st2, epsb], writes=[st2])
            S.op("act", lambda e: e.activation(out=st2[:, 4:5], in_=st2[:, 3:4], func=AF.Exp, scale=-0.5), reads=[st2], writes=[st2])
            for i, ps in enumerate(ps_list):
                S.op("dve", (lambda e, i=i, ps=ps: e.scalar_tensor_tensor(out=dst_ap[:, i * 512:(i + 1) * 512], in0=ps[:], scalar=st2[:, 4:5], in1=gain[:, i * 512:(i + 1) * 512],
                                                                         op0=ALU.mult, op1=ALU.mult)),
                     reads=[ps, st2, gain], writes=dst_res)
                S.op("pool", (lambda e, i=i: e.tensor_tensor(out=dst_ap[:, i * 512:(i + 1) * 512], in0=dst_ap[:, i * 512:(i + 1) * 512], in1=resid_ap[:, i * 512:(i + 1) * 512], op=ALU.add)),
                     reads=dst_res + resid_res, writes=dst_res)

        yjunk_b = xn
        yjunk = xn[:]

        ring = [psP[0], psP[1], psS, psO, psM]
        rctr = [0]

        def next_ring():
            bnk = ring[rctr[0] % len(ring)]
            rctr[0] += 1
            return bnk

        u2T_b = Buf(stage[:, 1, 0:512].bitcast(BF16).rearrange("p (a b) -> p a b", a=8), "u2T_b")
        u2T_b.r = [stage.r[1]]
        x1s = [(x1[:], x1.r[0]), (stage[:, 0, :], stage.r[0])]
        u2Ts = [u2T, u2T_b]

        def p2_front(scr_row, x_src, slot):
            osl = 0
            S.dma("sp", lambda e: e.dma_start(out=oc2[:, osl, :], in_=ocat_scr[scr_row * 128:(scr_row + 1) * 128, :]), writes=[oc2.r[osl]])
            xsl = xctr[0] % 2
            xctr[0] += 1
            S.dma("sp", lambda e: e.dma_start(out=xin[:, xsl, :], in_=x_src), writes=[xin.r[xsl]])
            S.op("pe", [(lambda e, k=k: e.transpose(psT[:, k * 128:(k + 1) * 128], oc2[:, osl, k * 128:(k + 1) * 128], identb[:])) for k in range(8)],
                 reads=[oc2.r[osl], identb], writes=[psT])
            S.op("act", lambda e: e.activation(out=oT[:], in_=psT[:].rearrange("p (a b) -> p a b", a=8), func=AF.Copy), reads=[psT], writes=[oT])
            proj_tm(wout_bf, 0, 512, oT, 0, psA)
            proj_tm(wout_bf, 512, 512, oT, 0, psB)
            x1a, x1r = x1s[slot]
            rms_finish([psA, psB], xin[:, xsl, :], [xin.r[xsl]], nmpost, x1a, [x1r])
            rms_stats(x1a, [x1r], xn[:], [xn], 1024)
            S.op("dve", lambda e: e.tensor_scalar(out=xn[:], in0=x1a, scalar1=st[:, 2:3], scalar2=None, op0=ALU.mult), reads=[x1r, st], writes=[xn])
            S.op("pe", [(lambda e, k=k: e.transpose(psT[:, k * 128:(k + 1) * 128], xn[:, k * 128:(k + 1) * 128], identb[:])) for k in range(8)],
                 reads=[xn, identb], writes=[psT])
            uu = u2Ts[slot]
            S.op("act", lambda e: e.activation(out=uu[:], in_=psT[:].rearrange("p (a b) -> p a b", a=8), func=AF.Copy), reads=[psT], writes=[uu])

        def p2_body(slot, y_dst, nseq, L, hook):
            NT = 128
            uu = u2Ts[slot]
            x1a, x1r = x1s[slot]
            st_ = {}

            def stageA(c):
                psg = next_ring()
                proj_fm(wgu_bf, c * 128, uu, NT, psg)
                fsl = c % 4
                gx = gxp[:, fsl, 0:nseq * (L + 2)].rearrange("p (s t) -> p s t", s=nseq)
                cvv = cv2[:, fsl, :].rearrange("p (s t) -> p s t", s=nseq)
                S.op("pool", (lambda e: e.tensor_copy(out=gx[:, :, 0:2], in_=gtail[:, c, 0:nseq, :])), reads=[gtail], writes=[gxp.r[fsl]])
                S.op("act", (lambda e: e.activation(out=gx[:, :, 2:2 + L], in_=psg[:, 0:NT].rearrange("p (s t) -> p s t", s=nseq), func=AF.Copy)),
                     reads=[psg], writes=[gxp.r[fsl]])
                S.op("pool", (lambda e: e.tensor_copy(out=gtail[:, c, 0:nseq, :], in_=gx[:, :, L:L + 2])), reads=[gxp.r[fsl]], writes=[gtail])
                psu = next_ring()
                proj_fm(wgu_bf, D_FF + c * 128, uu, NT, psu)
                S.op("act", (lambda e: e.activation(out=upb[:, fsl, :], in_=psu[:, 0:NT], func=AF.Copy)), reads=[psu], writes=[upb.r[fsl]])
                S.op("dve", (lambda e: e.tensor_scalar(out=cvv, in0=gx[:, :, 0:L], scalar1=fw[:, c, 0:1], scalar2=fb[:, c:c + 1], op0=ALU.mult, op1=ALU.add)),
                     reads=[gxp.r[fsl], fw, fb], writes=[cv2.r[fsl]])
                for i in (1, 2):
                    S.op("dve", (lambda e, i=i: e.scalar_tensor_tensor(out=cvv, in0=gx[:, :, i:i + L], scalar=fw[:, c, i:i + 1], in1=cvv, op0=ALU.mult, op1=ALU.add)),
                         reads=[gxp.r[fsl], fw, cv2.r[fsl]], writes=[cv2.r[fsl]])

            def stageB(c):
                fsl = c % 4
                S.op("act", (lambda e: e.activation(out=t2[:, fsl, :], in_=cv2[:, fsl, :], func=AF.Square, scale=0.044715 ** 0.5)), reads=[cv2.r[fsl]], writes=[t2.r[fsl]])
                S.op("dve", (lambda e: e.scalar_tensor_tensor(out=t2[:, fsl, :], in0=t2[:, fsl, :], scalar=1.0, in1=cv2[:, fsl, :], op0=ALU.add, op1=ALU.mult)),
                     reads=[t2.r[fsl], cv2.r[fsl]], writes=[t2.r[fsl]])

            def stageC(c):
                fsl = c % 4
                S.op("act", (lambda e: e.activation(out=sg2[:, fsl, :], in_=t2[:, fsl, :], func=AF.Sigmoid, scale=1.5957691216057308)), reads=[t2.r[fsl]], writes=[sg2.r[fsl]])
                S.op("pool", (lambda e: e.tensor_tensor(out=sg2[:, fsl, :], in0=sg2[:, fsl, :], in1=cv2[:, fsl, :], op=ALU.mult)),
                     reads=[sg2.r[fsl], cv2.r[fsl]], writes=[sg2.r[fsl]])
                S.op("dve", (lambda e: e.tensor_tensor(out=hT[:, c, :], in0=upb[:, fsl, :], in1=sg2[:, fsl, :], op=ALU.mult)),
                     reads=[upb.r[fsl], sg2.r[fsl]], writes=[hT])

            for i in range(NFF + 2):
                if i < NFF:
                    stageA(i)
                if 0 <= i - 1 < NFF:
                    stageB(i - 1)
                if 0 <= i - 2 < NFF:
                    stageC(i - 2)
                if i == 10 and hook is not None:
                    hook()
            S.op("pe", [(lambda e, c=c: e.matmul(psA[:], lhsT=hT[:, c, :], rhs=wd_bf[:, c, 0:512], start=(c == 0), stop=(c == NFF - 1))) for c in range(NFF)],
                 reads=[hT, wd_bf], writes=[psA])
            S.op("pe", [(lambda e, c=c: e.matmul(psB[:], lhsT=hT[:, c, :], rhs=wd_bf[:, c, 512:1024], start=(c == 0), stop=(c == NFF - 1))) for c in range(NFF)],
                 reads=[hT, wd_bf], writes=[psB])
            rms_finish([psA, psB], x1a, [x1r], nfpost, yb[:, 0, :], [yb.r[0]])
            if y_dst is not None:
                S.dma("pool", lambda e: e.dma_start(out=y_dst, in_=yb[:, 0, :]), reads=[yb.r[0]])

        p2_from = cfg.get("p2_from", first_own - 1)
        p2_to = cfg.get("p2_to", p_tiles)
        jobs = []
        for tl in range(p2_from, p2_to):
            own = tl - first_own
            jobs.append(dict(scr=tl - (p_tiles - NSCR + 1), x=xp_d[tl * 128:(tl + 1) * 128, :],
                             y=(y_p[own * 128:(own + 1) * 128, :] if own >= 0 else None), nseq=1, L=128, sample=False))
        if cfg.get("sample", True):
            jobs.append(dict(scr=NSCR - 1, x=xs_d[:, :], y=y_s[:, :], nseq=4, L=32, sample=True))
        p2_front(jobs[0]["scr"], jobs[0]["x"], 0)
        for ji, jb in enumerate(jobs):
            nxt = jobs[ji + 1] if ji + 1 < len(jobs) else None
            hook = (lambda nxt=nxt, ji=ji: p2_front(nxt["scr"], nxt["x"], (ji + 1) % 2)) if nxt is not None else None
            if jb["sample"]:
                S.dma("pool", lambda e: e.dma_start(out=fc_p[:, :].rearrange("p (a b) -> p a b", a=NFF), in_=gtail[:, :, 0, :]), reads=[gtail])
                load_small(gtail, fc0_d[:, :])
            p2_body(ji % 2, jb["y"], jb["nseq"], jb["L"], hook)
        if jobs[-1]["sample"]:
            S.dma("pool", lambda e: e.dma_start(out=fc_s[:, :], in_=flat(gtail[:])), reads=[gtail])
        else:
            S.dma("pool", lambda e: e.dma_start(out=fc_p[:, :].rearrange("p (a b) -> p a b", a=NFF), in_=gtail[:, :, 0, :]), reads=[gtail])

    S.finish()
    with nc.Block() as block:
        @block.tensor
        def _(e):
            S.replay("pe", e)

        @block.vector
        def _(e):
            S.replay("dve", e)

        @block.scalar
        def _(e):
            S.replay("act", e)

        @block.gpsimd
        def _(e):
            S.replay("pool", e)

        @block.sync
        def _(e):
            S.replay("sp", e)
    es.close()
    print("instructions:", S.ninstr, {k: len(v) for k, v in S.ops.items()}, "sems", len(S.sems))
    return nc


def _cpack(bs):
    m = np.arange(128)
    same = (m[:, None] // bs) == (m[None, :] // bs)
    tri = same & (m[:, None] <= m[None, :])
    strict = same & (m[:, None] > m[None, :])
    incl = same & (m[:, None] >= m[None, :])
    ident = np.eye(128, dtype=bool)
    nb = 128 // bs
    bm = np.zeros((128, 4), np.float32)
    cm = np.zeros((128, 4, 128), np.float32)
    for b in range(nb):
        bm[b * bs:(b + 1) * bs, b] = 1.0
        cm[:, b, :] = bm[:, b:b + 1]
    parts = [tri, same, np.tile(strict, (1, 4)), np.tile(incl, (1, 4)), np.tile(ident, (1, 4)), bm, -bm, cm.reshape(128, 512)]
    out = np.concatenate([np.asarray(p, np.float32) for p in parts], axis=1)
    assert out.shape == (128, CP_W)
    return np.ascontiguousarray(out)


def _bias_tables(rel_bias):
    tab = np.asarray(rel_bias, np.float32)
    kj = np.arange(128)[:, None, None]
    r = np.arange(5)[None, :, None]
    qi = np.arange(128)[None, None, :]
    rel_p = (4 - r) * 128 + qi - kj
    idx_p = np.clip(rel_p, -128, 128) + 128
    btp = tab[:, idx_p].transpose(1, 2, 0, 3)
    mask = ((r == 4) & (kj >= 64) & (qi < 64)) | ((r == 0) & (kj < 64) & (qi >= 64))
    btm = np.where(mask, np.float32(NEG), np.float32(0.0)).astype(np.float32)
    btm = np.broadcast_to(btm[:, :, None, :], (128, 5, 8, 128))
    rel_s = np.where(r < 4, (4 - r) * 128 + (qi % 32) - kj, (qi % 32) - (kj % 32))
    idx_s = np.clip(rel_s, -128, 128) + 128
    bts = tab[:, idx_s].transpose(1, 2, 0, 3)
    f = lambda a: np.ascontiguousarray(np.asarray(a, np.float32).reshape(128, 5 * 8 * 128))
    return f(btp), f(btm), f(bts)


def _smk():
    out = np.zeros((128, SMK_W), np.float32)
    q = np.arange(128)
    for s in range(4):
        cm = np.where(q // 32 == s, 0.0, NEG).astype(np.float32)
        out[0, s * 512:(s + 1) * 512] = np.tile(cm, 4)
        out[0, 2048 + s * 128:2048 + (s + 1) * 128] = cm
    out[0, 2560:2688] = 1.0
    out[0, 2688:3200] = 1.0
    return out


def make_in_maps(inp):
    f32 = lambda a: np.ascontiguousarray(np.asarray(a, np.float32))
    xpr, xsm = f32(inp["x_prompt"]), f32(inp["x_sample"])
    ckf = f32(inp["cache_band_k"])[0].reshape(32, 512, 512)
    cvf = f32(inp["cache_band_v"])[0].reshape(32, 512, 512)
    sdl = f32(inp["state_delta"])[0]
    sqc = f32(inp["state_qkv_conv"])[0]
    sfc = f32(inp["state_ffn_conv"])[0]
    rep = lambda v, n: np.ascontiguousarray(np.broadcast_to(f32(v).reshape(1, -1), (128, n)))
    pk = lambda v: np.ascontiguousarray(f32(v).reshape(-1, 128).T)
    btp, btm, bts = _bias_tables(inp["rel_bias"][0])
    common = dict(
        w_in=f32(inp["w_in"][0]), w_out=f32(inp["w_out"][0]), w_gu=f32(inp["w_gate_up"][0]), w_d=f32(inp["w_down"][0]),
        nmpre=pk(inp["norm_mix_pre"][0]), nfpre=pk(inp["norm_ffn_pre"][0]),
        nmpost=rep(inp["norm_mix_post"][0], 1024), nfpost=rep(inp["norm_ffn_post"][0], 1024),
        cw=np.ascontiguousarray(f32(inp["qkv_conv_w"][0]).reshape(4, 12, 128).transpose(2, 1, 0).reshape(128, 48)),
        fw=np.ascontiguousarray(f32(inp["ffn_conv_w"][0]).reshape(3, NFF, 128).transpose(2, 1, 0).reshape(128, NFF * 3)),
        fb=pk(inp["ffn_conv_b"][0]),
        alog=rep(inp["a_log"][0], 4), dtb=rep(inp["dt_bias"][0], 4),
        gnw=rep(np.tile(f32(inp["gdn_norm_w"][0]), 4), 512), anw=rep(np.tile(f32(inp["attn_norm_w"][0]), 8), 512),
        btp=btp, btm=btm, bts=bts, cpp=_cpack(64), cps=_cpack(32), smk=_smk(),
    )
    maps = []
    for c in range(8):
        s, half = c // 2, c % 2
        T0 = half * 2048
        xw = np.zeros((4096, 1024), np.float32)
        if half == 0:
            xw[2048:] = xpr[s, 0:2048]
        else:
            xw[:] = xpr[s]
        pos = T0 - 2048 + np.arange(4096)
        kval = (pos >= 0).astype(np.float32).reshape(32, 128).T
        sq = slice(4 * c, 4 * c + 4)
        m = dict(common)
        m.update(
            xp=xw, xs=np.ascontiguousarray(xsm[sq].reshape(128, 1024)), kvalid=np.ascontiguousarray(kval),
            ck=np.ascontiguousarray(ckf[sq]), cv=np.ascontiguousarray(cvf[sq]), s0=np.ascontiguousarray(sdl[sq]),
            qc0=np.ascontiguousarray(sqc[sq].reshape(4, 3, 12, 128).transpose(3, 2, 0, 1).reshape(128, 144)),
            fc0=np.ascontiguousarray(sfc[sq].reshape(4, 2, NFF, 128).transpose(3, 2, 0, 1).reshape(128, NFF * 8)),
        )
        maps.append(m)
    return maps


def assemble(results):
    y_prompt = np.zeros((4, 4096, 1024), np.float32)
    y_sample = np.zeros((32, 32, 1024), np.float32)
    bkp = np.zeros((1, 4, 512, 8, 64), np.float32)
    bvp = np.zeros((1, 4, 512, 8, 64), np.float32)
    dlp = np.zeros((1, 4, 4, 128, 128), np.float32)
    qcp = np.zeros((1, 4, 3, 1536), np.float32)
    fcp = np.zeros((1, 4, 2, D_FF), np.float32)
    bks = np.zeros((1, 32, 32, 8, 64), np.float32)
    bvs = np.zeros((1, 32, 32, 8, 64), np.float32)
    dls = np.zeros((1, 32, 4, 128, 128), np.float32)
    qcs = np.zeros((1, 32, 3, 1536), np.float32)
    fcs = np.zeros((1, 32, 2, D_FF), np.float32)
    for c, r in enumerate(results):
        s, half = c // 2, c % 2
        y_prompt[s, half * 2048:(half + 1) * 2048] = r["y_p"]
        if half == 1:
            bkp[0, s] = r["bk_p"].reshape(512, 8, 64)
            bvp[0, s] = r["bv_p"].reshape(512, 8, 64)
            dlp[0, s] = r["S_p"].reshape(128, 4, 128).transpose(1, 0, 2)
            qcp[0, s] = r["qc_p"].reshape(128, 12, 3).transpose(2, 1, 0).reshape(3, 1536)
            fcp[0, s] = r["fc_p"].reshape(128, NFF, 2).transpose(2, 1, 0).reshape(2, D_FF)
        sq = slice(4 * c, 4 * c + 4)
        y_sample[sq] = r["y_s"].reshape(4, 32, 1024)
        bks[0, sq] = r["bk_s"].reshape(4, 32, 8, 64)
        bvs[0, sq] = r["bv_s"].reshape(4, 32, 8, 64)
        dls[0, sq] = r["S_s"].reshape(128, 4, 4, 128).transpose(1, 2, 0, 3)
        qcs[0, sq] = r["qc_s"].reshape(128, 12, 4, 3).transpose(2, 3, 1, 0).reshape(4, 3, 1536)
        fcs[0, sq] = r["fc_s"].reshape(128, NFF, 4, 2).transpose(2, 3, 1, 0).reshape(4, 2, D_FF)
    return (y_prompt, y_sample, bkp, bvp, dlp, qcp, fcp, bks, bvs, dls, qcs, fcs)


def kernel(**inputs):
    nc = build_program()
    maps = make_in_maps(inputs)
    res = run_bass_kernel_spmd(nc, maps, core_ids=list(range(8)))
    return assemble(res.results)
```

```python
import os
import numpy as np
from contextlib import ExitStack
import ml_dtypes
import concourse.bass as bass
import concourse.mybir as mybir
from concourse.bass_utils import run_bass_kernel_spmd

F32 = mybir.dt.float32
BF16 = mybir.dt.bfloat16
F32R = mybir.dt.float32r
AF = mybir.ActivationFunctionType
ALU = mybir.AluOpType

D_MODEL = 1024
SEQ = 4096
GDN_H = 4
ATT_H = 8
D_FF = 2816
NFF = 22
IN_COLS = 3592
EPS = 1e-6
NEG = -30000.0
C_Q, C_K, C_V, C_Z, C_B, C_A, C_QB, C_KB, C_VB = 0, 512, 1024, 1536, 2048, 2052, 2056, 2568, 3080
CP_TRI, CP_SAME, CP_STRICT, CP_INCL, CP_IDENT, CP_BM, CP_NBM, CP_CM = 0, 128, 256, 768, 1280, 1792, 1796, 1800
CP_W = 1800 + 512


class Res:
    __slots__ = ("name", "lw", "rd")

    def __init__(self, name):
        self.name = name
        self.lw = None
        self.rd = {}


class Buf:
    def __init__(self, h, name, nslots=1):
        self.h = h
        self.r = [Res(f"{name}.{i}") for i in range(nslots)]

    def __getitem__(self, idx):
        return self.h[idx]


def _res(x):
    out = []
    for a in x:
        if isinstance(a, Buf):
            out.extend(a.r)
        else:
            out.append(a)
    return out


class Sched:
    CE = ("pe", "dve", "act", "pool")
    ROT = 20000

    def __init__(self, nc, es):
        self.nc, self.es = nc, es
        self.ops = {e: [] for e in ("pe", "dve", "act", "pool", "sp")}
        self.sems = []
        self.cur = {}
        self.cnt = {}
        for e in self.CE:
            self._newsem(e)
        self.waited = {e: {} for e in self.ops}
        self.dpool = {q: [] for q in ("sp", "pool", "act")}
        self.dnext = {q: 0 for q in self.dpool}
        for q, n in (("sp", 12), ("pool", 12), ("act", 4)):
            for i in range(n):
                s = es.enter_context(nc.semaphore(f"d_{q}{i}"))
                self.sems.append(s)
                self.dpool[q].append([len(self.sems) - 1, 0])
        self.ninstr = 0

    def _newsem(self, e):
        s = self.es.enter_context(self.nc.semaphore(f"s_{e}{len(self.sems)}"))
        self.sems.append(s)
        self.cur[e] = len(self.sems) - 1
        self.cnt[e] = 0

    def _deps(self, eng, reads, writes, strict=True):
        deps = {}

        def add(t):
            if t is None:
                return
            if deps.get(t[0], -1) < t[1]:
                deps[t[0]] = t[1]
        for r in reads:
            add(r.lw)
        for w in writes:
            add(w.lw)
            for k, v in w.rd.items():
                add((k, v))
        waits = []
        wd = self.waited[eng]
        for k, v in deps.items():
            if not strict and eng in self.cur and k == self.cur[eng]:
                continue
            if wd.get(k, -1) >= v:
                continue
            wd[k] = v
            waits.append((k, v))
        return waits

    def _mark(self, ticket, reads, writes):
        for w in writes:
            w.lw = ticket
            w.rd = {}
        for r in reads:
            if r.rd.get(ticket[0], -1) < ticket[1]:
                r.rd[ticket[0]] = ticket[1]

    def op(self, eng, fns, reads=(), writes=()):
        reads, writes = _res(reads), _res(writes)
        if not isinstance(fns, (list, tuple)):
            fns = [fns]
        waits = self._deps(eng, reads, writes, strict=(eng != "pe"))
        if self.cnt[eng] >= self.ROT:
            self._newsem(eng)
        self.cnt[eng] += 1
        ticket = (self.cur[eng], self.cnt[eng])
        n = len(fns)
        for i, f in enumerate(fns):
            self.ops[eng].append((waits if i == 0 else (), f, ticket[0] if i == n - 1 else None, 1))
        self.ninstr += n
        self._mark(ticket, reads, writes)

    def dma(self, q, fn, reads=(), writes=()):
        reads, writes = _res(reads), _res(writes)
        waits = self._deps(q, reads, writes)
        pool = self.dpool[q]
        slot = pool[self.dnext[q] % len(pool)]
        self.dnext[q] += 1
        wd = self.waited[q]
        if slot[1] > 0 and wd.get(slot[0], -1) < slot[1]:
            wd[slot[0]] = slot[1]
            waits.append((slot[0], slot[1]))
        slot[1] += 16
        ticket = (slot[0], slot[1])
        self.ops[q].append((waits, fn, slot[0], 16))
        self.ninstr += 1
        self._mark(ticket, reads, writes)

    def barrier(self):
        allw = []
        for q in self.dpool:
            for si, tgt in self.dpool[q]:
                if tgt > 0:
                    allw.append((si, tgt))
        for e in self.CE:
            if self.cnt[e] > 0:
                allw.append((self.cur[e], self.cnt[e]))
        for eng in self.ops:
            wd = self.waited[eng]
            waits = [(k, v) for k, v in allw if wd.get(k, -1) < v]
            for k, v in waits:
                wd[k] = v
            self.ops[eng].append((waits, None, None, 0))

    def finish(self):
        waits = []
        for q in self.dpool:
            for si, tgt in self.dpool[q]:
                if tgt > 0:
                    waits.append((si, tgt))
        for e in self.CE:
            if self.cnt[e] > 0:
                waits.append((self.cur[e], self.cnt[e]))
        self.ops["sp"].append((waits, None, None, 0))

    def replay(self, eng, e):
        sems = self.sems
        for waits, fn, inc, amt in self.ops[eng]:
            for k, v in waits:
                e.wait_ge(sems[k], v)
            if fn is None:
                continue
            ins = fn(e)
            if inc is not None:
                ins.then_inc(sems[inc], amt)


ARENA_F32 = 53000
SMK_W = 4 * 512 + 4 * 128 + 128 + 512
NSCR = 19


def build_program(cfg=None):
    cfg = cfg or {}
    nc = bass.Bass("TRN2", target_bir_lowering=False)
    es = ExitStack()

    def DI(name, shape, dt=F32):
        return nc.dram_tensor(name, list(shape), dt, kind="ExternalInput").ap()

    def DO(name, shape, dt=F32):
        return nc.dram_tensor(name, list(shape), dt, kind="ExternalOutput").ap()

    xp_d = DI("xp", [4096, 1024])
    xs_d = DI("xs", [128, 1024])
    kvalid_d = DI("kvalid", [128, 32])
    ck_d = DI("ck", [4, 512, 512])
    cv_d = DI("cv", [4, 512, 512])
    s0_d = DI("s0", [4, 4, 128, 128])
    qc0_d = DI("qc0", [128, 12 * 4 * 3])
    fc0_d = DI("fc0", [128, NFF * 4 * 2])
    w_in_d = DI("w_in", [1024, IN_COLS])
    w_out_d = DI("w_out", [1024, 1024])
    w_gu_d = DI("w_gu", [1024, 2 * D_FF])
    w_d_d = DI("w_d", [D_FF, 1024])
    nmpre_d = DI("nmpre", [128, 8])
    nfpre_d = DI("nfpre", [128, 8])
    nmpost_d = DI("nmpost", [128, 1024])
    nfpost_d = DI("nfpost", [128, 1024])
    cw_d = DI("cw", [128, 48])
    fw_d = DI("fw", [128, NFF * 3])
    fb_d = DI("fb", [128, NFF])
    alog_d = DI("alog", [128, 4])
    dtb_d = DI("dtb", [128, 4])
    gnw_d = DI("gnw", [128, 512])
    anw_d = DI("anw", [128, 512])
    btp_d = DI("btp", [128, 5 * 8 * 128])
    btm_d = DI("btm", [128, 5 * 8 * 128])
    bts_d = DI("bts", [128, 5 * 8 * 128])
    cpp_d = DI("cpp", [128, CP_W])
    cps_d = DI("cps", [128, CP_W])
    smk_d = DI("smk", [128, SMK_W])

    y_p = DO("y_p", [2048, 1024])
    bk_p = DO("bk_p", [512, 512])
    bv_p = DO("bv_p", [512, 512])
    S_p = DO("S_p", [128, 512])
    qc_p = DO("qc_p", [128, 36])
    fc_p = DO("fc_p", [128, NFF * 2])
    y_s = DO("y_s", [128, 1024])
    bk_s = DO("bk_s", [128, 512])
    bv_s = DO("bv_s", [128, 512])
    S_s = DO("S_s", [128, 4 * 512])
    qc_s = DO("qc_s", [128, 12 * 4 * 3])
    fc_s = DO("fc_s", [128, NFF * 4 * 2])
    ocat_scr = nc.dram_tensor("ocat_scr", [NSCR * 128, 1024], BF16).ap()
    dbg = {}
    for name, shape in (cfg.get("dbg") or {}).items():
        dbg[name] = DO("dbg_" + name, shape, BF16 if name.startswith(("ocat", "bf_")) else F32)

    S = Sched(nc, es)
    dump_tile = cfg.get("dump_tile", -1)

    def DUMP(name, ap, res, tag):
        if name in dbg and tag == dump_tile:
            S.dma("pool", lambda e: e.dma_start(out=dbg[name][:, :], in_=ap), reads=res)

    arena = es.enter_context(nc.sbuf_tensor("arena", [128, ARENA_F32], F32))
    aptr = [0]

    def SB(name, shape, dt=F32, nslots=1, at=None):
        n = int(np.prod(shape[1:]))
        nf = n if dt == F32 else (n + 1) // 2
        nf = (nf + 1) // 2 * 2
        if at is not None:
            off = at[0]
            at[0] += nf
            assert at[0] <= at[1], f"alias region overflow at {name}"
        else:
            off = aptr[0]
            aptr[0] += nf
        assert aptr[0] <= ARENA_F32, f"SBUF arena overflow at {name}: {aptr[0]}"
        v = arena[:, off:off + nf]
        if dt != F32:
            v = v.bitcast(dt)[:, 0:n]
        if len(shape) == 3:
            v = v.rearrange("p (a b) -> p a b", a=shape[1])
        elif len(shape) == 4:
            v = v.rearrange("p (a b c) -> p a b c", a=shape[1], b=shape[2])
        return Buf(v, name, nslots)

    def flat(ap):
        nd = len(ap.shape)
        if nd == 2:
            return ap
        if nd == 3:
            return ap.rearrange("p a b -> p (a b)")
        return ap.rearrange("p a b c -> p (a b c)")

    def PS(name, shape, dt=F32):
        h = es.enter_context(nc.psum_tensor(name, list(shape), dt))
        return Buf(h, name, 1)

    psT = PS("psT", [128, 1024], BF16)
    psP = [PS("psP0", [128, 512]), PS("psP1", [128, 512])]
    psA = PS("psA", [128, 512])
    psB = PS("psB", [128, 512])
    psS = PS("psS", [128, 512])
    psO = PS("psO", [128, 512])
    psM = PS("psM", [128, 512])
    pctr = [0]

    def next_psP():
        pctr[0] += 1
        return psP[pctr[0] % 2]

    identb = SB("identb", [128, 128], BF16)
    stage = SB("stage", [128, 2, 1024], F32, nslots=2)
    xin = SB("xin", [128, 2, 1024], F32, nslots=2)
    xn = SB("xn", [128, 1024], BF16)
    st = SB("st", [128, 8])
    epsb = SB("epsb", [128, 2])
    shared_end = aptr[0]

    stctr = [0]

    def load_cast_weight(dst_fn, src_ap_fn, ncols, nk, scale_ap_fn, dstbuf, pieces=None, piece_res=None):
        if pieces is None:
            order = [(k, c0, min(ncols, c0 + 1024), None) for k in range(nk) for c0 in range(0, ncols, 1024)]
        else:
            order = [(k, c0, c1, piece_res[pi]) for pi, (c0, c1) in enumerate(pieces) for k in range(nk)]
        for (k, c0, c1, pres) in order:
            if True:
                if pres is not None:
                    dstbuf = pres
                sl = stctr[0] % 2
                stctr[0] += 1
                w = c1 - c0
                S.dma("sp", (lambda e, sl=sl, k=k, c0=c0, c1=c1, w=w: e.dma_start(out=stage[:, sl, 0:w], in_=src_ap_fn(k, c0, c1))),
                      writes=[stage.r[sl]])
                sc = scale_ap_fn(k) if scale_ap_fn else None
                rds = [stage.r[sl]] + ([scale_ap_fn.buf] if scale_ap_fn else [])
                if stctr[0] % 2:
                    if sc is None:
                        S.op("act", (lambda e, sl=sl, k=k, c0=c0, c1=c1, w=w: e.activation(out=dst_fn(k, c0, c1), in_=stage[:, sl, 0:w], func=AF.Copy)),
                             reads=rds, writes=[dstbuf])
                    else:
                        S.op("act", (lambda e, sl=sl, k=k, c0=c0, c1=c1, w=w, sc=sc: e.activation(out=dst_fn(k, c0, c1), in_=stage[:, sl, 0:w], func=AF.Copy, scale=sc)),
                             reads=rds, writes=[dstbuf])
                else:
                    if sc is None:
                        S.op("dve", (lambda e, sl=sl, k=k, c0=c0, c1=c1, w=w: e.tensor_copy(out=dst_fn(k, c0, c1), in_=stage[:, sl, 0:w])),
                             reads=rds, writes=[dstbuf])
                    else:
                        S.op("dve", (lambda e, sl=sl, k=k, c0=c0, c1=c1, w=w, sc=sc: e.tensor_scalar(out=dst_fn(k, c0, c1), in0=stage[:, sl, 0:w], scalar1=sc, scalar2=None, op0=ALU.mult)),
                             reads=rds, writes=[dstbuf])

    def load_small(buf, dram_ap):
        S.dma("sp", (lambda e: e.dma_start(out=flat(buf[:]), in_=dram_ap)), writes=[buf])

    def rms_stats(src_ap, src_res, junk_ap, junk_res, ncols):
        S.op("act", lambda e: e.activation(out=junk_ap, in_=src_ap, func=AF.Square, accum_out=st[:, 0:1]),
             reads=src_res, writes=[st] + junk_res)
        S.op("act", lambda e: e.activation(out=st[:, 1:2], in_=st[:, 0:1], func=AF.Ln, scale=1.0 / ncols, bias=epsb[:, 0:1]),
             reads=[st, epsb], writes=[st])
        S.op("act", lambda e: e.activation(out=st[:, 2:3], in_=st[:, 1:2], func=AF.Exp, scale=-0.5),
             reads=[st], writes=[st])

    S.op("pool", lambda e: e.memset(epsb[:, 0:1], EPS), writes=[epsb])
    S.op("pool", lambda e: e.memset(epsb[:, 1:2], 1.0), writes=[epsb])

    do_p1 = cfg.get("p1", True)
    do_p2 = cfg.get("p2", True)
    NTL1 = cfg.get("ntl1", 2)
    p_tiles = 32
    first_own = 16
    p1_from = cfg.get("p1_from", 0)
    p1_to = cfg.get("p1_to", p_tiles)
    xctr = [0]

    def norm_transpose(src_dram_ap, dstT, col0, junk=None):
        sl = xctr[0] % 2
        xctr[0] += 1
        S.dma("sp", lambda e: e.dma_start(out=xin[:, sl, :], in_=src_dram_ap), writes=[xin.r[sl]])
        rms_stats(xin[:, sl, :], [xin.r[sl]], xn[:], [xn], 1024)
        S.op("dve", lambda e: e.tensor_scalar(out=xn[:], in0=xin[:, sl, :], scalar1=st[:, 2:3], scalar2=None, op0=ALU.mult),
             reads=[xin.r[sl], st], writes=[xn])
        S.op("pe", [(lambda e, k=k: e.transpose(psT[:, k * 128:(k + 1) * 128], xn[:, k * 128:(k + 1) * 128], identb[:])) for k in range(8)],
             reads=[xn, identb], writes=[psT])
        S.op("act", lambda e: e.activation(out=dstT[:, :, col0:col0 + 128], in_=psT[:].rearrange("p (a b) -> p a b", a=8), func=AF.Copy),
             reads=[psT], writes=[dstT])
        return sl

    def proj_fm(wbuf, wcol0, rhsT, NT, ps, nk=8, wres=None):
        S.op("pe", [(lambda e, k=k: e.matmul(ps[:, 0:NT], lhsT=wbuf[:, k, wcol0:wcol0 + 128], rhs=rhsT[:, k, 0:NT], start=(k == 0), stop=(k == nk - 1))) for k in range(nk)],
             reads=[wres if wres is not None else wbuf, rhsT], writes=[ps])

    def proj_tm(wbuf, wcol0, ncols, lhsT_buf, col0, ps, pcol0=0, nk=8):
        S.op("pe", [(lambda e, k=k: e.matmul(ps[:, pcol0:pcol0 + ncols], lhsT=lhsT_buf[:, k, col0:col0 + 128], rhs=wbuf[:, k, wcol0:wcol0 + ncols], start=(k == 0), stop=(k == nk - 1))) for k in range(nk)],
             reads=[wbuf, lhsT_buf], writes=[ps])

    if do_p1:
        aptr[0] = shared_end
        NT1 = NTL1 * 128
        cp = SB("cp", [128, CP_W])
        tri2 = cp[:, CP_TRI:CP_TRI + 128]
        same2 = cp[:, CP_SAME:CP_SAME + 128]
        identf = cp[:, CP_IDENT:CP_IDENT + 128]

        def c4(off):
            return cp[:, off:off + 512].rearrange("p (a b) -> p a b", a=4)

        strict4, incl4, ident4, cm4 = c4(CP_STRICT), c4(CP_INCL), c4(CP_IDENT), c4(CP_CM)
        w_in_bf = SB("w_in_bf", [128, 8, IN_COLS], BF16)
        nmpre = SB("nmpre", [128, 8])
        cw = SB("cw", [128, 12, 4])
        alog = SB("alog", [128, 4])
        nea = SB("nea", [128, 4])
        dtb = SB("dtb", [128, 4])
        gnw4 = SB("gnw4", [128, 4, 128])
        anw8 = SB("anw8", [128, 8, 64])
        kvalid = SB("kvalid", [128, 32])
        BT = SB("BT", [128, 5, 8, 128], BF16)
        uT0 = SB("uT", [128, 8, NT1], BF16)
        xpb = SB("xpb", [128, 2, NT1 + 16], F32, nslots=2)
        cvb = SB("cvb", [128, 4, NT1], F32, nslots=4)
        sqb = SB("sqb", [128, 2, NT1], F32, nslots=2)
        rnb = SB("rnb", [128, 2, NT1], F32, nslots=2)
        gta = SB("gta", [128, 16])
        xpt_r = [Res("xpt0"), Res("xpt1")]
        knT0 = SB("knT", [128, 4, NT1], BF16)
        qnT0 = SB("qnT", [128, 4, NT1], BF16)
        vT0 = SB("vT", [128, 4, NT1], BF16)
        zs0 = SB("zs", [128, NTL1, 512], BF16, nslots=NTL1)
        graw = SB("graw", [128, 2, NTL1, 8], F32, nslots=2 * NTL1)
        kbT = SB("kbT", [128, 4, 8, 128], BF16, nslots=8)
        V1 = SB("V1", [128, 8, 8, 64], BF16, nslots=8)
        vcol = SB("vcol", [128, 8], BF16, nslots=8)
        qbz0 = SB("qbz", [128, 4, 2, NT1], BF16)
        qtail = SB("qtail", [128, 12, 4, 3])
        ones128 = SB("ones128", [128, 128])
        gt = SB("gt", [128, 80])
        Rb = SB("Rb", [128, 4, 128])
        Eb = SB("Eb", [128, 4, 128])
        tmpb = SB("tmpb", [128, 4, 128])
        Qb = [SB("Qb0", [128, 4, 128]), SB("Qb1", [128, 4, 128])]
        Pb = [SB("Pb0", [128, 4, 128]), SB("Pb1", [128, 4, 128])]
        Xb = SB("Xb", [128, 4, 128])
        Xbf = SB("Xbf", [128, 4, 128], BF16)
        qkb = SB("qkb", [128, 4, 128], BF16)
        qkT = SB("qkT", [128, 4, 128], BF16)
        kbg = SB("kbg", [128, 4, 128], BF16)
        kgm = SB("kgm", [128, 4, 4, 128], BF16)
        vbb = SB("vbb", [128, 4, 128], BF16)
        vn32 = SB("vn32", [128, 4, 128])
        vnb = SB("vnb", [128, 4, 128], BF16)
        wTb = SB("wTb", [128, 4, 128], BF16)
        oacc = SB("oacc", [128, 4, 128])
        Sp = SB("Sp", [128, 4, 128])
        Spb = SB("Spb", [128, 4, 128], BF16)
        ureg0 = aptr[0]
        Ss = SB("Ss", [128, 4, 4, 128], F32, nslots=4)
        Ssb = SB("Ssb", [128, 4, 4, 128], BF16, nslots=4)
        smk = SB("smk", [128, SMK_W], BF16)
        ureg = [ureg0, aptr[0]]
        uT1 = SB("uT1", [128, 8, NT1], BF16, at=ureg)
        knT1 = SB("knT1", [128, 4, NT1], BF16, at=ureg)
        qnT1 = SB("qnT1", [128, 4, NT1], BF16, at=ureg)
        vT1 = SB("vT1", [128, 4, NT1], BF16, at=ureg)
        zs1 = SB("zs1", [128, NTL1, 512], BF16, nslots=NTL1, at=ureg)
        qbz1 = SB("qbz1", [128, 4, 2, NT1], BF16, at=ureg)
        uT, knT, qnT, vT, zs, qbz = [uT0, uT1], [knT0, knT1], [qnT0, qnT1], [vT0, vT1], [zs0, zs1], [qbz0, qbz1]
        scb = SB("scb", [128, 4, 128])
        PT = SB("PT", [128, 2, 4, 128], BF16, nslots=2)
        ob32 = SB("ob32", [128, 8, 64])
        ocat = SB("ocat", [128, 2, 1024], BF16, nslots=2)
        kvout = SB("kvout", [128, 1, 512], F32, nslots=1)
        print("phase1 arena used", aptr[0], "of", ARENA_F32)

        load_small(cp, cpp_d[:, :])
        S.op("dve", lambda e: e.tensor_copy(out=identb[:], in_=identf), reads=[cp], writes=[identb])
        load_small(nmpre, nmpre_d[:, :])
        load_small(cw, cw_d[:, :])
        load_small(alog, alog_d[:, :])
        load_small(dtb, dtb_d[:, :])
        load_small(gnw4, gnw_d[:, :])
        load_small(anw8, anw_d[:, :])
        load_small(kvalid, kvalid_d[:, :])
        S.op("act", lambda e: e.activation(out=nea[:], in_=alog[:], func=AF.Exp), reads=[alog], writes=[nea])
        S.op("dve", lambda e: e.tensor_scalar(out=nea[:], in0=nea[:], scalar1=-1.0, scalar2=None, op0=ALU.mult), reads=[nea], writes=[nea])
        S.op("pool", lambda e: e.memset(ones128[:], 1.0), writes=[ones128])
        S.op("pool", lambda e: e.memset(flat(qbz0[:]), 0.0), writes=[qbz0])
        S.op("pool", lambda e: e.memset(flat(qbz1[:]), 0.0), writes=[qbz1])
        S.op("pool", lambda e: e.memset(flat(qtail[:]), 0.0), writes=[qtail])
        S.op("pool", lambda e: e.memset(flat(Sp[:]), 0.0), writes=[Sp])
        S.op("pool", lambda e: e.memset(flat(Spb[:]), 0.0), writes=[Spb])
        S.op("pool", lambda e: e.tensor_copy(out=vcol[:], in_=kvalid[:, 0:8]), reads=[kvalid], writes=[vcol])

        def load_BT(tab_d, mask_d):
            BTf = flat(BT[:])
            for i in range(5):
                S.dma("sp", (lambda e, i=i: e.dma_start(out=stage[:, 0, :], in_=tab_d[:, i * 1024:(i + 1) * 1024])), writes=[stage.r[0]])
                if mask_d is not None:
                    S.dma("sp", (lambda e, i=i: e.dma_start(out=stage[:, 1, :], in_=mask_d[:, i * 1024:(i + 1) * 1024])), writes=[stage.r[1]])
                    S.op("dve", (lambda e, i=i: e.tensor_tensor(out=BTf[:, i * 1024:(i + 1) * 1024], in0=stage[:, 0, :], in1=stage[:, 1, :], op=ALU.add)),
                         reads=[stage], writes=[BT])
                else:
                    S.op("dve", (lambda e, i=i: e.tensor_copy(out=BTf[:, i * 1024:(i + 1) * 1024], in_=stage[:, 0, :])),
                         reads=[stage.r[0]], writes=[BT])

        load_BT(btp_d, btm_d)
        sfn = lambda k: nmpre[:, k:k + 1]
        sfn.buf = nmpre
        load_cast_weight(lambda k, c0, c1: w_in_bf[:, k, c0:c1],
                         lambda k, c0, c1: w_in_d[k * 128:(k + 1) * 128, c0:c1],
                         IN_COLS, 8, sfn, w_in_bf)

        cvctr = [0]

        def stA(c, nseq, L, NT, xsl, csl, need_conv, par):
            ps = next_psP()
            proj_fm(w_in_bf, c * 128, uT[par], NT, ps)
            xpv = xpb[:, xsl, 0:nseq * (L + 3)].rearrange("p (s t) -> p s t", s=nseq)
            cvv = cvb[:, csl, 0:NT].rearrange("p (s t) -> p s t", s=nseq)
            S.op("pool", lambda e: e.tensor_copy(out=xpv[:, :, 0:3], in_=qtail[:, c, 0:nseq, :]), reads=[qtail], writes=[xpt_r[xsl]])
            S.op("act", lambda e: e.activation(out=xpv[:, :, 3:3 + L], in_=ps[:, 0:NT].rearrange("p (s t) -> p s t", s=nseq), func=AF.Copy),
                 reads=[ps], writes=[xpb.r[xsl]])
            S.op("pool", lambda e: e.tensor_copy(out=qtail[:, c, 0:nseq, :], in_=xpv[:, :, L:L + 3]), reads=[xpb.r[xsl]], writes=[qtail])
            if not need_conv:
                return
            S.op("dve", lambda e: e.tensor_scalar(out=cvv, in0=xpv[:, :, 0:L], scalar1=cw[:, c, 0:1], scalar2=None, op0=ALU.mult),
                 reads=[xpb.r[xsl], xpt_r[xsl], cw], writes=[cvb.r[csl]])
            for i in range(1, 4):
                S.op("dve", (lambda e, i=i: e.scalar_tensor_tensor(out=cvv, in0=xpv[:, :, i:i + L], scalar=cw[:, c, i:i + 1], in1=cvv, op0=ALU.mult, op1=ALU.add)),
                     reads=[xpb.r[xsl], xpt_r[xsl], cw, cvb.r[csl]], writes=[cvb.r[csl]])

        def stB(c, NT, csl, qsl, par):
            sg = sqb[:, qsl, 0:NT]
            cvs = cvb[:, csl, 0:NT]
            S.op("act", lambda e: e.activation(out=sg, in_=cvs, func=AF.Exp, scale=-1.0), reads=[cvb.r[csl]], writes=[sqb.r[qsl]])
            S.op("act", lambda e: e.activation(out=sg, in_=sg, func=AF.Ln, bias=epsb[:, 1:2]), reads=[sqb.r[qsl], epsb], writes=[sqb.r[qsl]])
            S.op("act", lambda e: e.activation(out=sg, in_=sg, func=AF.Exp, scale=-1.0), reads=[sqb.r[qsl]], writes=[sqb.r[qsl]])
            if c >= 8:
                S.op("pool", lambda e: e.tensor_tensor(out=vT[par][:, c - 8, 0:NT], in0=cvs, in1=sg, op=ALU.mult), reads=[cvb.r[csl], sqb.r[qsl]], writes=[vT[par]])
                return
            S.op("pool", lambda e: e.tensor_tensor(out=cvs, in0=cvs, in1=sg, op=ALU.mult), reads=[cvb.r[csl], sqb.r[qsl]], writes=[cvb.r[csl]])
            S.op("pool", lambda e: e.tensor_tensor(out=sg, in0=cvs, in1=cvs, op=ALU.mult), reads=[cvb.r[csl]], writes=[sqb.r[qsl]])

        def stC(c, NT, csl, qsl, par):
            if c >= 8:
                return
            psn = next_psP()
            S.op("pe", lambda e: e.matmul(psn[:, 0:NT], lhsT=ones128[:], rhs=sqb[:, qsl, 0:NT], start=True, stop=True), reads=[ones128, sqb.r[qsl]], writes=[psn])
            S.op("act", lambda e: e.activation(out=rnb[:, qsl, 0:NT], in_=psn[:, 0:NT], func=AF.Ln, bias=epsb[:, 0:1]), reads=[psn, epsb], writes=[rnb.r[qsl]])

        def stD(c, NT, csl, qsl, par):
            if c >= 8:
                return
            dstT, h, scale = (qnT[par], c, 128.0 ** -0.5) if c < 4 else (knT[par], c - 4, 1.0)
            S.op("act", lambda e: e.activation(out=rnb[:, qsl, 0:NT], in_=rnb[:, qsl, 0:NT], func=AF.Exp, scale=-0.5), reads=[rnb.r[qsl]], writes=[rnb.r[qsl]])
            S.op("dve", lambda e: e.scalar_tensor_tensor(out=dstT[:, h, 0:NT], in0=cvb[:, csl, 0:NT], scalar=float(scale), in1=rnb[:, qsl, 0:NT], op0=ALU.mult, op1=ALU.mult),
                 reads=[cvb.r[csl], rnb.r[qsl]], writes=[dstT])

        def bc(ap2, n, w):
            return ap2.unsqueeze(2).to_broadcast([128, n, w])

        def gdn_tile(tcol, nb, full, S_in, Sb_in, S_out, Sb_out, zslot, oc_slot, par, tag=-1):
            tc = slice(tcol, tcol + 128)
            beta, t1, g, Gs, eG, ekg, ckbg, nbeta = (gt[:, 0:4], gt[:, 4:8], gt[:, 8:12], gt[:, 12:20], gt[:, 20:24], gt[:, 24:28], gt[:, 28:32], gt[:, 32:36])
            dlb = gt[:, 36:36 + 4 * nb]
            ssq = gt[:, 52:56]
            cog = gt[:, 56:60]
            gr = graw[:, par, zslot, :]
            grr = graw.r[par * NTL1 + zslot]
            S.op("act", lambda e: e.activation(out=beta, in_=gr[:, 0:4], func=AF.Exp, scale=-1.0), reads=[grr], writes=[gt])
            S.op("dve", lambda e: e.tensor_scalar(out=beta, in0=beta, scalar1=1.0, scalar2=None, op0=ALU.add), reads=[gt], writes=[gt])
            S.op("dve", lambda e: e.reciprocal(out=beta, in_=beta), reads=[gt], writes=[gt])
            S.op("dve", lambda e: e.tensor_tensor(out=t1, in0=gr[:, 4:8], in1=dtb[:], op=ALU.add), reads=[grr, dtb], writes=[gt])
            S.op("act", lambda e: e.activation(out=t1, in_=t1, func=AF.Exp), reads=[gt], writes=[gt])
            S.op("act", lambda e: e.activation(out=t1, in_=t1, func=AF.Ln, bias=epsb[:, 1:2]), reads=[gt, epsb], writes=[gt])
            S.op("dve", lambda e: e.tensor_tensor(out=g, in0=t1, in1=nea[:], op=ALU.mult), reads=[gt, nea], writes=[gt])
            fns = [lambda e: e.matmul(psM[:, 16:20], lhsT=tri2, rhs=g, start=True, stop=True),
                   lambda e: e.matmul(psM[:, 20:24], lhsT=same2, rhs=g, start=True, stop=True)]
            for b in range(nb):
                fns.append(lambda e, b=b: e.matmul(psM[:, 24 + 4 * b:28 + 4 * b], lhsT=cm4[:, b, :], rhs=g, start=True, stop=True))
            S.op("pe", fns, reads=[cp, gt], writes=[psM])
            yield
            S.op("act", lambda e: e.activation(out=Gs, in_=psM[:, 16:24], func=AF.Copy), reads=[psM], writes=[gt])
            S.op("act", lambda e: e.activation(out=dlb, in_=psM[:, 24:24 + 4 * nb], func=AF.Exp), reads=[psM], writes=[gt])
            S.op("dve", lambda e: e.tensor_tensor(out=ekg, in0=Gs[:, 4:8], in1=Gs[:, 0:4], op=ALU.subtract), reads=[gt], writes=[gt])
            S.op("act", lambda e: e.activation(out=ekg, in_=ekg, func=AF.Exp), reads=[gt], writes=[gt])
            S.op("act", lambda e: e.activation(out=eG, in_=Gs[:, 0:4], func=AF.Exp), reads=[gt], writes=[gt])
            S.op("dve", lambda e: e.tensor_tensor(out=ckbg, in0=beta, in1=eG, op=ALU.mult), reads=[gt], writes=[gt])
            S.op("dve", lambda e: e.tensor_scalar(out=nbeta, in0=beta, scalar1=-1.0, scalar2=None, op0=ALU.mult), reads=[gt], writes=[gt])
            DUMP("gt", gt[:, 0:64], [gt], tag)
            DUMP("bf_knT", flat(knT[par][:]), [knT[par]], tag)
            DUMP("bf_vT", flat(vT[par][:]), [vT[par]], tag)
            S.op("pool", lambda e: e.tensor_tensor(out=Rb[:], in0=strict4, in1=bc(g, 4, 128), op=ALU.mult), reads=[cp, gt], writes=[Rb])
            S.op("pe", [(lambda e, h=h: e.matmul(psA[:, h * 128:(h + 1) * 128], lhsT=tri2, rhs=Rb[:, h, :], start=True, stop=True)) for h in range(4)],
                 reads=[cp, Rb], writes=[psA])
            yield
            S.op("act", lambda e: e.activation(out=flat(Eb[:]), in_=psA[:], func=AF.Exp), reads=[psA], writes=[Eb])
            S.op("pe", [(lambda e, h=h: e.matmul(psB[:, h * 128:(h + 1) * 128], lhsT=knT[par][:, h, tc], rhs=knT[par][:, h, tc], start=True, stop=True)) for h in range(4)],
                 reads=[knT[par]], writes=[psB])
            yield
            S.op("dve", lambda e: e.tensor_tensor(out=flat(tmpb[:]), in0=psB[:], in1=flat(Eb[:]), op=ALU.mult), reads=[psB, Eb], writes=[tmpb])
            S.op("dve", lambda e: e.tensor_tensor(out=tmpb[:], in0=tmpb[:], in1=bc(nbeta, 4, 128), op=ALU.mult), reads=[tmpb, gt], writes=[tmpb])
            Q, Q2, P, P2 = Qb[0], Qb[1], Pb[0], Pb[1]
            DUMP("E", flat(Eb[:]), [Eb], tag)
            S.op("dve", lambda e, Q=Q: e.tensor_tensor(out=Q[:], in0=tmpb[:], in1=strict4, op=ALU.mult), reads=[tmpb, cp], writes=[Q])
            DUMP("Q0", flat(Q[:]), [Q], tag)
            S.op("pe", [(lambda e, h=h, Q=Q: e.transpose(psA[:, h * 128:(h + 1) * 128], Q[:, h, :], identf)) for h in range(4)],
                 reads=[Q, cp], writes=[psA])
            yield
            S.op("act", lambda e, P=P: e.activation(out=flat(P[:]), in_=psA[:], func=AF.Copy), reads=[psA], writes=[P])
            if full:
                S.op("pe", [(lambda e, h=h: e.matmul(psB[:, h * 128:(h + 1) * 128], lhsT=qnT[par][:, h, tc], rhs=knT[par][:, h, tc], start=True, stop=True)) for h in range(4)],
                     reads=[qnT[par], knT[par]], writes=[psB])
                yield
                S.op("dve", lambda e: e.tensor_tensor(out=flat(tmpb[:]), in0=psB[:], in1=flat(Eb[:]), op=ALU.mult), reads=[psB, Eb], writes=[tmpb])
                S.op("pool", lambda e: e.tensor_tensor(out=qkb[:], in0=tmpb[:], in1=incl4, op=ALU.mult), reads=[tmpb, cp], writes=[qkb])
                S.op("pe", [(lambda e, h=h: e.transpose(psT[:, h * 128:(h + 1) * 128], qkb[:, h, :], identb[:])) for h in range(4)],
                     reads=[qkb, identb], writes=[psT])
                yield
                S.op("act", lambda e: e.activation(out=flat(qkT[:]), in_=psT[:, 0:512], func=AF.Copy), reads=[psT], writes=[qkT])
            DUMP("P0", flat(P[:]), [P], tag)
            S.op("dve", lambda e, P=P: e.tensor_tensor(out=Xb[:], in0=P[:], in1=ident4, op=ALU.add), reads=[P, cp], writes=[Xb])
            nlev = 5
            for lv in range(nlev):
                S.op("pe", [(lambda e, h=h, P=P, Q=Q: e.matmul(psB[:, h * 128:(h + 1) * 128], lhsT=P[:, h, :], rhs=Q[:, h, :], start=True, stop=True)) for h in range(4)],
                     reads=[P, Q], writes=[psB])
                if lv < nlev - 1:
                    S.op("pe", [(lambda e, h=h, P=P, Q=Q: e.matmul(psA[:, h * 128:(h + 1) * 128], lhsT=Q[:, h, :], rhs=P[:, h, :], start=True, stop=True)) for h in range(4)],
                         reads=[P, Q], writes=[psA])
                yield
                S.op("act", (lambda e, Q2=Q2: e.activation(out=flat(Q2[:]), in_=psB[:], func=AF.Copy)), reads=[psB], writes=[Q2])
                if lv < nlev - 1:
                    S.op("dve", (lambda e, P2=P2: e.tensor_copy(out=flat(P2[:]), in_=psA[:])), reads=[psA], writes=[P2])
                S.op("pe", [(lambda e, h=h, Q2=Q2: e.matmul(psB[:, h * 128:(h + 1) * 128], lhsT=Q2[:, h, :], rhs=Xb[:, h, :], start=True, stop=True)) for h in range(4)],
                     reads=[Q2, Xb], writes=[psB])
                yield
                S.op("dve", lambda e: e.tensor_tensor(out=flat(Xb[:]), in0=flat(Xb[:]), in1=psB[:], op=ALU.add), reads=[Xb, psB], writes=[Xb])
                if lv < nlev - 1:
                    P, P2 = P2, P
                Q, Q2 = Q2, Q
            DUMP("X", flat(Xb[:]), [Xb], tag)
            S.op("act", lambda e: e.activation(out=Xbf[:], in_=Xb[:], func=AF.Copy), reads=[Xb], writes=[Xbf])
            S.op("pe", [(lambda e, h=h: e.transpose(psT[:, h * 128:(h + 1) * 128], knT[par][:, h, tc], identb[:])) for h in range(4)] +
                 [(lambda e, h=h: e.transpose(psT[:, 512 + h * 128:512 + (h + 1) * 128], vT[par][:, h, tc], identb[:])) for h in range(4)],
                 reads=[knT[par], vT[par], identb], writes=[psT])
            yield
            kTv = psT[:, 0:512].rearrange("p (a b) -> p a b", a=4)
            vTv = psT[:, 512:1024].rearrange("p (a b) -> p a b", a=4)
            S.op("dve", lambda e: e.tensor_tensor(out=kbg[:], in0=kTv, in1=bc(ckbg, 4, 128), op=ALU.mult), reads=[psT, gt], writes=[kbg])
            S.op("dve", lambda e: e.tensor_tensor(out=vbb[:], in0=vTv, in1=bc(beta, 4, 128), op=ALU.mult), reads=[psT, gt], writes=[vbb])
            for b in range(nb):
                cof = gt[:, 60 + 4 * b:64 + 4 * b]
                S.op("dve", (lambda e, b=b, cof=cof: e.tensor_scalar(out=cof, in0=ekg, scalar1=cp[:, CP_BM + b:CP_BM + b + 1], scalar2=None, op0=ALU.mult)),
                     reads=[gt, cp], writes=[gt])
                S.op("dve", (lambda e, b=b, cof=cof: e.tensor_tensor(out=kgm[:, b], in0=kTv, in1=bc(cof, 4, 128), op=ALU.mult)),
                     reads=[psT, gt], writes=[kgm])
            S.op("pe", [(lambda e, h=h: e.matmul(psA[:, h * 128:(h + 1) * 128], lhsT=Xbf[:, h, :], rhs=vbb[:, h, :], start=True, stop=True)) for h in range(4)],
                 reads=[Xbf, vbb], writes=[psA])
            yield
            S.op("act", lambda e: e.activation(out=flat(vn32[:]), in_=psA[:], func=AF.Copy), reads=[psA], writes=[vn32])
            S.op("pe", [(lambda e, h=h: e.matmul(psB[:, h * 128:(h + 1) * 128], lhsT=kbg[:, h, :], rhs=Xbf[:, h, :], start=True, stop=True)) for h in range(4)],
                 reads=[kbg, Xbf], writes=[psB])
            yield
            S.op("act", lambda e: e.activation(out=flat(wTb[:]), in_=psB[:], func=AF.Copy), reads=[psB], writes=[wTb])
            DUMP("u", flat(vn32[:]), [vn32], tag)
            DUMP("bf_wT", flat(wTb[:]), [wTb], tag)
            DUMP("bf_kbg", flat(kbg[:]), [kbg], tag)
            DUMP("bf_vb", flat(vbb[:]), [vbb], tag)
            for b in range(nb):
                Si, Sbi, So, Sbo = S_in(b), Sb_in(b), S_out(b), Sb_out(b)
                S.op("pe", [(lambda e, h=h, Sbi=Sbi: e.matmul(psA[:, h * 128:(h + 1) * 128], lhsT=wTb[:, h, :], rhs=Sbi[0][:, h, :], start=True, stop=True)) for h in range(4)],
                     reads=[wTb, Sbi[1]], writes=[psA])
                yield
                S.op("dve", (lambda e, b=b: e.scalar_tensor_tensor(out=flat(vn32[:]), in0=psA[:], scalar=cp[:, CP_NBM + b:CP_NBM + b + 1],
                                                                    in1=flat(vn32[:]), op0=ALU.mult, op1=ALU.add)),
                     reads=[psA, cp, vn32], writes=[vn32])
                S.op("act", lambda e: e.activation(out=vnb[:], in_=vn32[:], func=AF.Copy), reads=[vn32], writes=[vnb])
                if full:
                    S.op("dve", (lambda e, b=b: e.tensor_scalar(out=cog, in0=eG, scalar1=cp[:, CP_BM + b:CP_BM + b + 1], scalar2=None, op0=ALU.mult)),
                         reads=[gt, cp], writes=[gt])
                    S.op("pe", [(lambda e, h=h, Sbi=Sbi: e.matmul(psB[:, h * 128:(h + 1) * 128], lhsT=qnT[par][:, h, tc], rhs=Sbi[0][:, h, :], start=True, stop=True)) for h in range(4)],
                         reads=[qnT[par], Sbi[1]], writes=[psB])
                    yield
                    psBv = psB[:].rearrange("p (a b) -> p a b", a=4)
                    if b == 0:
                        S.op("dve", lambda e: e.tensor_tensor(out=oacc[:], in0=psBv, in1=bc(cog, 4, 128), op=ALU.mult), reads=[psB, gt], writes=[oacc])
                    else:
                        S.op("dve", lambda e: e.tensor_tensor(out=tmpb[:], in0=psBv, in1=bc(cog, 4, 128), op=ALU.mult), reads=[psB, gt], writes=[tmpb])
                        S.op("pool", lambda e: e.tensor_tensor(out=oacc[:], in0=oacc[:], in1=tmpb[:], op=ALU.add), reads=[oacc, tmpb], writes=[oacc])
                S.op("pe", [(lambda e, h=h, b=b: e.matmul(psA[:, h * 128:(h + 1) * 128], lhsT=kgm[:, b, h, :], rhs=vnb[:, h, :], start=True, stop=True)) for h in range(4)],
                     reads=[kgm, vnb], writes=[psA])
                yield
                for h in range(4):
                    S.op("dve", (lambda e, h=h, b=b, Si=Si, So=So: e.scalar_tensor_tensor(out=So[0][:, h, :], in0=Si[0][:, h, :], scalar=dlb[:, 4 * b + h:4 * b + h + 1],
                                                                                      in1=psA[:, h * 128:(h + 1) * 128], op0=ALU.mult, op1=ALU.add)),
                         reads=[Si[1], gt, psA], writes=[So[1]])
                if Sbo is not None:
                    S.op("act", (lambda e, So=So, Sbo=Sbo: e.activation(out=Sbo[0], in_=So[0], func=AF.Copy)), reads=[So[1]], writes=[Sbo[1]])
            DUMP("vn", flat(vn32[:]), [vn32], tag)
            DUMP("Safter", flat(S_out(nb - 1)[0]), [S_out(nb - 1)[1]], tag)
            if full:
                S.op("pe", [(lambda e, h=h: e.matmul(psB[:, h * 128:(h + 1) * 128], lhsT=qkT[:, h, :], rhs=vnb[:, h, :], start=True, stop=True)) for h in range(4)],
                     reads=[qkT, vnb], writes=[psB])
                yield
                S.op("dve", lambda e: e.tensor_tensor(out=flat(oacc[:]), in0=flat(oacc[:]), in1=psB[:], op=ALU.add), reads=[oacc, psB], writes=[oacc])
                for h in range(4):
                    S.op("act", (lambda e, h=h: e.activation(out=tmpb[:, h, :], in_=oacc[:, h, :], func=AF.Square, accum_out=ssq[:, h:h + 1])),
                         reads=[oacc], writes=[tmpb, gt])
                S.op("act", lambda e: e.activation(out=ssq, in_=ssq, func=AF.Ln, scale=1.0 / 128, bias=epsb[:, 0:1]), reads=[gt, epsb], writes=[gt])
                S.op("act", lambda e: e.activation(out=ssq, in_=ssq, func=AF.Exp, scale=-0.5), reads=[gt], writes=[gt])
                DUMP("o", flat(oacc[:]), [oacc], tag)
                S.op("dve", lambda e: e.tensor_tensor(out=tmpb[:], in0=oacc[:], in1=bc(ssq, 4, 128), op=ALU.mult), reads=[oacc, gt], writes=[tmpb])
                S.op("pool", lambda e: e.tensor_tensor(out=tmpb[:], in0=tmpb[:], in1=gnw4[:], op=ALU.mult), reads=[tmpb, gnw4], writes=[tmpb])
                S.op("dve", lambda e: e.tensor_tensor(out=ocat[:, oc_slot, 0:512], in0=flat(tmpb[:]), in1=zs[par][:, zslot, :], op=ALU.mult),
                     reads=[tmpb, zs[par].r[zslot]], writes=[ocat.r[oc_slot]])

        def attn_tile(pieces, qcol, oc_slot, par):
            qc = slice(qcol, qcol + 128)
            npc = len(pieces)
            first_pv = [True]
            pctr2 = [0]
            for pi, pc in enumerate(pieces):
                if pc.get("prep"):
                    pc["prep"]()
                for hh in range(2):
                    fns = []
                    first = True
                    if pc.get("seq") is not None:
                        s = pc["seq"]
                        fns.append(lambda e, s=s: e.matmul(psS[:], lhsT=smk[:, 2560:2688], rhs=smk[:, s * 512:(s + 1) * 512], start=True, stop=False, skip_group_check=True))
                        first = False
                        if pc.get("newk"):
                            fns.append(lambda e, s=s: e.matmul(psS[:], lhsT=smk[:, 2048 + s * 128:2048 + (s + 1) * 128], rhs=smk[:, 2688:3200], start=False, stop=False, skip_group_check=True))
                    for j in range(4):
                        h = 4 * hh + j
                        fns.append(lambda e, h=h, j=j, pc=pc, first=first: e.matmul(psS[:, j * 128:(j + 1) * 128], lhsT=kbT[:, h // 2, pc["kslot"], :],
                                                                                   rhs=qbz[par][:, h // 2, h % 2, qc], start=first, stop=True, skip_group_check=True))
                    S.op("pe", fns, reads=[kbT.r[pc["kslot"]], qbz[par], smk], writes=[psS])
                    yield
                    S.op("dve", (lambda e, pc=pc, hh=hh: e.tensor_tensor(out=scb[:], in0=psS[:].rearrange("p (a b) -> p a b", a=4), in1=BT[:, pc["r"], 4 * hh:4 * hh + 4, :], op=ALU.add)),
                         reads=[psS, BT], writes=[scb])
                    psl = pctr2[0] % 2
                    pctr2[0] += 1
                    S.op("act", (lambda e, psl=psl: e.activation(out=PT[:, psl], in_=scb[:], func=AF.Exp)), reads=[scb], writes=[PT.r[psl]])
                    fns = []
                    for j in range(4):
                        h = 4 * hh + j
                        stt = first_pv[0]
                        first_pv[0] = False
                        fns.append(lambda e, j=j, h=h, pc=pc, psl=psl, stt=stt: e.matmul(psO[:, h * 64:(h + 1) * 64], lhsT=PT[:, psl, j, :], rhs=V1[:, pc["vslot"], h, :],
                                                                                        start=stt, stop=False, skip_group_check=True))
                        fns.append(lambda e, j=j, h=h, pc=pc, psl=psl, pi=pi: e.matmul(psM[:, 64 + pi * 8 + h:64 + pi * 8 + h + 1], lhsT=PT[:, psl, j, :], rhs=vcol[:, pc["vslot"]:pc["vslot"] + 1],
                                                                                      start=True, stop=True, skip_group_check=True))
                    S.op("pe", fns, reads=[PT.r[psl], V1.r[pc["vslot"]], vcol.r[pc["vslot"]]], writes=[psO, psM])
                    yield
            rden = gta[:, 0:8]
            ss8 = gta[:, 8:16]
            S.op("dve", lambda e: e.tensor_reduce(out=rden, in_=psM[:, 64:64 + npc * 8].rearrange("p (a b) -> p b a", b=8), axis=mybir.AxisListType.X, op=ALU.add),
                 reads=[psM], writes=[gta])
            S.op("dve", lambda e: e.tensor_scalar(out=rden, in0=rden, scalar1=1e-30, scalar2=None, op0=ALU.max), reads=[gta], writes=[gta])
            S.op("dve", lambda e: e.reciprocal(out=rden, in_=rden), reads=[gta], writes=[gta])
            S.op("dve", lambda e: e.tensor_tensor(out=ob32[:], in0=psO[:].rearrange("p (a b) -> p a b", a=8), in1=bc(rden, 8, 64), op=ALU.mult),
                 reads=[psO, gta], writes=[ob32])
            for h in range(8):
                S.op("act", (lambda e, h=h: e.activation(out=scb[:, 0, 0:64], in_=ob32[:, h, :], func=AF.Square, accum_out=ss8[:, h:h + 1])),
                     reads=[ob32], writes=[scb, gta])
            S.op("act", lambda e: e.activation(out=ss8, in_=ss8, func=AF.Ln, scale=1.0 / 64, bias=epsb[:, 0:1]), reads=[gta, epsb], writes=[gta])
            S.op("act", lambda e: e.activation(out=ss8, in_=ss8, func=AF.Exp, scale=-0.5), reads=[gta], writes=[gta])
            S.op("dve", lambda e: e.tensor_tensor(out=ob32[:], in0=ob32[:], in1=bc(ss8, 8, 64), op=ALU.mult), reads=[ob32, gta], writes=[ob32])
            S.op("pool", lambda e: e.tensor_tensor(out=ocat[:, oc_slot, 512:1024].rearrange("p (a b) -> p a b", a=8), in0=ob32[:], in1=anw8[:], op=ALU.mult),
                 reads=[ob32, anw8], writes=[ocat.r[oc_slot]])

        occtr = [0]
        kvctr = [0]

        def run_all(*gens):
            gens = [g for g in gens if g is not None]
            while gens:
                for g in list(gens):
                    try:
                        next(g)
                    except StopIteration:
                        gens.remove(g)

        def front_gen(tls, gq, gkv, full, sample, par):
            ntl = len(tls)
            NT = ntl * 128
            nseq, L = (4, 32) if sample else (1, NT)
            for ti, tl in enumerate(tls):
                src = xs_d[:, :] if sample else xp_d[tl * 128:(tl + 1) * 128, :]
                norm_transpose(src, uT[par], ti * 128)
                yield
            chunks = list(range(12)) if gq else list(range(4, 12))
            nch = len(chunks)
            ok = lambda c: (c >= 4 or full)
            for i in range(nch + 3):
                if i < nch:
                    c = chunks[i]
                    stA(c, nseq, L, NT, i % 2, i % 4, ok(c), par)
                if 0 <= i - 1 < nch and ok(chunks[i - 1]):
                    stB(chunks[i - 1], NT, (i - 1) % 4, (i - 1) % 2, par)
                if 0 <= i - 2 < nch and ok(chunks[i - 2]):
                    stC(chunks[i - 2], NT, (i - 2) % 4, (i - 2) % 2, par)
                if 0 <= i - 3 < nch and ok(chunks[i - 3]):
                    stD(chunks[i - 3], NT, (i - 3) % 4, (i - 3) % 2, par)
                yield
            if full:
                for c in range(4):
                    ps = next_psP()
                    proj_fm(w_in_bf, C_QB + c * 128, uT[par], NT, ps)
                    S.op("act", (lambda e, c=c, ps=ps: e.activation(out=qbz[par][0:64, c, 0, 0:NT], in_=ps[0:64, 0:NT], func=AF.Copy, scale=0.125)), reads=[ps], writes=[qbz[par]])
                    S.op("act", (lambda e, c=c, ps=ps: e.activation(out=qbz[par][64:128, c, 1, 0:NT], in_=ps[64:128, 0:NT], func=AF.Copy, scale=0.125)), reads=[ps], writes=[qbz[par]])
                    yield
            if gkv:
                for c in range(4):
                    ps = next_psP()
                    proj_fm(w_in_bf, C_KB + c * 128, uT[par], NT, ps)
                    for ti, tl in enumerate(tls):
                        slot = tl % 8
                        S.op("act", (lambda e, c=c, ps=ps, ti=ti, slot=slot: e.activation(out=kbT[:, c, slot, :], in_=ps[:, ti * 128:(ti + 1) * 128], func=AF.Copy)),
                             reads=[ps], writes=[kbT.r[slot]])
                    yield
            for ti, tl in enumerate(tls):
                tcol = ti * 128
                if gkv:
                    slot = tl % 8
                    ps = next_psP()
                    proj_tm(w_in_bf, C_VB, 512, uT[par], tcol, ps)
                    S.op("act", (lambda e, ps=ps, slot=slot: e.activation(out=V1[:, slot], in_=ps[:].rearrange("p (a b) -> p a b", a=8), func=AF.Copy)),
                         reads=[ps], writes=[V1.r[slot]])
                    if sample:
                        S.op("pool", (lambda e, slot=slot: e.memset(vcol[:, slot:slot + 1], 1.0)), writes=[vcol.r[slot]])
                    else:
                        S.op("pool", (lambda e, slot=slot, tl=tl: e.tensor_copy(out=vcol[:, slot:slot + 1], in_=kvalid[:, tl:tl + 1])),
                             reads=[kvalid], writes=[vcol.r[slot]])
                    yield
                    want_out = sample or (tl >= p_tiles - 4)
                    if want_out:
                        orow = 0 if sample else (tl - (p_tiles - 4)) * 128
                        kd, vd = (bk_s, bv_s) if sample else (bk_p, bv_p)
                        S.op("act", (lambda e, ps=ps: e.activation(out=kvout[:, 0, :], in_=ps[:], func=AF.Copy)), reads=[ps], writes=[kvout.r[0]])
                        S.dma("pool", (lambda e, vd=vd, orow=orow: e.dma_start(out=vd[orow:orow + 128, :], in_=kvout[:, 0, :])), reads=[kvout.r[0]])
                        ps2 = next_psP()
                        proj_tm(w_in_bf, C_KB, 512, uT[par], tcol, ps2)
                        S.op("act", (lambda e, ps2=ps2: e.activation(out=kvout[:, 0, :], in_=ps2[:], func=AF.Copy)), reads=[ps2], writes=[kvout.r[0]])
                        S.dma("pool", (lambda e, kd=kd, orow=orow: e.dma_start(out=kd[orow:orow + 128, :], in_=kvout[:, 0, :])), reads=[kvout.r[0]])
                        yield
                if full:
                    ps = next_psP()
                    proj_tm(w_in_bf, C_Z, 512, uT[par], tcol, ps)
                    zt = kvout[:, 0, :]
                    S.op("act", (lambda e, ps=ps: e.activation(out=zt, in_=ps[:], func=AF.Exp, scale=-1.0)), reads=[ps], writes=[kvout.r[0]])
                    S.op("act", (lambda e: e.activation(out=zt, in_=zt, func=AF.Ln, bias=epsb[:, 1:2])), reads=[kvout.r[0], epsb], writes=[kvout.r[0]])
                    S.op("act", (lambda e: e.activation(out=zt, in_=zt, func=AF.Exp, scale=-1.0)), reads=[kvout.r[0]], writes=[kvout.r[0]])
                    S.op("dve", (lambda e, ps=ps, ti=ti: e.tensor_tensor(out=zs[par][:, ti, :], in0=ps[:], in1=zt, op=ALU.mult)), reads=[ps, kvout.r[0]], writes=[zs[par].r[ti]])
                    yield
                proj_tm(w_in_bf, C_B, 8, uT[par], tcol, psM, 0)
                S.op("act", (lambda e, ti=ti: e.activation(out=graw[:, par, ti, :], in_=psM[:, 0:8], func=AF.Copy)), reads=[psM], writes=[graw.r[par * NTL1 + ti]])
                yield

        def work_gen(tls, full, sample, par, res):
            for ti, tl in enumerate(tls):
                tcol = ti * 128
                oc_slot = occtr[0] % 2
                if full:
                    occtr[0] += 1
                res["oc_slot"] = oc_slot
                if sample:
                    gens = [gdn_tile(tcol, 4, True,
                                     lambda b: (Ss[:, b], Ss.r[b]), lambda b: (Ssb[:, b], Ssb.r[b]),
                                     lambda b: (Ss[:, b], Ss.r[b]), lambda b: None, ti, oc_slot, par)]
                else:
                    gens = [gdn_tile(tcol, 2, full,
                                     lambda b: (Sp[:], Sp.r[0]), lambda b: (Spb[:], Spb.r[0]),
                                     lambda b: (Sp[:], Sp.r[0]), lambda b: (Spb[:], Spb.r[0]), ti, oc_slot, par, tag=tl)]
                    if full:
                        pieces = [dict(kslot=(tl - 4 + r) % 8, vslot=(tl - 4 + r) % 8, r=r) for r in range(5)]
                        gens.append(attn_tile(pieces, tcol, oc_slot, par))
                while gens:
                    for g in list(gens):
                        try:
                            next(g)
                        except StopIteration:
                            gens.remove(g)
                    yield
                if full and not sample:
                    scr_row = tl - (p_tiles - NSCR + 1)
                    if scr_row >= 0:
                        S.dma("pool", (lambda e, oc_slot=oc_slot, scr_row=scr_row: e.dma_start(out=ocat_scr[scr_row * 128:(scr_row + 1) * 128, :], in_=ocat[:, oc_slot, :])),
                              reads=[ocat.r[oc_slot]])
                    if "ocat" in dbg and tl >= first_own:
                        S.dma("pool", (lambda e, oc_slot=oc_slot, tl=tl: e.dma_start(out=dbg["ocat"][(tl - first_own) * 128:(tl - first_own + 1) * 128, :], in_=ocat[:, oc_slot, :])),
                              reads=[ocat.r[oc_slot]])

        first_full_tile = first_own - 2
        first_kv_tile = first_full_tile - 4
        tl = p1_from
        pending = None
        gi = 0
        while tl < p1_to:
            tls = list(range(tl, min(tl + NTL1, p1_to)))
            full = tls[0] >= first_full_tile
            gkv = tls[0] >= first_kv_tile
            gq = tls[0] >= first_full_tile - NTL1
            par = gi % 2
            run_all(front_gen(tls, gq, gkv, full, False, par), pending)
            pending = work_gen(tls, full, False, par, {})
            tl += NTL1
            gi += 1
        run_all(pending)
        S.dma("pool", lambda e: e.dma_start(out=S_p[:, :], in_=flat(Sp[:])), reads=[Sp])
        S.dma("pool", lambda e: e.dma_start(out=qc_p[:, :].rearrange("p (a b) -> p a b", a=12), in_=qtail[:, :, 0, :]), reads=[qtail])

        if cfg.get("sample", True):
            S.barrier()
            load_small(cp, cps_d[:, :])
            load_BT(bts_d, None)
            for i, (a, b) in enumerate(((0, 1024), (1024, 2048), (2048, 3072), (3072, SMK_W))):
                S.dma("sp", (lambda e, i=i, a=a, b=b: e.dma_start(out=stage[:, i % 2, 0:b - a], in_=smk_d[:, a:b])), writes=[stage.r[i % 2]])
                S.op("dve", (lambda e, i=i, a=a, b=b: e.tensor_copy(out=smk[:, a:b], in_=stage[:, i % 2, 0:b - a])), reads=[stage.r[i % 2]], writes=[smk])
            load_small(qtail, qc0_d[:, :])
            for s in range(4):
                S.dma("sp", (lambda e, s=s: e.dma_start(out=Ss[:, s], in_=s0_d[s].rearrange("h k v -> k h v"))), writes=[Ss.r[s]])
                S.op("act", (lambda e, s=s: e.activation(out=Ssb[:, s], in_=Ss[:, s], func=AF.Copy)), reads=[Ss.r[s]], writes=[Ssb.r[s]])
            run_all(front_gen([0], True, True, True, True, 0))
            sres = {}
            run_all(work_gen([0], True, True, 0, sres))
            ocs = sres["oc_slot"]
            S.dma("pool", lambda e: e.dma_start(out=S_s[:, :], in_=flat(Ss[:])), reads=[Ss])
            S.dma("pool", lambda e: e.dma_start(out=qc_s[:, :], in_=flat(qtail[:])), reads=[qtail])

            def mk_prep(s, r, slot, sl):
                def prep():
                    S.dma("sp", (lambda e: e.dma_start(out=stage[:, sl, 0:512], in_=ck_d[s, r * 128:(r + 1) * 128, :])), writes=[stage.r[sl]])
                    S.dma("sp", (lambda e: e.dma_start(out=stage[:, sl, 512:1024], in_=cv_d[s, r * 128:(r + 1) * 128, :])), writes=[stage.r[sl]])
                    S.op("dve", (lambda e: e.tensor_copy(out=xn[:, 0:512], in_=stage[:, sl, 0:512])), reads=[stage.r[sl]], writes=[xn])
                    S.op("pe", [(lambda e, c=c: e.transpose(psT[:, c * 128:(c + 1) * 128], xn[:, c * 128:(c + 1) * 128], identb[:])) for c in range(4)],
                         reads=[xn, identb], writes=[psT])
                    S.op("act", (lambda e: e.activation(out=kbT[:, :, slot, :], in_=psT[:, 0:512].rearrange("p (a b) -> p a b", a=4), func=AF.Copy)),
                         reads=[psT], writes=[kbT.r[slot]])
                    S.op("act", (lambda e: e.activation(out=V1[:, slot], in_=stage[:, sl, 512:1024].rearrange("p (a b) -> p a b", a=8), func=AF.Copy)),
                         reads=[stage.r[sl]], writes=[V1.r[slot]])
                    S.op("pool", (lambda e: e.memset(vcol[:, slot:slot + 1], 1.0)), writes=[vcol.r[slot]])
                return prep

            pieces = []
            for s in range(4):
                for r in range(4):
                    i = s * 4 + r
                    slot = i % 7 + 1
                    pieces.append(dict(kslot=slot, vslot=slot, r=r, seq=s, prep=mk_prep(s, r, slot, i % 2)))
                pieces.append(dict(kslot=0, vslot=0, r=4, seq=s, newk=True))
            run_all(attn_tile(pieces, 0, ocs, 0))
            S.dma("pool", lambda e: e.dma_start(out=ocat_scr[(NSCR - 1) * 128:NSCR * 128, :], in_=ocat[:, ocs, :]), reads=[ocat.r[ocs]])
            if "ocat_s" in dbg:
                S.dma("pool", lambda e: e.dma_start(out=dbg["ocat_s"][:, :], in_=ocat[:, ocs, :]), reads=[ocat.r[ocs]])

    if do_p2:
        S.barrier()
        aptr[0] = shared_end
        wgu_bf = SB("wgu_bf", [128, 8, 2 * D_FF], BF16)
        wd_bf = SB("wd_bf", [128, NFF, 1024], BF16)
        wout_bf = SB("wout_bf", [128, 8, 1024], BF16)
        nfpre = SB("nfpre", [128, 8])
        nmpost = SB("nmpost", [128, 1024])
        nfpost = SB("nfpost", [128, 1024])
        fw = SB("fw", [128, NFF, 3])
        fb = SB("fb", [128, NFF])
        gtail = SB("gtail", [128, NFF, 4, 2])
        oc2 = SB("oc2", [128, 1, 1024], BF16, nslots=1)
        upb = SB("upb", [128, 4, 128], F32, nslots=4)
        oT = SB("oT", [128, 8, 128], BF16)
        x1 = SB("x1", [128, 1024])
        u2T = SB("u2T", [128, 8, 128], BF16)
        gxp = SB("gxp", [128, 4, 144], F32, nslots=4)
        cv2 = SB("cv2", [128, 4, 128], F32, nslots=4)
        t2 = SB("t2", [128, 4, 128], F32, nslots=4)
        sg2 = SB("sg2", [128, 4, 128], F32, nslots=4)
        junk2 = SB("junk2", [128, 512], BF16)
        gxt_r = [Res("gxt%d" % i) for i in range(4)]
        hT = SB("hT", [128, NFF, 128], BF16)
        yb = SB("yb", [128, 1, 1024], F32, nslots=1)
        st2 = SB("st2", [128, 8])
        print("phase2 arena used", aptr[0], "of", ARENA_F32)

        load_small(nfpre, nfpre_d[:, :])
        load_small(nmpost, nmpost_d[:, :])
        load_small(nfpost, nfpost_d[:, :])
        load_small(fw, fw_d[:, :])
        load_small(fb, fb_d[:, :])
        S.op("pool", lambda e: e.memset(flat(gtail[:]), 0.0), writes=[gtail])
        if not do_p1:
            S.dma("sp", lambda e: e.dma_start(out=stage[:, 0, 0:128], in_=cpp_d[:, CP_IDENT:CP_IDENT + 128]), writes=[stage.r[0]])
            S.op("dve", lambda e: e.tensor_copy(out=identb[:], in_=stage[:, 0, 0:128]), reads=[stage.r[0]], writes=[identb])
        load_cast_weight(lambda k, c0, c1: wout_bf[:, k, c0:c1], lambda k, c0, c1: w_out_d[k * 128:(k + 1) * 128, c0:c1], 1024, 8, None, wout_bf)
        sfn2 = lambda k: nfpre[:, k:k + 1]
        sfn2.buf = nfpre
        wgu_pieces = [(0, 1024), (2816, 3840), (1024, 2048), (3840, 4864), (2048, 2816), (4864, 5632)]
        wgu_res = [Res("wgu%d" % i) for i in range(6)]

        def wgu_r(col):
            for (c0, c1), r in zip(wgu_pieces, wgu_res):
                if c0 <= col < c1:
                    return r
            raise AssertionError(col)
        load_cast_weight(lambda k, c0, c1: wgu_bf[:, k, c0:c1], lambda k, c0, c1: w_gu_d[k * 128:(k + 1) * 128, c0:c1], 2 * D_FF, 8, sfn2, wgu_bf,
                         pieces=wgu_pieces, piece_res=wgu_res)
        load_cast_weight(lambda k, c0, c1: wd_bf[:, k, c0:c1], lambda k, c0, c1: w_d_d[k * 128:(k + 1) * 128, c0:c1], 1024, NFF, None, wd_bf)

        o2ctr = [0]
        yctr = [0]
        fctr = [0]

        def rms_finish(ps_list, resid_ap, resid_res, gain, dst_ap, dst_res):
            for i, ps in enumerate(ps_list):
                S.op("act", (lambda e, i=i, ps=ps: e.activation(out=junk2[:], in_=ps[:], func=AF.Square, accum_out=st2[:, i:i + 1])),
                     reads=[ps], writes=[st2, junk2])
            S.op("dve", lambda e: e.tensor_tensor(out=st2[:, 2:3], in0=st2[:, 0:1], in1=st2[:, 1:2], op=ALU.add), reads=[st2], writes=[st2])
            S.op("act", lambda e: e.activation(out=st2[:, 3:4], in_=st2[:, 2:3], func=AF.Ln, scale=1.0 / 1024, bias=epsb[:, 0:1]), reads=[st2, epsb], writes=[st2])
            S.op("act", lambda e: e.activation(out=st2[:, 4:5], in_=st2[:, 3:4], func=AF.Exp, scale=-0.5), reads=[st2], writes=[st2])
            for i, ps in enumerate(ps_list):
                S.op("dve", (lambda e, i=i, ps=ps: e.scalar_tensor_tensor(out=dst_ap[:, i * 512:(i + 1) * 512], in0=ps[:], scalar=st2[:, 4:5], in1=gain[:, i * 512:(i + 1) * 512],
                                                                         op0=ALU.mult, op1=ALU.mult)),
                     reads=[ps, st2, gain], writes=dst_res)
                S.op("pool", (lambda e, i=i: e.tensor_tensor(out=dst_ap[:, i * 512:(i + 1) * 512], in0=dst_ap[:, i * 512:(i + 1) * 512], in1=resid_ap[:, i * 512:(i + 1) * 512], op=ALU.add)),
                     reads=dst_res + resid_res, writes=dst_res)

        yjunk_b = xn
        yjunk = xn[:]

        ring = [psP[0], psP[1], psS, psO, psM]
        rctr = [0]

        def next_ring():
            bnk = ring[rctr[0] % len(ring)]
            rctr[0] += 1
            return bnk

        u2T_b = Buf(stage[:, 1, 0:512].bitcast(BF16).rearrange("p (a b) -> p a b", a=8), "u2T_b")
        u2T_b.r = [stage.r[1]]
        x1s = [(x1[:], x1.r[0]), (stage[:, 0, :], stage.r[0])]
        u2Ts = [u2T, u2T_b]

        def rms_finish_gen(ps_list, resid_ap, resid_res, gain, dst_ap, dst_res):
            for i, ps in enumerate(ps_list):
                S.op("act", (lambda e, i=i, ps=ps: e.activation(out=junk2[:], in_=ps[:], func=AF.Square, accum_out=st2[:, i:i + 1])),
                     reads=[ps], writes=[st2, junk2])
            yield
            S.op("dve", lambda e: e.tensor_tensor(out=st2[:, 2:3], in0=st2[:, 0:1], in1=st2[:, 1:2], op=ALU.add), reads=[st2], writes=[st2])
            yield
            S.op("act", lambda e: e.activation(out=st2[:, 3:4], in_=st2[:, 2:3], func=AF.Ln, scale=1.0 / 1024, bias=epsb[:, 0:1]), reads=[st2, epsb], writes=[st2])
            S.op("act", lambda e: e.activation(out=st2[:, 4:5], in_=st2[:, 3:4], func=AF.Exp, scale=-0.5), reads=[st2], writes=[st2])
            yield
            for i, ps in enumerate(ps_list):
                S.op("dve", (lambda e, i=i, ps=ps: e.scalar_tensor_tensor(out=dst_ap[:, i * 512:(i + 1) * 512], in0=ps[:], scalar=st2[:, 4:5], in1=gain[:, i * 512:(i + 1) * 512],
                                                                         op0=ALU.mult, op1=ALU.mult)),
                     reads=[ps, st2, gain], writes=dst_res)
            yield
            for i, ps in enumerate(ps_list):
                S.op("pool", (lambda e, i=i: e.tensor_tensor(out=dst_ap[:, i * 512:(i + 1) * 512], in0=dst_ap[:, i * 512:(i + 1) * 512], in1=resid_ap[:, i * 512:(i + 1) * 512], op=ALU.add)),
                     reads=dst_res + resid_res, writes=dst_res)
            yield

        def p2_front(scr_row, x_src, slot):
            osl = 0
            S.dma("act", lambda e: e.dma_start(out=oc2[:, osl, :], in_=ocat_scr[scr_row * 128:(scr_row + 1) * 128, :]), writes=[oc2.r[osl]])
            xsl = xctr[0] % 2
            xctr[0] += 1
            S.dma("act", lambda e: e.dma_start(out=xin[:, xsl, :], in_=x_src), writes=[xin.r[xsl]])
            yield
            S.op("pe", [(lambda e, k=k: e.transpose(psT[:, k * 128:(k + 1) * 128], oc2[:, osl, k * 128:(k + 1) * 128], identb[:])) for k in range(8)],
                 reads=[oc2.r[osl], identb], writes=[psT])
            yield
            S.op("act", lambda e: e.activation(out=oT[:], in_=psT[:].rearrange("p (a b) -> p a b", a=8), func=AF.Copy), reads=[psT], writes=[oT])
            yield
            proj_tm(wout_bf, 0, 512, oT, 0, psA)
            proj_tm(wout_bf, 512, 512, oT, 0, psB)
            yield
            x1a, x1r = x1s[slot]
            yield from rms_finish_gen([psA, psB], xin[:, xsl, :], [xin.r[xsl]], nmpost, x1a, [x1r])
            S.op("act", lambda e: e.activation(out=xn[:], in_=x1a, func=AF.Square, accum_out=st[:, 0:1]), reads=[x1r], writes=[st, xn])
            yield
            S.op("act", lambda e: e.activation(out=st[:, 1:2], in_=st[:, 0:1], func=AF.Ln, scale=1.0 / 1024, bias=epsb[:, 0:1]), reads=[st, epsb], writes=[st])
            S.op("act", lambda e: e.activation(out=st[:, 2:3], in_=st[:, 1:2], func=AF.Exp, scale=-0.5), reads=[st], writes=[st])
            yield
            S.op("dve", lambda e: e.tensor_scalar(out=xn[:], in0=x1a, scalar1=st[:, 2:3], scalar2=None, op0=ALU.mult), reads=[x1r, st], writes=[xn])
            yield
            S.op("pe", [(lambda e, k=k: e.transpose(psT[:, k * 128:(k + 1) * 128], xn[:, k * 128:(k + 1) * 128], identb[:])) for k in range(8)],
                 reads=[xn, identb], writes=[psT])
            yield
            uu = u2Ts[slot]
            S.op("act", lambda e: e.activation(out=uu[:], in_=psT[:].rearrange("p (a b) -> p a b", a=8), func=AF.Copy), reads=[psT], writes=[uu])
            yield

        def p2_body(slot, y_dst, nseq, L):
            NT = 128
            uu = u2Ts[slot]
            x1a, x1r = x1s[slot]
            st_ = {}

            def stageA(c):
                psg = next_ring()
                proj_fm(wgu_bf, c * 128, uu, NT, psg, wres=wgu_r(c * 128))
                fsl = c % 4
                gx = gxp[:, fsl, 0:nseq * (L + 2)].rearrange("p (s t) -> p s t", s=nseq)
                cvv = cv2[:, fsl, :].rearrange("p (s t) -> p s t", s=nseq)
                S.op("pool", (lambda e: e.tensor_copy(out=gx[:, :, 0:2], in_=gtail[:, c, 0:nseq, :])), reads=[gtail], writes=[gxt_r[fsl]])
                S.op("act", (lambda e: e.activation(out=gx[:, :, 2:2 + L], in_=psg[:, 0:NT].rearrange("p (s t) -> p s t", s=nseq), func=AF.Copy)),
                     reads=[psg], writes=[gxp.r[fsl]])
                S.op("pool", (lambda e: e.tensor_copy(out=gtail[:, c, 0:nseq, :], in_=gx[:, :, L:L + 2])), reads=[gxp.r[fsl]], writes=[gtail])
                psu = next_ring()
                proj_fm(wgu_bf, D_FF + c * 128, uu, NT, psu, wres=wgu_r(D_FF + c * 128))
                S.op("act", (lambda e: e.activation(out=upb[:, fsl, :], in_=psu[:, 0:NT], func=AF.Copy)), reads=[psu], writes=[upb.r[fsl]])
                S.op("dve", (lambda e: e.tensor_scalar(out=cvv, in0=gx[:, :, 0:L], scalar1=fw[:, c, 0:1], scalar2=fb[:, c:c + 1], op0=ALU.mult, op1=ALU.add)),
                     reads=[gxp.r[fsl], gxt_r[fsl], fw, fb], writes=[cv2.r[fsl]])
                for i in (1, 2):
                    S.op("dve", (lambda e, i=i: e.scalar_tensor_tensor(out=cvv, in0=gx[:, :, i:i + L], scalar=fw[:, c, i:i + 1], in1=cvv, op0=ALU.mult, op1=ALU.add)),
                         reads=[gxp.r[fsl], gxt_r[fsl], fw, cv2.r[fsl]], writes=[cv2.r[fsl]])

            def stageB(c):
                fsl = c % 4
                S.op("act", (lambda e: e.activation(out=t2[:, fsl, :], in_=cv2[:, fsl, :], func=AF.Square, scale=0.044715 ** 0.5)), reads=[cv2.r[fsl]], writes=[t2.r[fsl]])
                S.op("dve", (lambda e: e.scalar_tensor_tensor(out=t2[:, fsl, :], in0=t2[:, fsl, :], scalar=1.0, in1=cv2[:, fsl, :], op0=ALU.add, op1=ALU.mult)),
                     reads=[t2.r[fsl], cv2.r[fsl]], writes=[t2.r[fsl]])

            def stageC(c):
                fsl = c % 4
                S.op("act", (lambda e: e.activation(out=sg2[:, fsl, :], in_=t2[:, fsl, :], func=AF.Sigmoid, scale=1.5957691216057308)), reads=[t2.r[fsl]], writes=[sg2.r[fsl]])
                S.op("pool", (lambda e: e.tensor_tensor(out=sg2[:, fsl, :], in0=sg2[:, fsl, :], in1=cv2[:, fsl, :], op=ALU.mult)),
                     reads=[sg2.r[fsl], cv2.r[fsl]], writes=[sg2.r[fsl]])
                S.op("dve", (lambda e: e.tensor_tensor(out=hT[:, c, :], in0=upb[:, fsl, :], in1=sg2[:, fsl, :], op=ALU.mult)),
                     reads=[upb.r[fsl], sg2.r[fsl]], writes=[hT])

            for i in range(NFF + 2):
                if i < NFF:
                    stageA(i)
                if 0 <= i - 1 < NFF:
                    stageB(i - 1)
                if 0 <= i - 2 < NFF:
                    stageC(i - 2)
                yield

        def p2_finish(slot, y_dst):
            x1a, x1r = x1s[slot]
            S.op("pe", [(lambda e, c=c: e.matmul(psA[:], lhsT=hT[:, c, :], rhs=wd_bf[:, c, 0:512], start=(c == 0), stop=(c == NFF - 1))) for c in range(NFF)],
                 reads=[hT, wd_bf], writes=[psA])
            S.op("pe", [(lambda e, c=c: e.matmul(psB[:], lhsT=hT[:, c, :], rhs=wd_bf[:, c, 512:1024], start=(c == 0), stop=(c == NFF - 1))) for c in range(NFF)],
                 reads=[hT, wd_bf], writes=[psB])
            yield
            yield from rms_finish_gen([psA, psB], x1a, [x1r], nfpost, yb[:, 0, :], [yb.r[0]])
            if y_dst is not None:
                S.dma("pool", lambda e: e.dma_start(out=y_dst, in_=yb[:, 0, :]), reads=[yb.r[0]])

        p2_from = cfg.get("p2_from", first_own - 1)
        p2_to = cfg.get("p2_to", p_tiles)
        jobs = []
        for tl in range(p2_from, p2_to):
            own = tl - first_own
            jobs.append(dict(scr=tl - (p_tiles - NSCR + 1), x=xp_d[tl * 128:(tl + 1) * 128, :],
                             y=(y_p[own * 128:(own + 1) * 128, :] if own >= 0 else None), nseq=1, L=128, sample=False))
        if cfg.get("sample", True):
            jobs.append(dict(scr=NSCR - 1, x=xs_d[:, :], y=y_s[:, :], nseq=4, L=32, sample=True))
        def step(g):
            if g is None:
                return None
            try:
                next(g)
                return g
            except StopIteration:
                return None

        def drain(g):
            while g is not None:
                g = step(g)

        PRE = 2
        drain(p2_front(jobs[0]["scr"], jobs[0]["x"], 0))
        body = p2_body(0, jobs[0]["y"], jobs[0]["nseq"], jobs[0]["L"])
        nsteps_done = 0
        for ji, jb in enumerate(jobs):
            slot = ji % 2
            nxt = jobs[ji + 1] if ji + 1 < len(jobs) else None
            fgen = p2_front(nxt["scr"], nxt["x"], (ji + 1) % 2) if nxt is not None else None
            k = nsteps_done
            while body is not None:
                body = step(body)
                k += 1
                if k >= 4:
                    fgen = step(fgen)
            drain(fgen)
            fin = p2_finish(slot, jb["y"])
            nbody = None
            nsteps_done = 0
            if nxt is not None:
                if nxt["sample"]:
                    S.dma("pool", lambda e: e.dma_start(out=fc_p[:, :].rearrange("p (a b) -> p a b", a=NFF), in_=gtail[:, :, 0, :]), reads=[gtail])
                    load_small(gtail, fc0_d[:, :])
                nbody = p2_body((ji + 1) % 2, nxt["y"], nxt["nseq"], nxt["L"])
                for _ in range(PRE):
                    nbody = step(nbody)
                    nsteps_done += 1
            while fin is not None:
                fin = step(fin)
                if nbody is not None and nsteps_done < 8:
                    nbody = step(nbody)
                    nsteps_done += 1
            body = nbody
        if jobs[-1]["sample"]:
            S.dma("pool", lambda e: e.dma_start(out=fc_s[:, :], in_=flat(gtail[:])), reads=[gtail])
        else:
            S.dma("pool", lambda e: e.dma_start(out=fc_p[:, :].rearrange("p (a b) -> p a b", a=NFF), in_=gtail[:, :, 0, :]), reads=[gtail])

    S.finish()
    with nc.Block() as block:
        @block.tensor
        def _(e):
            S.replay("pe", e)

        @block.vector
        def _(e):
            S.replay("dve", e)

        @block.scalar
        def _(e):
            S.replay("act", e)

        @block.gpsimd
        def _(e):
            S.replay("pool", e)

        @block.sync
        def _(e):
            S.replay("sp", e)
    es.close()
    print("instructions:", S.ninstr, {k: len(v) for k, v in S.ops.items()}, "sems", len(S.sems))
    return nc


def _cpack(bs):
    m = np.arange(128)
    same = (m[:, None] // bs) == (m[None, :] // bs)
    tri = same & (m[:, None] <= m[None, :])
    strict = same & (m[:, None] > m[None, :])
    incl = same & (m[:, None] >= m[None, :])
    ident = np.eye(128, dtype=bool)
    nb = 128 // bs
    bm = np.zeros((128, 4), np.float32)
    cm = np.zeros((128, 4, 128), np.float32)
    for b in range(nb):
        bm[b * bs:(b + 1) * bs, b] = 1.0
        cm[:, b, :] = bm[:, b:b + 1]
    parts = [tri, same, np.tile(strict, (1, 4)), np.tile(incl, (1, 4)), np.tile(ident, (1, 4)), bm, -bm, cm.reshape(128, 512)]
    out = np.concatenate([np.asarray(p, np.float32) for p in parts], axis=1)
    assert out.shape == (128, CP_W)
    return np.ascontiguousarray(out)


def _bias_tables(rel_bias):
    tab = np.asarray(rel_bias, np.float32)
    kj = np.arange(128)[:, None, None]
    r = np.arange(5)[None, :, None]
    qi = np.arange(128)[None, None, :]
    rel_p = (4 - r) * 128 + qi - kj
    idx_p = np.clip(rel_p, -128, 128) + 128
    btp = tab[:, idx_p].transpose(1, 2, 0, 3)
    mask = ((r == 4) & (kj >= 64) & (qi < 64)) | ((r == 0) & (kj < 64) & (qi >= 64))
    btm = np.where(mask, np.float32(NEG), np.float32(0.0)).astype(np.float32)
    btm = np.broadcast_to(btm[:, :, None, :], (128, 5, 8, 128))
    rel_s = np.where(r < 4, (4 - r) * 128 + (qi % 32) - kj, (qi % 32) - (kj % 32))
    idx_s = np.clip(rel_s, -128, 128) + 128
    bts = tab[:, idx_s].transpose(1, 2, 0, 3)
    f = lambda a: np.ascontiguousarray(np.asarray(a, np.float32).reshape(128, 5 * 8 * 128))
    return f(btp), f(btm), f(bts)


def _smk():
    out = np.zeros((128, SMK_W), np.float32)
    q = np.arange(128)
    for s in range(4):
        cm = np.where(q // 32 == s, 0.0, NEG).astype(np.float32)
        out[0, s * 512:(s + 1) * 512] = np.tile(cm, 4)
        out[0, 2048 + s * 128:2048 + (s + 1) * 128] = cm
    out[0, 2560:2688] = 1.0
    out[0, 2688:3200] = 1.0
    return out


def make_in_maps(inp):
    f32 = lambda a: np.ascontiguousarray(np.asarray(a, np.float32))
    xpr, xsm = f32(inp["x_prompt"]), f32(inp["x_sample"])
    ckf = f32(inp["cache_band_k"])[0].reshape(32, 512, 512)
    cvf = f32(inp["cache_band_v"])[0].reshape(32, 512, 512)
    sdl = f32(inp["state_delta"])[0]
    sqc = f32(inp["state_qkv_conv"])[0]
    sfc = f32(inp["state_ffn_conv"])[0]
    rep = lambda v, n: np.ascontiguousarray(np.broadcast_to(f32(v).reshape(1, -1), (128, n)))
    pk = lambda v: np.ascontiguousarray(f32(v).reshape(-1, 128).T)
    btp, btm, bts = _bias_tables(inp["rel_bias"][0])
    common = dict(
        w_in=f32(inp["w_in"][0]), w_out=f32(inp["w_out"][0]), w_gu=f32(inp["w_gate_up"][0]), w_d=f32(inp["w_down"][0]),
        nmpre=pk(inp["norm_mix_pre"][0]), nfpre=pk(inp["norm_ffn_pre"][0]),
        nmpost=rep(inp["norm_mix_post"][0], 1024), nfpost=rep(inp["norm_ffn_post"][0], 1024),
        cw=np.ascontiguousarray(f32(inp["qkv_conv_w"][0]).reshape(4, 12, 128).transpose(2, 1, 0).reshape(128, 48)),
        fw=np.ascontiguousarray(f32(inp["ffn_conv_w"][0]).reshape(3, NFF, 128).transpose(2, 1, 0).reshape(128, NFF * 3)),
        fb=pk(inp["ffn_conv_b"][0]),
        alog=rep(inp["a_log"][0], 4), dtb=rep(inp["dt_bias"][0], 4),
        gnw=rep(np.tile(f32(inp["gdn_norm_w"][0]), 4), 512), anw=rep(np.tile(f32(inp["attn_norm_w"][0]), 8), 512),
        btp=btp, btm=btm, bts=bts, cpp=_cpack(64), cps=_cpack(32), smk=_smk(),
    )
    maps = []
    for c in range(8):
        s, half = c // 2, c % 2
        T0 = half * 2048
        xw = np.zeros((4096, 1024), np.float32)
        if half == 0:
            xw[2048:] = xpr[s, 0:2048]
        else:
            xw[:] = xpr[s]
        pos = T0 - 2048 + np.arange(4096)
        kval = (pos >= 0).astype(np.float32).reshape(32, 128).T
        sq = slice(4 * c, 4 * c + 4)
        m = dict(common)
        m.update(
            xp=xw, xs=np.ascontiguousarray(xsm[sq].reshape(128, 1024)), kvalid=np.ascontiguousarray(kval),
            ck=np.ascontiguousarray(ckf[sq]), cv=np.ascontiguousarray(cvf[sq]), s0=np.ascontiguousarray(sdl[sq]),
            qc0=np.ascontiguousarray(sqc[sq].reshape(4, 3, 12, 128).transpose(3, 2, 0, 1).reshape(128, 144)),
            fc0=np.ascontiguousarray(sfc[sq].reshape(4, 2, NFF, 128).transpose(3, 2, 0, 1).reshape(128, NFF * 8)),
        )
        maps.append(m)
    return maps


def assemble(results):
    y_prompt = np.zeros((4, 4096, 1024), np.float32)
    y_sample = np.zeros((32, 32, 1024), np.float32)
    bkp = np.zeros((1, 4, 512, 8, 64), np.float32)
    bvp = np.zeros((1, 4, 512, 8, 64), np.float32)
    dlp = np.zeros((1, 4, 4, 128, 128), np.float32)
    qcp = np.zeros((1, 4, 3, 1536), np.float32)
    fcp = np.zeros((1, 4, 2, D_FF), np.float32)
    bks = np.zeros((1, 32, 32, 8, 64), np.float32)
    bvs = np.zeros((1, 32, 32, 8, 64), np.float32)
    dls = np.zeros((1, 32, 4, 128, 128), np.float32)
    qcs = np.zeros((1, 32, 3, 1536), np.float32)
    fcs = np.zeros((1, 32, 2, D_FF), np.float32)
    for c, r in enumerate(results):
        s, half = c // 2, c % 2
        y_prompt[s, half * 2048:(half + 1) * 2048] = r["y_p"]
        if half == 1:
            bkp[0, s] = r["bk_p"].reshape(512, 8, 64)
            bvp[0, s] = r["bv_p"].reshape(512, 8, 64)
            dlp[0, s] = r["S_p"].reshape(128, 4, 128).transpose(1, 0, 2)
            qcp[0, s] = r["qc_p"].reshape(128, 12, 3).transpose(2, 1, 0).reshape(3, 1536)
            fcp[0, s] = r["fc_p"].reshape(128, NFF, 2).transpose(2, 1, 0).reshape(2, D_FF)
        sq = slice(4 * c, 4 * c + 4)
        y_sample[sq] = r["y_s"].reshape(4, 32, 1024)
        bks[0, sq] = r["bk_s"].reshape(4, 32, 8, 64)
        bvs[0, sq] = r["bv_s"].reshape(4, 32, 8, 64)
        dls[0, sq] = r["S_s"].reshape(128, 4, 4, 128).transpose(1, 2, 0, 3)
        qcs[0, sq] = r["qc_s"].reshape(128, 12, 4, 3).transpose(2, 3, 1, 0).reshape(4, 3, 1536)
        fcs[0, sq] = r["fc_s"].reshape(128, NFF, 4, 2).transpose(2, 3, 1, 0).reshape(4, 2, D_FF)
    return (y_prompt, y_sample, bkp, bvp, dlp, qcp, fcp, bks, bvs, dls, qcs, fcs)


def kernel(**inputs):
    nc = build_program()
    maps = make_in_maps(inputs)
    res = run_bass_kernel_spmd(nc, maps, core_ids=list(range(8)))
    return assemble(res.results)
```

```python
import os
import numpy as np
from contextlib import ExitStack
import ml_dtypes
import concourse.bass as bass
import concourse.mybir as mybir
from concourse.bass_utils import run_bass_kernel_spmd

F32 = mybir.dt.float32
BF16 = mybir.dt.bfloat16
F32R = mybir.dt.float32r
AF = mybir.ActivationFunctionType
ALU = mybir.AluOpType

D_MODEL = 1024
SEQ = 4096
GDN_H = 4
ATT_H = 8
D_FF = 2816
NFF = 22
IN_COLS = 3592
EPS = 1e-6
NEG = -30000.0
C_Q, C_K, C_V, C_Z, C_B, C_A, C_QB, C_KB, C_VB = 0, 512, 1024, 1536, 2048, 2052, 2056, 2568, 3080
CP_TRI, CP_SAME, CP_STRICT, CP_INCL, CP_IDENT, CP_BM, CP_NBM, CP_CM = 0, 128, 256, 768, 1280, 1792, 1796, 1800
CP_W = 1800 + 512


class Res:
    __slots__ = ("name", "lw", "rd")

    def __init__(self, name):
        self.name = name
        self.lw = None
        self.rd = {}


class Buf:
    def __init__(self, h, name, nslots=1):
        self.h = h
        self.r = [Res(f"{name}.{i}") for i in range(nslots)]

    def __getitem__(self, idx):
        return self.h[idx]


def _res(x):
    out = []
    for a in x:
        if isinstance(a, Buf):
            out.extend(a.r)
        else:
            out.append(a)
    return out


class Sched:
    CE = ("pe", "dve", "act", "pool")
    ROT = 20000

    def __init__(self, nc, es):
        self.nc, self.es = nc, es
        self.ops = {e: [] for e in ("pe", "dve", "act", "pool", "sp")}
        self.sems = []
        self.cur = {}
        self.cnt = {}
        for e in self.CE:
            self._newsem(e)
        self.waited = {e: {} for e in self.ops}
        self.dpool = {q: [] for q in ("sp", "pool", "act")}
        self.dnext = {q: 0 for q in self.dpool}
        for q, n in (("sp", 12), ("pool", 12), ("act", 4)):
            for i in range(n):
                s = es.enter_context(nc.semaphore(f"d_{q}{i}"))
                self.sems.append(s)
                self.dpool[q].append([len(self.sems) - 1, 0])
        self.ninstr = 0

    def _newsem(self, e):
        s = self.es.enter_context(self.nc.semaphore(f"s_{e}{len(self.sems)}"))
        self.sems.append(s)
        self.cur[e] = len(self.sems) - 1
        self.cnt[e] = 0

    def _deps(self, eng, reads, writes, strict=True):
        deps = {}

        def add(t):
            if t is None:
                return
            if deps.get(t[0], -1) < t[1]:
                deps[t[0]] = t[1]
        for r in reads:
            add(r.lw)
        for w in writes:
            add(w.lw)
            for k, v in w.rd.items():
                add((k, v))
        waits = []
        wd = self.waited[eng]
        for k, v in deps.items():
            if not strict and eng in self.cur and k == self.cur[eng]:
                continue
            if wd.get(k, -1) >= v:
                continue
            wd[k] = v
            waits.append((k, v))
        return waits

    def _mark(self, ticket, reads, writes):
        for w in writes:
            w.lw = ticket
            w.rd = {}
        for r in reads:
            if r.rd.get(ticket[0], -1) < ticket[1]:
                r.rd[ticket[0]] = ticket[1]

    def op(self, eng, fns, reads=(), writes=()):
        reads, writes = _res(reads), _res(writes)
        if not isinstance(fns, (list, tuple)):
            fns = [fns]
        waits = self._deps(eng, reads, writes, strict=(eng != "pe"))
        if self.cnt[eng] >= self.ROT:
            self._newsem(eng)
        self.cnt[eng] += 1
        ticket = (self.cur[eng], self.cnt[eng])
        n = len(fns)
        for i, f in enumerate(fns):
            self.ops[eng].append((waits if i == 0 else (), f, ticket[0] if i == n - 1 else None, 1))
        self.ninstr += n
        self._mark(ticket, reads, writes)

    def dma(self, q, fn, reads=(), writes=()):
        reads, writes = _res(reads), _res(writes)
        waits = self._deps(q, reads, writes)
        pool = self.dpool[q]
        slot = pool[self.dnext[q] % len(pool)]
        self.dnext[q] += 1
        wd = self.waited[q]
        if slot[1] > 0 and wd.get(slot[0], -1) < slot[1]:
            wd[slot[0]] = slot[1]
            waits.append((slot[0], slot[1]))
        slot[1] += 16
        ticket = (slot[0], slot[1])
        self.ops[q].append((waits, fn, slot[0], 16))
        self.ninstr += 1
        self._mark(ticket, reads, writes)

    def barrier(self):
        allw = []
        for q in self.dpool:
            for si, tgt in self.dpool[q]:
                if tgt > 0:
                    allw.append((si, tgt))
        for e in self.CE:
            if self.cnt[e] > 0:
                allw.append((self.cur[e], self.cnt[e]))
        for eng in self.ops:
            wd = self.waited[eng]
            waits = [(k, v) for k, v in allw if wd.get(k, -1) < v]
            for k, v in waits:
                wd[k] = v
            self.ops[eng].append((waits, None, None, 0))

    def finish(self):
        waits = []
        for q in self.dpool:
            for si, tgt in self.dpool[q]:
                if tgt > 0:
                    waits.append((si, tgt))
        for e in self.CE:
            if self.cnt[e] > 0:
                waits.append((self.cur[e], self.cnt[e]))
        self.ops["sp"].append((waits, None, None, 0))

    def replay(self, eng, e):
        sems = self.sems
        for waits, fn, inc, amt in self.ops[eng]:
            for k, v in waits:
                e.wait_ge(sems[k], v)
            if fn is None:
                continue
            ins = fn(e)
            if inc is not None:
                ins.then_inc(sems[inc], amt)


ARENA_F32 = 53000
SMK_W = 4 * 512 + 4 * 128 + 128 + 512
NSCR = 19


def build_program(cfg=None):
    cfg = cfg or {}
    nc = bass.Bass("TRN2", target_bir_lowering=False)
    es = ExitStack()

    def DI(name, shape, dt=F32):
        return nc.dram_tensor(name, list(shape), dt, kind="ExternalInput").ap()

    def DO(name, shape, dt=F32):
        return nc.dram_tensor(name, list(shape), dt, kind="ExternalOutput").ap()

    xp_d = DI("xp", [4096, 1024])
    xs_d = DI("xs", [128, 1024])
    kvalid_d = DI("kvalid", [128, 32])
    ck_d = DI("ck", [4, 512, 512])
    cv_d = DI("cv", [4, 512, 512])
    s0_d = DI("s0", [4, 4, 128, 128])
    qc0_d = DI("qc0", [128, 12 * 4 * 3])
    fc0_d = DI("fc0", [128, NFF * 4 * 2])
    w_in_d = DI("w_in", [1024, IN_COLS])
    w_out_d = DI("w_out", [1024, 1024])
    w_gu_d = DI("w_gu", [1024, 2 * D_FF])
    w_d_d = DI("w_d", [D_FF, 1024])
    nmpre_d = DI("nmpre", [128, 8])
    nfpre_d = DI("nfpre", [128, 8])
    nmpost_d = DI("nmpost", [128, 1024])
    nfpost_d = DI("nfpost", [128, 1024])
    cw_d = DI("cw", [128, 48])
    fw_d = DI("fw", [128, NFF * 3])
    fb_d = DI("fb", [128, NFF])
    alog_d = DI("alog", [128, 4])
    dtb_d = DI("dtb", [128, 4])
    gnw_d = DI("gnw", [128, 512])
    anw_d = DI("anw", [128, 512])
    btp_d = DI("btp", [128, 5 * 8 * 128])
    btm_d = DI("btm", [128, 5 * 8 * 128])
    bts_d = DI("bts", [128, 5 * 8 * 128])
    cpp_d = DI("cpp", [128, CP_W])
    cps_d = DI("cps", [128, CP_W])
    smk_d = DI("smk", [128, SMK_W])

    y_p = DO("y_p", [2048, 1024])
    bk_p = DO("bk_p", [512, 512])
    bv_p = DO("bv_p", [512, 512])
    S_p = DO("S_p", [128, 512])
    qc_p = DO("qc_p", [128, 36])
    fc_p = DO("fc_p", [128, NFF * 2])
    y_s = DO("y_s", [128, 1024])
    bk_s = DO("bk_s", [128, 512])
    bv_s = DO("bv_s", [128, 512])
    S_s = DO("S_s", [128, 4 * 512])
    qc_s = DO("qc_s", [128, 12 * 4 * 3])
    fc_s = DO("fc_s", [128, NFF * 4 * 2])
    ocat_scr = nc.dram_tensor("ocat_scr", [NSCR * 128, 1024], BF16).ap()
    dbg = {}
    for name, shape in (cfg.get("dbg") or {}).items():
        dbg[name] = DO("dbg_" + name, shape, BF16 if name.startswith(("ocat", "bf_")) else F32)

    S = Sched(nc, es)
    dump_tile = cfg.get("dump_tile", -1)

    def DUMP(name, ap, res, tag):
        if name in dbg and tag == dump_tile:
            S.dma("pool", lambda e: e.dma_start(out=dbg[name][:, :], in_=ap), reads=res)

    arena = es.enter_context(nc.sbuf_tensor("arena", [128, ARENA_F32], F32))
    aptr = [0]

    def SB(name, shape, dt=F32, nslots=1, at=None):
        n = int(np.prod(shape[1:]))
        nf = n if dt == F32 else (n + 1) // 2
        nf = (nf + 1) // 2 * 2
        if at is not None:
            off = at[0]
            at[0] += nf
            assert at[0] <= at[1], f"alias region overflow at {name}"
        else:
            off = aptr[0]
            aptr[0] += nf
        assert aptr[0] <= ARENA_F32, f"SBUF arena overflow at {name}: {aptr[0]}"
        v = arena[:, off:off + nf]
        if dt != F32:
            v = v.bitcast(dt)[:, 0:n]
        if len(shape) == 3:
            v = v.rearrange("p (a b) -> p a b", a=shape[1])
        elif len(shape) == 4:
            v = v.rearrange("p (a b c) -> p a b c", a=shape[1], b=shape[2])
        return Buf(v, name, nslots)

    def flat(ap):
        nd = len(ap.shape)
        if nd == 2:
            return ap
        if nd == 3:
            return ap.rearrange("p a b -> p (a b)")
        return ap.rearrange("p a b c -> p (a b c)")

    def PS(name, shape, dt=F32):
        h = es.enter_context(nc.psum_tensor(name, list(shape), dt))
        return Buf(h, name, 1)

    psT = PS("psT", [128, 1024], BF16)
    psP = [PS("psP0", [128, 512]), PS("psP1", [128, 512])]
    psA = PS("psA", [128, 512])
    psB = PS("psB", [128, 512])
    psS = PS("psS", [128, 512])
    psO = PS("psO", [128, 512])
    psM = PS("psM", [128, 512])
    pctr = [0]

    def next_psP():
        pctr[0] += 1
        return psP[pctr[0] % 2]

    identb = SB("identb", [128, 128], BF16)
    stage = SB("stage", [128, 2, 1024], F32, nslots=2)
    xin = SB("xin", [128, 2, 1024], F32, nslots=2)
    xn = SB("xn", [128, 1024], BF16)
    st = SB("st", [128, 8])
    epsb = SB("epsb", [128, 2])
    shared_end = aptr[0]

    stctr = [0]

    def load_cast_weight(dst_fn, src_ap_fn, ncols, nk, scale_ap_fn, dstbuf, pieces=None, piece_res=None):
        if pieces is None:
            order = [(k, c0, min(ncols, c0 + 1024), None) for k in range(nk) for c0 in range(0, ncols, 1024)]
        else:
            order = [(k, c0, c1, piece_res[pi]) for pi, (c0, c1) in enumerate(pieces) for k in range(nk)]
        for (k, c0, c1, pres) in order:
            if True:
                if pres is not None:
                    dstbuf = pres
                sl = stctr[0] % 2
                stctr[0] += 1
                w = c1 - c0
                S.dma("sp", (lambda e, sl=sl, k=k, c0=c0, c1=c1, w=w: e.dma_start(out=stage[:, sl, 0:w], in_=src_ap_fn(k, c0, c1))),
                      writes=[stage.r[sl]])
                sc = scale_ap_fn(k) if scale_ap_fn else None
                rds = [stage.r[sl]] + ([scale_ap_fn.buf] if scale_ap_fn else [])
                if stctr[0] % 2:
                    if sc is None:
                        S.op("act", (lambda e, sl=sl, k=k, c0=c0, c1=c1, w=w: e.activation(out=dst_fn(k, c0, c1), in_=stage[:, sl, 0:w], func=AF.Copy)),
                             reads=rds, writes=[dstbuf])
                    else:
                        S.op("act", (lambda e, sl=sl, k=k, c0=c0, c1=c1, w=w, sc=sc: e.activation(out=dst_fn(k, c0, c1), in_=stage[:, sl, 0:w], func=AF.Copy, scale=sc)),
                             reads=rds, writes=[dstbuf])
                else:
                    if sc is None:
                        S.op("dve", (lambda e, sl=sl, k=k, c0=c0, c1=c1, w=w: e.tensor_copy(out=dst_fn(k, c0, c1), in_=stage[:, sl, 0:w])),
                             reads=rds, writes=[dstbuf])
                    else:
                        S.op("dve", (lambda e, sl=sl, k=k, c0=c0, c1=c1, w=w, sc=sc: e.tensor_scalar(out=dst_fn(k, c0, c1), in0=stage[:, sl, 0:w], scalar1=sc, scalar2=None, op0=ALU.mult)),
                             reads=rds, writes=[dstbuf])

    def load_small(buf, dram_ap):
        S.dma("sp", (lambda e: e.dma_start(out=flat(buf[:]), in_=dram_ap)), writes=[buf])

    def rms_stats(src_ap, src_res, junk_ap, junk_res, ncols):
        S.op("act", lambda e: e.activation(out=junk_ap, in_=src_ap, func=AF.Square, accum_out=st[:, 0:1]),
             reads=src_res, writes=[st] + junk_res)
        S.op("act", lambda e: e.activation(out=st[:, 1:2], in_=st[:, 0:1], func=AF.Ln, scale=1.0 / ncols, bias=epsb[:, 0:1]),
             reads=[st, epsb], writes=[st])
        S.op("act", lambda e: e.activation(out=st[:, 2:3], in_=st[:, 1:2], func=AF.Exp, scale=-0.5),
             reads=[st], writes=[st])

    S.op("pool", lambda e: e.memset(epsb[:, 0:1], EPS), writes=[epsb])
    S.op("pool", lambda e: e.memset(epsb[:, 1:2], 1.0), writes=[epsb])

    do_p1 = cfg.get("p1", True)
    do_p2 = cfg.get("p2", True)
    NTL1 = cfg.get("ntl1", 2)
    p_tiles = 32
    first_own = 16
    p1_from = cfg.get("p1_from", 0)
    p1_to = cfg.get("p1_to", p_tiles)
    xctr = [0]

    def norm_transpose(src_dram_ap, dstT, col0, junk=None):
        sl = xctr[0] % 2
        xctr[0] += 1
        S.dma("sp", lambda e: e.dma_start(out=xin[:, sl, :], in_=src_dram_ap), writes=[xin.r[sl]])
        rms_stats(xin[:, sl, :], [xin.r[sl]], xn[:], [xn], 1024)
        S.op("dve", lambda e: e.tensor_scalar(out=xn[:], in0=xin[:, sl, :], scalar1=st[:, 2:3], scalar2=None, op0=ALU.mult),
             reads=[xin.r[sl], st], writes=[xn])
        S.op("pe", [(lambda e, k=k: e.transpose(psT[:, k * 128:(k + 1) * 128], xn[:, k * 128:(k + 1) * 128], identb[:])) for k in range(8)],
             reads=[xn, identb], writes=[psT])
        S.op("act", lambda e: e.activation(out=dstT[:, :, col0:col0 + 128], in_=psT[:].rearrange("p (a b) -> p a b", a=8), func=AF.Copy),
             reads=[psT], writes=[dstT])
        return sl

    def proj_fm(wbuf, wcol0, rhsT, NT, ps, nk=8, wres=None):
        S.op("pe", [(lambda e, k=k: e.matmul(ps[:, 0:NT], lhsT=wbuf[:, k, wcol0:wcol0 + 128], rhs=rhsT[:, k, 0:NT], start=(k == 0), stop=(k == nk - 1))) for k in range(nk)],
             reads=[wres if wres is not None else wbuf, rhsT], writes=[ps])

    def proj_tm(wbuf, wcol0, ncols, lhsT_buf, col0, ps, pcol0=0, nk=8):
        S.op("pe", [(lambda e, k=k: e.matmul(ps[:, pcol0:pcol0 + ncols], lhsT=lhsT_buf[:, k, col0:col0 + 128], rhs=wbuf[:, k, wcol0:wcol0 + ncols], start=(k == 0), stop=(k == nk - 1))) for k in range(nk)],
             reads=[wbuf, lhsT_buf], writes=[ps])

    if do_p1:
        aptr[0] = shared_end
        NT1 = NTL1 * 128
        cp = SB("cp", [128, CP_W])
        tri2 = cp[:, CP_TRI:CP_TRI + 128]
        same2 = cp[:, CP_SAME:CP_SAME + 128]
        identf = cp[:, CP_IDENT:CP_IDENT + 128]

        def c4(off):
            return cp[:, off:off + 512].rearrange("p (a b) -> p a b", a=4)

        strict4, incl4, ident4, cm4 = c4(CP_STRICT), c4(CP_INCL), c4(CP_IDENT), c4(CP_CM)
        w_in_bf = SB("w_in_bf", [128, 8, IN_COLS], BF16)
        nmpre = SB("nmpre", [128, 8])
        cw = SB("cw", [128, 12, 4])
        alog = SB("alog", [128, 4])
        nea = SB("nea", [128, 4])
        dtb = SB("dtb", [128, 4])
        gnw4 = SB("gnw4", [128, 4, 128])
        anw8 = SB("anw8", [128, 8, 64])
        kvalid = SB("kvalid", [128, 32])
        BT = SB("BT", [128, 5, 8, 128], BF16)
        uT0 = SB("uT", [128, 8, NT1], BF16)
        xpb = SB("xpb", [128, 2, NT1 + 16], F32, nslots=2)
        cvb = SB("cvb", [128, 4, NT1], F32, nslots=4)
        sqb = SB("sqb", [128, 2, NT1], F32, nslots=2)
        rnb = SB("rnb", [128, 2, NT1], F32, nslots=2)
        gta = SB("gta", [128, 16])
        xpt_r = [Res("xpt0"), Res("xpt1")]
        knT0 = SB("knT", [128, 4, NT1], BF16)
        qnT0 = SB("qnT", [128, 4, NT1], BF16)
        vT0 = SB("vT", [128, 4, NT1], BF16)
        zs0 = SB("zs", [128, NTL1, 512], BF16, nslots=NTL1)
        graw = SB("graw", [128, 2, NTL1, 8], F32, nslots=2 * NTL1)
        kbT = SB("kbT", [128, 4, 8, 128], BF16, nslots=8)
        V1 = SB("V1", [128, 8, 8, 64], BF16, nslots=8)
        vcol = SB("vcol", [128, 8], BF16, nslots=8)
        qbz0 = SB("qbz", [128, 4, 2, NT1], BF16)
        qtail = SB("qtail", [128, 12, 4, 3])
        ones128 = SB("ones128", [128, 128])
        gt = SB("gt", [128, 80])
        Rb = SB("Rb", [128, 4, 128])
        Eb = SB("Eb", [128, 4, 128])
        tmpb = SB("tmpb", [128, 4, 128])
        Qb = [SB("Qb0", [128, 4, 128]), SB("Qb1", [128, 4, 128])]
        Pb = [SB("Pb0", [128, 4, 128]), SB("Pb1", [128, 4, 128])]
        Xb = SB("Xb", [128, 4, 128])
        Xbf = SB("Xbf", [128, 4, 128], BF16)
        qkb = SB("qkb", [128, 4, 128], BF16)
        qkT = SB("qkT", [128, 4, 128], BF16)
        kbg = SB("kbg", [128, 4, 128], BF16)
        kgm = SB("kgm", [128, 4, 4, 128], BF16)
        vbb = SB("vbb", [128, 4, 128], BF16)
        vn32 = SB("vn32", [128, 4, 128])
        vnb = SB("vnb", [128, 4, 128], BF16)
        wTb = SB("wTb", [128, 4, 128], BF16)
        oacc = SB("oacc", [128, 4, 128])
        Sp = SB("Sp", [128, 4, 128])
        Spb = SB("Spb", [128, 4, 128], BF16)
        ureg0 = aptr[0]
        Ss = SB("Ss", [128, 4, 4, 128], F32, nslots=4)
        Ssb = SB("Ssb", [128, 4, 4, 128], BF16, nslots=4)
        smk = SB("smk", [128, SMK_W], BF16)
        ureg = [ureg0, aptr[0]]
        uT1 = SB("uT1", [128, 8, NT1], BF16, at=ureg)
        knT1 = SB("knT1", [128, 4, NT1], BF16, at=ureg)
        qnT1 = SB("qnT1", [128, 4, NT1], BF16, at=ureg)
        vT1 = SB("vT1", [128, 4, NT1], BF16, at=ureg)
        zs1 = SB("zs1", [128, NTL1, 512], BF16, nslots=NTL1, at=ureg)
        qbz1 = SB("qbz1", [128, 4, 2, NT1], BF16, at=ureg)
        uT, knT, qnT, vT, zs, qbz = [uT0, uT1], [knT0, knT1], [qnT0, qnT1], [vT0, vT1], [zs0, zs1], [qbz0, qbz1]
        scb = SB("scb", [128, 4, 128])
        PT = SB("PT", [128, 2, 4, 128], BF16, nslots=2)
        ob32 = SB("ob32", [128, 8, 64])
        ocat = SB("ocat", [128, 2, 1024], BF16, nslots=2)
        kvout = SB("kvout", [128, 1, 512], F32, nslots=1)
        print("phase1 arena used", aptr[0], "of", ARENA_F32)

        load_small(cp, cpp_d[:, :])
        S.op("dve", lambda e: e.tensor_copy(out=identb[:], in_=identf), reads=[cp], writes=[identb])
        load_small(nmpre, nmpre_d[:, :])
        load_small(cw, cw_d[:, :])
        load_small(alog, alog_d[:, :])
        load_small(dtb, dtb_d[:, :])
        load_small(gnw4, gnw_d[:, :])
        load_small(anw8, anw_d[:, :])
        load_small(kvalid, kvalid_d[:, :])
        S.op("act", lambda e: e.activation(out=nea[:], in_=alog[:], func=AF.Exp), reads=[alog], writes=[nea])
        S.op("dve", lambda e: e.tensor_scalar(out=nea[:], in0=nea[:], scalar1=-1.0, scalar2=None, op0=ALU.mult), reads=[nea], writes=[nea])
        S.op("pool", lambda e: e.memset(ones128[:], 1.0), writes=[ones128])
        S.op("pool", lambda e: e.memset(flat(qbz0[:]), 0.0), writes=[qbz0])
        S.op("pool", lambda e: e.memset(flat(qbz1[:]), 0.0), writes=[qbz1])
        S.op("pool", lambda e: e.memset(flat(qtail[:]), 0.0), writes=[qtail])
        S.op("pool", lambda e: e.memset(flat(Sp[:]), 0.0), writes=[Sp])
        S.op("pool", lambda e: e.memset(flat(Spb[:]), 0.0), writes=[Spb])
        S.op("pool", lambda e: e.tensor_copy(out=vcol[:], in_=kvalid[:, 0:8]), reads=[kvalid], writes=[vcol])

        def load_BT(tab_d, mask_d):
            BTf = flat(BT[:])
            for i in range(5):
                S.dma("sp", (lambda e, i=i: e.dma_start(out=stage[:, 0, :], in_=tab_d[:, i * 1024:(i + 1) * 1024])), writes=[stage.r[0]])
                if mask_d is not None:
                    S.dma("sp", (lambda e, i=i: e.dma_start(out=stage[:, 1, :], in_=mask_d[:, i * 1024:(i + 1) * 1024])), writes=[stage.r[1]])
                    S.op("dve", (lambda e, i=i: e.tensor_tensor(out=BTf[:, i * 1024:(i + 1) * 1024], in0=stage[:, 0, :], in1=stage[:, 1, :], op=ALU.add)),
                         reads=[stage], writes=[BT])
                else:
                    S.op("dve", (lambda e, i=i: e.tensor_copy(out=BTf[:, i * 1024:(i + 1) * 1024], in_=stage[:, 0, :])),
                         reads=[stage.r[0]], writes=[BT])

        load_BT(btp_d, btm_d)
        sfn = lambda k: nmpre[:, k:k + 1]
        sfn.buf = nmpre
        load_cast_weight(lambda k, c0, c1: w_in_bf[:, k, c0:c1],
                         lambda k, c0, c1: w_in_d[k * 128:(k + 1) * 128, c0:c1],
                         IN_COLS, 8, sfn, w_in_bf)

        cvctr = [0]

        def stA(c, nseq, L, NT, xsl, csl, need_conv, par):
            ps = next_psP()
            proj_fm(w_in_bf, c * 128, uT[par], NT, ps)
            xpv = xpb[:, xsl, 0:nseq * (L + 3)].rearrange("p (s t) -> p s t", s=nseq)
            cvv = cvb[:, csl, 0:NT].rearrange("p (s t) -> p s t", s=nseq)
            S.op("pool", lambda e: e.tensor_copy(out=xpv[:, :, 0:3], in_=qtail[:, c, 0:nseq, :]), reads=[qtail], writes=[xpt_r[xsl]])
            S.op("act", lambda e: e.activation(out=xpv[:, :, 3:3 + L], in_=ps[:, 0:NT].rearrange("p (s t) -> p s t", s=nseq), func=AF.Copy),
                 reads=[ps], writes=[xpb.r[xsl]])
            S.op("pool", lambda e: e.tensor_copy(out=qtail[:, c, 0:nseq, :], in_=xpv[:, :, L:L + 3]), reads=[xpb.r[xsl]], writes=[qtail])
            if not need_conv:
                return
            S.op("dve", lambda e: e.tensor_scalar(out=cvv, in0=xpv[:, :, 0:L], scalar1=cw[:, c, 0:1], scalar2=None, op0=ALU.mult),
                 reads=[xpb.r[xsl], xpt_r[xsl], cw], writes=[cvb.r[csl]])
            for i in range(1, 4):
                S.op("dve", (lambda e, i=i: e.scalar_tensor_tensor(out=cvv, in0=xpv[:, :, i:i + L], scalar=cw[:, c, i:i + 1], in1=cvv, op0=ALU.mult, op1=ALU.add)),
                     reads=[xpb.r[xsl], xpt_r[xsl], cw, cvb.r[csl]], writes=[cvb.r[csl]])

        def stB(c, NT, csl, qsl, par):
            sg = sqb[:, qsl, 0:NT]
            cvs = cvb[:, csl, 0:NT]
            S.op("act", lambda e: e.activation(out=sg, in_=cvs, func=AF.Exp, scale=-1.0), reads=[cvb.r[csl]], writes=[sqb.r[qsl]])
            S.op("act", lambda e: e.activation(out=sg, in_=sg, func=AF.Ln, bias=epsb[:, 1:2]), reads=[sqb.r[qsl], epsb], writes=[sqb.r[qsl]])
            S.op("act", lambda e: e.activation(out=sg, in_=sg, func=AF.Exp, scale=-1.0), reads=[sqb.r[qsl]], writes=[sqb.r[qsl]])
            if c >= 8:
                S.op("pool", lambda e: e.tensor_tensor(out=vT[par][:, c - 8, 0:NT], in0=cvs, in1=sg, op=ALU.mult), reads=[cvb.r[csl], sqb.r[qsl]], writes=[vT[par]])
                return
            S.op("pool", lambda e: e.tensor_tensor(out=cvs, in0=cvs, in1=sg, op=ALU.mult), reads=[cvb.r[csl], sqb.r[qsl]], writes=[cvb.r[csl]])
            S.op("pool", lambda e: e.tensor_tensor(out=sg, in0=cvs, in1=cvs, op=ALU.mult), reads=[cvb.r[csl]], writes=[sqb.r[qsl]])

        def stC(c, NT, csl, qsl, par):
            if c >= 8:
                return
            psn = next_psP()
            S.op("pe", lambda e: e.matmul(psn[:, 0:NT], lhsT=ones128[:], rhs=sqb[:, qsl, 0:NT], start=True, stop=True), reads=[ones128, sqb.r[qsl]], writes=[psn])
            S.op("act", lambda e: e.activation(out=rnb[:, qsl, 0:NT], in_=psn[:, 0:NT], func=AF.Ln, bias=epsb[:, 0:1]), reads=[psn, epsb], writes=[rnb.r[qsl]])

        def stD(c, NT, csl, qsl, par):
            if c >= 8:
                return
            dstT, h, scale = (qnT[par], c, 128.0 ** -0.5) if c < 4 else (knT[par], c - 4, 1.0)
            S.op("act", lambda e: e.activation(out=rnb[:, qsl, 0:NT], in_=rnb[:, qsl, 0:NT], func=AF.Exp, scale=-0.5), reads=[rnb.r[qsl]], writes=[rnb.r[qsl]])
            S.op("dve", lambda e: e.scalar_tensor_tensor(out=dstT[:, h, 0:NT], in0=cvb[:, csl, 0:NT], scalar=float(scale), in1=rnb[:, qsl, 0:NT], op0=ALU.mult, op1=ALU.mult),
                 reads=[cvb.r[csl], rnb.r[qsl]], writes=[dstT])

        def bc(ap2, n, w):
            return ap2.unsqueeze(2).to_broadcast([128, n, w])

        NDUM = cfg.get("ndum", 0)

        def pe_keepwarm(n=None):
            for _ in range(NDUM if n is None else n):
                S.ops["pe"].append(((), (lambda e: e.matmul(psM[:, 256:512], lhsT=identb[:], rhs=xn[:, 0:256], start=True, stop=True)), None, 0))

        def gdn_tile(tcol, nb, full, S_in, Sb_in, S_out, Sb_out, zslot, oc_slot, par, tag=-1):
            tc = slice(tcol, tcol + 128)
            beta, t1, g, Gs, eG, ekg, ckbg, nbeta = (gt[:, 0:4], gt[:, 4:8], gt[:, 8:12], gt[:, 12:20], gt[:, 20:24], gt[:, 24:28], gt[:, 28:32], gt[:, 32:36])
            dlb = gt[:, 36:36 + 4 * nb]
            ssq = gt[:, 52:56]
            cog = gt[:, 56:60]
            gr = graw[:, par, zslot, :]
            grr = graw.r[par * NTL1 + zslot]
            S.op("act", lambda e: e.activation(out=beta, in_=gr[:, 0:4], func=AF.Exp, scale=-1.0), reads=[grr], writes=[gt])
            S.op("dve", lambda e: e.tensor_scalar(out=beta, in0=beta, scalar1=1.0, scalar2=None, op0=ALU.add), reads=[gt], writes=[gt])
            S.op("dve", lambda e: e.reciprocal(out=beta, in_=beta), reads=[gt], writes=[gt])
            S.op("dve", lambda e: e.tensor_tensor(out=t1, in0=gr[:, 4:8], in1=dtb[:], op=ALU.add), reads=[grr, dtb], writes=[gt])
            S.op("act", lambda e: e.activation(out=t1, in_=t1, func=AF.Exp), reads=[gt], writes=[gt])
            S.op("act", lambda e: e.activation(out=t1, in_=t1, func=AF.Ln, bias=epsb[:, 1:2]), reads=[gt, epsb], writes=[gt])
            S.op("dve", lambda e: e.tensor_tensor(out=g, in0=t1, in1=nea[:], op=ALU.mult), reads=[gt, nea], writes=[gt])
            fns = [lambda e: e.matmul(psM[:, 16:20], lhsT=tri2, rhs=g, start=True, stop=True),
                   lambda e: e.matmul(psM[:, 20:24], lhsT=same2, rhs=g, start=True, stop=True)]
            for b in range(nb):
                fns.append(lambda e, b=b: e.matmul(psM[:, 24 + 4 * b:28 + 4 * b], lhsT=cm4[:, b, :], rhs=g, start=True, stop=True))
            S.op("pe", fns, reads=[cp, gt], writes=[psM])
            yield
            S.op("act", lambda e: e.activation(out=Gs, in_=psM[:, 16:24], func=AF.Copy), reads=[psM], writes=[gt])
            S.op("act", lambda e: e.activation(out=dlb, in_=psM[:, 24:24 + 4 * nb], func=AF.Exp), reads=[psM], writes=[gt])
            S.op("dve", lambda e: e.tensor_tensor(out=ekg, in0=Gs[:, 4:8], in1=Gs[:, 0:4], op=ALU.subtract), reads=[gt], writes=[gt])
            S.op("act", lambda e: e.activation(out=ekg, in_=ekg, func=AF.Exp), reads=[gt], writes=[gt])
            S.op("act", lambda e: e.activation(out=eG, in_=Gs[:, 0:4], func=AF.Exp), reads=[gt], writes=[gt])
            S.op("dve", lambda e: e.tensor_tensor(out=ckbg, in0=beta, in1=eG, op=ALU.mult), reads=[gt], writes=[gt])
            S.op("dve", lambda e: e.tensor_scalar(out=nbeta, in0=beta, scalar1=-1.0, scalar2=None, op0=ALU.mult), reads=[gt], writes=[gt])
            DUMP("gt", gt[:, 0:64], [gt], tag)
            DUMP("bf_knT", flat(knT[par][:]), [knT[par]], tag)
            DUMP("bf_vT", flat(vT[par][:]), [vT[par]], tag)
            S.op("pool", lambda e: e.tensor_tensor(out=Rb[:], in0=strict4, in1=bc(g, 4, 128), op=ALU.mult), reads=[cp, gt], writes=[Rb])
            S.op("pe", [(lambda e, h=h: e.matmul(psA[:, h * 128:(h + 1) * 128], lhsT=tri2, rhs=Rb[:, h, :], start=True, stop=True)) for h in range(4)],
                 reads=[cp, Rb], writes=[psA])
            yield
            S.op("act", lambda e: e.activation(out=flat(Eb[:]), in_=psA[:], func=AF.Exp), reads=[psA], writes=[Eb])
            S.op("pe", [(lambda e, h=h: e.matmul(psB[:, h * 128:(h + 1) * 128], lhsT=knT[par][:, h, tc], rhs=knT[par][:, h, tc], start=True, stop=True)) for h in range(4)],
                 reads=[knT[par]], writes=[psB])
            yield
            S.op("dve", lambda e: e.tensor_tensor(out=flat(tmpb[:]), in0=psB[:], in1=flat(Eb[:]), op=ALU.mult), reads=[psB, Eb], writes=[tmpb])
            S.op("dve", lambda e: e.tensor_tensor(out=tmpb[:], in0=tmpb[:], in1=bc(nbeta, 4, 128), op=ALU.mult), reads=[tmpb, gt], writes=[tmpb])
            Q, Q2, P, P2 = Qb[0], Qb[1], Pb[0], Pb[1]
            DUMP("E", flat(Eb[:]), [Eb], tag)
            S.op("dve", lambda e, Q=Q: e.tensor_tensor(out=Q[:], in0=tmpb[:], in1=strict4, op=ALU.mult), reads=[tmpb, cp], writes=[Q])
            DUMP("Q0", flat(Q[:]), [Q], tag)
            S.op("pe", [(lambda e, h=h, Q=Q: e.transpose(psA[:, h * 128:(h + 1) * 128], Q[:, h, :], identf)) for h in range(4)],
                 reads=[Q, cp], writes=[psA])
            yield
            S.op("act", lambda e, P=P: e.activation(out=flat(P[:]), in_=psA[:], func=AF.Copy), reads=[psA], writes=[P])
            if full:
                S.op("pe", [(lambda e, h=h: e.matmul(psB[:, h * 128:(h + 1) * 128], lhsT=qnT[par][:, h, tc], rhs=knT[par][:, h, tc], start=True, stop=True)) for h in range(4)],
                     reads=[qnT[par], knT[par]], writes=[psB])
                yield
                S.op("dve", lambda e: e.tensor_tensor(out=flat(tmpb[:]), in0=psB[:], in1=flat(Eb[:]), op=ALU.mult), reads=[psB, Eb], writes=[tmpb])
                S.op("pool", lambda e: e.tensor_tensor(out=qkb[:], in0=tmpb[:], in1=incl4, op=ALU.mult), reads=[tmpb, cp], writes=[qkb])
                S.op("pe", [(lambda e, h=h: e.transpose(psT[:, h * 128:(h + 1) * 128], qkb[:, h, :], identb[:])) for h in range(4)],
                     reads=[qkb, identb], writes=[psT])
                yield
                S.op("act", lambda e: e.activation(out=flat(qkT[:]), in_=psT[:, 0:512], func=AF.Copy), reads=[psT], writes=[qkT])
            DUMP("P0", flat(P[:]), [P], tag)
            S.op("dve", lambda e, P=P: e.tensor_tensor(out=Xb[:], in0=P[:], in1=ident4, op=ALU.add), reads=[P, cp], writes=[Xb])
            nlev = 5
            for lv in range(nlev):
                S.op("pe", [(lambda e, h=h, P=P, Q=Q: e.matmul(psB[:, h * 128:(h + 1) * 128], lhsT=P[:, h, :], rhs=Q[:, h, :], start=True, stop=True)) for h in range(4)],
                     reads=[P, Q], writes=[psB])
                if lv < nlev - 1:
                    S.op("pe", [(lambda e, h=h, P=P, Q=Q: e.matmul(psA[:, h * 128:(h + 1) * 128], lhsT=Q[:, h, :], rhs=P[:, h, :], start=True, stop=True)) for h in range(4)],
                         reads=[P, Q], writes=[psA])
                pe_keepwarm()
                yield
                S.op("act", (lambda e, Q2=Q2: e.activation(out=flat(Q2[:]), in_=psB[:], func=AF.Copy)), reads=[psB], writes=[Q2])
                if lv < nlev - 1:
                    S.op("dve", (lambda e, P2=P2: e.tensor_copy(out=flat(P2[:]), in_=psA[:])), reads=[psA], writes=[P2])
                S.op("pe", [(lambda e, h=h, Q2=Q2: e.matmul(psB[:, h * 128:(h + 1) * 128], lhsT=Q2[:, h, :], rhs=Xb[:, h, :], start=True, stop=True)) for h in range(4)],
                     reads=[Q2, Xb], writes=[psB])
                pe_keepwarm()
                yield
                S.op("dve", lambda e: e.tensor_tensor(out=flat(Xb[:]), in0=flat(Xb[:]), in1=psB[:], op=ALU.add), reads=[Xb, psB], writes=[Xb])
                if lv < nlev - 1:
                    P, P2 = P2, P
                Q, Q2 = Q2, Q
            DUMP("X", flat(Xb[:]), [Xb], tag)
            S.op("act", lambda e: e.activation(out=Xbf[:], in_=Xb[:], func=AF.Copy), reads=[Xb], writes=[Xbf])
            S.op("pe", [(lambda e, h=h: e.transpose(psT[:, h * 128:(h + 1) * 128], knT[par][:, h, tc], identb[:])) for h in range(4)] +
                 [(lambda e, h=h: e.transpose(psT[:, 512 + h * 128:512 + (h + 1) * 128], vT[par][:, h, tc], identb[:])) for h in range(4)],
                 reads=[knT[par], vT[par], identb], writes=[psT])
            yield
            kTv = psT[:, 0:512].rearrange("p (a b) -> p a b", a=4)
            vTv = psT[:, 512:1024].rearrange("p (a b) -> p a b", a=4)
            S.op("dve", lambda e: e.tensor_tensor(out=kbg[:], in0=kTv, in1=bc(ckbg, 4, 128), op=ALU.mult), reads=[psT, gt], writes=[kbg])
            S.op("dve", lambda e: e.tensor_tensor(out=vbb[:], in0=vTv, in1=bc(beta, 4, 128), op=ALU.mult), reads=[psT, gt], writes=[vbb])
            for b in range(nb):
                cof = gt[:, 60 + 4 * b:64 + 4 * b]
                S.op("dve", (lambda e, b=b, cof=cof: e.tensor_scalar(out=cof, in0=ekg, scalar1=cp[:, CP_BM + b:CP_BM + b + 1], scalar2=None, op0=ALU.mult)),
                     reads=[gt, cp], writes=[gt])
                S.op("dve", (lambda e, b=b, cof=cof: e.tensor_tensor(out=kgm[:, b], in0=kTv, in1=bc(cof, 4, 128), op=ALU.mult)),
                     reads=[psT, gt], writes=[kgm])
            S.op("pe", [(lambda e, h=h: e.matmul(psA[:, h * 128:(h + 1) * 128], lhsT=Xbf[:, h, :], rhs=vbb[:, h, :], start=True, stop=True)) for h in range(4)],
                 reads=[Xbf, vbb], writes=[psA])
            yield
            S.op("act", lambda e: e.activation(out=flat(vn32[:]), in_=psA[:], func=AF.Copy), reads=[psA], writes=[vn32])
            S.op("pe", [(lambda e, h=h: e.matmul(psB[:, h * 128:(h + 1) * 128], lhsT=kbg[:, h, :], rhs=Xbf[:, h, :], start=True, stop=True)) for h in range(4)],
                 reads=[kbg, Xbf], writes=[psB])
            yield
            S.op("act", lambda e: e.activation(out=flat(wTb[:]), in_=psB[:], func=AF.Copy), reads=[psB], writes=[wTb])
            DUMP("u", flat(vn32[:]), [vn32], tag)
            DUMP("bf_wT", flat(wTb[:]), [wTb], tag)
            DUMP("bf_kbg", flat(kbg[:]), [kbg], tag)
            DUMP("bf_vb", flat(vbb[:]), [vbb], tag)
            for b in range(nb):
                Si, Sbi, So, Sbo = S_in(b), Sb_in(b), S_out(b), Sb_out(b)
                S.op("pe", [(lambda e, h=h, Sbi=Sbi: e.matmul(psA[:, h * 128:(h + 1) * 128], lhsT=wTb[:, h, :], rhs=Sbi[0][:, h, :], start=True, stop=True)) for h in range(4)],
                     reads=[wTb, Sbi[1]], writes=[psA])
                yield
                S.op("dve", (lambda e, b=b: e.scalar_tensor_tensor(out=flat(vn32[:]), in0=psA[:], scalar=cp[:, CP_NBM + b:CP_NBM + b + 1],
                                                                    in1=flat(vn32[:]), op0=ALU.mult, op1=ALU.add)),
                     reads=[psA, cp, vn32], writes=[vn32])
                S.op("act", lambda e: e.activation(out=vnb[:], in_=vn32[:], func=AF.Copy), reads=[vn32], writes=[vnb])
                if full:
                    S.op("dve", (lambda e, b=b: e.tensor_scalar(out=cog, in0=eG, scalar1=cp[:, CP_BM + b:CP_BM + b + 1], scalar2=None, op0=ALU.mult)),
                         reads=[gt, cp], writes=[gt])
                    S.op("pe", [(lambda e, h=h, Sbi=Sbi: e.matmul(psB[:, h * 128:(h + 1) * 128], lhsT=qnT[par][:, h, tc], rhs=Sbi[0][:, h, :], start=True, stop=True)) for h in range(4)],
                         reads=[qnT[par], Sbi[1]], writes=[psB])
                    yield
                    psBv = psB[:].rearrange("p (a b) -> p a b", a=4)
                    if b == 0:
                        S.op("dve", lambda e: e.tensor_tensor(out=oacc[:], in0=psBv, in1=bc(cog, 4, 128), op=ALU.mult), reads=[psB, gt], writes=[oacc])
                    else:
                        S.op("dve", lambda e: e.tensor_tensor(out=tmpb[:], in0=psBv, in1=bc(cog, 4, 128), op=ALU.mult), reads=[psB, gt], writes=[tmpb])
                        S.op("pool", lambda e: e.tensor_tensor(out=oacc[:], in0=oacc[:], in1=tmpb[:], op=ALU.add), reads=[oacc, tmpb], writes=[oacc])
                S.op("pe", [(lambda e, h=h, b=b: e.matmul(psA[:, h * 128:(h + 1) * 128], lhsT=kgm[:, b, h, :], rhs=vnb[:, h, :], start=True, stop=True)) for h in range(4)],
                     reads=[kgm, vnb], writes=[psA])
                yield
                for h in range(4):
                    S.op("dve", (lambda e, h=h, b=b, Si=Si, So=So: e.scalar_tensor_tensor(out=So[0][:, h, :], in0=Si[0][:, h, :], scalar=dlb[:, 4 * b + h:4 * b + h + 1],
                                                                                      in1=psA[:, h * 128:(h + 1) * 128], op0=ALU.mult, op1=ALU.add)),
                         reads=[Si[1], gt, psA], writes=[So[1]])
                if Sbo is not None:
                    S.op("act", (lambda e, So=So, Sbo=Sbo: e.activation(out=Sbo[0], in_=So[0], func=AF.Copy)), reads=[So[1]], writes=[Sbo[1]])
            DUMP("vn", flat(vn32[:]), [vn32], tag)
            DUMP("Safter", flat(S_out(nb - 1)[0]), [S_out(nb - 1)[1]], tag)
            if full:
                S.op("pe", [(lambda e, h=h: e.matmul(psB[:, h * 128:(h + 1) * 128], lhsT=qkT[:, h, :], rhs=vnb[:, h, :], start=True, stop=True)) for h in range(4)],
                     reads=[qkT, vnb], writes=[psB])
                yield
                S.op("dve", lambda e: e.tensor_tensor(out=flat(oacc[:]), in0=flat(oacc[:]), in1=psB[:], op=ALU.add), reads=[oacc, psB], writes=[oacc])
                for h in range(4):
                    S.op("act", (lambda e, h=h: e.activation(out=tmpb[:, h, :], in_=oacc[:, h, :], func=AF.Square, accum_out=ssq[:, h:h + 1])),
                         reads=[oacc], writes=[tmpb, gt])
                S.op("act", lambda e: e.activation(out=ssq, in_=ssq, func=AF.Ln, scale=1.0 / 128, bias=epsb[:, 0:1]), reads=[gt, epsb], writes=[gt])
                S.op("act", lambda e: e.activation(out=ssq, in_=ssq, func=AF.Exp, scale=-0.5), reads=[gt], writes=[gt])
                DUMP("o", flat(oacc[:]), [oacc], tag)
                S.op("dve", lambda e: e.tensor_tensor(out=tmpb[:], in0=oacc[:], in1=bc(ssq, 4, 128), op=ALU.mult), reads=[oacc, gt], writes=[tmpb])
                S.op("pool", lambda e: e.tensor_tensor(out=tmpb[:], in0=tmpb[:], in1=gnw4[:], op=ALU.mult), reads=[tmpb, gnw4], writes=[tmpb])
                S.op("dve", lambda e: e.tensor_tensor(out=ocat[:, oc_slot, 0:512], in0=flat(tmpb[:]), in1=zs[par][:, zslot, :], op=ALU.mult),
                     reads=[tmpb, zs[par].r[zslot]], writes=[ocat.r[oc_slot]])

        def attn_tile(pieces, qcol, oc_slot, par):
            qc = slice(qcol, qcol + 128)
            npc = len(pieces)
            first_pv = [True]
            pctr2 = [0]
            for pi, pc in enumerate(pieces):
                if pc.get("prep"):
                    pc["prep"]()
                for hh in range(2):
                    fns = []
                    first = True
                    if pc.get("seq") is not None:
                        s = pc["seq"]
                        fns.append(lambda e, s=s: e.matmul(psS[:], lhsT=smk[:, 2560:2688], rhs=smk[:, s * 512:(s + 1) * 512], start=True, stop=False, skip_group_check=True))
                        first = False
                        if pc.get("newk"):
                            fns.append(lambda e, s=s: e.matmul(psS[:], lhsT=smk[:, 2048 + s * 128:2048 + (s + 1) * 128], rhs=smk[:, 2688:3200], start=False, stop=False, skip_group_check=True))
                    for j in range(4):
                        h = 4 * hh + j
                        fns.append(lambda e, h=h, j=j, pc=pc, first=first: e.matmul(psS[:, j * 128:(j + 1) * 128], lhsT=kbT[:, h // 2, pc["kslot"], :],
                                                                                   rhs=qbz[par][:, h // 2, h % 2, qc], start=first, stop=True, skip_group_check=True))
                    S.op("pe", fns, reads=[kbT.r[pc["kslot"]], qbz[par], smk], writes=[psS])
                    yield
                    S.op("dve", (lambda e, pc=pc, hh=hh: e.tensor_tensor(out=scb[:], in0=psS[:].rearrange("p (a b) -> p a b", a=4), in1=BT[:, pc["r"], 4 * hh:4 * hh + 4, :], op=ALU.add)),
                         reads=[psS, BT], writes=[scb])
                    psl = pctr2[0] % 2
                    pctr2[0] += 1
                    S.op("act", (lambda e, psl=psl: e.activation(out=PT[:, psl], in_=scb[:], func=AF.Exp)), reads=[scb], writes=[PT.r[psl]])
                    fns = []
                    for j in range(4):
                        h = 4 * hh + j
                        stt = first_pv[0]
                        first_pv[0] = False
                        fns.append(lambda e, j=j, h=h, pc=pc, psl=psl, stt=stt: e.matmul(psO[:, h * 64:(h + 1) * 64], lhsT=PT[:, psl, j, :], rhs=V1[:, pc["vslot"], h, :],
                                                                                        start=stt, stop=False, skip_group_check=True))
                        fns.append(lambda e, j=j, h=h, pc=pc, psl=psl, pi=pi: e.matmul(psM[:, 64 + pi * 8 + h:64 + pi * 8 + h + 1], lhsT=PT[:, psl, j, :], rhs=vcol[:, pc["vslot"]:pc["vslot"] + 1],
                                                                                      start=True, stop=True, skip_group_check=True))
                    S.op("pe", fns, reads=[PT.r[psl], V1.r[pc["vslot"]], vcol.r[pc["vslot"]]], writes=[psO, psM])
                    yield
            rden = gta[:, 0:8]
            ss8 = gta[:, 8:16]
            S.op("dve", lambda e: e.tensor_reduce(out=rden, in_=psM[:, 64:64 + npc * 8].rearrange("p (a b) -> p b a", b=8), axis=mybir.AxisListType.X, op=ALU.add),
                 reads=[psM], writes=[gta])
            S.op("dve", lambda e: e.tensor_scalar(out=rden, in0=rden, scalar1=1e-30, scalar2=None, op0=ALU.max), reads=[gta], writes=[gta])
            S.op("dve", lambda e: e.reciprocal(out=rden, in_=rden), reads=[gta], writes=[gta])
            S.op("dve", lambda e: e.tensor_tensor(out=ob32[:], in0=psO[:].rearrange("p (a b) -> p a b", a=8), in1=bc(rden, 8, 64), op=ALU.mult),
                 reads=[psO, gta], writes=[ob32])
            for h in range(8):
                S.op("act", (lambda e, h=h: e.activation(out=scb[:, 0, 0:64], in_=ob32[:, h, :], func=AF.Square, accum_out=ss8[:, h:h + 1])),
                     reads=[ob32], writes=[scb, gta])
            S.op("act", lambda e: e.activation(out=ss8, in_=ss8, func=AF.Ln, scale=1.0 / 64, bias=epsb[:, 0:1]), reads=[gta, epsb], writes=[gta])
            S.op("act", lambda e: e.activation(out=ss8, in_=ss8, func=AF.Exp, scale=-0.5), reads=[gta], writes=[gta])
            S.op("dve", lambda e: e.tensor_tensor(out=ob32[:], in0=ob32[:], in1=bc(ss8, 8, 64), op=ALU.mult), reads=[ob32, gta], writes=[ob32])
            S.op("pool", lambda e: e.tensor_tensor(out=ocat[:, oc_slot, 512:1024].rearrange("p (a b) -> p a b", a=8), in0=ob32[:], in1=anw8[:], op=ALU.mult),
                 reads=[ob32, anw8], writes=[ocat.r[oc_slot]])

        occtr = [0]
        kvctr = [0]

        def run_all(*gens, weights=None):
            pairs = [(g, (weights[i] if weights else 1)) for i, g in enumerate(gens) if g is not None]
            while pairs:
                for (g, w) in list(pairs):
                    for _ in range(w):
                        try:
                            next(g)
                        except StopIteration:
                            pairs.remove((g, w))
                            break

        def front_gen(tls, gq, gkv, full, sample, par):
            ntl = len(tls)
            NT = ntl * 128
            nseq, L = (4, 32) if sample else (1, NT)
            for ti, tl in enumerate(tls):
                src = xs_d[:, :] if sample else xp_d[tl * 128:(tl + 1) * 128, :]
                norm_transpose(src, uT[par], ti * 128)
                yield
            chunks = list(range(12)) if gq else list(range(4, 12))
            nch = len(chunks)
            ok = lambda c: (c >= 4 or full)
            for i in range(nch + 3):
                if i < nch:
                    c = chunks[i]
                    stA(c, nseq, L, NT, i % 2, i % 4, ok(c), par)
                if 0 <= i - 1 < nch and ok(chunks[i - 1]):
                    stB(chunks[i - 1], NT, (i - 1) % 4, (i - 1) % 2, par)
                if 0 <= i - 2 < nch and ok(chunks[i - 2]):
                    stC(chunks[i - 2], NT, (i - 2) % 4, (i - 2) % 2, par)
                if 0 <= i - 3 < nch and ok(chunks[i - 3]):
                    stD(chunks[i - 3], NT, (i - 3) % 4, (i - 3) % 2, par)
                yield
            if full:
                for c in range(4):
                    ps = next_psP()
                    proj_fm(w_in_bf, C_QB + c * 128, uT[par], NT, ps)
                    S.op("act", (lambda e, c=c, ps=ps: e.activation(out=qbz[par][0:64, c, 0, 0:NT], in_=ps[0:64, 0:NT], func=AF.Copy, scale=0.125)), reads=[ps], writes=[qbz[par]])
                    S.op("act", (lambda e, c=c, ps=ps: e.activation(out=qbz[par][64:128, c, 1, 0:NT], in_=ps[64:128, 0:NT], func=AF.Copy, scale=0.125)), reads=[ps], writes=[qbz[par]])
                    yield
            if gkv:
                for c in range(4):
                    ps = next_psP()
                    proj_fm(w_in_bf, C_KB + c * 128, uT[par], NT, ps)
                    for ti, tl in enumerate(tls):
                        slot = tl % 8
                        S.op("act", (lambda e, c=c, ps=ps, ti=ti, slot=slot: e.activation(out=kbT[:, c, slot, :], in_=ps[:, ti * 128:(ti + 1) * 128], func=AF.Copy)),
                             reads=[ps], writes=[kbT.r[slot]])
                    yield
            for ti, tl in enumerate(tls):
                tcol = ti * 128
                if gkv:
                    slot = tl % 8
                    ps = next_psP()
                    proj_tm(w_in_bf, C_VB, 512, uT[par], tcol, ps)
                    S.op("act", (lambda e, ps=ps, slot=slot: e.activation(out=V1[:, slot], in_=ps[:].rearrange("p (a b) -> p a b", a=8), func=AF.Copy)),
                         reads=[ps], writes=[V1.r[slot]])
                    if sample:
                        S.op("pool", (lambda e, slot=slot: e.memset(vcol[:, slot:slot + 1], 1.0)), writes=[vcol.r[slot]])
                    else:
                        S.op("pool", (lambda e, slot=slot, tl=tl: e.tensor_copy(out=vcol[:, slot:slot + 1], in_=kvalid[:, tl:tl + 1])),
                             reads=[kvalid], writes=[vcol.r[slot]])
                    yield
                    want_out = sample or (tl >= p_tiles - 4)
                    if want_out:
                        orow = 0 if sample else (tl - (p_tiles - 4)) * 128
                        kd, vd = (bk_s, bv_s) if sample else (bk_p, bv_p)
                        S.op("act", (lambda e, ps=ps: e.activation(out=kvout[:, 0, :], in_=ps[:], func=AF.Copy)), reads=[ps], writes=[kvout.r[0]])
                        S.dma("pool", (lambda e, vd=vd, orow=orow: e.dma_start(out=vd[orow:orow + 128, :], in_=kvout[:, 0, :])), reads=[kvout.r[0]])
                        ps2 = next_psP()
                        proj_tm(w_in_bf, C_KB, 512, uT[par], tcol, ps2)
                        S.op("act", (lambda e, ps2=ps2: e.activation(out=kvout[:, 0, :], in_=ps2[:], func=AF.Copy)), reads=[ps2], writes=[kvout.r[0]])
                        S.dma("pool", (lambda e, kd=kd, orow=orow: e.dma_start(out=kd[orow:orow + 128, :], in_=kvout[:, 0, :])), reads=[kvout.r[0]])
                        yield
                if full:
                    ps = next_psP()
                    proj_tm(w_in_bf, C_Z, 512, uT[par], tcol, ps)
                    zt = kvout[:, 0, :]
                    S.op("act", (lambda e, ps=ps: e.activation(out=zt, in_=ps[:], func=AF.Exp, scale=-1.0)), reads=[ps], writes=[kvout.r[0]])
                    S.op("act", (lambda e: e.activation(out=zt, in_=zt, func=AF.Ln, bias=epsb[:, 1:2])), reads=[kvout.r[0], epsb], writes=[kvout.r[0]])
                    S.op("act", (lambda e: e.activation(out=zt, in_=zt, func=AF.Exp, scale=-1.0)), reads=[kvout.r[0]], writes=[kvout.r[0]])
                    S.op("dve", (lambda e, ps=ps, ti=ti: e.tensor_tensor(out=zs[par][:, ti, :], in0=ps[:], in1=zt, op=ALU.mult)), reads=[ps, kvout.r[0]], writes=[zs[par].r[ti]])
                    yield
                proj_tm(w_in_bf, C_B, 8, uT[par], tcol, psM, 0)
                S.op("act", (lambda e, ti=ti: e.activation(out=graw[:, par, ti, :], in_=psM[:, 0:8], func=AF.Copy)), reads=[psM], writes=[graw.r[par * NTL1 + ti]])
                yield

        def work_gen(tls, full, sample, par, res):
            for ti, tl in enumerate(tls):
                tcol = ti * 128
                oc_slot = occtr[0] % 2
                if full:
                    occtr[0] += 1
                res["oc_slot"] = oc_slot
                if sample:
                    gens = [gdn_tile(tcol, 4, True,
                                     lambda b: (Ss[:, b], Ss.r[b]), lambda b: (Ssb[:, b], Ssb.r[b]),
                                     lambda b: (Ss[:, b], Ss.r[b]), lambda b: None, ti, oc_slot, par)]
                else:
                    gens = [gdn_tile(tcol, 2, full,
                                     lambda b: (Sp[:], Sp.r[0]), lambda b: (Spb[:], Spb.r[0]),
                                     lambda b: (Sp[:], Sp.r[0]), lambda b: (Spb[:], Spb.r[0]), ti, oc_slot, par, tag=tl)]
                    if full:
                        pieces = [dict(kslot=(tl - 4 + r) % 8, vslot=(tl - 4 + r) % 8, r=r) for r in range(5)]
                        gens.append(attn_tile(pieces, tcol, oc_slot, par))
                while gens:
                    for g in list(gens):
                        try:
                            next(g)
                        except StopIteration:
                            gens.remove(g)
                    yield
                if full and not sample:
                    scr_row = tl - (p_tiles - NSCR + 1)
                    if scr_row >= 0:
                        S.dma("pool", (lambda e, oc_slot=oc_slot, scr_row=scr_row: e.dma_start(out=ocat_scr[scr_row * 128:(scr_row + 1) * 128, :], in_=ocat[:, oc_slot, :])),
                              reads=[ocat.r[oc_slot]])
                    if "ocat" in dbg and tl >= first_own:
                        S.dma("pool", (lambda e, oc_slot=oc_slot, tl=tl: e.dma_start(out=dbg["ocat"][(tl - first_own) * 128:(tl - first_own + 1) * 128, :], in_=ocat[:, oc_slot, :])),
                              reads=[ocat.r[oc_slot]])

        first_full_tile = first_own - 2
        first_kv_tile = first_full_tile - 4
        tl = p1_from
        pending = None
        gi = 0
        while tl < p1_to:
            tls = list(range(tl, min(tl + NTL1, p1_to)))
            full = tls[0] >= first_full_tile
            gkv = tls[0] >= first_kv_tile
            gq = tls[0] >= first_full_tile - NTL1
            par = gi % 2
            run_all(front_gen(tls, gq, gkv, full, False, par), pending, weights=cfg.get('w_fp', (2, 1)))
            pending = work_gen(tls, full, False, par, {})
            tl += NTL1
            gi += 1
        run_all(pending)
        S.dma("pool", lambda e: e.dma_start(out=S_p[:, :], in_=flat(Sp[:])), reads=[Sp])
        S.dma("pool", lambda e: e.dma_start(out=qc_p[:, :].rearrange("p (a b) -> p a b", a=12), in_=qtail[:, :, 0, :]), reads=[qtail])

        if cfg.get("sample", True):
            S.barrier()
            load_small(cp, cps_d[:, :])
            load_BT(bts_d, None)
            for i, (a, b) in enumerate(((0, 1024), (1024, 2048), (2048, 3072), (3072, SMK_W))):
                S.dma("sp", (lambda e, i=i, a=a, b=b: e.dma_start(out=stage[:, i % 2, 0:b - a], in_=smk_d[:, a:b])), writes=[stage.r[i % 2]])
                S.op("dve", (lambda e, i=i, a=a, b=b: e.tensor_copy(out=smk[:, a:b], in_=stage[:, i % 2, 0:b - a])), reads=[stage.r[i % 2]], writes=[smk])
            load_small(qtail, qc0_d[:, :])
            for s in range(4):
                S.dma("sp", (lambda e, s=s: e.dma_start(out=Ss[:, s], in_=s0_d[s].rearrange("h k v -> k h v"))), writes=[Ss.r[s]])
                S.op("act", (lambda e, s=s: e.activation(out=Ssb[:, s], in_=Ss[:, s], func=AF.Copy)), reads=[Ss.r[s]], writes=[Ssb.r[s]])
            run_all(front_gen([0], True, True, True, True, 0))
            sres = {}
            run_all(work_gen([0], True, True, 0, sres))
            ocs = sres["oc_slot"]
            S.dma("pool", lambda e: e.dma_start(out=S_s[:, :], in_=flat(Ss[:])), reads=[Ss])
            S.dma("pool", lambda e: e.dma_start(out=qc_s[:, :], in_=flat(qtail[:])), reads=[qtail])

            def mk_prep(s, r, slot, sl):
                def prep():
                    S.dma("sp", (lambda e: e.dma_start(out=stage[:, sl, 0:512], in_=ck_d[s, r * 128:(r + 1) * 128, :])), writes=[stage.r[sl]])
                    S.dma("sp", (lambda e: e.dma_start(out=stage[:, sl, 512:1024], in_=cv_d[s, r * 128:(r + 1) * 128, :])), writes=[stage.r[sl]])
                    S.op("dve", (lambda e: e.tensor_copy(out=xn[:, 0:512], in_=stage[:, sl, 0:512])), reads=[stage.r[sl]], writes=[xn])
                    S.op("pe", [(lambda e, c=c: e.transpose(psT[:, c * 128:(c + 1) * 128], xn[:, c * 128:(c + 1) * 128], identb[:])) for c in range(4)],
                         reads=[xn, identb], writes=[psT])
                    S.op("act", (lambda e: e.activation(out=kbT[:, :, slot, :], in_=psT[:, 0:512].rearrange("p (a b) -> p a b", a=4), func=AF.Copy)),
                         reads=[psT], writes=[kbT.r[slot]])
                    S.op("act", (lambda e: e.activation(out=V1[:, slot], in_=stage[:, sl, 512:1024].rearrange("p (a b) -> p a b", a=8), func=AF.Copy)),
                         reads=[stage.r[sl]], writes=[V1.r[slot]])
                    S.op("pool", (lambda e: e.memset(vcol[:, slot:slot + 1], 1.0)), writes=[vcol.r[slot]])
                return prep

            pieces = []
            for s in range(4):
                for r in range(4):
                    i = s * 4 + r
                    slot = i % 7 + 1
                    pieces.append(dict(kslot=slot, vslot=slot, r=r, seq=s, prep=mk_prep(s, r, slot, i % 2)))
                pieces.append(dict(kslot=0, vslot=0, r=4, seq=s, newk=True))
            run_all(attn_tile(pieces, 0, ocs, 0))
            S.dma("pool", lambda e: e.dma_start(out=ocat_scr[(NSCR - 1) * 128:NSCR * 128, :], in_=ocat[:, ocs, :]), reads=[ocat.r[ocs]])
            if "ocat_s" in dbg:
                S.dma("pool", lambda e: e.dma_start(out=dbg["ocat_s"][:, :], in_=ocat[:, ocs, :]), reads=[ocat.r[ocs]])

    if do_p2:
        S.barrier()
        aptr[0] = shared_end
        wgu_bf = SB("wgu_bf", [128, 8, 2 * D_FF], BF16)
        wd_bf = SB("wd_bf", [128, NFF, 1024], BF16)
        wout_bf = SB("wout_bf", [128, 8, 1024], BF16)
        nfpre = SB("nfpre", [128, 8])
        nmpost = SB("nmpost", [128, 1024])
        nfpost = SB("nfpost", [128, 1024])
        fw = SB("fw", [128, NFF, 3])
        fb = SB("fb", [128, NFF])
        gtail = SB("gtail", [128, NFF, 4, 2])
        oc2 = SB("oc2", [128, 1, 1024], BF16, nslots=1)
        upb = SB("upb", [128, 4, 128], F32, nslots=4)
        oT = SB("oT", [128, 8, 128], BF16)
        x1 = SB("x1", [128, 1024])
        u2T = SB("u2T", [128, 8, 128], BF16)
        gxp = SB("gxp", [128, 4, 144], F32, nslots=4)
        cv2 = SB("cv2", [128, 4, 128], F32, nslots=4)
        t2 = SB("t2", [128, 4, 128], F32, nslots=4)
        sg2 = SB("sg2", [128, 4, 128], F32, nslots=4)
        junk2 = SB("junk2", [128, 512], BF16)
        gxt_r = [Res("gxt%d" % i) for i in range(4)]
        hT = SB("hT", [128, NFF, 128], BF16)
        yb = SB("yb", [128, 1, 1024], F32, nslots=1)
        st2 = SB("st2", [128, 8])
        print("phase2 arena used", aptr[0], "of", ARENA_F32)

        load_small(nfpre, nfpre_d[:, :])
        load_small(nmpost, nmpost_d[:, :])
        load_small(nfpost, nfpost_d[:, :])
        load_small(fw, fw_d[:, :])
        load_small(fb, fb_d[:, :])
        S.op("pool", lambda e: e.memset(flat(gtail[:]), 0.0), writes=[gtail])
        if not do_p1:
            S.dma("sp", lambda e: e.dma_start(out=stage[:, 0, 0:128], in_=cpp_d[:, CP_IDENT:CP_IDENT + 128]), writes=[stage.r[0]])
            S.op("dve", lambda e: e.tensor_copy(out=identb[:], in_=stage[:, 0, 0:128]), reads=[stage.r[0]], writes=[identb])
        load_cast_weight(lambda k, c0, c1: wout_bf[:, k, c0:c1], lambda k, c0, c1: w_out_d[k * 128:(k + 1) * 128, c0:c1], 1024, 8, None, wout_bf)
        sfn2 = lambda k: nfpre[:, k:k + 1]
        sfn2.buf = nfpre
        wgu_pieces = [(0, 1024), (2816, 3840), (1024, 2048), (3840, 4864), (2048, 2816), (4864, 5632)]
        wgu_res = [Res("wgu%d" % i) for i in range(6)]

        def wgu_r(col):
            for (c0, c1), r in zip(wgu_pieces, wgu_res):
                if c0 <= col < c1:
                    return r
            raise AssertionError(col)
        load_cast_weight(lambda k, c0, c1: wgu_bf[:, k, c0:c1], lambda k, c0, c1: w_gu_d[k * 128:(k + 1) * 128, c0:c1], 2 * D_FF, 8, sfn2, wgu_bf,
                         pieces=wgu_pieces, piece_res=wgu_res)
        load_cast_weight(lambda k, c0, c1: wd_bf[:, k, c0:c1], lambda k, c0, c1: w_d_d[k * 128:(k + 1) * 128, c0:c1], 1024, NFF, None, wd_bf)

        o2ctr = [0]
        yctr = [0]
        fctr = [0]

        def rms_finish(ps_list, resid_ap, resid_res, gain, dst_ap, dst_res):
            for i, ps in enumerate(ps_list):
                S.op("act", (lambda e, i=i, ps=ps: e.activation(out=junk2[:], in_=ps[:], func=AF.Square, accum_out=st2[:, i:i + 1])),
                     reads=[ps], writes=[st2, junk2])
            S.op("dve", lambda e: e.tensor_tensor(out=st2[:, 2:3], in0=st2[:, 0:1], in1=st2[:, 1:2], op=ALU.add), reads=[st2], writes=[st2])
            S.op("act", lambda e: e.activation(out=st2[:, 3:4], in_=st2[:, 2:3], func=AF.Ln, scale=1.0 / 1024, bias=epsb[:, 0:1]), reads=[st2, epsb], writes=[st2])
            S.op("act", lambda e: e.activation(out=st2[:, 4:5], in_=st2[:, 3:4], func=AF.Exp, scale=-0.5), reads=[st2], writes=[st2])
            for i, ps in enumerate(ps_list):
                S.op("dve", (lambda e, i=i, ps=ps: e.scalar_tensor_tensor(out=dst_ap[:, i * 512:(i + 1) * 512], in0=ps[:], scalar=st2[:, 4:5], in1=gain[:, i * 512:(i + 1) * 512],
                                                                         op0=ALU.mult, op1=ALU.mult)),
                     reads=[ps, st2, gain], writes=dst_res)
                S.op("pool", (lambda e, i=i: e.tensor_tensor(out=dst_ap[:, i * 512:(i + 1) * 512], in0=dst_ap[:, i * 512:(i + 1) * 512], in1=resid_ap[:, i * 512:(i + 1) * 512], op=ALU.add)),
                     reads=dst_res + resid_res, writes=dst_res)

        yjunk_b = xn
        yjunk = xn[:]

        ring = [psP[0], psP[1], psS, psO, psM]
        rctr = [0]

        def next_ring():
            bnk = ring[rctr[0] % len(ring)]
            rctr[0] += 1
            return bnk

        u2T_b = Buf(stage[:, 1, 0:512].bitcast(BF16).rearrange("p (a b) -> p a b", a=8), "u2T_b")
        u2T_b.r = [stage.r[1]]
        x1s = [(x1[:], x1.r[0]), (stage[:, 0, :], stage.r[0])]
        u2Ts = [u2T, u2T_b]

        def rms_finish_gen(ps_list, resid_ap, resid_res, gain, dst_ap, dst_res):
            for i, ps in enumerate(ps_list):
                S.op("act", (lambda e, i=i, ps=ps: e.activation(out=junk2[:], in_=ps[:], func=AF.Square, accum_out=st2[:, i:i + 1])),
                     reads=[ps], writes=[st2, junk2])
            yield
            S.op("dve", lambda e: e.tensor_tensor(out=st2[:, 2:3], in0=st2[:, 0:1], in1=st2[:, 1:2], op=ALU.add), reads=[st2], writes=[st2])
            yield
            S.op("act", lambda e: e.activation(out=st2[:, 3:4], in_=st2[:, 2:3], func=AF.Ln, scale=1.0 / 1024, bias=epsb[:, 0:1]), reads=[st2, epsb], writes=[st2])
            S.op("act", lambda e: e.activation(out=st2[:, 4:5], in_=st2[:, 3:4], func=AF.Exp, scale=-0.5), reads=[st2], writes=[st2])
            yield
            for i, ps in enumerate(ps_list):
                S.op("dve", (lambda e, i=i, ps=ps: e.scalar_tensor_tensor(out=dst_ap[:, i * 512:(i + 1) * 512], in0=ps[:], scalar=st2[:, 4:5], in1=gain[:, i * 512:(i + 1) * 512],
                                                                         op0=ALU.mult, op1=ALU.mult)),
                     reads=[ps, st2, gain], writes=dst_res)
            yield
            for i, ps in enumerate(ps_list):
                S.op("pool", (lambda e, i=i: e.tensor_tensor(out=dst_ap[:, i * 512:(i + 1) * 512], in0=dst_ap[:, i * 512:(i + 1) * 512], in1=resid_ap[:, i * 512:(i + 1) * 512], op=ALU.add)),
                     reads=dst_res + resid_res, writes=dst_res)
            yield

        def p2_front(scr_row, x_src, slot):
            osl = 0
            S.dma("act", lambda e: e.dma_start(out=oc2[:, osl, :], in_=ocat_scr[scr_row * 128:(scr_row + 1) * 128, :]), writes=[oc2.r[osl]])
            xsl = xctr[0] % 2
            xctr[0] += 1
            S.dma("act", lambda e: e.dma_start(out=xin[:, xsl, :], in_=x_src), writes=[xin.r[xsl]])
            yield
            S.op("pe", [(lambda e, k=k: e.transpose(psT[:, k * 128:(k + 1) * 128], oc2[:, osl, k * 128:(k + 1) * 128], identb[:])) for k in range(8)],
                 reads=[oc2.r[osl], identb], writes=[psT])
            yield
            S.op("act", lambda e: e.activation(out=oT[:], in_=psT[:].rearrange("p (a b) -> p a b", a=8), func=AF.Copy), reads=[psT], writes=[oT])
            yield
            proj_tm(wout_bf, 0, 512, oT, 0, psA)
            proj_tm(wout_bf, 512, 512, oT, 0, psB)
            yield
            x1a, x1r = x1s[slot]
            yield from rms_finish_gen([psA, psB], xin[:, xsl, :], [xin.r[xsl]], nmpost, x1a, [x1r])
            S.op("act", lambda e: e.activation(out=xn[:], in_=x1a, func=AF.Square, accum_out=st[:, 0:1]), reads=[x1r], writes=[st, xn])
            yield
            S.op("act", lambda e: e.activation(out=st[:, 1:2], in_=st[:, 0:1], func=AF.Ln, scale=1.0 / 1024, bias=epsb[:, 0:1]), reads=[st, epsb], writes=[st])
            S.op("act", lambda e: e.activation(out=st[:, 2:3], in_=st[:, 1:2], func=AF.Exp, scale=-0.5), reads=[st], writes=[st])
            yield
            S.op("dve", lambda e: e.tensor_scalar(out=xn[:], in0=x1a, scalar1=st[:, 2:3], scalar2=None, op0=ALU.mult), reads=[x1r, st], writes=[xn])
            yield
            S.op("pe", [(lambda e, k=k: e.transpose(psT[:, k * 128:(k + 1) * 128], xn[:, k * 128:(k + 1) * 128], identb[:])) for k in range(8)],
                 reads=[xn, identb], writes=[psT])
            yield
            uu = u2Ts[slot]
            S.op("act", lambda e: e.activation(out=uu[:], in_=psT[:].rearrange("p (a b) -> p a b", a=8), func=AF.Copy), reads=[psT], writes=[uu])
            yield

        def p2_body(slot, y_dst, nseq, L):
            NT = 128
            uu = u2Ts[slot]
            x1a, x1r = x1s[slot]
            st_ = {}

            def stageA(c):
                psg = next_ring()
                proj_fm(wgu_bf, c * 128, uu, NT, psg, wres=wgu_r(c * 128))
                fsl = c % 4
                gx = gxp[:, fsl, 0:nseq * (L + 2)].rearrange("p (s t) -> p s t", s=nseq)
                cvv = cv2[:, fsl, :].rearrange("p (s t) -> p s t", s=nseq)
                S.op("pool", (lambda e: e.tensor_copy(out=gx[:, :, 0:2], in_=gtail[:, c, 0:nseq, :])), reads=[gtail], writes=[gxt_r[fsl]])
                S.op("act", (lambda e: e.activation(out=gx[:, :, 2:2 + L], in_=psg[:, 0:NT].rearrange("p (s t) -> p s t", s=nseq), func=AF.Copy)),
                     reads=[psg], writes=[gxp.r[fsl]])
                S.op("pool", (lambda e: e.tensor_copy(out=gtail[:, c, 0:nseq, :], in_=gx[:, :, L:L + 2])), reads=[gxp.r[fsl]], writes=[gtail])
                psu = next_ring()
                proj_fm(wgu_bf, D_FF + c * 128, uu, NT, psu, wres=wgu_r(D_FF + c * 128))
                S.op("act", (lambda e: e.activation(out=upb[:, fsl, :], in_=psu[:, 0:NT], func=AF.Copy)), reads=[psu], writes=[upb.r[fsl]])
                S.op("dve", (lambda e: e.tensor_scalar(out=cvv, in0=gx[:, :, 0:L], scalar1=fw[:, c, 0:1], scalar2=fb[:, c:c + 1], op0=ALU.mult, op1=ALU.add)),
                     reads=[gxp.r[fsl], gxt_r[fsl], fw, fb], writes=[cv2.r[fsl]])
                for i in (1, 2):
                    S.op("dve", (lambda e, i=i: e.scalar_tensor_tensor(out=cvv, in0=gx[:, :, i:i + L], scalar=fw[:, c, i:i + 1], in1=cvv, op0=ALU.mult, op1=ALU.add)),
                         reads=[gxp.r[fsl], gxt_r[fsl], fw, cv2.r[fsl]], writes=[cv2.r[fsl]])

            def stageB(c):
                fsl = c % 4
                S.op("act", (lambda e: e.activation(out=t2[:, fsl, :], in_=cv2[:, fsl, :], func=AF.Square, scale=0.044715 ** 0.5)), reads=[cv2.r[fsl]], writes=[t2.r[fsl]])
                S.op("dve", (lambda e: e.scalar_tensor_tensor(out=t2[:, fsl, :], in0=t2[:, fsl, :], scalar=1.0, in1=cv2[:, fsl, :], op0=ALU.add, op1=ALU.mult)),
                     reads=[t2.r[fsl], cv2.r[fsl]], writes=[t2.r[fsl]])

            def stageC(c):
                fsl = c % 4
                S.op("act", (lambda e: e.activation(out=sg2[:, fsl, :], in_=t2[:, fsl, :], func=AF.Sigmoid, scale=1.5957691216057308)), reads=[t2.r[fsl]], writes=[sg2.r[fsl]])
                S.op("pool", (lambda e: e.tensor_tensor(out=sg2[:, fsl, :], in0=sg2[:, fsl, :], in1=cv2[:, fsl, :], op=ALU.mult)),
                     reads=[sg2.r[fsl], cv2.r[fsl]], writes=[sg2.r[fsl]])
                S.op("dve", (lambda e: e.tensor_tensor(out=hT[:, c, :], in0=upb[:, fsl, :], in1=sg2[:, fsl, :], op=ALU.mult)),
                     reads=[upb.r[fsl], sg2.r[fsl]], writes=[hT])

            for i in range(NFF + 2):
                if i < NFF:
                    stageA(i)
                if 0 <= i - 1 < NFF:
                    stageB(i - 1)
                if 0 <= i - 2 < NFF:
                    stageC(i - 2)
                yield

        def p2_finish(slot, y_dst):
            x1a, x1r = x1s[slot]
            S.op("pe", [(lambda e, c=c: e.matmul(psA[:], lhsT=hT[:, c, :], rhs=wd_bf[:, c, 0:512], start=(c == 0), stop=(c == NFF - 1))) for c in range(NFF)],
                 reads=[hT, wd_bf], writes=[psA])
            S.op("pe", [(lambda e, c=c: e.matmul(psB[:], lhsT=hT[:, c, :], rhs=wd_bf[:, c, 512:1024], start=(c == 0), stop=(c == NFF - 1))) for c in range(NFF)],
                 reads=[hT, wd_bf], writes=[psB])
            yield
            yield from rms_finish_gen([psA, psB], x1a, [x1r], nfpost, yb[:, 0, :], [yb.r[0]])
            if y_dst is not None:
                S.dma("pool", lambda e: e.dma_start(out=y_dst, in_=yb[:, 0, :]), reads=[yb.r[0]])

        p2_from = cfg.get("p2_from", first_own - 1)
        p2_to = cfg.get("p2_to", p_tiles)
        jobs = []
        for tl in range(p2_from, p2_to):
            own = tl - first_own
            jobs.append(dict(scr=tl - (p_tiles - NSCR + 1), x=xp_d[tl * 128:(tl + 1) * 128, :],
                             y=(y_p[own * 128:(own + 1) * 128, :] if own >= 0 else None), nseq=1, L=128, sample=False))
        if cfg.get("sample", True):
            jobs.append(dict(scr=NSCR - 1, x=xs_d[:, :], y=y_s[:, :], nseq=4, L=32, sample=True))
        def step(g):
            if g is None:
                return None
            try:
                next(g)
                return g
            except StopIteration:
                return None

        def drain(g):
            while g is not None:
                g = step(g)

        PRE = 2
        drain(p2_front(jobs[0]["scr"], jobs[0]["x"], 0))
        body = p2_body(0, jobs[0]["y"], jobs[0]["nseq"], jobs[0]["L"])
        nsteps_done = 0
        for ji, jb in enumerate(jobs):
            slot = ji % 2
            nxt = jobs[ji + 1] if ji + 1 < len(jobs) else None
            fgen = p2_front(nxt["scr"], nxt["x"], (ji + 1) % 2) if nxt is not None else None
            k = nsteps_done
            while body is not None:
                body = step(body)
                k += 1
                if k >= 4:
                    fgen = step(fgen)
            drain(fgen)
            fin = p2_finish(slot, jb["y"])
            nbody = None
            nsteps_done = 0
            if nxt is not None:
                if nxt["sample"]:
                    S.dma("pool", lambda e: e.dma_start(out=fc_p[:, :].rearrange("p (a b) -> p a b", a=NFF), in_=gtail[:, :, 0, :]), reads=[gtail])
                    load_small(gtail, fc0_d[:, :])
                nbody = p2_body((ji + 1) % 2, nxt["y"], nxt["nseq"], nxt["L"])
                for _ in range(PRE):
                    nbody = step(nbody)
                    nsteps_done += 1
            while fin is not None:
                fin = step(fin)
                if nbody is not None and nsteps_done < 8:
                    nbody = step(nbody)
                    nsteps_done += 1
            body = nbody
        if jobs[-1]["sample"]:
            S.dma("pool", lambda e: e.dma_start(out=fc_s[:, :], in_=flat(gtail[:])), reads=[gtail])
        else:
            S.dma("pool", lambda e: e.dma_start(out=fc_p[:, :].rearrange("p (a b) -> p a b", a=NFF), in_=gtail[:, :, 0, :]), reads=[gtail])

    S.finish()
    with nc.Block() as block:
        @block.tensor
        def _(e):
            S.replay("pe", e)

        @block.vector
        def _(e):
            S.replay("dve", e)

        @block.scalar
        def _(e):
            S.replay("act", e)

        @block.gpsimd
        def _(e):
            S.replay("pool", e)

        @block.sync
        def _(e):
            S.replay("sp", e)
    es.close()
    print("instructions:", S.ninstr, {k: len(v) for k, v in S.ops.items()}, "sems", len(S.sems))
    return nc


def _cpack(bs):
    m = np.arange(128)
    same = (m[:, None] // bs) == (m[None, :] // bs)
    tri = same & (m[:, None] <= m[None, :])
    strict = same & (m[:, None] > m[None, :])
    incl = same & (m[:, None] >= m[None, :])
    ident = np.eye(128, dtype=bool)
    nb = 128 // bs
    bm = np.zeros((128, 4), np.float32)
    cm = np.zeros((128, 4, 128), np.float32)
    for b in range(nb):
        bm[b * bs:(b + 1) * bs, b] = 1.0
        cm[:, b, :] = bm[:, b:b + 1]
    parts = [tri, same, np.tile(strict, (1, 4)), np.tile(incl, (1, 4)), np.tile(ident, (1, 4)), bm, -bm, cm.reshape(128, 512)]
    out = np.concatenate([np.asarray(p, np.float32) for p in parts], axis=1)
    assert out.shape == (128, CP_W)
    return np.ascontiguousarray(out)


def _bias_tables(rel_bias):
    tab = np.asarray(rel_bias, np.float32)
    kj = np.arange(128)[:, None, None]
    r = np.arange(5)[None, :, None]
    qi = np.arange(128)[None, None, :]
    rel_p = (4 - r) * 128 + qi - kj
    idx_p = np.clip(rel_p, -128, 128) + 128
    btp = tab[:, idx_p].transpose(1, 2, 0, 3)
    mask = ((r == 4) & (kj >= 64) & (qi < 64)) | ((r == 0) & (kj < 64) & (qi >= 64))
    btm = np.where(mask, np.float32(NEG), np.float32(0.0)).astype(np.float32)
    btm = np.broadcast_to(btm[:, :, None, :], (128, 5, 8, 128))
    rel_s = np.where(r < 4, (4 - r) * 128 + (qi % 32) - kj, (qi % 32) - (kj % 32))
    idx_s = np.clip(rel_s, -128, 128) + 128
    bts = tab[:, idx_s].transpose(1, 2, 0, 3)
    f = lambda a: np.ascontiguousarray(np.asarray(a, np.float32).reshape(128, 5 * 8 * 128))
    return f(btp), f(btm), f(bts)


def _smk():
    out = np.zeros((128, SMK_W), np.float32)
    q = np.arange(128)
    for s in range(4):
        cm = np.where(q // 32 == s, 0.0, NEG).astype(np.float32)
        out[0, s * 512:(s + 1) * 512] = np.tile(cm, 4)
        out[0, 2048 + s * 128:2048 + (s + 1) * 128] = cm
    out[0, 2560:2688] = 1.0
    out[0, 2688:3200] = 1.0
    return out


def make_in_maps(inp):
    f32 = lambda a: np.ascontiguousarray(np.asarray(a, np.float32))
    xpr, xsm = f32(inp["x_prompt"]), f32(inp["x_sample"])
    ckf = f32(inp["cache_band_k"])[0].reshape(32, 512, 512)
    cvf = f32(inp["cache_band_v"])[0].reshape(32, 512, 512)
    sdl = f32(inp["state_delta"])[0]
    sqc = f32(inp["state_qkv_conv"])[0]
    sfc = f32(inp["state_ffn_conv"])[0]
    rep = lambda v, n: np.ascontiguousarray(np.broadcast_to(f32(v).reshape(1, -1), (128, n)))
    pk = lambda v: np.ascontiguousarray(f32(v).reshape(-1, 128).T)
    btp, btm, bts = _bias_tables(inp["rel_bias"][0])
    common = dict(
        w_in=f32(inp["w_in"][0]), w_out=f32(inp["w_out"][0]), w_gu=f32(inp["w_gate_up"][0]), w_d=f32(inp["w_down"][0]),
        nmpre=pk(inp["norm_mix_pre"][0]), nfpre=pk(inp["norm_ffn_pre"][0]),
        nmpost=rep(inp["norm_mix_post"][0], 1024), nfpost=rep(inp["norm_ffn_post"][0], 1024),
        cw=np.ascontiguousarray(f32(inp["qkv_conv_w"][0]).reshape(4, 12, 128).transpose(2, 1, 0).reshape(128, 48)),
        fw=np.ascontiguousarray(f32(inp["ffn_conv_w"][0]).reshape(3, NFF, 128).transpose(2, 1, 0).reshape(128, NFF * 3)),
        fb=pk(inp["ffn_conv_b"][0]),
        alog=rep(inp["a_log"][0], 4), dtb=rep(inp["dt_bias"][0], 4),
        gnw=rep(np.tile(f32(inp["gdn_norm_w"][0]), 4), 512), anw=rep(np.tile(f32(inp["attn_norm_w"][0]), 8), 512),
        btp=btp, btm=btm, bts=bts, cpp=_cpack(64), cps=_cpack(32), smk=_smk(),
    )
    maps = []
    for c in range(8):
        s, half = c // 2, c % 2
        T0 = half * 2048
        xw = np.zeros((4096, 1024), np.float32)
        if half == 0:
            xw[2048:] = xpr[s, 0:2048]
        else:
            xw[:] = xpr[s]
        pos = T0 - 2048 + np.arange(4096)
        kval = (pos >= 0).astype(np.float32).reshape(32, 128).T
        sq = slice(4 * c, 4 * c + 4)
        m = dict(common)
        m.update(
            xp=xw, xs=np.ascontiguousarray(xsm[sq].reshape(128, 1024)), kvalid=np.ascontiguousarray(kval),
            ck=np.ascontiguousarray(ckf[sq]), cv=np.ascontiguousarray(cvf[sq]), s0=np.ascontiguousarray(sdl[sq]),
            qc0=np.ascontiguousarray(sqc[sq].reshape(4, 3, 12, 128).transpose(3, 2, 0, 1).reshape(128, 144)),
            fc0=np.ascontiguousarray(sfc[sq].reshape(4, 2, NFF, 128).transpose(3, 2, 0, 1).reshape(128, NFF * 8)),
        )
        maps.append(m)
    return maps


def assemble(results):
    y_prompt = np.zeros((4, 4096, 1024), np.float32)
    y_sample = np.zeros((32, 32, 1024), np.float32)
    bkp = np.zeros((1, 4, 512, 8, 64), np.float32)
    bvp = np.zeros((1, 4, 512, 8, 64), np.float32)
    dlp = np.zeros((1, 4, 4, 128, 128), np.float32)
    qcp = np.zeros((1, 4, 3, 1536), np.float32)
    fcp = np.zeros((1, 4, 2, D_FF), np.float32)
    bks = np.zeros((1, 32, 32, 8, 64), np.float32)
    bvs = np.zeros((1, 32, 32, 8, 64), np.float32)
    dls = np.zeros((1, 32, 4, 128, 128), np.float32)
    qcs = np.zeros((1, 32, 3, 1536), np.float32)
    fcs = np.zeros((1, 32, 2, D_FF), np.float32)
    for c, r in enumerate(results):
        s, half = c // 2, c % 2
        y_prompt[s, half * 2048:(half + 1) * 2048] = r["y_p"]
        if half == 1:
            bkp[0, s] = r["bk_p"].reshape(512, 8, 64)
            bvp[0, s] = r["bv_p"].reshape(512, 8, 64)
            dlp[0, s] = r["S_p"].reshape(128, 4, 128).transpose(1, 0, 2)
            qcp[0, s] = r["qc_p"].reshape(128, 12, 3).transpose(2, 1, 0).reshape(3, 1536)
            fcp[0, s] = r["fc_p"].reshape(128, NFF, 2).transpose(2, 1, 0).reshape(2, D_FF)
        sq = slice(4 * c, 4 * c + 4)
        y_sample[sq] = r["y_s"].reshape(4, 32, 1024)
        bks[0, sq] = r["bk_s"].reshape(4, 32, 8, 64)
        bvs[0, sq] = r["bv_s"].reshape(4, 32, 8, 64)
        dls[0, sq] = r["S_s"].reshape(128, 4, 4, 128).transpose(1, 2, 0, 3)
        qcs[0, sq] = r["qc_s"].reshape(128, 12, 4, 3).transpose(2, 3, 1, 0).reshape(4, 3, 1536)
        fcs[0, sq] = r["fc_s"].reshape(128, NFF, 4, 2).transpose(2, 3, 1, 0).reshape(4, 2, D_FF)
    return (y_prompt, y_sample, bkp, bvp, dlp, qcp, fcp, bks, bvs, dls, qcs, fcs)


def kernel(**inputs):
    nc = build_program()
    maps = make_in_maps(inputs)
    res = run_bass_kernel_spmd(nc, maps, core_ids=list(range(8)))
    return assemble(res.results)
```

```python
import os
import numpy as np
from contextlib import ExitStack
import ml_dtypes
import concourse.bass as bass
import concourse.mybir as mybir
from concourse.bass_utils import run_bass_kernel_spmd

F32 = mybir.dt.float32
BF16 = mybir.dt.bfloat16
F32R = mybir.dt.float32r
AF = mybir.ActivationFunctionType
ALU = mybir.AluOpType

D_MODEL = 1024
SEQ = 4096
GDN_H = 4
ATT_H = 8
D_FF = 2816
NFF = 22
IN_COLS = 3592
EPS = 1e-6
NEG = -30000.0
C_Q, C_K, C_V, C_Z, C_B, C_A, C_QB, C_KB, C_VB = 0, 512, 1024, 1536, 2048, 2052, 2056, 2568, 3080
CP_TRI, CP_SAME, CP_STRICT, CP_INCL, CP_IDENT, CP_BM, CP_NBM, CP_CM = 0, 128, 256, 768, 1280, 1792, 1796, 1800
CP_W = 1800 + 512


class Res:
    __slots__ = ("name", "lw", "rd")

    def __init__(self, name):
        self.name = name
        self.lw = None
        self.rd = {}


class Buf:
    def __init__(self, h, name, nslots=1):
        self.h = h
        self.r = [Res(f"{name}.{i}") for i in range(nslots)]

    def __getitem__(self, idx):
        return self.h[idx]


def _res(x):
    out = []
    for a in x:
        if isinstance(a, Buf):
            out.extend(a.r)
        else:
            out.append(a)
    return out


class Sched:
    CE = ("pe", "dve", "act", "pool")
    ROT = 20000

    def __init__(self, nc, es):
        self.nc, self.es = nc, es
        self.ops = {e: [] for e in ("pe", "dve", "act", "pool", "sp")}
        self.sems = []
        self.cur = {}
        self.cnt = {}
        for e in self.CE:
            self._newsem(e)
        self.waited = {e: {} for e in self.ops}
        self.dpool = {q: [] for q in ("sp", "pool", "act")}
        self.dnext = {q: 0 for q in self.dpool}
        for q, n in (("sp", 12), ("pool", 12), ("act", 4)):
            for i in range(n):
                s = es.enter_context(nc.semaphore(f"d_{q}{i}"))
                self.sems.append(s)
                self.dpool[q].append([len(self.sems) - 1, 0])
        self.ninstr = 0

    def _newsem(self, e):
        s = self.es.enter_context(self.nc.semaphore(f"s_{e}{len(self.sems)}"))
        self.sems.append(s)
        self.cur[e] = len(self.sems) - 1
        self.cnt[e] = 0

    def _deps(self, eng, reads, writes, strict=True):
        deps = {}

        def add(t):
            if t is None:
                return
            if deps.get(t[0], -1) < t[1]:
                deps[t[0]] = t[1]
        for r in reads:
            add(r.lw)
        for w in writes:
            add(w.lw)
            for k, v in w.rd.items():
                add((k, v))
        waits = []
        wd = self.waited[eng]
        for k, v in deps.items():
            if not strict and eng in self.cur and k == self.cur[eng]:
                continue
            if wd.get(k, -1) >= v:
                continue
            wd[k] = v
            waits.append((k, v))
        return waits

    def _mark(self, ticket, reads, writes):
        for w in writes:
            w.lw = ticket
            w.rd = {}
        for r in reads:
            if r.rd.get(ticket[0], -1) < ticket[1]:
                r.rd[ticket[0]] = ticket[1]

    def op(self, eng, fns, reads=(), writes=()):
        reads, writes = _res(reads), _res(writes)
        if not isinstance(fns, (list, tuple)):
            fns = [fns]
        waits = self._deps(eng, reads, writes, strict=(eng != "pe"))
        if self.cnt[eng] >= self.ROT:
            self._newsem(eng)
        self.cnt[eng] += 1
        ticket = (self.cur[eng], self.cnt[eng])
        n = len(fns)
        for i, f in enumerate(fns):
            self.ops[eng].append((waits if i == 0 else (), f, ticket[0] if i == n - 1 else None, 1))
        self.ninstr += n
        self._mark(ticket, reads, writes)

    def dma(self, q, fn, reads=(), writes=()):
        reads, writes = _res(reads), _res(writes)
        waits = self._deps(q, reads, writes)
        pool = self.dpool[q]
        slot = pool[self.dnext[q] % len(pool)]
        self.dnext[q] += 1
        wd = self.waited[q]
        if slot[1] > 0 and wd.get(slot[0], -1) < slot[1]:
            wd[slot[0]] = slot[1]
            waits.append((slot[0], slot[1]))
        slot[1] += 16
        ticket = (slot[0], slot[1])
        self.ops[q].append((waits, fn, slot[0], 16))
        self.ninstr += 1
        self._mark(ticket, reads, writes)

    def barrier(self):
        allw = []
        for q in self.dpool:
            for si, tgt in self.dpool[q]:
                if tgt > 0:
                    allw.append((si, tgt))
        for e in self.CE:
            if self.cnt[e] > 0:
                allw.append((self.cur[e], self.cnt[e]))
        for eng in self.ops:
            wd = self.waited[eng]
            waits = [(k, v) for k, v in allw if wd.get(k, -1) < v]
            for k, v in waits:
                wd[k] = v
            self.ops[eng].append((waits, None, None, 0))

    def finish(self):
        waits = []
        for q in self.dpool:
            for si, tgt in self.dpool[q]:
                if tgt > 0:
                    waits.append((si, tgt))
        for e in self.CE:
            if self.cnt[e] > 0:
                waits.append((self.cur[e], self.cnt[e]))
        self.ops["sp"].append((waits, None, None, 0))

    def replay(self, eng, e):
        sems = self.sems
        for waits, fn, inc, amt in self.ops[eng]:
            for k, v in waits:
                e.wait_ge(sems[k], v)
            if fn is None:
                continue
            ins = fn(e)
            if inc is not None:
                ins.then_inc(sems[inc], amt)


ARENA_F32 = 53000
SMK_W = 4 * 512 + 4 * 128 + 128 + 512
NSCR = 19


def build_program(cfg=None):
    cfg = cfg or {}
    nc = bass.Bass("TRN2", target_bir_lowering=False)
    es = ExitStack()

    def DI(name, shape, dt=F32):
        return nc.dram_tensor(name, list(shape), dt, kind="ExternalInput").ap()

    def DO(name, shape, dt=F32):
        return nc.dram_tensor(name, list(shape), dt, kind="ExternalOutput").ap()

    xp_d = DI("xp", [4096, 1024])
    xs_d = DI("xs", [128, 1024])
    kvalid_d = DI("kvalid", [128, 32])
    ck_d = DI("ck", [4, 512, 512])
    cv_d = DI("cv", [4, 512, 512])
    s0_d = DI("s0", [4, 4, 128, 128])
    qc0_d = DI("qc0", [128, 12 * 4 * 3])
    fc0_d = DI("fc0", [128, NFF * 4 * 2])
    w_in_d = DI("w_in", [1024, IN_COLS])
    w_out_d = DI("w_out", [1024, 1024])
    w_gu_d = DI("w_gu", [1024, 2 * D_FF])
    w_d_d = DI("w_d", [D_FF, 1024])
    nmpre_d = DI("nmpre", [128, 8])
    nfpre_d = DI("nfpre", [128, 8])
    nmpost_d = DI("nmpost", [128, 1024])
    nfpost_d = DI("nfpost", [128, 1024])
    cw_d = DI("cw", [128, 48])
    fw_d = DI("fw", [128, NFF * 3])
    fb_d = DI("fb", [128, NFF])
    alog_d = DI("alog", [128, 4])
    dtb_d = DI("dtb", [128, 4])
    gnw_d = DI("gnw", [128, 512])
    anw_d = DI("anw", [128, 512])
    btp_d = DI("btp", [128, 5 * 8 * 128])
    btm_d = DI("btm", [128, 5 * 8 * 128])
    bts_d = DI("bts", [128, 5 * 8 * 128])
    cpp_d = DI("cpp", [128, CP_W])
    cps_d = DI("cps", [128, CP_W])
    smk_d = DI("smk", [128, SMK_W])

    y_p = DO("y_p", [2048, 1024])
    bk_p = DO("bk_p", [512, 512])
    bv_p = DO("bv_p", [512, 512])
    S_p = DO("S_p", [128, 512])
    qc_p = DO("qc_p", [128, 36])
    fc_p = DO("fc_p", [128, NFF * 2])
    y_s = DO("y_s", [128, 1024])
    bk_s = DO("bk_s", [128, 512])
    bv_s = DO("bv_s", [128, 512])
    S_s = DO("S_s", [128, 4 * 512])
    qc_s = DO("qc_s", [128, 12 * 4 * 3])
    fc_s = DO("fc_s", [128, NFF * 4 * 2])
    ocat_scr = nc.dram_tensor("ocat_scr", [NSCR * 128, 1024], BF16).ap()
    dbg = {}
    for name, shape in (cfg.get("dbg") or {}).items():
        dbg[name] = DO("dbg_" + name, shape, BF16 if name.startswith(("ocat", "bf_")) else F32)

    S = Sched(nc, es)
    dump_tile = cfg.get("dump_tile", -1)

    def DUMP(name, ap, res, tag):
        if name in dbg and tag == dump_tile:
            S.dma("pool", lambda e: e.dma_start(out=dbg[name][:, :], in_=ap), reads=res)

    arena = es.enter_context(nc.sbuf_tensor("arena", [128, ARENA_F32], F32))
    aptr = [0]

    def SB(name, shape, dt=F32, nslots=1, at=None):
        n = int(np.prod(shape[1:]))
        nf = n if dt == F32 else (n + 1) // 2
        nf = (nf + 1) // 2 * 2
        if at is not None:
            off = at[0]
            at[0] += nf
            assert at[0] <= at[1], f"alias region overflow at {name}"
        else:
            off = aptr[0]
            aptr[0] += nf
        assert aptr[0] <= ARENA_F32, f"SBUF arena overflow at {name}: {aptr[0]}"
        v = arena[:, off:off + nf]
        if dt != F32:
            v = v.bitcast(dt)[:, 0:n]
        if len(shape) == 3:
            v = v.rearrange("p (a b) -> p a b", a=shape[1])
        elif len(shape) == 4:
            v = v.rearrange("p (a b c) -> p a b c", a=shape[1], b=shape[2])
        return Buf(v, name, nslots)

    def flat(ap):
        nd = len(ap.shape)
        if nd == 2:
            return ap
        if nd == 3:
            return ap.rearrange("p a b -> p (a b)")
        return ap.rearrange("p a b c -> p (a b c)")

    def PS(name, shape, dt=F32):
        h = es.enter_context(nc.psum_tensor(name, list(shape), dt))
        return Buf(h, name, 1)

    psT = PS("psT", [128, 1024], BF16)
    psP = [PS("psP0", [128, 512]), PS("psP1", [128, 512])]
    psA = PS("psA", [128, 512])
    psB = PS("psB", [128, 512])
    psS = PS("psS", [128, 512])
    psO = PS("psO", [128, 512])
    psM = PS("psM", [128, 512])
    pctr = [0]

    def next_psP():
        pctr[0] += 1
        return psP[pctr[0] % 2]

    identb = SB("identb", [128, 128], BF16)
    stage = SB("stage", [128, 2, 1024], F32, nslots=2)
    xin = SB("xin", [128, 2, 1024], F32, nslots=2)
    xn = SB("xn", [128, 1024], BF16)
    st = SB("st", [128, 8])
    epsb = SB("epsb", [128, 2])
    shared_end = aptr[0]

    stctr = [0]

    def load_cast_weight(dst_fn, src_ap_fn, ncols, nk, scale_ap_fn, dstbuf, pieces=None, piece_res=None):
        if pieces is None:
            order = [(k, c0, min(ncols, c0 + 1024), None) for k in range(nk) for c0 in range(0, ncols, 1024)]
        else:
            order = [(k, c0, c1, piece_res[pi]) for pi, (c0, c1) in enumerate(pieces) for k in range(nk)]
        for (k, c0, c1, pres) in order:
            if True:
                if pres is not None:
                    dstbuf = pres
                sl = stctr[0] % 2
                stctr[0] += 1
                w = c1 - c0
                S.dma("sp", (lambda e, sl=sl, k=k, c0=c0, c1=c1, w=w: e.dma_start(out=stage[:, sl, 0:w], in_=src_ap_fn(k, c0, c1))),
                      writes=[stage.r[sl]])
                sc = scale_ap_fn(k) if scale_ap_fn else None
                rds = [stage.r[sl]] + ([scale_ap_fn.buf] if scale_ap_fn else [])
                if stctr[0] % 2:
                    if sc is None:
                        S.op("act", (lambda e, sl=sl, k=k, c0=c0, c1=c1, w=w: e.activation(out=dst_fn(k, c0, c1), in_=stage[:, sl, 0:w], func=AF.Copy)),
                             reads=rds, writes=[dstbuf])
                    else:
                        S.op("act", (lambda e, sl=sl, k=k, c0=c0, c1=c1, w=w, sc=sc: e.activation(out=dst_fn(k, c0, c1), in_=stage[:, sl, 0:w], func=AF.Copy, scale=sc)),
                             reads=rds, writes=[dstbuf])
                else:
                    if sc is None:
                        S.op("dve", (lambda e, sl=sl, k=k, c0=c0, c1=c1, w=w: e.tensor_copy(out=dst_fn(k, c0, c1), in_=stage[:, sl, 0:w])),
                             reads=rds, writes=[dstbuf])
                    else:
                        S.op("dve", (lambda e, sl=sl, k=k, c0=c0, c1=c1, w=w, sc=sc: e.tensor_scalar(out=dst_fn(k, c0, c1), in0=stage[:, sl, 0:w], scalar1=sc, scalar2=None, op0=ALU.mult)),
                             reads=rds, writes=[dstbuf])

    def load_small(buf, dram_ap):
        S.dma("sp", (lambda e: e.dma_start(out=flat(buf[:]), in_=dram_ap)), writes=[buf])

    def rms_stats(src_ap, src_res, junk_ap, junk_res, ncols):
        S.op("act", lambda e: e.activation(out=junk_ap, in_=src_ap, func=AF.Square, accum_out=st[:, 0:1]),
             reads=src_res, writes=[st] + junk_res)
        S.op("act", lambda e: e.activation(out=st[:, 1:2], in_=st[:, 0:1], func=AF.Ln, scale=1.0 / ncols, bias=epsb[:, 0:1]),
             reads=[st, epsb], writes=[st])
        S.op("act", lambda e: e.activation(out=st[:, 2:3], in_=st[:, 1:2], func=AF.Exp, scale=-0.5),
             reads=[st], writes=[st])

    S.op("pool", lambda e: e.memset(epsb[:, 0:1], EPS), writes=[epsb])
    S.op("pool", lambda e: e.memset(epsb[:, 1:2], 1.0), writes=[epsb])

    do_p1 = cfg.get("p1", True)
    do_p2 = cfg.get("p2", True)
    NTL1 = cfg.get("ntl1", 2)
    p_tiles = 32
    first_own = 16
    p1_from = cfg.get("p1_from", 0)
    p1_to = cfg.get("p1_to", p_tiles)
    xctr = [0]

    def norm_transpose(src_dram_ap, dstT, col0, junk=None):
        sl = xctr[0] % 2
        xctr[0] += 1
        S.dma("sp", lambda e: e.dma_start(out=xin[:, sl, :], in_=src_dram_ap), writes=[xin.r[sl]])
        rms_stats(xin[:, sl, :], [xin.r[sl]], xn[:], [xn], 1024)
        S.op("dve", lambda e: e.tensor_scalar(out=xn[:], in0=xin[:, sl, :], scalar1=st[:, 2:3], scalar2=None, op0=ALU.mult),
             reads=[xin.r[sl], st], writes=[xn])
        S.op("pe", [(lambda e, k=k: e.transpose(psT[:, k * 128:(k + 1) * 128], xn[:, k * 128:(k + 1) * 128], identb[:])) for k in range(8)],
             reads=[xn, identb], writes=[psT])
        S.op("act", lambda e: e.activation(out=dstT[:, :, col0:col0 + 128], in_=psT[:].rearrange("p (a b) -> p a b", a=8), func=AF.Copy),
             reads=[psT], writes=[dstT])
        return sl

    def proj_fm(wbuf, wcol0, rhsT, NT, ps, nk=8, wres=None):
        S.op("pe", [(lambda e, k=k: e.matmul(ps[:, 0:NT], lhsT=wbuf[:, k, wcol0:wcol0 + 128], rhs=rhsT[:, k, 0:NT], start=(k == 0), stop=(k == nk - 1))) for k in range(nk)],
             reads=[wres if wres is not None else wbuf, rhsT], writes=[ps])

    def proj_tm(wbuf, wcol0, ncols, lhsT_buf, col0, ps, pcol0=0, nk=8):
        S.op("pe", [(lambda e, k=k: e.matmul(ps[:, pcol0:pcol0 + ncols], lhsT=lhsT_buf[:, k, col0:col0 + 128], rhs=wbuf[:, k, wcol0:wcol0 + ncols], start=(k == 0), stop=(k == nk - 1))) for k in range(nk)],
             reads=[wbuf, lhsT_buf], writes=[ps])

    if do_p1:
        aptr[0] = shared_end
        NT1 = NTL1 * 128
        cp = SB("cp", [128, CP_W])
        tri2 = cp[:, CP_TRI:CP_TRI + 128]
        same2 = cp[:, CP_SAME:CP_SAME + 128]
        identf = cp[:, CP_IDENT:CP_IDENT + 128]

        def c4(off):
            return cp[:, off:off + 512].rearrange("p (a b) -> p a b", a=4)

        strict4, incl4, ident4, cm4 = c4(CP_STRICT), c4(CP_INCL), c4(CP_IDENT), c4(CP_CM)
        w_in_bf = SB("w_in_bf", [128, 8, IN_COLS], BF16)
        nmpre = SB("nmpre", [128, 8])
        cw = SB("cw", [128, 12, 4])
        alog = SB("alog", [128, 4])
        nea = SB("nea", [128, 4])
        dtb = SB("dtb", [128, 4])
        gnw4 = SB("gnw4", [128, 4, 128])
        anw8 = SB("anw8", [128, 8, 64])
        kvalid = SB("kvalid", [128, 32])
        BT = SB("BT", [128, 5, 8, 128], BF16)
        uT0 = SB("uT", [128, 8, NT1], BF16)
        xpb = SB("xpb", [128, 2, NT1 + 16], F32, nslots=2)
        cvb = SB("cvb", [128, 4, NT1], F32, nslots=4)
        sqb = SB("sqb", [128, 2, NT1], F32, nslots=2)
        rnb = SB("rnb", [128, 2, NT1], F32, nslots=2)
        gta = SB("gta", [128, 16])
        xpt_r = [Res("xpt0"), Res("xpt1")]
        knT0 = SB("knT", [128, 4, NT1], BF16)
        qnT0 = SB("qnT", [128, 4, NT1], BF16)
        vT0 = SB("vT", [128, 4, NT1], BF16)
        zs0 = SB("zs", [128, NTL1, 512], BF16, nslots=NTL1)
        graw = SB("graw", [128, 2, NTL1, 8], F32, nslots=2 * NTL1)
        kbT = SB("kbT", [128, 4, 8, 128], BF16, nslots=8)
        V1 = SB("V1", [128, 8, 8, 64], BF16, nslots=8)
        vcol = SB("vcol", [128, 8], BF16, nslots=8)
        qbz0 = SB("qbz", [128, 4, 2, NT1], BF16)
        qtail = SB("qtail", [128, 12, 4, 3])
        ones128 = SB("ones128", [128, 128])
        gt = SB("gt", [128, 80])
        Rb = SB("Rb", [128, 4, 128])
        Eb = SB("Eb", [128, 4, 128])
        tmpb = SB("tmpb", [128, 4, 128])
        Qb = [SB("Qb0", [128, 4, 128]), SB("Qb1", [128, 4, 128])]
        Pb = [SB("Pb0", [128, 4, 128]), SB("Pb1", [128, 4, 128])]
        Xb = SB("Xb", [128, 4, 128])
        Xbf = SB("Xbf", [128, 4, 128], BF16)
        qkb = SB("qkb", [128, 4, 128], BF16)
        qkT = SB("qkT", [128, 4, 128], BF16)
        kbg = SB("kbg", [128, 4, 128], BF16)
        kgm = SB("kgm", [128, 4, 4, 128], BF16)
        vbb = SB("vbb", [128, 4, 128], BF16)
        vn32 = SB("vn32", [128, 4, 128])
        vnb = SB("vnb", [128, 4, 128], BF16)
        wTb = SB("wTb", [128, 4, 128], BF16)
        oacc = SB("oacc", [128, 4, 128])
        Sp = SB("Sp", [128, 4, 128])
        Spb = SB("Spb", [128, 4, 128], BF16)
        ureg0 = aptr[0]
        Ss = SB("Ss", [128, 4, 4, 128], F32, nslots=4)
        Ssb = SB("Ssb", [128, 4, 4, 128], BF16, nslots=4)
        smk = SB("smk", [128, SMK_W], BF16)
        ureg = [ureg0, aptr[0]]
        uT1 = SB("uT1", [128, 8, NT1], BF16, at=ureg)
        knT1 = SB("knT1", [128, 4, NT1], BF16, at=ureg)
        qnT1 = SB("qnT1", [128, 4, NT1], BF16, at=ureg)
        vT1 = SB("vT1", [128, 4, NT1], BF16, at=ureg)
        zs1 = SB("zs1", [128, NTL1, 512], BF16, nslots=NTL1, at=ureg)
        qbz1 = SB("qbz1", [128, 4, 2, NT1], BF16, at=ureg)
        uT, knT, qnT, vT, zs, qbz = [uT0, uT1], [knT0, knT1], [qnT0, qnT1], [vT0, vT1], [zs0, zs1], [qbz0, qbz1]
        scb = SB("scb", [128, 4, 128])
        PT = SB("PT", [128, 2, 4, 128], BF16, nslots=2)
        ob32 = SB("ob32", [128, 8, 64])
        ocat = SB("ocat", [128, 2, 1024], BF16, nslots=2)
        kvout = SB("kvout", [128, 1, 512], F32, nslots=1)
        print("phase1 arena used", aptr[0], "of", ARENA_F32)

        class BSet:
            pass
        B1 = BSet()
        B1.Rb, B1.Eb, B1.tmpb, B1.Qb, B1.Pb, B1.Xb, B1.Xbf, B1.kbg, B1.kgm, B1.vbb, B1.vn32, B1.vnb, B1.wTb, B1.gt = (
            Rb, Eb, tmpb, Qb, Pb, Xb, Xbf, kbg, kgm, vbb, vn32, vnb, wTb, gt)
        B1.psA, B1.psB, B1.gcol = psA, psB, 16

        def carve(base_buf, off_f32, name, shape, dt=F32):
            n = int(np.prod(shape[1:]))
            nf = n if dt == F32 else (n + 1) // 2
            v = base_buf["flat32"][:, off_f32:off_f32 + nf]
            if dt != F32:
                v = v.bitcast(dt)[:, 0:n]
            if len(shape) == 3:
                v = v.rearrange("p (a b) -> p a b", a=shape[1])
            elif len(shape) == 4:
                v = v.rearrange("p (a b c) -> p a b c", a=shape[1], b=shape[2])
            return Buf(v, name, 1), off_f32 + nf

        def flat32(buf, nf32):
            a = buf[:]
            a = flat(a)
            if a.dtype != F32:
                a = a.bitcast(F32)
            return {"flat32": a[:, 0:nf32]}
        B2 = BSet()
        rBT = flat32(BT, 2560)
        o = 0
        B2.tmpb, o = carve(rBT, o, "tmpb2", [128, 4, 128])
        B2.Rb = B2.tmpb
        B2.Eb, o = carve(rBT, o, "Eb2", [128, 4, 128])
        q0, o = carve(rBT, o, "Qb20", [128, 4, 128])
        q1, o = carve(rBT, o, "Qb21", [128, 4, 128])
        p0, o = carve(rBT, o, "Pb20", [128, 4, 128])
        B2.Qb = [q0, q1]
        assert o <= 2560
        roc = flat32(ocat, 1024)
        o = 0
        p1, o = carve(roc, o, "Pb21", [128, 4, 128])
        B2.Pb = [p0, p1]
        B2.Xb, o = carve(roc, o, "Xb2", [128, 4, 128])
        rsc = flat32(scb, 512)
        B2.vn32, _ = carve(rsc, 0, "vn322", [128, 4, 128])
        rob = flat32(ob32, 512)
        o = 0
        B2.Xbf, o = carve(rob, o, "Xbf2", [128, 4, 128], BF16)
        B2.kbg, o = carve(rob, o, "kbg2", [128, 4, 128], BF16)
        rpt = flat32(PT, 512)
        o = 0
        B2.kgm, o = carve(rpt, o, "kgm2", [128, 2, 4, 128], BF16)
        roa = flat32(oacc, 512)
        o = 0
        B2.vbb, o = carve(roa, o, "vbb2", [128, 4, 128], BF16)
        B2.vnb, o = carve(roa, o, "vnb2", [128, 4, 128], BF16)
        rqk = flat32(qkb, 256)
        B2.wTb, _ = carve(rqk, 0, "wTb2", [128, 4, 128], BF16)
        rqt = flat32(qkT, 256)
        B2.gt, _ = carve(rqt, 0, "gt2", [128, 80])
        B2.psA, B2.psB, B2.gcol = psS, psO, 40

        load_small(cp, cpp_d[:, :])
        S.op("dve", lambda e: e.tensor_copy(out=identb[:], in_=identf), reads=[cp], writes=[identb])
        load_small(nmpre, nmpre_d[:, :])
        load_small(cw, cw_d[:, :])
        load_small(alog, alog_d[:, :])
        load_small(dtb, dtb_d[:, :])
        load_small(gnw4, gnw_d[:, :])
        load_small(anw8, anw_d[:, :])
        load_small(kvalid, kvalid_d[:, :])
        S.op("act", lambda e: e.activation(out=nea[:], in_=alog[:], func=AF.Exp), reads=[alog], writes=[nea])
        S.op("dve", lambda e: e.tensor_scalar(out=nea[:], in0=nea[:], scalar1=-1.0, scalar2=None, op0=ALU.mult), reads=[nea], writes=[nea])
        S.op("pool", lambda e: e.memset(ones128[:], 1.0), writes=[ones128])
        S.op("pool", lambda e: e.memset(flat(qbz0[:]), 0.0), writes=[qbz0])
        S.op("pool", lambda e: e.memset(flat(qbz1[:]), 0.0), writes=[qbz1])
        S.op("pool", lambda e: e.memset(flat(qtail[:]), 0.0), writes=[qtail])
        S.op("pool", lambda e: e.memset(flat(Sp[:]), 0.0), writes=[Sp])
        S.op("pool", lambda e: e.memset(flat(Spb[:]), 0.0), writes=[Spb])
        S.op("pool", lambda e: e.tensor_copy(out=vcol[:], in_=kvalid[:, 0:8]), reads=[kvalid], writes=[vcol])

        def load_BT(tab_d, mask_d):
            BTf = flat(BT[:])
            for i in range(5):
                S.dma("sp", (lambda e, i=i: e.dma_start(out=stage[:, 0, :], in_=tab_d[:, i * 1024:(i + 1) * 1024])), writes=[stage.r[0]])
                if mask_d is not None:
                    S.dma("sp", (lambda e, i=i: e.dma_start(out=stage[:, 1, :], in_=mask_d[:, i * 1024:(i + 1) * 1024])), writes=[stage.r[1]])
                    S.op("dve", (lambda e, i=i: e.tensor_tensor(out=BTf[:, i * 1024:(i + 1) * 1024], in0=stage[:, 0, :], in1=stage[:, 1, :], op=ALU.add)),
                         reads=[stage], writes=[BT])
                else:
                    S.op("dve", (lambda e, i=i: e.tensor_copy(out=BTf[:, i * 1024:(i + 1) * 1024], in_=stage[:, 0, :])),
                         reads=[stage.r[0]], writes=[BT])

        sfn = lambda k: nmpre[:, k:k + 1]
        sfn.buf = nmpre
        load_cast_weight(lambda k, c0, c1: w_in_bf[:, k, c0:c1],
                         lambda k, c0, c1: w_in_d[k * 128:(k + 1) * 128, c0:c1],
                         IN_COLS, 8, sfn, w_in_bf)

        cvctr = [0]

        def stA(c, nseq, L, NT, xsl, csl, need_conv, par):
            ps = next_psP()
            proj_fm(w_in_bf, c * 128, uT[par], NT, ps)
            xpv = xpb[:, xsl, 0:nseq * (L + 3)].rearrange("p (s t) -> p s t", s=nseq)
            cvv = cvb[:, csl, 0:NT].rearrange("p (s t) -> p s t", s=nseq)
            S.op("pool", lambda e: e.tensor_copy(out=xpv[:, :, 0:3], in_=qtail[:, c, 0:nseq, :]), reads=[qtail], writes=[xpt_r[xsl]])
            S.op("act", lambda e: e.activation(out=xpv[:, :, 3:3 + L], in_=ps[:, 0:NT].rearrange("p (s t) -> p s t", s=nseq), func=AF.Copy),
                 reads=[ps], writes=[xpb.r[xsl]])
            S.op("pool", lambda e: e.tensor_copy(out=qtail[:, c, 0:nseq, :], in_=xpv[:, :, L:L + 3]), reads=[xpb.r[xsl]], writes=[qtail])
            if not need_conv:
                return
            S.op("dve", lambda e: e.tensor_scalar(out=cvv, in0=xpv[:, :, 0:L], scalar1=cw[:, c, 0:1], scalar2=None, op0=ALU.mult),
                 reads=[xpb.r[xsl], xpt_r[xsl], cw], writes=[cvb.r[csl]])
            for i in range(1, 4):
                S.op("dve", (lambda e, i=i: e.scalar_tensor_tensor(out=cvv, in0=xpv[:, :, i:i + L], scalar=cw[:, c, i:i + 1], in1=cvv, op0=ALU.mult, op1=ALU.add)),
                     reads=[xpb.r[xsl], xpt_r[xsl], cw, cvb.r[csl]], writes=[cvb.r[csl]])

        def stB(c, NT, csl, qsl, par):
            sg = sqb[:, qsl, 0:NT]
            cvs = cvb[:, csl, 0:NT]
            S.op("act", lambda e: e.activation(out=sg, in_=cvs, func=AF.Exp, scale=-1.0), reads=[cvb.r[csl]], writes=[sqb.r[qsl]])
            S.op("act", lambda e: e.activation(out=sg, in_=sg, func=AF.Ln, bias=epsb[:, 1:2]), reads=[sqb.r[qsl], epsb], writes=[sqb.r[qsl]])
            S.op("act", lambda e: e.activation(out=sg, in_=sg, func=AF.Exp, scale=-1.0), reads=[sqb.r[qsl]], writes=[sqb.r[qsl]])
            if c >= 8:
                S.op("pool", lambda e: e.tensor_tensor(out=vT[par][:, c - 8, 0:NT], in0=cvs, in1=sg, op=ALU.mult), reads=[cvb.r[csl], sqb.r[qsl]], writes=[vT[par]])
                return
            S.op("pool", lambda e: e.tensor_tensor(out=cvs, in0=cvs, in1=sg, op=ALU.mult), reads=[cvb.r[csl], sqb.r[qsl]], writes=[cvb.r[csl]])
            S.op("pool", lambda e: e.tensor_tensor(out=sg, in0=cvs, in1=cvs, op=ALU.mult), reads=[cvb.r[csl]], writes=[sqb.r[qsl]])

        def stC(c, NT, csl, qsl, par):
            if c >= 8:
                return
            psn = next_psP()
            S.op("pe", lambda e: e.matmul(psn[:, 0:NT], lhsT=ones128[:], rhs=sqb[:, qsl, 0:NT], start=True, stop=True), reads=[ones128, sqb.r[qsl]], writes=[psn])
            S.op("act", lambda e: e.activation(out=rnb[:, qsl, 0:NT], in_=psn[:, 0:NT], func=AF.Ln, bias=epsb[:, 0:1]), reads=[psn, epsb], writes=[rnb.r[qsl]])

        def stD(c, NT, csl, qsl, par):
            if c >= 8:
                return
            dstT, h, scale = (qnT[par], c, 128.0 ** -0.5) if c < 4 else (knT[par], c - 4, 1.0)
            S.op("act", lambda e: e.activation(out=rnb[:, qsl, 0:NT], in_=rnb[:, qsl, 0:NT], func=AF.Exp, scale=-0.5), reads=[rnb.r[qsl]], writes=[rnb.r[qsl]])
            S.op("dve", lambda e: e.scalar_tensor_tensor(out=dstT[:, h, 0:NT], in0=cvb[:, csl, 0:NT], scalar=float(scale), in1=rnb[:, qsl, 0:NT], op0=ALU.mult, op1=ALU.mult),
                 reads=[cvb.r[csl], rnb.r[qsl]], writes=[dstT])

        def bc(ap2, n, w):
            return ap2.unsqueeze(2).to_broadcast([128, n, w])

        NDUM = cfg.get("ndum", 0)

        def pe_keepwarm(n=None):
            for _ in range(NDUM if n is None else n):
                S.ops["pe"].append(((), (lambda e: e.matmul(psM[:, 256:512], lhsT=identb[:], rhs=xn[:, 0:256], start=True, stop=True)), None, 0))

        def gdn_tile(tcol, nb, full, S_in, Sb_in, S_out, Sb_out, zslot, oc_slot, par, tag=-1, B=None):
            tc = slice(tcol, tcol + 128)
            B = B or B1
            GC = B.gcol
            beta, t1, g, Gs, eG, ekg, ckbg, nbeta = (B.gt[:, 0:4], B.gt[:, 4:8], B.gt[:, 8:12], B.gt[:, 12:20], B.gt[:, 20:24], B.gt[:, 24:28], B.gt[:, 28:32], B.gt[:, 32:36])
            dlb = B.gt[:, 36:36 + 4 * nb]
            ssq = B.gt[:, 52:56]
            cog = B.gt[:, 56:60]
            gr = graw[:, par, zslot, :]
            grr = graw.r[par * NTL1 + zslot]
            S.op("act", lambda e: e.activation(out=beta, in_=gr[:, 0:4], func=AF.Exp, scale=-1.0), reads=[grr], writes=[B.gt])
            S.op("dve", lambda e: e.tensor_scalar(out=beta, in0=beta, scalar1=1.0, scalar2=None, op0=ALU.add), reads=[B.gt], writes=[B.gt])
            S.op("dve", lambda e: e.reciprocal(out=beta, in_=beta), reads=[B.gt], writes=[B.gt])
            S.op("dve", lambda e: e.tensor_tensor(out=t1, in0=gr[:, 4:8], in1=dtb[:], op=ALU.add), reads=[grr, dtb], writes=[B.gt])
            S.op("act", lambda e: e.activation(out=t1, in_=t1, func=AF.Exp), reads=[B.gt], writes=[B.gt])
            S.op("act", lambda e: e.activation(out=t1, in_=t1, func=AF.Ln, bias=epsb[:, 1:2]), reads=[B.gt, epsb], writes=[B.gt])
            S.op("dve", lambda e: e.tensor_tensor(out=g, in0=t1, in1=nea[:], op=ALU.mult), reads=[B.gt, nea], writes=[B.gt])
            fns = [lambda e: e.matmul(psM[:, GC:GC + 4], lhsT=tri2, rhs=g, start=True, stop=True),
                   lambda e: e.matmul(psM[:, GC + 4:GC + 8], lhsT=same2, rhs=g, start=True, stop=True)]
            for b in range(nb):
                fns.append(lambda e, b=b: e.matmul(psM[:, GC + 8 + 4 * b:GC + 12 + 4 * b], lhsT=cm4[:, b, :], rhs=g, start=True, stop=True))
            S.op("pe", fns, reads=[cp, B.gt], writes=[psM])
            yield
            S.op("act", lambda e: e.activation(out=Gs, in_=psM[:, GC:GC + 8], func=AF.Copy), reads=[psM], writes=[B.gt])
            S.op("act", lambda e: e.activation(out=dlb, in_=psM[:, GC + 8:GC + 8 + 4 * nb], func=AF.Exp), reads=[psM], writes=[B.gt])
            S.op("dve", lambda e: e.tensor_tensor(out=ekg, in0=Gs[:, 4:8], in1=Gs[:, 0:4], op=ALU.subtract), reads=[B.gt], writes=[B.gt])
            S.op("act", lambda e: e.activation(out=ekg, in_=ekg, func=AF.Exp), reads=[B.gt], writes=[B.gt])
            S.op("act", lambda e: e.activation(out=eG, in_=Gs[:, 0:4], func=AF.Exp), reads=[B.gt], writes=[B.gt])
            S.op("dve", lambda e: e.tensor_tensor(out=ckbg, in0=beta, in1=eG, op=ALU.mult), reads=[B.gt], writes=[B.gt])
            S.op("dve", lambda e: e.tensor_scalar(out=nbeta, in0=beta, scalar1=-1.0, scalar2=None, op0=ALU.mult), reads=[B.gt], writes=[B.gt])
            DUMP("gt", B.gt[:, 0:64], [B.gt], tag)
            DUMP("bf_knT", flat(knT[par][:]), [knT[par]], tag)
            DUMP("bf_vT", flat(vT[par][:]), [vT[par]], tag)
            S.op("pool", lambda e: e.tensor_tensor(out=B.Rb[:], in0=strict4, in1=bc(g, 4, 128), op=ALU.mult), reads=[cp, B.gt], writes=[B.Rb])
            S.op("pe", [(lambda e, h=h: e.matmul(B.psA[:, h * 128:(h + 1) * 128], lhsT=tri2, rhs=B.Rb[:, h, :], start=True, stop=True)) for h in range(4)],
                 reads=[cp, B.Rb], writes=[B.psA])
            yield
            S.op("act", lambda e: e.activation(out=flat(B.Eb[:]), in_=B.psA[:], func=AF.Exp), reads=[B.psA], writes=[B.Eb])
            S.op("pe", [(lambda e, h=h: e.matmul(B.psB[:, h * 128:(h + 1) * 128], lhsT=knT[par][:, h, tc], rhs=knT[par][:, h, tc], start=True, stop=True)) for h in range(4)],
                 reads=[knT[par]], writes=[B.psB])
            yield
            S.op("dve", lambda e: e.tensor_tensor(out=flat(B.tmpb[:]), in0=B.psB[:], in1=flat(B.Eb[:]), op=ALU.mult), reads=[B.psB, B.Eb], writes=[B.tmpb])
            S.op("dve", lambda e: e.tensor_tensor(out=B.tmpb[:], in0=B.tmpb[:], in1=bc(nbeta, 4, 128), op=ALU.mult), reads=[B.tmpb, B.gt], writes=[B.tmpb])
            Q, Q2, P, P2 = B.Qb[0], B.Qb[1], B.Pb[0], B.Pb[1]
            DUMP("E", flat(B.Eb[:]), [B.Eb], tag)
            S.op("dve", lambda e, Q=Q: e.tensor_tensor(out=Q[:], in0=B.tmpb[:], in1=strict4, op=ALU.mult), reads=[B.tmpb, cp], writes=[Q])
            DUMP("Q0", flat(Q[:]), [Q], tag)
            S.op("pe", [(lambda e, h=h, Q=Q: e.transpose(B.psA[:, h * 128:(h + 1) * 128], Q[:, h, :], identf)) for h in range(4)],
                 reads=[Q, cp], writes=[B.psA])
            yield
            S.op("act", lambda e, P=P: e.activation(out=flat(P[:]), in_=B.psA[:], func=AF.Copy), reads=[B.psA], writes=[P])
            if full:
                S.op("pe", [(lambda e, h=h: e.matmul(B.psB[:, h * 128:(h + 1) * 128], lhsT=qnT[par][:, h, tc], rhs=knT[par][:, h, tc], start=True, stop=True)) for h in range(4)],
                     reads=[qnT[par], knT[par]], writes=[B.psB])
                yield
                S.op("dve", lambda e: e.tensor_tensor(out=flat(B.tmpb[:]), in0=B.psB[:], in1=flat(B.Eb[:]), op=ALU.mult), reads=[B.psB, B.Eb], writes=[B.tmpb])
                S.op("pool", lambda e: e.tensor_tensor(out=qkb[:], in0=B.tmpb[:], in1=incl4, op=ALU.mult), reads=[B.tmpb, cp], writes=[qkb])
                S.op("pe", [(lambda e, h=h: e.transpose(psT[:, h * 128:(h + 1) * 128], qkb[:, h, :], identb[:])) for h in range(4)],
                     reads=[qkb, identb], writes=[psT])
                yield
                S.op("act", lambda e: e.activation(out=flat(qkT[:]), in_=psT[:, 0:512], func=AF.Copy), reads=[psT], writes=[qkT])
            DUMP("P0", flat(P[:]), [P], tag)
            S.op("dve", lambda e, P=P: e.tensor_tensor(out=B.Xb[:], in0=P[:], in1=ident4, op=ALU.add), reads=[P, cp], writes=[B.Xb])
            nlev = 5
            for lv in range(nlev):
                S.op("pe", [(lambda e, h=h, P=P, Q=Q: e.matmul(B.psB[:, h * 128:(h + 1) * 128], lhsT=P[:, h, :], rhs=Q[:, h, :], start=True, stop=True)) for h in range(4)],
                     reads=[P, Q], writes=[B.psB])
                if lv < nlev - 1:
                    S.op("pe", [(lambda e, h=h, P=P, Q=Q: e.matmul(B.psA[:, h * 128:(h + 1) * 128], lhsT=Q[:, h, :], rhs=P[:, h, :], start=True, stop=True)) for h in range(4)],
                         reads=[P, Q], writes=[B.psA])
                pe_keepwarm()
                yield
                S.op("act", (lambda e, Q2=Q2: e.activation(out=flat(Q2[:]), in_=B.psB[:], func=AF.Copy)), reads=[B.psB], writes=[Q2])
                if lv < nlev - 1:
                    S.op("dve", (lambda e, P2=P2: e.tensor_copy(out=flat(P2[:]), in_=B.psA[:])), reads=[B.psA], writes=[P2])
                S.op("pe", [(lambda e, h=h, Q2=Q2: e.matmul(B.psB[:, h * 128:(h + 1) * 128], lhsT=Q2[:, h, :], rhs=B.Xb[:, h, :], start=True, stop=True)) for h in range(4)],
                     reads=[Q2, B.Xb], writes=[B.psB])
                pe_keepwarm()
                yield
                S.op("dve", lambda e: e.tensor_tensor(out=flat(B.Xb[:]), in0=flat(B.Xb[:]), in1=B.psB[:], op=ALU.add), reads=[B.Xb, B.psB], writes=[B.Xb])
                if lv < nlev - 1:
                    P, P2 = P2, P
                Q, Q2 = Q2, Q
            DUMP("X", flat(B.Xb[:]), [B.Xb], tag)
            S.op("act", lambda e: e.activation(out=B.Xbf[:], in_=B.Xb[:], func=AF.Copy), reads=[B.Xb], writes=[B.Xbf])
            S.op("pe", [(lambda e, h=h: e.transpose(psT[:, h * 128:(h + 1) * 128], knT[par][:, h, tc], identb[:])) for h in range(4)] +
                 [(lambda e, h=h: e.transpose(psT[:, 512 + h * 128:512 + (h + 1) * 128], vT[par][:, h, tc], identb[:])) for h in range(4)],
                 reads=[knT[par], vT[par], identb], writes=[psT])
            kTv = psT[:, 0:512].rearrange("p (a b) -> p a b", a=4)
            vTv = psT[:, 512:1024].rearrange("p (a b) -> p a b", a=4)
            S.op("dve", lambda e: e.tensor_tensor(out=B.kbg[:], in0=kTv, in1=bc(ckbg, 4, 128), op=ALU.mult), reads=[psT, B.gt], writes=[B.kbg])
            S.op("dve", lambda e: e.tensor_tensor(out=B.vbb[:], in0=vTv, in1=bc(beta, 4, 128), op=ALU.mult), reads=[psT, B.gt], writes=[B.vbb])
            for b in range(nb):
                cof = B.gt[:, 60 + 4 * b:64 + 4 * b]
                S.op("dve", (lambda e, b=b, cof=cof: e.tensor_scalar(out=cof, in0=ekg, scalar1=cp[:, CP_BM + b:CP_BM + b + 1], scalar2=None, op0=ALU.mult)),
                     reads=[B.gt, cp], writes=[B.gt])
                S.op("dve", (lambda e, b=b, cof=cof: e.tensor_tensor(out=B.kgm[:, b], in0=kTv, in1=bc(cof, 4, 128), op=ALU.mult)),
                     reads=[psT, B.gt], writes=[B.kgm])
            yield
            S.op("pe", [(lambda e, h=h: e.matmul(B.psA[:, h * 128:(h + 1) * 128], lhsT=B.Xbf[:, h, :], rhs=B.vbb[:, h, :], start=True, stop=True)) for h in range(4)],
                 reads=[B.Xbf, B.vbb], writes=[B.psA])
            yield
            S.op("act", lambda e: e.activation(out=flat(B.vn32[:]), in_=B.psA[:], func=AF.Copy), reads=[B.psA], writes=[B.vn32])
            S.op("pe", [(lambda e, h=h: e.matmul(B.psB[:, h * 128:(h + 1) * 128], lhsT=B.kbg[:, h, :], rhs=B.Xbf[:, h, :], start=True, stop=True)) for h in range(4)],
                 reads=[B.kbg, B.Xbf], writes=[B.psB])
            yield
            S.op("act", lambda e: e.activation(out=flat(B.wTb[:]), in_=B.psB[:], func=AF.Copy), reads=[B.psB], writes=[B.wTb])
            DUMP("u", flat(B.vn32[:]), [B.vn32], tag)
            DUMP("bf_wT", flat(B.wTb[:]), [B.wTb], tag)
            DUMP("bf_kbg", flat(B.kbg[:]), [B.kbg], tag)
            DUMP("bf_vb", flat(B.vbb[:]), [B.vbb], tag)
            yield "S"
            for b in range(nb):
                Si, Sbi, So, Sbo = S_in(b), Sb_in(b), S_out(b), Sb_out(b)
                S.op("pe", [(lambda e, h=h, Sbi=Sbi: e.matmul(B.psA[:, h * 128:(h + 1) * 128], lhsT=B.wTb[:, h, :], rhs=Sbi[0][:, h, :], start=True, stop=True)) for h in range(4)],
                     reads=[B.wTb, Sbi[1]], writes=[B.psA])
                yield
                S.op("dve", (lambda e, b=b: e.scalar_tensor_tensor(out=flat(B.vn32[:]), in0=B.psA[:], scalar=cp[:, CP_NBM + b:CP_NBM + b + 1],
                                                                    in1=flat(B.vn32[:]), op0=ALU.mult, op1=ALU.add)),
                     reads=[B.psA, cp, B.vn32], writes=[B.vn32])
                S.op("act", lambda e: e.activation(out=B.vnb[:], in_=B.vn32[:], func=AF.Copy), reads=[B.vn32], writes=[B.vnb])
                if full:
                    S.op("dve", (lambda e, b=b: e.tensor_scalar(out=cog, in0=eG, scalar1=cp[:, CP_BM + b:CP_BM + b + 1], scalar2=None, op0=ALU.mult)),
                         reads=[B.gt, cp], writes=[B.gt])
                    S.op("pe", [(lambda e, h=h, Sbi=Sbi: e.matmul(B.psB[:, h * 128:(h + 1) * 128], lhsT=qnT[par][:, h, tc], rhs=Sbi[0][:, h, :], start=True, stop=True)) for h in range(4)],
                         reads=[qnT[par], Sbi[1]], writes=[B.psB])
                    yield
                    psBv = B.psB[:].rearrange("p (a b) -> p a b", a=4)
                    if b == 0:
                        S.op("dve", lambda e: e.tensor_tensor(out=oacc[:], in0=psBv, in1=bc(cog, 4, 128), op=ALU.mult), reads=[B.psB, B.gt], writes=[oacc])
                    else:
                        S.op("dve", lambda e: e.tensor_tensor(out=B.tmpb[:], in0=psBv, in1=bc(cog, 4, 128), op=ALU.mult), reads=[B.psB, B.gt], writes=[B.tmpb])
                        S.op("pool", lambda e: e.tensor_tensor(out=oacc[:], in0=oacc[:], in1=B.tmpb[:], op=ALU.add), reads=[oacc, B.tmpb], writes=[oacc])
                S.op("pe", [(lambda e, h=h, b=b: e.matmul(B.psA[:, h * 128:(h + 1) * 128], lhsT=B.kgm[:, b, h, :], rhs=B.vnb[:, h, :], start=True, stop=True)) for h in range(4)],
                     reads=[B.kgm, B.vnb], writes=[B.psA])
                yield
                for h in range(4):
                    S.op("dve", (lambda e, h=h, b=b, Si=Si, So=So: e.scalar_tensor_tensor(out=So[0][:, h, :], in0=Si[0][:, h, :], scalar=dlb[:, 4 * b + h:4 * b + h + 1],
                                                                                      in1=B.psA[:, h * 128:(h + 1) * 128], op0=ALU.mult, op1=ALU.add)),
                         reads=[Si[1], B.gt, B.psA], writes=[So[1]])
                if Sbo is not None:
                    S.op("act", (lambda e, So=So, Sbo=Sbo: e.activation(out=Sbo[0], in_=So[0], func=AF.Copy)), reads=[So[1]], writes=[Sbo[1]])
            DUMP("vn", flat(B.vn32[:]), [B.vn32], tag)
            DUMP("Safter", flat(S_out(nb - 1)[0]), [S_out(nb - 1)[1]], tag)
            if full:
                S.op("pe", [(lambda e, h=h: e.matmul(B.psB[:, h * 128:(h + 1) * 128], lhsT=qkT[:, h, :], rhs=B.vnb[:, h, :], start=True, stop=True)) for h in range(4)],
                     reads=[qkT, B.vnb], writes=[B.psB])
                yield
                S.op("dve", lambda e: e.tensor_tensor(out=flat(oacc[:]), in0=flat(oacc[:]), in1=B.psB[:], op=ALU.add), reads=[oacc, B.psB], writes=[oacc])
                for h in range(4):
                    S.op("act", (lambda e, h=h: e.activation(out=B.tmpb[:, h, :], in_=oacc[:, h, :], func=AF.Square, accum_out=ssq[:, h:h + 1])),
                         reads=[oacc], writes=[B.tmpb, B.gt])
                S.op("act", lambda e: e.activation(out=ssq, in_=ssq, func=AF.Ln, scale=1.0 / 128, bias=epsb[:, 0:1]), reads=[B.gt, epsb], writes=[B.gt])
                S.op("act", lambda e: e.activation(out=ssq, in_=ssq, func=AF.Exp, scale=-0.5), reads=[B.gt], writes=[B.gt])
                DUMP("o", flat(oacc[:]), [oacc], tag)
                S.op("dve", lambda e: e.tensor_tensor(out=B.tmpb[:], in0=oacc[:], in1=bc(ssq, 4, 128), op=ALU.mult), reads=[oacc, B.gt], writes=[B.tmpb])
                S.op("pool", lambda e: e.tensor_tensor(out=B.tmpb[:], in0=B.tmpb[:], in1=gnw4[:], op=ALU.mult), reads=[B.tmpb, gnw4], writes=[B.tmpb])
                S.op("dve", lambda e: e.tensor_tensor(out=ocat[:, oc_slot, 0:512], in0=flat(B.tmpb[:]), in1=zs[par][:, zslot, :], op=ALU.mult),
                     reads=[B.tmpb, zs[par].r[zslot]], writes=[ocat.r[oc_slot]])

        def attn_tile(pieces, qcol, oc_slot, par):
            qc = slice(qcol, qcol + 128)
            npc = len(pieces)
            first_pv = [True]
            pctr2 = [0]
            for pi, pc in enumerate(pieces):
                if pc.get("prep"):
                    pc["prep"]()
                for hh in range(2):
                    fns = []
                    first = True
                    if pc.get("seq") is not None:
                        s = pc["seq"]
                        fns.append(lambda e, s=s: e.matmul(psS[:], lhsT=smk[:, 2560:2688], rhs=smk[:, s * 512:(s + 1) * 512], start=True, stop=False, skip_group_check=True))
                        first = False
                        if pc.get("newk"):
                            fns.append(lambda e, s=s: e.matmul(psS[:], lhsT=smk[:, 2048 + s * 128:2048 + (s + 1) * 128], rhs=smk[:, 2688:3200], start=False, stop=False, skip_group_check=True))
                    for j in range(4):
                        h = 4 * hh + j
                        fns.append(lambda e, h=h, j=j, pc=pc, first=first: e.matmul(psS[:, j * 128:(j + 1) * 128], lhsT=kbT[:, h // 2, pc["kslot"], :],
                                                                                   rhs=qbz[par][:, h // 2, h % 2, qc], start=first, stop=True, skip_group_check=True))
                    S.op("pe", fns, reads=[kbT.r[pc["kslot"]], qbz[par], smk], writes=[psS])
                    yield
                    S.op("dve", (lambda e, pc=pc, hh=hh: e.tensor_tensor(out=scb[:], in0=psS[:].rearrange("p (a b) -> p a b", a=4), in1=BT[:, pc["r"], 4 * hh:4 * hh + 4, :], op=ALU.add)),
                         reads=[psS, BT], writes=[scb])
                    psl = pctr2[0] % 2
                    pctr2[0] += 1
                    S.op("act", (lambda e, psl=psl: e.activation(out=PT[:, psl], in_=scb[:], func=AF.Exp)), reads=[scb], writes=[PT.r[psl]])
                    fns = []
                    for j in range(4):
                        h = 4 * hh + j
                        stt = first_pv[0]
                        first_pv[0] = False
                        fns.append(lambda e, j=j, h=h, pc=pc, psl=psl, stt=stt: e.matmul(psO[:, h * 64:(h + 1) * 64], lhsT=PT[:, psl, j, :], rhs=V1[:, pc["vslot"], h, :],
                                                                                        start=stt, stop=False, skip_group_check=True))
                        fns.append(lambda e, j=j, h=h, pc=pc, psl=psl, pi=pi: e.matmul(psM[:, 64 + pi * 8 + h:64 + pi * 8 + h + 1], lhsT=PT[:, psl, j, :], rhs=vcol[:, pc["vslot"]:pc["vslot"] + 1],
                                                                                      start=True, stop=True, skip_group_check=True))
                    S.op("pe", fns, reads=[PT.r[psl], V1.r[pc["vslot"]], vcol.r[pc["vslot"]]], writes=[psO, psM])
                    yield
            rden = gta[:, 0:8]
            ss8 = gta[:, 8:16]
            S.op("dve", lambda e: e.tensor_reduce(out=rden, in_=psM[:, 64:64 + npc * 8].rearrange("p (a b) -> p b a", b=8), axis=mybir.AxisListType.X, op=ALU.add),
                 reads=[psM], writes=[gta])
            S.op("dve", lambda e: e.tensor_scalar(out=rden, in0=rden, scalar1=1e-30, scalar2=None, op0=ALU.max), reads=[gta], writes=[gta])
            S.op("dve", lambda e: e.reciprocal(out=rden, in_=rden), reads=[gta], writes=[gta])
            S.op("dve", lambda e: e.tensor_tensor(out=ob32[:], in0=psO[:].rearrange("p (a b) -> p a b", a=8), in1=bc(rden, 8, 64), op=ALU.mult),
                 reads=[psO, gta], writes=[ob32])
            for h in range(8):
                S.op("act", (lambda e, h=h: e.activation(out=scb[:, 0, 0:64], in_=ob32[:, h, :], func=AF.Square, accum_out=ss8[:, h:h + 1])),
                     reads=[ob32], writes=[scb, gta])
            S.op("act", lambda e: e.activation(out=ss8, in_=ss8, func=AF.Ln, scale=1.0 / 64, bias=epsb[:, 0:1]), reads=[gta, epsb], writes=[gta])
            S.op("act", lambda e: e.activation(out=ss8, in_=ss8, func=AF.Exp, scale=-0.5), reads=[gta], writes=[gta])
            S.op("dve", lambda e: e.tensor_tensor(out=ob32[:], in0=ob32[:], in1=bc(ss8, 8, 64), op=ALU.mult), reads=[ob32, gta], writes=[ob32])
            S.op("pool", lambda e: e.tensor_tensor(out=ocat[:, oc_slot, 512:1024].rearrange("p (a b) -> p a b", a=8), in0=ob32[:], in1=anw8[:], op=ALU.mult),
                 reads=[ob32, anw8], writes=[ocat.r[oc_slot]])

        occtr = [0]
        kvctr = [0]

        def run_all(*gens, weights=None):
            pairs = [(g, (weights[i] if weights else 1)) for i, g in enumerate(gens) if g is not None]
            while pairs:
                for (g, w) in list(pairs):
                    for _ in range(w):
                        try:
                            next(g)
                        except StopIteration:
                            pairs.remove((g, w))
                            break

        def front_gen(tls, gq, gkv, full, sample, par):
            ntl = len(tls)
            NT = ntl * 128
            nseq, L = (4, 32) if sample else (1, NT)
            for ti, tl in enumerate(tls):
                src = xs_d[:, :] if sample else xp_d[tl * 128:(tl + 1) * 128, :]
                norm_transpose(src, uT[par], ti * 128)
                yield
            chunks = list(range(12)) if gq else list(range(4, 12))
            nch = len(chunks)
            ok = lambda c: (c >= 4 or full)
            for i in range(nch + 3):
                if i < nch:
                    c = chunks[i]
                    stA(c, nseq, L, NT, i % 2, i % 4, ok(c), par)
                if 0 <= i - 1 < nch and ok(chunks[i - 1]):
                    stB(chunks[i - 1], NT, (i - 1) % 4, (i - 1) % 2, par)
                if 0 <= i - 2 < nch and ok(chunks[i - 2]):
                    stC(chunks[i - 2], NT, (i - 2) % 4, (i - 2) % 2, par)
                if 0 <= i - 3 < nch and ok(chunks[i - 3]):
                    stD(chunks[i - 3], NT, (i - 3) % 4, (i - 3) % 2, par)
                yield
            if full:
                for c in range(4):
                    ps = next_psP()
                    proj_fm(w_in_bf, C_QB + c * 128, uT[par], NT, ps)
                    S.op("act", (lambda e, c=c, ps=ps: e.activation(out=qbz[par][0:64, c, 0, 0:NT], in_=ps[0:64, 0:NT], func=AF.Copy, scale=0.125)), reads=[ps], writes=[qbz[par]])
                    S.op("act", (lambda e, c=c, ps=ps: e.activation(out=qbz[par][64:128, c, 1, 0:NT], in_=ps[64:128, 0:NT], func=AF.Copy, scale=0.125)), reads=[ps], writes=[qbz[par]])
                    yield
            if gkv:
                for c in range(4):
                    ps = next_psP()
                    proj_fm(w_in_bf, C_KB + c * 128, uT[par], NT, ps)
                    for ti, tl in enumerate(tls):
                        slot = tl % 8
                        S.op("act", (lambda e, c=c, ps=ps, ti=ti, slot=slot: e.activation(out=kbT[:, c, slot, :], in_=ps[:, ti * 128:(ti + 1) * 128], func=AF.Copy)),
                             reads=[ps], writes=[kbT.r[slot]])
                    yield
            for ti, tl in enumerate(tls):
                tcol = ti * 128
                if gkv:
                    slot = tl % 8
                    ps = next_psP()
                    proj_tm(w_in_bf, C_VB, 512, uT[par], tcol, ps)
                    S.op("act", (lambda e, ps=ps, slot=slot: e.activation(out=V1[:, slot], in_=ps[:].rearrange("p (a b) -> p a b", a=8), func=AF.Copy)),
                         reads=[ps], writes=[V1.r[slot]])
                    if sample:
                        S.op("pool", (lambda e, slot=slot: e.memset(vcol[:, slot:slot + 1], 1.0)), writes=[vcol.r[slot]])
                    else:
                        S.op("pool", (lambda e, slot=slot, tl=tl: e.tensor_copy(out=vcol[:, slot:slot + 1], in_=kvalid[:, tl:tl + 1])),
                             reads=[kvalid], writes=[vcol.r[slot]])
                    yield
                    want_out = sample or (tl >= p_tiles - 4)
                    if want_out:
                        orow = 0 if sample else (tl - (p_tiles - 4)) * 128
                        kd, vd = (bk_s, bv_s) if sample else (bk_p, bv_p)
                        S.op("act", (lambda e, ps=ps: e.activation(out=kvout[:, 0, :], in_=ps[:], func=AF.Copy)), reads=[ps], writes=[kvout.r[0]])
                        S.dma("pool", (lambda e, vd=vd, orow=orow: e.dma_start(out=vd[orow:orow + 128, :], in_=kvout[:, 0, :])), reads=[kvout.r[0]])
                        ps2 = next_psP()
                        proj_tm(w_in_bf, C_KB, 512, uT[par], tcol, ps2)
                        S.op("act", (lambda e, ps2=ps2: e.activation(out=kvout[:, 0, :], in_=ps2[:], func=AF.Copy)), reads=[ps2], writes=[kvout.r[0]])
                        S.dma("pool", (lambda e, kd=kd, orow=orow: e.dma_start(out=kd[orow:orow + 128, :], in_=kvout[:, 0, :])), reads=[kvout.r[0]])
                        yield
                if full:
                    ps = next_psP()
                    proj_tm(w_in_bf, C_Z, 512, uT[par], tcol, ps)
                    zt = kvout[:, 0, :]
                    S.op("act", (lambda e, ps=ps: e.activation(out=zt, in_=ps[:], func=AF.Exp, scale=-1.0)), reads=[ps], writes=[kvout.r[0]])
                    S.op("act", (lambda e: e.activation(out=zt, in_=zt, func=AF.Ln, bias=epsb[:, 1:2])), reads=[kvout.r[0], epsb], writes=[kvout.r[0]])
                    S.op("act", (lambda e: e.activation(out=zt, in_=zt, func=AF.Exp, scale=-1.0)), reads=[kvout.r[0]], writes=[kvout.r[0]])
                    S.op("dve", (lambda e, ps=ps, ti=ti: e.tensor_tensor(out=zs[par][:, ti, :], in0=ps[:], in1=zt, op=ALU.mult)), reads=[ps, kvout.r[0]], writes=[zs[par].r[ti]])
                    yield
                proj_tm(w_in_bf, C_B, 8, uT[par], tcol, psM, 0)
                S.op("act", (lambda e, ti=ti: e.activation(out=graw[:, par, ti, :], in_=psM[:, 0:8], func=AF.Copy)), reads=[psM], writes=[graw.r[par * NTL1 + ti]])
                yield

        def work_gen(tls, full, sample, par, res):
            for ti, tl in enumerate(tls):
                tcol = ti * 128
                oc_slot = occtr[0] % 2
                if full:
                    occtr[0] += 1
                res["oc_slot"] = oc_slot
                if sample:
                    gens = [gdn_tile(tcol, 4, True,
                                     lambda b: (Ss[:, b], Ss.r[b]), lambda b: (Ssb[:, b], Ssb.r[b]),
                                     lambda b: (Ss[:, b], Ss.r[b]), lambda b: None, ti, oc_slot, par)]
                else:
                    if (not full) and cfg.get("pair_w", True):
                        if ti == 1:
                            continue
                        ga = gdn_tile(0, 2, False, lambda b: (Sp[:], Sp.r[0]), lambda b: (Spb[:], Spb.r[0]),
                                      lambda b: (Sp[:], Sp.r[0]), lambda b: (Spb[:], Spb.r[0]), 0, oc_slot, par, tag=tls[0], B=B1)
                        gb = gdn_tile(128, 2, False, lambda b: (Sp[:], Sp.r[0]), lambda b: (Spb[:], Spb.r[0]),
                                      lambda b: (Sp[:], Sp.r[0]), lambda b: (Spb[:], Spb.r[0]), 1, oc_slot, par, tag=tls[1], B=B2) if len(tls) > 1 else None
                        a_done, b_wait = False, False
                        while ga is not None or gb is not None:
                            if ga is not None:
                                try:
                                    next(ga)
                                except StopIteration:
                                    ga, a_done = None, True
                            if gb is not None and not (b_wait and ga is not None):
                                try:
                                    if next(gb) == "S" and ga is not None:
                                        b_wait = True
                                except StopIteration:
                                    gb = None
                            yield
                        continue
                    gens = [gdn_tile(tcol, 2, full,
                                     lambda b: (Sp[:], Sp.r[0]), lambda b: (Spb[:], Spb.r[0]),
                                     lambda b: (Sp[:], Sp.r[0]), lambda b: (Spb[:], Spb.r[0]), ti, oc_slot, par, tag=tl)]
                    if full:
                        pieces = [dict(kslot=(tl - 4 + r) % 8, vslot=(tl - 4 + r) % 8, r=r) for r in range(5)]
                        gens.append(attn_tile(pieces, tcol, oc_slot, par))
                while gens:
                    for g in list(gens):
                        try:
                            next(g)
                        except StopIteration:
                            gens.remove(g)
                    yield
                if full and not sample:
                    scr_row = tl - (p_tiles - NSCR + 1)
                    if scr_row >= 0:
                        S.dma("pool", (lambda e, oc_slot=oc_slot, scr_row=scr_row: e.dma_start(out=ocat_scr[scr_row * 128:(scr_row + 1) * 128, :], in_=ocat[:, oc_slot, :])),
                              reads=[ocat.r[oc_slot]])
                    if "ocat" in dbg and tl >= first_own:
                        S.dma("pool", (lambda e, oc_slot=oc_slot, tl=tl: e.dma_start(out=dbg["ocat"][(tl - first_own) * 128:(tl - first_own + 1) * 128, :], in_=ocat[:, oc_slot, :])),
                              reads=[ocat.r[oc_slot]])

        first_full_tile = first_own - 2
        first_kv_tile = first_full_tile - 4
        tl = p1_from
        pending = None
        gi = 0
        bt_loaded = [False]
        while tl < p1_to:
            tls = list(range(tl, min(tl + NTL1, p1_to)))
            full = tls[0] >= first_full_tile
            gkv = tls[0] >= first_kv_tile
            gq = tls[0] >= first_full_tile - NTL1
            par = gi % 2
            if full and not bt_loaded[0]:
                run_all(pending)
                pending = None
                S.barrier()
                load_BT(btp_d, btm_d)
                bt_loaded[0] = True
            run_all(front_gen(tls, gq, gkv, full, False, par), pending, weights=cfg.get('w_fp', (2, 1)))
            pending = work_gen(tls, full, False, par, {})
            tl += NTL1
            gi += 1
        run_all(pending)
        S.dma("pool", lambda e: e.dma_start(out=S_p[:, :], in_=flat(Sp[:])), reads=[Sp])
        S.dma("pool", lambda e: e.dma_start(out=qc_p[:, :].rearrange("p (a b) -> p a b", a=12), in_=qtail[:, :, 0, :]), reads=[qtail])

        if cfg.get("sample", True):
            S.barrier()
            load_small(cp, cps_d[:, :])
            load_BT(bts_d, None)
            for i, (a, b) in enumerate(((0, 1024), (1024, 2048), (2048, 3072), (3072, SMK_W))):
                S.dma("sp", (lambda e, i=i, a=a, b=b: e.dma_start(out=stage[:, i % 2, 0:b - a], in_=smk_d[:, a:b])), writes=[stage.r[i % 2]])
                S.op("dve", (lambda e, i=i, a=a, b=b: e.tensor_copy(out=smk[:, a:b], in_=stage[:, i % 2, 0:b - a])), reads=[stage.r[i % 2]], writes=[smk])
            load_small(qtail, qc0_d[:, :])
            for s in range(4):
                S.dma("sp", (lambda e, s=s: e.dma_start(out=Ss[:, s], in_=s0_d[s].rearrange("h k v -> k h v"))), writes=[Ss.r[s]])
                S.op("act", (lambda e, s=s: e.activation(out=Ssb[:, s], in_=Ss[:, s], func=AF.Copy)), reads=[Ss.r[s]], writes=[Ssb.r[s]])
            run_all(front_gen([0], True, True, True, True, 0))
            sres = {}
            run_all(work_gen([0], True, True, 0, sres))
            ocs = sres["oc_slot"]
            S.dma("pool", lambda e: e.dma_start(out=S_s[:, :], in_=flat(Ss[:])), reads=[Ss])
            S.dma("pool", lambda e: e.dma_start(out=qc_s[:, :], in_=flat(qtail[:])), reads=[qtail])

            def mk_prep(s, r, slot, sl):
                def prep():
                    S.dma("sp", (lambda e: e.dma_start(out=stage[:, sl, 0:512], in_=ck_d[s, r * 128:(r + 1) * 128, :])), writes=[stage.r[sl]])
                    S.dma("sp", (lambda e: e.dma_start(out=stage[:, sl, 512:1024], in_=cv_d[s, r * 128:(r + 1) * 128, :])), writes=[stage.r[sl]])
                    S.op("dve", (lambda e: e.tensor_copy(out=xn[:, 0:512], in_=stage[:, sl, 0:512])), reads=[stage.r[sl]], writes=[xn])
                    S.op("pe", [(lambda e, c=c: e.transpose(psT[:, c * 128:(c + 1) * 128], xn[:, c * 128:(c + 1) * 128], identb[:])) for c in range(4)],
                         reads=[xn, identb], writes=[psT])
                    S.op("act", (lambda e: e.activation(out=kbT[:, :, slot, :], in_=psT[:, 0:512].rearrange("p (a b) -> p a b", a=4), func=AF.Copy)),
                         reads=[psT], writes=[kbT.r[slot]])
                    S.op("act", (lambda e: e.activation(out=V1[:, slot], in_=stage[:, sl, 512:1024].rearrange("p (a b) -> p a b", a=8), func=AF.Copy)),
                         reads=[stage.r[sl]], writes=[V1.r[slot]])
                    S.op("pool", (lambda e: e.memset(vcol[:, slot:slot + 1], 1.0)), writes=[vcol.r[slot]])
                return prep

            pieces = []
            for s in range(4):
                for r in range(4):
                    i = s * 4 + r
                    slot = i % 7 + 1
                    pieces.append(dict(kslot=slot, vslot=slot, r=r, seq=s, prep=mk_prep(s, r, slot, i % 2)))
                pieces.append(dict(kslot=0, vslot=0, r=4, seq=s, newk=True))
            run_all(attn_tile(pieces, 0, ocs, 0))
            S.dma("pool", lambda e: e.dma_start(out=ocat_scr[(NSCR - 1) * 128:NSCR * 128, :], in_=ocat[:, ocs, :]), reads=[ocat.r[ocs]])
            if "ocat_s" in dbg:
                S.dma("pool", lambda e: e.dma_start(out=dbg["ocat_s"][:, :], in_=ocat[:, ocs, :]), reads=[ocat.r[ocs]])

    if do_p2:
        S.barrier()
        aptr[0] = shared_end
        wgu_bf = SB("wgu_bf", [128, 8, 2 * D_FF], BF16)
        wd_bf = SB("wd_bf", [128, NFF, 1024], BF16)
        wout_bf = SB("wout_bf", [128, 8, 1024], BF16)
        nfpre = SB("nfpre", [128, 8])
        nmpost = SB("nmpost", [128, 1024])
        nfpost = SB("nfpost", [128, 1024])
        fw = SB("fw", [128, NFF, 3])
        fb = SB("fb", [128, NFF])
        gtail = SB("gtail", [128, NFF, 4, 2])
        oc2 = SB("oc2", [128, 1, 1024], BF16, nslots=1)
        upb = SB("upb", [128, 4, 128], F32, nslots=4)
        oT = SB("oT", [128, 8, 128], BF16)
        x1 = SB("x1", [128, 1024])
        u2T = SB("u2T", [128, 8, 128], BF16)
        gxp = SB("gxp", [128, 4, 144], F32, nslots=4)
        cv2 = SB("cv2", [128, 4, 128], F32, nslots=4)
        t2 = SB("t2", [128, 4, 128], F32, nslots=4)
        sg2 = SB("sg2", [128, 4, 128], F32, nslots=4)
        junk2 = SB("junk2", [128, 512], BF16)
        gxt_r = [Res("gxt%d" % i) for i in range(4)]
        hT = SB("hT", [128, NFF, 128], BF16)
        yb = SB("yb", [128, 1, 1024], F32, nslots=1)
        st2 = SB("st2", [128, 8])
        print("phase2 arena used", aptr[0], "of", ARENA_F32)

        load_small(nfpre, nfpre_d[:, :])
        load_small(nmpost, nmpost_d[:, :])
        load_small(nfpost, nfpost_d[:, :])
        load_small(fw, fw_d[:, :])
        load_small(fb, fb_d[:, :])
        S.op("pool", lambda e: e.memset(flat(gtail[:]), 0.0), writes=[gtail])
        if not do_p1:
            S.dma("sp", lambda e: e.dma_start(out=stage[:, 0, 0:128], in_=cpp_d[:, CP_IDENT:CP_IDENT + 128]), writes=[stage.r[0]])
            S.op("dve", lambda e: e.tensor_copy(out=identb[:], in_=stage[:, 0, 0:128]), reads=[stage.r[0]], writes=[identb])
        load_cast_weight(lambda k, c0, c1: wout_bf[:, k, c0:c1], lambda k, c0, c1: w_out_d[k * 128:(k + 1) * 128, c0:c1], 1024, 8, None, wout_bf)
        sfn2 = lambda k: nfpre[:, k:k + 1]
        sfn2.buf = nfpre
        wgu_pieces = [(0, 1024), (2816, 3840), (1024, 2048), (3840, 4864), (2048, 2816), (4864, 5632)]
        wgu_res = [Res("wgu%d" % i) for i in range(6)]

        def wgu_r(col):
            for (c0, c1), r in zip(wgu_pieces, wgu_res):
                if c0 <= col < c1:
                    return r
            raise AssertionError(col)
        load_cast_weight(lambda k, c0, c1: wgu_bf[:, k, c0:c1], lambda k, c0, c1: w_gu_d[k * 128:(k + 1) * 128, c0:c1], 2 * D_FF, 8, sfn2, wgu_bf,
                         pieces=wgu_pieces, piece_res=wgu_res)
        load_cast_weight(lambda k, c0, c1: wd_bf[:, k, c0:c1], lambda k, c0, c1: w_d_d[k * 128:(k + 1) * 128, c0:c1], 1024, NFF, None, wd_bf)

        o2ctr = [0]
        yctr = [0]
        fctr = [0]

        def rms_finish(ps_list, resid_ap, resid_res, gain, dst_ap, dst_res):
            for i, ps in enumerate(ps_list):
                S.op("act", (lambda e, i=i, ps=ps: e.activation(out=junk2[:], in_=ps[:], func=AF.Square, accum_out=st2[:, i:i + 1])),
                     reads=[ps], writes=[st2, junk2])
            S.op("dve", lambda e: e.tensor_tensor(out=st2[:, 2:3], in0=st2[:, 0:1], in1=st2[:, 1:2], op=ALU.add), reads=[st2], writes=[st2])
            S.op("act", lambda e: e.activation(out=st2[:, 3:4], in_=st2[:, 2:3], func=AF.Ln, scale=1.0 / 1024, bias=epsb[:, 0:1]), reads=[st2, epsb], writes=[st2])
            S.op("act", lambda e: e.activation(out=st2[:, 4:5], in_=st2[:, 3:4], func=AF.Exp, scale=-0.5), reads=[st2], writes=[st2])
            for i, ps in enumerate(ps_list):
                S.op("dve", (lambda e, i=i, ps=ps: e.scalar_tensor_tensor(out=dst_ap[:, i * 512:(i + 1) * 512], in0=ps[:], scalar=st2[:, 4:5], in1=gain[:, i * 512:(i + 1) * 512],
                                                                         op0=ALU.mult, op1=ALU.mult)),
                     reads=[ps, st2, gain], writes=dst_res)
                S.op("pool", (lambda e, i=i: e.tensor_tensor(out=dst_ap[:, i * 512:(i + 1) * 512], in0=dst_ap[:, i * 512:(i + 1) * 512], in1=resid_ap[:, i * 512:(i + 1) * 512], op=ALU.add)),
                     reads=dst_res + resid_res, writes=dst_res)

        yjunk_b = xn
        yjunk = xn[:]

        ring = [psP[0], psP[1], psS, psO, psM]
        rctr = [0]

        def next_ring():
            bnk = ring[rctr[0] % len(ring)]
            rctr[0] += 1
            return bnk

        u2T_b = Buf(stage[:, 1, 0:512].bitcast(BF16).rearrange("p (a b) -> p a b", a=8), "u2T_b")
        u2T_b.r = [stage.r[1]]
        x1s = [(x1[:], x1.r[0]), (stage[:, 0, :], stage.r[0])]
        u2Ts = [u2T, u2T_b]

        def rms_finish_gen(ps_list, resid_ap, resid_res, gain, dst_ap, dst_res):
            for i, ps in enumerate(ps_list):
                S.op("act", (lambda e, i=i, ps=ps: e.activation(out=junk2[:], in_=ps[:], func=AF.Square, accum_out=st2[:, i:i + 1])),
                     reads=[ps], writes=[st2, junk2])
            yield
            S.op("dve", lambda e: e.tensor_tensor(out=st2[:, 2:3], in0=st2[:, 0:1], in1=st2[:, 1:2], op=ALU.add), reads=[st2], writes=[st2])
            yield
            S.op("act", lambda e: e.activation(out=st2[:, 3:4], in_=st2[:, 2:3], func=AF.Ln, scale=1.0 / 1024, bias=epsb[:, 0:1]), reads=[st2, epsb], writes=[st2])
            S.op("act", lambda e: e.activation(out=st2[:, 4:5], in_=st2[:, 3:4], func=AF.Exp, scale=-0.5), reads=[st2], writes=[st2])
            yield
            for i, ps in enumerate(ps_list):
                S.op("dve", (lambda e, i=i, ps=ps: e.scalar_tensor_tensor(out=dst_ap[:, i * 512:(i + 1) * 512], in0=ps[:], scalar=st2[:, 4:5], in1=gain[:, i * 512:(i + 1) * 512],
                                                                         op0=ALU.mult, op1=ALU.mult)),
                     reads=[ps, st2, gain], writes=dst_res)
            yield
            for i, ps in enumerate(ps_list):
                S.op("pool", (lambda e, i=i: e.tensor_tensor(out=dst_ap[:, i * 512:(i + 1) * 512], in0=dst_ap[:, i * 512:(i + 1) * 512], in1=resid_ap[:, i * 512:(i + 1) * 512], op=ALU.add)),
                     reads=dst_res + resid_res, writes=dst_res)
            yield

        def p2_front(scr_row, x_src, slot):
            osl = 0
            S.dma("act", lambda e: e.dma_start(out=oc2[:, osl, :], in_=ocat_scr[scr_row * 128:(scr_row + 1) * 128, :]), writes=[oc2.r[osl]])
            xsl = xctr[0] % 2
            xctr[0] += 1
            S.dma("act", lambda e: e.dma_start(out=xin[:, xsl, :], in_=x_src), writes=[xin.r[xsl]])
            yield
            S.op("pe", [(lambda e, k=k: e.transpose(psT[:, k * 128:(k + 1) * 128], oc2[:, osl, k * 128:(k + 1) * 128], identb[:])) for k in range(8)],
                 reads=[oc2.r[osl], identb], writes=[psT])
            yield
            S.op("act", lambda e: e.activation(out=oT[:], in_=psT[:].rearrange("p (a b) -> p a b", a=8), func=AF.Copy), reads=[psT], writes=[oT])
            yield
            proj_tm(wout_bf, 0, 512, oT, 0, psA)
            proj_tm(wout_bf, 512, 512, oT, 0, psB)
            yield
            x1a, x1r = x1s[slot]
            yield from rms_finish_gen([psA, psB], xin[:, xsl, :], [xin.r[xsl]], nmpost, x1a, [x1r])
            S.op("act", lambda e: e.activation(out=xn[:], in_=x1a, func=AF.Square, accum_out=st[:, 0:1]), reads=[x1r], writes=[st, xn])
            yield
            S.op("act", lambda e: e.activation(out=st[:, 1:2], in_=st[:, 0:1], func=AF.Ln, scale=1.0 / 1024, bias=epsb[:, 0:1]), reads=[st, epsb], writes=[st])
            S.op("act", lambda e: e.activation(out=st[:, 2:3], in_=st[:, 1:2], func=AF.Exp, scale=-0.5), reads=[st], writes=[st])
            yield
            S.op("dve", lambda e: e.tensor_scalar(out=xn[:], in0=x1a, scalar1=st[:, 2:3], scalar2=None, op0=ALU.mult), reads=[x1r, st], writes=[xn])
            yield
            S.op("pe", [(lambda e, k=k: e.transpose(psT[:, k * 128:(k + 1) * 128], xn[:, k * 128:(k + 1) * 128], identb[:])) for k in range(8)],
                 reads=[xn, identb], writes=[psT])
            yield
            uu = u2Ts[slot]
            S.op("act", lambda e: e.activation(out=uu[:], in_=psT[:].rearrange("p (a b) -> p a b", a=8), func=AF.Copy), reads=[psT], writes=[uu])
            yield

        def p2_body(slot, y_dst, nseq, L):
            NT = 128
            uu = u2Ts[slot]
            x1a, x1r = x1s[slot]
            st_ = {}

            def stageA(c):
                psg = next_ring()
                proj_fm(wgu_bf, c * 128, uu, NT, psg, wres=wgu_r(c * 128))
                fsl = c % 4
                gx = gxp[:, fsl, 0:nseq * (L + 2)].rearrange("p (s t) -> p s t", s=nseq)
                cvv = cv2[:, fsl, :].rearrange("p (s t) -> p s t", s=nseq)
                S.op("pool", (lambda e: e.tensor_copy(out=gx[:, :, 0:2], in_=gtail[:, c, 0:nseq, :])), reads=[gtail], writes=[gxt_r[fsl]])
                S.op("act", (lambda e: e.activation(out=gx[:, :, 2:2 + L], in_=psg[:, 0:NT].rearrange("p (s t) -> p s t", s=nseq), func=AF.Copy)),
                     reads=[psg], writes=[gxp.r[fsl]])
                S.op("pool", (lambda e: e.tensor_copy(out=gtail[:, c, 0:nseq, :], in_=gx[:, :, L:L + 2])), reads=[gxp.r[fsl]], writes=[gtail])
                psu = next_ring()
                proj_fm(wgu_bf, D_FF + c * 128, uu, NT, psu, wres=wgu_r(D_FF + c * 128))
                S.op("act", (lambda e: e.activation(out=upb[:, fsl, :], in_=psu[:, 0:NT], func=AF.Copy)), reads=[psu], writes=[upb.r[fsl]])
                S.op("dve", (lambda e: e.tensor_scalar(out=cvv, in0=gx[:, :, 0:L], scalar1=fw[:, c, 0:1], scalar2=fb[:, c:c + 1], op0=ALU.mult, op1=ALU.add)),
                     reads=[gxp.r[fsl], gxt_r[fsl], fw, fb], writes=[cv2.r[fsl]])
                for i in (1, 2):
                    S.op("dve", (lambda e, i=i: e.scalar_tensor_tensor(out=cvv, in0=gx[:, :, i:i + L], scalar=fw[:, c, i:i + 1], in1=cvv, op0=ALU.mult, op1=ALU.add)),
                         reads=[gxp.r[fsl], gxt_r[fsl], fw, cv2.r[fsl]], writes=[cv2.r[fsl]])

            def stageB(c):
                fsl = c % 4
                S.op("act", (lambda e: e.activation(out=t2[:, fsl, :], in_=cv2[:, fsl, :], func=AF.Square, scale=0.044715 ** 0.5)), reads=[cv2.r[fsl]], writes=[t2.r[fsl]])
                S.op("dve", (lambda e: e.scalar_tensor_tensor(out=t2[:, fsl, :], in0=t2[:, fsl, :], scalar=1.0, in1=cv2[:, fsl, :], op0=ALU.add, op1=ALU.mult)),
                     reads=[t2.r[fsl], cv2.r[fsl]], writes=[t2.r[fsl]])

            def stageC(c):
                fsl = c % 4
                S.op("act", (lambda e: e.activation(out=sg2[:, fsl, :], in_=t2[:, fsl, :], func=AF.Sigmoid, scale=1.5957691216057308)), reads=[t2.r[fsl]], writes=[sg2.r[fsl]])
                S.op("pool", (lambda e: e.tensor_tensor(out=sg2[:, fsl, :], in0=sg2[:, fsl, :], in1=cv2[:, fsl, :], op=ALU.mult)),
                     reads=[sg2.r[fsl], cv2.r[fsl]], writes=[sg2.r[fsl]])
                S.op("dve", (lambda e: e.tensor_tensor(out=hT[:, c, :], in0=upb[:, fsl, :], in1=sg2[:, fsl, :], op=ALU.mult)),
                     reads=[upb.r[fsl], sg2.r[fsl]], writes=[hT])

            for i in range(NFF + 2):
                if i < NFF:
                    stageA(i)
                if 0 <= i - 1 < NFF:
                    stageB(i - 1)
                if 0 <= i - 2 < NFF:
                    stageC(i - 2)
                yield

        def p2_finish(slot, y_dst):
            x1a, x1r = x1s[slot]
            S.op("pe", [(lambda e, c=c: e.matmul(psA[:], lhsT=hT[:, c, :], rhs=wd_bf[:, c, 0:512], start=(c == 0), stop=(c == NFF - 1))) for c in range(NFF)],
                 reads=[hT, wd_bf], writes=[psA])
            S.op("pe", [(lambda e, c=c: e.matmul(psB[:], lhsT=hT[:, c, :], rhs=wd_bf[:, c, 512:1024], start=(c == 0), stop=(c == NFF - 1))) for c in range(NFF)],
                 reads=[hT, wd_bf], writes=[psB])
            yield
            yield from rms_finish_gen([psA, psB], x1a, [x1r], nfpost, yb[:, 0, :], [yb.r[0]])
            if y_dst is not None:
                S.dma("pool", lambda e: e.dma_start(out=y_dst, in_=yb[:, 0, :]), reads=[yb.r[0]])

        p2_from = cfg.get("p2_from", first_own - 1)
        p2_to = cfg.get("p2_to", p_tiles)
        jobs = []
        for tl in range(p2_from, p2_to):
            own = tl - first_own
            jobs.append(dict(scr=tl - (p_tiles - NSCR + 1), x=xp_d[tl * 128:(tl + 1) * 128, :],
                             y=(y_p[own * 128:(own + 1) * 128, :] if own >= 0 else None), nseq=1, L=128, sample=False))
        if cfg.get("sample", True):
            jobs.append(dict(scr=NSCR - 1, x=xs_d[:, :], y=y_s[:, :], nseq=4, L=32, sample=True))
        def step(g):
            if g is None:
                return None
            try:
                next(g)
                return g
            except StopIteration:
                return None

        def drain(g):
            while g is not None:
                g = step(g)

        PRE = 2
        drain(p2_front(jobs[0]["scr"], jobs[0]["x"], 0))
        body = p2_body(0, jobs[0]["y"], jobs[0]["nseq"], jobs[0]["L"])
        nsteps_done = 0
        for ji, jb in enumerate(jobs):
            slot = ji % 2
            nxt = jobs[ji + 1] if ji + 1 < len(jobs) else None
            fgen = p2_front(nxt["scr"], nxt["x"], (ji + 1) % 2) if nxt is not None else None
            k = nsteps_done
            while body is not None:
                body = step(body)
                k += 1
                if k >= 4:
                    fgen = step(fgen)
            drain(fgen)
            fin = p2_finish(slot, jb["y"])
            nbody = None
            nsteps_done = 0
            if nxt is not None:
                if nxt["sample"]:
                    S.dma("pool", lambda e: e.dma_start(out=fc_p[:, :].rearrange("p (a b) -> p a b", a=NFF), in_=gtail[:, :, 0, :]), reads=[gtail])
                    load_small(gtail, fc0_d[:, :])
                nbody = p2_body((ji + 1) % 2, nxt["y"], nxt["nseq"], nxt["L"])
                for _ in range(PRE):
                    nbody = step(nbody)
                    nsteps_done += 1
            while fin is not None:
                fin = step(fin)
                if nbody is not None and nsteps_done < 8:
                    nbody = step(nbody)
                    nsteps_done += 1
            body = nbody
        if jobs[-1]["sample"]:
            S.dma("pool", lambda e: e.dma_start(out=fc_s[:, :], in_=flat(gtail[:])), reads=[gtail])
        else:
            S.dma("pool", lambda e: e.dma_start(out=fc_p[:, :].rearrange("p (a b) -> p a b", a=NFF), in_=gtail[:, :, 0, :]), reads=[gtail])

    S.finish()
    with nc.Block() as block:
        @block.tensor
        def _(e):
            S.replay("pe", e)

        @block.vector
        def _(e):
            S.replay("dve", e)

        @block.scalar
        def _(e):
            S.replay("act", e)

        @block.gpsimd
        def _(e):
            S.replay("pool", e)

        @block.sync
        def _(e):
            S.replay("sp", e)
    es.close()
    print("instructions:", S.ninstr, {k: len(v) for k, v in S.ops.items()}, "sems", len(S.sems))
    return nc


def _cpack(bs):
    m = np.arange(128)
    same = (m[:, None] // bs) == (m[None, :] // bs)
    tri = same & (m[:, None] <= m[None, :])
    strict = same & (m[:, None] > m[None, :])
    incl = same & (m[:, None] >= m[None, :])
    ident = np.eye(128, dtype=bool)
    nb = 128 // bs
    bm = np.zeros((128, 4), np.float32)
    cm = np.zeros((128, 4, 128), np.float32)
    for b in range(nb):
        bm[b * bs:(b + 1) * bs, b] = 1.0
        cm[:, b, :] = bm[:, b:b + 1]
    parts = [tri, same, np.tile(strict, (1, 4)), np.tile(incl, (1, 4)), np.tile(ident, (1, 4)), bm, -bm, cm.reshape(128, 512)]
    out = np.concatenate([np.asarray(p, np.float32) for p in parts], axis=1)
    assert out.shape == (128, CP_W)
    return np.ascontiguousarray(out)


def _bias_tables(rel_bias):
    tab = np.asarray(rel_bias, np.float32)
    kj = np.arange(128)[:, None, None]
    r = np.arange(5)[None, :, None]
    qi = np.arange(128)[None, None, :]
    rel_p = (4 - r) * 128 + qi - kj
    idx_p = np.clip(rel_p, -128, 128) + 128
    btp = tab[:, idx_p].transpose(1, 2, 0, 3)
    mask = ((r == 4) & (kj >= 64) & (qi < 64)) | ((r == 0) & (kj < 64) & (qi >= 64))
    btm = np.where(mask, np.float32(NEG), np.float32(0.0)).astype(np.float32)
    btm = np.broadcast_to(btm[:, :, None, :], (128, 5, 8, 128))
    rel_s = np.where(r < 4, (4 - r) * 128 + (qi % 32) - kj, (qi % 32) - (kj % 32))
    idx_s = np.clip(rel_s, -128, 128) + 128
    bts = tab[:, idx_s].transpose(1, 2, 0, 3)
    f = lambda a: np.ascontiguousarray(np.asarray(a, np.float32).reshape(128, 5 * 8 * 128))
    return f(btp), f(btm), f(bts)


def _smk():
    out = np.zeros((128, SMK_W), np.float32)
    q = np.arange(128)
    for s in range(4):
        cm = np.where(q // 32 == s, 0.0, NEG).astype(np.float32)
        out[0, s * 512:(s + 1) * 512] = np.tile(cm, 4)
        out[0, 2048 + s * 128:2048 + (s + 1) * 128] = cm
    out[0, 2560:2688] = 1.0
    out[0, 2688:3200] = 1.0
    return out


def make_in_maps(inp):
    f32 = lambda a: np.ascontiguousarray(np.asarray(a, np.float32))
    xpr, xsm = f32(inp["x_prompt"]), f32(inp["x_sample"])
    ckf = f32(inp["cache_band_k"])[0].reshape(32, 512, 512)
    cvf = f32(inp["cache_band_v"])[0].reshape(32, 512, 512)
    sdl = f32(inp["state_delta"])[0]
    sqc = f32(inp["state_qkv_conv"])[0]
    sfc = f32(inp["state_ffn_conv"])[0]
    rep = lambda v, n: np.ascontiguousarray(np.broadcast_to(f32(v).reshape(1, -1), (128, n)))
    pk = lambda v: np.ascontiguousarray(f32(v).reshape(-1, 128).T)
    btp, btm, bts = _bias_tables(inp["rel_bias"][0])
    common = dict(
        w_in=f32(inp["w_in"][0]), w_out=f32(inp["w_out"][0]), w_gu=f32(inp["w_gate_up"][0]), w_d=f32(inp["w_down"][0]),
        nmpre=pk(inp["norm_mix_pre"][0]), nfpre=pk(inp["norm_ffn_pre"][0]),
        nmpost=rep(inp["norm_mix_post"][0], 1024), nfpost=rep(inp["norm_ffn_post"][0], 1024),
        cw=np.ascontiguousarray(f32(inp["qkv_conv_w"][0]).reshape(4, 12, 128).transpose(2, 1, 0).reshape(128, 48)),
        fw=np.ascontiguousarray(f32(inp["ffn_conv_w"][0]).reshape(3, NFF, 128).transpose(2, 1, 0).reshape(128, NFF * 3)),
        fb=pk(inp["ffn_conv_b"][0]),
        alog=rep(inp["a_log"][0], 4), dtb=rep(inp["dt_bias"][0], 4),
        gnw=rep(np.tile(f32(inp["gdn_norm_w"][0]), 4), 512), anw=rep(np.tile(f32(inp["attn_norm_w"][0]), 8), 512),
        btp=btp, btm=btm, bts=bts, cpp=_cpack(64), cps=_cpack(32), smk=_smk(),
    )
    maps = []
    for c in range(8):
        s, half = c // 2, c % 2
        T0 = half * 2048
        xw = np.zeros((4096, 1024), np.float32)
        if half == 0:
            xw[2048:] = xpr[s, 0:2048]
        else:
            xw[:] = xpr[s]
        pos = T0 - 2048 + np.arange(4096)
        kval = (pos >= 0).astype(np.float32).reshape(32, 128).T
        sq = slice(4 * c, 4 * c + 4)
        m = dict(common)
        m.update(
            xp=xw, xs=np.ascontiguousarray(xsm[sq].reshape(128, 1024)), kvalid=np.ascontiguousarray(kval),
            ck=np.ascontiguousarray(ckf[sq]), cv=np.ascontiguousarray(cvf[sq]), s0=np.ascontiguousarray(sdl[sq]),
            qc0=np.ascontiguousarray(sqc[sq].reshape(4, 3, 12, 128).transpose(3, 2, 0, 1).reshape(128, 144)),
            fc0=np.ascontiguousarray(sfc[sq].reshape(4, 2, NFF, 128).transpose(3, 2, 0, 1).reshape(128, NFF * 8)),
        )
        maps.append(m)
    return maps


def assemble(results):
    y_prompt = np.zeros((4, 4096, 1024), np.float32)
    y_sample = np.zeros((32, 32, 1024), np.float32)
    bkp = np.zeros((1, 4, 512, 8, 64), np.float32)
    bvp = np.zeros((1, 4, 512, 8, 64), np.float32)
    dlp = np.zeros((1, 4, 4, 128, 128), np.float32)
    qcp = np.zeros((1, 4, 3, 1536), np.float32)
    fcp = np.zeros((1, 4, 2, D_FF), np.float32)
    bks = np.zeros((1, 32, 32, 8, 64), np.float32)
    bvs = np.zeros((1, 32, 32, 8, 64), np.float32)
    dls = np.zeros((1, 32, 4, 128, 128), np.float32)
    qcs = np.zeros((1, 32, 3, 1536), np.float32)
    fcs = np.zeros((1, 32, 2, D_FF), np.float32)
    for c, r in enumerate(results):
        s, half = c // 2, c % 2
        y_prompt[s, half * 2048:(half + 1) * 2048] = r["y_p"]
        if half == 1:
            bkp[0, s] = r["bk_p"].reshape(512, 8, 64)
            bvp[0, s] = r["bv_p"].reshape(512, 8, 64)
            dlp[0, s] = r["S_p"].reshape(128, 4, 128).transpose(1, 0, 2)
            qcp[0, s] = r["qc_p"].reshape(128, 12, 3).transpose(2, 1, 0).reshape(3, 1536)
            fcp[0, s] = r["fc_p"].reshape(128, NFF, 2).transpose(2, 1, 0).reshape(2, D_FF)
        sq = slice(4 * c, 4 * c + 4)
        y_sample[sq] = r["y_s"].reshape(4, 32, 1024)
        bks[0, sq] = r["bk_s"].reshape(4, 32, 8, 64)
        bvs[0, sq] = r["bv_s"].reshape(4, 32, 8, 64)
        dls[0, sq] = r["S_s"].reshape(128, 4, 4, 128).transpose(1, 2, 0, 3)
        qcs[0, sq] = r["qc_s"].reshape(128, 12, 4, 3).transpose(2, 3, 1, 0).reshape(4, 3, 1536)
        fcs[0, sq] = r["fc_s"].reshape(128, NFF, 4, 2).transpose(2, 3, 1, 0).reshape(4, 2, D_FF)
    return (y_prompt, y_sample, bkp, bvp, dlp, qcp, fcp, bks, bvs, dls, qcs, fcs)


def kernel(**inputs):
    nc = build_program()
    maps = make_in_maps(inputs)
    res = run_bass_kernel_spmd(nc, maps, core_ids=list(range(8)))
    return assemble(res.results)
```
